# Optimizing a Trainium2 kernel written in Bass

```python
import math
import jax, jax.numpy as jnp
from jax import lax
import numpy as np

D_MODEL = 2048
BATCH = 1
SEQ = 16384
DEPTH = 1

CHUNK = 64
PLE_DIM = 256
D_FF = 5632
DN_HEADS = 16
DN_HEAD_DIM = 128
DN_WIDTH = DN_HEADS * DN_HEAD_DIM
DN_CONV = 4
SB_HEADS = 16
SB_HEAD_DIM = 128
SB_WIDTH = SB_HEADS * SB_HEAD_DIM
SB_BLOCK = 128
IN_WIDTH = 4 * DN_WIDTH + 2 * DN_HEADS + 3 * SB_WIDTH + 2 * D_MODEL
NORM_EPS = 1e-6
L2_EPS = 1e-6

kernel_name = "hybrid_deltanet_stickbreaking_macaron_block"


def _rms_norm(x, g):
    xf = x.astype(jnp.float32)
    y = xf * lax.rsqrt(jnp.mean(xf * xf, axis=-1, keepdims=True) + NORM_EPS)
    return (y * g.astype(jnp.float32)).astype(x.dtype)


def _l2norm(t):
    tf = t.astype(jnp.float32)
    return tf * lax.rsqrt(jnp.sum(tf * tf, axis=-1, keepdims=True) + L2_EPS)


def _swiglu(u, w_gate, w_up, w_down):
    return (jax.nn.silu(u @ w_gate) * (u @ w_up)) @ w_down


def _causal_dwconv(x, w):
    K, C = w.shape
    return lax.conv_general_dilated(
        x, w[:, None, :].astype(x.dtype), window_strides=(1,), padding=[(K - 1, 0)],
        dimension_numbers=("NWC", "WIO", "NWC"), feature_group_count=C)


def _gated_delta_rule(q, k, v, g, beta):
    B, S, H, dk = q.shape
    dv = v.shape[-1]
    C = CHUNK
    N = S // C

    def chunks(t):
        return t.reshape((B, N, C, H) + t.shape[3:]).swapaxes(2, 3)

    q = chunks(q.astype(jnp.float32)) * (dk ** -0.5)
    k = chunks(k.astype(jnp.float32))
    v = chunks(v.astype(jnp.float32))
    g = chunks(g.astype(jnp.float32))
    beta = chunks(beta.astype(jnp.float32))

    gc = jnp.cumsum(g, axis=-1)
    idx = jnp.arange(C)
    lower = idx[:, None] >= idx[None, :]
    strict = idx[:, None] > idx[None, :]
    decay = jnp.exp(jnp.where(lower, gc[..., :, None] - gc[..., None, :], -jnp.inf))

    kb = k * beta[..., None]
    M = jnp.where(strict, jnp.einsum("bnhcd,bnhed->bnhce", kb, k) * decay, 0.0)
    lhs = M + jnp.eye(C, dtype=jnp.float32)
    rhs = jnp.concatenate([v * beta[..., None], kb * jnp.exp(gc)[..., None]], axis=-1)
    sol = lax.linalg.triangular_solve(lhs, rhs, left_side=True, lower=True)
    u, w = sol[..., :dv], sol[..., dv:]

    attn = jnp.einsum("bnhcd,bnhed->bnhce", q, k) * decay
    qg = q * jnp.exp(gc)[..., None]
    kg = k * jnp.exp(gc[..., -1:] - gc)[..., None]
    glast = jnp.exp(gc[..., -1])

    def step(state, xs):
        u_c, w_c, attn_c, qg_c, kg_c, gl_c = xs
        v_new = u_c - jnp.einsum("bhcd,bhde->bhce", w_c, state)
        o = jnp.einsum("bhcd,bhde->bhce", qg_c, state) + jnp.einsum("bhce,bhef->bhcf", attn_c, v_new)
        state = state * gl_c[..., None, None] + jnp.einsum("bhcd,bhce->bhde", kg_c, v_new)
        return state, o

    xs = tuple(jnp.moveaxis(t, 1, 0) for t in (u, w, attn, qg, kg, glast))
    s0 = jnp.zeros((B, H, dk, dv), jnp.float32)
    _, o = lax.scan(step, s0, xs)
    return o.transpose(1, 0, 3, 2, 4).reshape(B, S, H, dv)


def _stick_breaking(q, k, v):
    B, S, H, Dh = q.shape
    nb = S // SB_BLOCK
    scale = Dh ** -0.5
    qb = q.reshape(B, nb, SB_BLOCK, H, Dh).transpose(1, 0, 3, 2, 4)
    kpos = jnp.arange(S)

    def block(args):
        qi, bi = args
        z = jnp.einsum("bhqd,bkhd->bhqk", qi, k, preferred_element_type=jnp.float32) * scale
        qpos = bi * SB_BLOCK + jnp.arange(SB_BLOCK)
        mask = kpos[None, :] < qpos[:, None]
        ls = jax.nn.log_sigmoid(z)
        l1m = jnp.where(mask, ls - z, 0.0)
        between = lax.cumsum(l1m, axis=3, reverse=True) - l1m
        A = jnp.where(mask, jnp.exp(ls + between), 0.0)
        return jnp.einsum("bhqk,bkhd->bqhd", A, v.astype(jnp.float32)).astype(q.dtype)

    out = lax.map(block, (qb, jnp.arange(nb)))
    return out.transpose(1, 0, 2, 3, 4).reshape(B, S, H * Dh)


def _mixer(u, w_in, dn_conv_w, dn_A_log, dn_dt_bias, dn_out_norm, w_branch_dn, w_branch_sb, w_out):
    B, S, _ = u.shape
    proj = u @ w_in
    o1 = 3 * DN_WIDTH
    o2 = o1 + DN_WIDTH
    o3 = o2 + DN_HEADS
    o4 = o3 + DN_HEADS
    o5 = o4 + 3 * SB_WIDTH
    o6 = o5 + D_MODEL
    dn_qkv, z, b, a = proj[..., :o1], proj[..., o1:o2], proj[..., o2:o3], proj[..., o3:o4]
    sb_qkv, gate_dn, gate_sb = proj[..., o4:o5], proj[..., o5:o6], proj[..., o6:]

    dn_qkv = jax.nn.silu(_causal_dwconv(dn_qkv, dn_conv_w))
    dq, dk, dv = jnp.split(dn_qkv, 3, axis=-1)
    dq = _l2norm(dq.reshape(B, S, DN_HEADS, DN_HEAD_DIM))
    dk = _l2norm(dk.reshape(B, S, DN_HEADS, DN_HEAD_DIM))
    dv = dv.reshape(B, S, DN_HEADS, DN_HEAD_DIM)
    beta = jax.nn.sigmoid(b.astype(jnp.float32))
    g = -jnp.exp(dn_A_log.astype(jnp.float32)) * jax.nn.softplus(
        a.astype(jnp.float32) + dn_dt_bias.astype(jnp.float32))
    o_dn = _gated_delta_rule(dq, dk, dv, g, beta)
    zf = z.reshape(B, S, DN_HEADS, DN_HEAD_DIM).astype(jnp.float32)
    o_dn = (_rms_norm(o_dn, dn_out_norm) * jax.nn.silu(zf)).astype(u.dtype).reshape(B, S, DN_WIDTH)

    sq, sk, sv = jnp.split(sb_qkv, 3, axis=-1)
    heads = lambda t: t.reshape(B, S, SB_HEADS, SB_HEAD_DIM)
    o_sb = _stick_breaking(heads(sq), heads(sk), heads(sv))

    merged = (jax.nn.sigmoid(gate_dn) * (o_dn @ w_branch_dn)
              + jax.nn.sigmoid(gate_sb) * (o_sb @ w_branch_sb))
    return merged @ w_out


def setup_inputs(seed: int = 0) -> dict:
    key = jax.random.key(seed)
    ks = jax.random.split(key, 32)
    L, D = DEPTH, D_MODEL

    def w(k, shape, fan_in):
        return jax.random.normal(k, shape, jnp.float32) * (fan_in ** -0.5)

    def gain(k, n):
        return 1.0 + 0.02 * jax.random.normal(k, (L, n), jnp.float32)

    dt = jnp.exp(jax.random.uniform(ks[20], (L, DN_HEADS), jnp.float32,
                                    minval=math.log(1e-3), maxval=math.log(1e-1)))
    return {
        "x": jax.random.normal(ks[0], (BATCH, SEQ, D), jnp.float32),
        "p": jax.random.normal(ks[1], (L, BATCH, SEQ, PLE_DIM), jnp.float32),
        "ffn1_norm_pre": gain(ks[2], D),
        "ffn1_w_gate": w(ks[3], (L, D, D_FF), D),
        "ffn1_w_up": w(ks[4], (L, D, D_FF), D),
        "ffn1_w_down": w(ks[5], (L, D_FF, D), D_FF),
        "ffn1_norm_post": gain(ks[6], D),
        "mix_norm_pre": gain(ks[7], D),
        "w_in": w(ks[8], (L, D, IN_WIDTH), D),
        "dn_conv_w": w(ks[9], (L, DN_CONV, 3 * DN_WIDTH), DN_CONV),
        "dn_A_log": jnp.log(jax.random.uniform(ks[10], (L, DN_HEADS), jnp.float32, minval=1.0, maxval=16.0)),
        "dn_dt_bias": dt + jnp.log(-jnp.expm1(-dt)),
        "dn_out_norm": gain(ks[11], DN_HEAD_DIM),
        "w_branch_dn": w(ks[12], (L, DN_WIDTH, D), DN_WIDTH),
        "w_branch_sb": w(ks[13], (L, SB_WIDTH, D), SB_WIDTH),
        "w_out": w(ks[14], (L, D, D), D),
        "mix_norm_post": gain(ks[15], D),
        "ffn2_norm_pre": gain(ks[16], D),
        "ffn2_w_gate": w(ks[17], (L, D, D_FF), D),
        "ffn2_w_up": w(ks[18], (L, D, D_FF), D),
        "ffn2_w_down": w(ks[19], (L, D_FF, D), D_FF),
        "ffn2_norm_post": gain(ks[21], D),
        "ple_norm_pre": gain(ks[22], D),
        "ple_w_gate": w(ks[23], (L, D, D), D),
        "ple_w_proj": w(ks[24], (L, PLE_DIM, D), PLE_DIM),
        "ple_norm_post": gain(ks[25], D),
    }


def reference(x, p, ffn1_norm_pre, ffn1_w_gate, ffn1_w_up, ffn1_w_down, ffn1_norm_post,
              mix_norm_pre, w_in, dn_conv_w, dn_A_log, dn_dt_bias, dn_out_norm,
              w_branch_dn, w_branch_sb, w_out, mix_norm_post,
              ffn2_norm_pre, ffn2_w_gate, ffn2_w_up, ffn2_w_down, ffn2_norm_post,
              ple_norm_pre, ple_w_gate, ple_w_proj, ple_norm_post):
    h = x
    for i in range(DEPTH):
        f = _swiglu(_rms_norm(h, ffn1_norm_pre[i]), ffn1_w_gate[i], ffn1_w_up[i], ffn1_w_down[i])
        h = h + 0.5 * _rms_norm(f, ffn1_norm_post[i])
        m = _mixer(_rms_norm(h, mix_norm_pre[i]), w_in[i], dn_conv_w[i], dn_A_log[i], dn_dt_bias[i],
                   dn_out_norm[i], w_branch_dn[i], w_branch_sb[i], w_out[i])
        h = h + _rms_norm(m, mix_norm_post[i])
        f = _swiglu(_rms_norm(h, ffn2_norm_pre[i]), ffn2_w_gate[i], ffn2_w_up[i], ffn2_w_down[i])
        h = h + 0.5 * _rms_norm(f, ffn2_norm_post[i])
        gate = jax.nn.sigmoid(_rms_norm(h, ple_norm_pre[i]) @ ple_w_gate[i])
        h = h + _rms_norm(gate * (p[i] @ ple_w_proj[i]), ple_norm_post[i])
    return h
```

```python
import contextlib
import numpy as np
import concourse.bass as bass
import concourse.mybir as mybir
from concourse.bass_utils import run_bass_kernel_spmd

F32 = mybir.dt.float32
BF16 = mybir.dt.bfloat16
AF = mybir.ActivationFunctionType
ALU = mybir.AluOpType

D = 2048
SEQ = 16384
NCORE = 8
TOK = SEQ // NCORE
T = 512
DFF = 5632
PLE = 256
KC = D // 128
EPS = 1e-6


class Tok:
    __slots__ = ("name", "last_w", "readers", "sem", "dma_cnt", "last_dma")

    def __init__(self, name):
        self.name = name
        self.last_w = None
        self.readers = []
        self.sem = None
        self.dma_cnt = 0
        self.last_dma = None


class Op:
    __slots__ = ("eng", "fn", "deps", "dma", "signal", "sem", "count", "tok")

    def __init__(self, eng, fn, dma):
        self.eng = eng
        self.fn = fn
        self.deps = []
        self.dma = dma
        self.signal = dma
        self.sem = None
        self.count = 0
        self.tok = None


ENGS = ("pe", "act", "dve", "pool", "sp")


class Rec:
    def __init__(self, nc, stack):
        self.nc = nc
        self.stack = stack
        self.ops = {e: [] for e in ENGS}
        self.toks = []
        self.dma_toks = []
        self.nsb = 0

    def tok(self, name="t"):
        t = Tok(name)
        self.toks.append(t)
        return t

    def sb(self, shape, dt, name=None):
        self.nsb += 1
        return self.stack.enter_context(self.nc.sbuf_tensor("s_" + (name or f"sb{self.nsb}"), list(shape), dt))

    def ps(self, shape, dt, name=None):
        self.nsb += 1
        return self.stack.enter_context(self.nc.psum_tensor("p_" + (name or f"ps{self.nsb}"), list(shape), dt))

    def add(self, eng, fn, reads=(), writes=(), dma_tok=None):
        op = Op(eng, fn, dma_tok is not None)
        deps = []
        for r in reads:
            if r.last_w is not None:
                deps.append(r.last_w)
        for w in writes:
            if w.last_w is not None:
                deps.append(w.last_w)
            deps.extend(w.readers)
        if dma_tok is not None:
            if dma_tok.sem is None:
                dma_tok.sem = self.stack.enter_context(self.nc.semaphore(f"dq{len(self.dma_toks)}"))
                dma_tok.last_dma = None
                self.dma_toks.append(dma_tok)
            if getattr(dma_tok, "last_dma", None) is not None:
                deps.append(dma_tok.last_dma)
            dma_tok.dma_cnt += 16
            op.sem = dma_tok.sem
            op.count = dma_tok.dma_cnt
            op.tok = dma_tok
            dma_tok.last_dma = op
        seen = set()
        for d in deps:
            if d is op or id(d) in seen:
                continue
            seen.add(id(d))
            if (not d.dma) and d.eng == eng and eng == "pe":
                continue
            op.deps.append(d)
            d.signal = True
        for r in reads:
            r.readers.append(op)
        for w in writes:
            w.last_w = op
            w.readers = []
        self.ops[eng].append(op)
        return op

    def emit(self):
        nc = self.nc
        esem = {}
        for e in ("pe", "act", "dve", "pool"):
            esem[e] = self.stack.enter_context(nc.semaphore(f"es_{e}"))
        for e in ("pe", "act", "dve", "pool", "sp"):
            c = 0
            for op in self.ops[e]:
                if op.dma:
                    continue
                if op.signal:
                    c += 1
                    op.sem = esem.get(e)
                    op.count = c
                    assert e != "sp"
        final = [(t.sem, t.dma_cnt) for t in self.dma_toks]

        def run(e, eng):
            waited = {}
            for op in self.ops[e]:
                for d in op.deps:
                    key = id(d.sem)
                    if waited.get(key, 0) >= d.count:
                        continue
                    eng.wait_ge(d.sem, d.count)
                    waited[key] = d.count
                ins = op.fn(eng)
                if op.signal:
                    ins.then_inc(op.sem, 16 if op.dma else 1)
            if e == "sp":
                for s, c in final:
                    eng.wait_ge(s, c)

        with nc.Block() as block:
            @block.tensor
            def _(eng):
                run("pe", eng)

            @block.scalar
            def _(eng):
                run("act", eng)

            @block.vector
            def _(eng):
                run("dve", eng)

            @block.gpsimd
            def _(eng):
                run("pool", eng)

            @block.sync
            def _(eng):
                run("sp", eng)


class Ctx:
    def __init__(self, nc, stack):
        self.nc = nc
        self.R = Rec(nc, stack)
        R = self.R
        self.banks = [R.ps([128, 512], F32, name=f"bank{i}") for i in range(8)]
        self.bank_tok = [R.tok(f"bank{i}") for i in range(8)]
        self.bank_rr = 0
        self.ones_bf = R.sb([128, 128], BF16, "ones_bf")
        self.t_const = R.tok("const")
        R.add("pool", lambda e: e.memset(self.ones_bf[:], 1.0), writes=[self.t_const])
        self.evac_rr = 0
        self.reserved = set()

    def bank(self):
        while True:
            i = self.bank_rr % 8
            self.bank_rr += 1
            if id(self.banks[i]) not in self.reserved:
                return self.banks[i], self.bank_tok[i]


def load_vec_fm(cx, dram_vec, name):
    R = cx.R
    t = R.sb([128, dram_vec.shape[1]], F32, name)
    tk = R.tok(name)
    R.add("sp", lambda e: e.dma_start(out=t[:], in_=dram_vec[:, :]), writes=[tk], dma_tok=tk)
    return t, tk


def rms_fm(cx, src, t_src, nch, gain, t_gain, dst, t_dst, sq, t_sq, rstd, t_rstd, dim, post=None):
    R = cx.R
    R.add("act", lambda e: e.activation(out=sq[:, 0:nch, :], in_=src[:, 0:nch, :], func=AF.Square),
          reads=[t_src], writes=[t_sq])
    bk, tb = cx.bank()
    for c in range(nch):
        R.add("pe", lambda e, c=c: e.matmul(bk[:, :], lhsT=cx.ones_bf[:], rhs=sq[:, c, :],
                                             start=(c == 0), stop=(c == nch - 1)),
              reads=[t_sq, cx.t_const], writes=[tb])
    R.add("dve", lambda e: e.tensor_scalar(out=rstd[:], in0=bk[:, :], scalar1=1.0 / dim, scalar2=EPS,
                                           op0=ALU.mult, op1=ALU.add), reads=[tb], writes=[t_rstd])
    R.add("act", lambda e: e.activation(out=rstd[:], in_=rstd[:], func=AF.Sqrt), reads=[t_rstd], writes=[t_rstd])
    R.add("dve", lambda e: e.reciprocal(out=rstd[:], in_=rstd[:]), reads=[t_rstd], writes=[t_rstd])
    if post is None:
        for c in range(nch):
            R.add("dve", lambda e, c=c: e.scalar_tensor_tensor(out=dst[:, c, :], in0=src[:, c, :],
                                                                scalar=gain[:, c:c + 1], in1=rstd[:],
                                                                op0=ALU.mult, op1=ALU.mult),
                  reads=[t_src, t_gain, t_rstd], writes=[t_dst])
    else:
        post()


def transpose_in(cx, x_dram, r0, xtok, t_xtok, xT, t_xT, ident, ncol_chunks, alias=()):
    R = cx.R
    W = ncol_chunks * 128
    alias = list(alias)
    for s in range(T // 128):
        R.add("sp", lambda e, s=s: e.dma_start(out=xtok[:, s, 0:W], in_=x_dram[r0 + s * 128:r0 + (s + 1) * 128, :]),
              writes=[t_xtok[s]] + alias, dma_tok=t_xtok[s])
    for c in range(ncol_chunks):
        bk, tb = cx.bank()
        for s in range(T // 128):
            R.add("pe", lambda e, c=c, s=s, bk=bk: e.transpose(out=bk[:, s * 128:(s + 1) * 128],
                                                                in_=xtok[:, s, c * 128:(c + 1) * 128], identity=ident[:]),
                  reads=[t_xtok[s], cx.t_const] + alias, writes=[tb])
        evac(cx, xT[:, c, :], bk[:, :], [tb], [t_xT])


def evac(cx, dst, src, reads, writes):
    R = cx.R
    cx.evac_rr += 1
    if cx.evac_rr % 2 == 0:
        R.add("act", lambda e: e.copy(out=dst, in_=src), reads=reads, writes=writes)
    else:
        R.add("dve", lambda e: e.tensor_copy(out=dst, in_=src), reads=reads, writes=writes)


def transpose_out(cx, srcT, t_src, nch, out_dram, r0, otok, t_otok, ident):
    R = cx.R
    for s in range(T // 128):
        for c4 in range(0, nch, 4):
            bk, tb = cx.bank()
            for j in range(4):
                c = c4 + j
                R.add("pe", lambda e, c=c, s=s, j=j, bk=bk: e.transpose(out=bk[:, j * 128:(j + 1) * 128],
                                                                   in_=srcT[:, c, s * 128:(s + 1) * 128],
                                                                   identity=ident[:]),
                      reads=[t_src, cx.t_const], writes=[tb])
            evac(cx, otok[:, s, c4 * 128:(c4 + 4) * 128], bk[:, :], [tb], [t_otok[s]])
        R.add("sp", lambda e, s=s: e.dma_start(out=out_dram[r0 + s * 128:r0 + (s + 1) * 128, :],
                                               in_=otok[:, s, 0:nch * 128]),
              reads=[t_otok[s]], dma_tok=t_otok[s])


class WStream:
    def __init__(self, cx, kch_max, nbuf, name):
        R = cx.R
        self.cx = cx
        self.bufs = [R.sb([128, kch_max, 512], BF16, f"{name}{i}") for i in range(nbuf)]
        self.toks = [R.tok(f"{name}{i}") for i in range(nbuf)]
        self.rr = 0

    def load(self, w_dram, k0, kch, c0, ncols):
        R = self.cx.R
        i = self.rr % len(self.bufs)
        self.rr += 1
        buf, tk = self.bufs[i], self.toks[i]
        src = w_dram[k0 * 128:(k0 + kch) * 128, c0:c0 + ncols].rearrange("(k p) n -> p k n", p=128)
        R.add("pool", lambda e: e.dma_start(out=buf[:, 0:kch, 0:ncols], in_=src), writes=[tk], dma_tok=tk)
        return buf, tk


def ffn_fm(cx, uT, t_u, wg, wu, wd, actT, t_act, ws, fT, t_f, tmp, t_tmp):
    R = cx.R
    NJ = DFF // 128
    for j4 in range(0, NJ, 4):
        gw, tg = ws.load(wg, 0, KC, j4 * 128, 512)
        uw, tu = ws.load(wu, 0, KC, j4 * 128, 512)
        for j in range(4):
            bg, tbg = cx.bank()
            bu, tbu = cx.bank()
            for k in range(KC):
                R.add("pe", lambda e, k=k, j=j, bg=bg, gw=gw: e.matmul(bg[:, :], lhsT=gw[:, k, j * 128:(j + 1) * 128],
                                                                        rhs=uT[:, k, :], start=(k == 0), stop=(k == KC - 1)),
                      reads=[tg, t_u], writes=[tbg])
            for k in range(KC):
                R.add("pe", lambda e, k=k, j=j, bu=bu, uw=uw: e.matmul(bu[:, :], lhsT=uw[:, k, j * 128:(j + 1) * 128],
                                                                        rhs=uT[:, k, :], start=(k == 0), stop=(k == KC - 1)),
                      reads=[tu, t_u], writes=[tbu])
            jj = j4 + j
            tt = t_tmp[jj % 2]
            tm = tmp[jj % 2]
            R.add("act", lambda e, bg=bg, tm=tm: e.activation(out=tm[:], in_=bg[:, :], func=AF.Silu),
                  reads=[tbg], writes=[tt])
            R.add("dve", lambda e, bu=bu, tm=tm, jj=jj: e.tensor_tensor(out=actT[:, jj, :], in0=tm[:], in1=bu[:, :],
                                                                         op=ALU.mult),
                  reads=[tt, tbu], writes=[t_act])
    for dg in range(4):
        bks = [cx.bank() for _ in range(4)]
        for q in range(4):
            dwt, tdw = ws.load(wd, q * 11, 11, dg * 512, 512)
            for kk in range(11):
                k = q * 11 + kk
                for dd in range(4):
                    bk, tb = bks[dd]
                    R.add("pe", lambda e, kk=kk, k=k, dd=dd, bk=bk, dwt=dwt: e.matmul(
                        bk[:, :], lhsT=dwt[:, kk, dd * 128:(dd + 1) * 128], rhs=actT[:, k, :],
                        start=(k == 0), stop=(k == NJ - 1)), reads=[tdw, t_act], writes=[tb])
        for dd in range(4):
            bk, tb = bks[dd]
            evac(cx, fT[:, dg * 4 + dd, :], bk[:, :], [tb], [t_f])


def resid_norm_add(cx, fT, t_f, gain, t_gain, hT, t_h, sq, t_sq, rstd, t_rstd, coef):
    R = cx.R

    def post():
        for c in range(KC):
            R.add("dve", lambda e, c=c: e.scalar_tensor_tensor(out=fT[:, c, :], in0=fT[:, c, :],
                                                                scalar=gain[:, c:c + 1], in1=rstd[:],
                                                                op0=ALU.mult, op1=ALU.mult),
                  reads=[t_f, t_gain, t_rstd], writes=[t_f])
            R.add("dve", lambda e, c=c: e.scalar_tensor_tensor(out=hT[:, c, :], in0=fT[:, c, :],
                                                                scalar=float(coef), in1=hT[:, c, :],
                                                                op0=ALU.mult, op1=ALU.add),
                  reads=[t_f, t_h], writes=[t_h])

    rms_fm(cx, fT, t_f, KC, gain, t_gain, None, None, sq, t_sq, rstd, t_rstd, D, post=post)


def build_stage1(ntiles=TOK // T):
    nc = bass.Bass("TRN2", target_bir_lowering=False)
    x = nc.dram_tensor("x", [TOK, D], F32, kind="ExternalInput").ap()
    wg = nc.dram_tensor("wg", [D, DFF], F32, kind="ExternalInput").ap()
    wu = nc.dram_tensor("wu", [D, DFF], F32, kind="ExternalInput").ap()
    wd = nc.dram_tensor("wd", [DFF, D], F32, kind="ExternalInput").ap()
    gains = nc.dram_tensor("gains", [128, 3 * KC], F32, kind="ExternalInput").ap()
    ident_d = nc.dram_tensor("ident", [128, 128], F32, kind="ExternalInput").ap()
    h1T = nc.dram_tensor("h1T", [D, TOK], F32, kind="ExternalOutput").ap()
    uT_o = nc.dram_tensor("uT", [D, TOK], BF16, kind="ExternalOutput").ap()
    with contextlib.ExitStack() as stack:
        cx = Ctx(nc, stack)
        R = cx.R
        g_all, t_g = load_vec_fm(cx, gains, "gains_sb")
        ident = R.sb([128, 128], F32, "ident")
        R.add("sp", lambda e: e.dma_start(out=ident[:], in_=ident_d[:, :]), writes=[cx.t_const], dma_tok=cx.t_const)
        big = R.sb([128, KC * T], F32, "big")
        xtok = big[:].rearrange("p (s d) -> p s d", s=4)
        fT = big[:].rearrange("p (c t) -> p c t", c=KC)
        t_xtok = [R.tok(f"xtok{s}") for s in range(4)]
        t_f = R.tok("fT")
        hT = R.sb([128, KC, T], F32, "hT")
        t_h = R.tok("hT")
        uT = R.sb([128, KC, T], BF16, "uT")
        t_u = R.tok("uT")
        sq = R.sb([128, KC, T], BF16, "sq")
        t_sq = R.tok("sq")
        rstd = R.sb([128, T], F32, "rstd")
        t_rstd = R.tok("rstd")
        actT = R.sb([128, DFF // 128, T], BF16, "actT")
        t_act = R.tok("actT")
        tmp = [R.sb([128, T], F32, f"tmp{i}") for i in range(2)]
        t_tmp = [R.tok(f"tmp{i}") for i in range(2)]
        ws = WStream(cx, KC, 3, "w")
        h1v = h1T.rearrange("(c p) t -> p c t", p=128)
        uv = uT_o.rearrange("(c p) t -> p c t", p=128)
        for it in range(ntiles):
            r0 = it * T
            transpose_in(cx, x, r0, xtok, t_xtok, hT, t_h, ident, KC, alias=[t_f])
            rms_fm(cx, hT, t_h, KC, g_all[:, 0:KC], t_g, uT, t_u, sq, t_sq, rstd, t_rstd, D)
            ffn_fm(cx, uT, t_u, wg, wu, wd, actT, t_act, ws, fT, t_f, tmp, t_tmp)
            resid_norm_add(cx, fT, t_f, g_all[:, KC:2 * KC], t_g, hT, t_h, sq, t_sq, rstd, t_rstd, 0.5)
            R.add("sp", lambda e, r0=r0: e.dma_start(out=h1v[:, :, r0:r0 + T], in_=hT[:]), reads=[t_h], dma_tok=t_h)
            rms_fm(cx, hT, t_h, KC, g_all[:, 2 * KC:3 * KC], t_g, uT, t_u, sq, t_sq, rstd, t_rstd, D)
            R.add("sp", lambda e, r0=r0: e.dma_start(out=uv[:, :, r0:r0 + T], in_=uT[:]), reads=[t_u], dma_tok=t_u)
        R.emit()
    return nc


def fm_vec(v):
    return np.ascontiguousarray(np.asarray(v, np.float32).reshape(-1, 128).T)


HD = 128
CH = 64
NCHK = T // CH
C_ID, C_U, C_SL, C_BU, C_BL = 0, 128, 192, 256, 320


def dn_consts():
    c = np.zeros((128, 384), np.float32)
    c[:, 0:128] = np.eye(128, dtype=np.float32)
    p = np.arange(64)[:, None]
    f = np.arange(64)[None, :]
    c[0:64, C_U:C_U + 64] = (p <= f)
    c[0:64, C_SL:C_SL + 64] = (f < p)
    c[0:64, C_BU:C_BU + 64] = 1e4 * (f > p)
    c[0:64, C_BL:C_BL + 64] = 1e4 * (f < p)
    return c


def bc_mid(ap2, n):
    return ap2.unsqueeze(1).to_broadcast([ap2.shape[0], n, ap2.shape[1]])


def bc_last(ap2, n):
    return ap2.unsqueeze(2).to_broadcast([ap2.shape[0], ap2.shape[1], n])


def build_stage2_dn(ntiles=SEQ // T):
    S = ntiles * T
    nc = bass.Bass("TRN2", target_bir_lowering=False)
    uT_all = nc.dram_tensor("uT_all", [D, S], BF16, kind="ExternalInput").ap()
    wdn = nc.dram_tensor("wdn", [D, 1024], F32, kind="ExternalInput").ap()
    wba = nc.dram_tensor("wba", [D, 4], F32, kind="ExternalInput").ap()
    convw_d = nc.dram_tensor("convw", [128, 24], F32, kind="ExternalInput").ap()
    gon_d = nc.dram_tensor("gon", [128, 1], F32, kind="ExternalInput").ap()
    hv_d = nc.dram_tensor("hv", [64, 4], F32, kind="ExternalInput").ap()
    consts_d = nc.dram_tensor("consts", [128, 384], F32, kind="ExternalInput").ap()
    odnT = nc.dram_tensor("odnT", [256, S], BF16, kind="ExternalOutput").ap()
    with contextlib.ExitStack() as stack:
        cx = Ctx(nc, stack)
        R = cx.R
        A = R.add
        tc = cx.t_const
        consts = R.sb([128, 384], F32, "consts")
        A("sp", lambda e: e.dma_start(out=consts[:], in_=consts_d[:, :]), writes=[tc], dma_tok=tc)
        ident = consts[:, 0:128]
        id64 = consts[0:64, 0:64]
        Um = consts[0:64, C_U:C_U + 64]
        SLm = consts[0:64, C_SL:C_SL + 64]
        BUm = consts[0:64, C_BU:C_BU + 64]
        BLm = consts[0:64, C_BL:C_BL + 64]
        convw, t_cw = load_vec_fm(cx, convw_d, "convw_sb")
        gon, t_gon = load_vec_fm(cx, gon_d, "gon_sb")
        hv = R.sb([64, 4], F32, "hv")
        t_hv = R.tok("hv")
        A("sp", lambda e: e.dma_start(out=hv[:], in_=hv_d[:, :]), writes=[t_hv], dma_tok=t_hv)
        ones_f = R.sb([64, 128], F32, "ones_f")
        A("pool", lambda e: e.memset(ones_f[:], 1.0), writes=[tc])
        expA = R.sb([64, 2], F32, "expA")
        A("act", lambda e: e.activation(out=expA[:], in_=hv[:, 2:4], func=AF.Exp), reads=[t_hv], writes=[t_hv])
        wres = R.sb([128, KC, 1024], BF16, "wres")
        t_w = R.tok("wres")
        wv = wdn.rearrange("(k p) n -> p k n", p=128)
        for half in range(2):
            A("pool", lambda e, half=half: e.dma_start(out=wres[:, :, half * 512:(half + 1) * 512],
                                                      in_=wv[:, :, half * 512:(half + 1) * 512]),
              writes=[t_w], dma_tok=t_w)
        wbar = R.sb([128, KC, 4], BF16, "wbar")
        t_wb = R.tok("wbar")
        A("pool", lambda e: e.dma_start(out=wbar[:], in_=wba.rearrange("(k p) n -> p k n", p=128)),
          writes=[t_wb], dma_tok=t_wb)
        u_t = [R.sb([128, KC, T], BF16, f"u_t{i}") for i in range(2)]
        t_ut = [R.tok(f"u_t{i}") for i in range(2)]
        rawbuf = R.sb([128, 6, T + 3], F32, "rawbuf")
        t_raw = [R.tok(f"raw{c}") for c in range(6)]
        A("pool", lambda e: e.memset(rawbuf[:, :, 0:3], 0.0), writes=t_raw)
        cv = R.sb([128, 6, T], F32, "cv")
        t_cv = [R.tok(f"cv{c}") for c in range(6)]
        qk = R.sb([128, 4, T], F32, "qk")
        t_qk = [R.tok(f"qk{c}") for c in range(4)]
        siluz = R.sb([128, 2, T], F32, "siluz")
        t_sz = [R.tok(f"sz{c}") for c in range(2)]
        sqb = R.sb([128, T], BF16, "sqb")
        t_sqb = R.tok("sqb")
        rstd = R.sb([128, T], F32, "rstd")
        t_rstd = R.tok("rstd")
        ba = R.sb([64, NCHK, 4], F32, "ba")
        t_ba = R.tok("ba")
        beta = R.sb([64, NCHK, 2], F32, "beta")
        t_beta = R.tok("beta")
        gt = R.sb([64, NCHK, 2], F32, "gt")
        t_g = R.tok("gt")
        spx = R.sb([64, NCHK, 2], F32, "spx")
        t_spx = R.tok("spx")

        def sbt(shape, name, dt=F32):
            return R.sb(shape, dt, name), R.tok(name)

        gc_col, t_gcc = sbt([64, NCHK], "gc_col")
        egc, t_egc = sbt([64, NCHK], "egc")
        be, t_be = sbt([64, NCHK], "be")
        kgs, t_kgs = sbt([64, NCHK], "kgs")
        Ug, t_Ug = sbt([64, NCHK, 64], "Ug")
        gcrow, t_gcr = sbt([128, NCHK, 64], "gcrow")
        egrow, t_egr = sbt([128, NCHK, 64], "egrow")
        dl, t_dl = sbt([64, NCHK, 64], "dl")
        tmpd, t_tmpd = sbt([64, NCHK, 64], "tmpd")
        decay, t_dec = sbt([64, NCHK, 64], "decay")
        decayT, t_decT = sbt([64, NCHK, 64], "decayT")
        Ktok, t_Kt = sbt([64, NCHK, 128], "Ktok")
        Vtok, t_Vt = sbt([64, NCHK, 128], "Vtok")
        Pb = [sbt([64, NCHK, 64], f"P{i}") for i in range(2)]
        Ptb = [sbt([64, NCHK, 64], f"Pt{i}") for i in range(2)]
        At, t_At = sbt([64, NCHK, 64], "At")
        Vb, t_Vb = sbt([64, NCHK, 128], "Vb")
        Kbg, t_Kbg = sbt([64, NCHK, 128], "Kbg")
        u_sb, t_usb = sbt([64, NCHK, 128], "u_sb")
        wT, t_wT = sbt([128, NCHK, 64], "wT")
        attnT, t_att = sbt([64, NCHK, 64], "attnT")
        qgT, t_qg = sbt([128, NCHK, 64], "qgT")
        kg, t_kg = sbt([64, NCHK, 128], "kg")
        Sst = [[sbt([128, 128], f"S{h}_{i}") for i in range(2)] for h in range(2)]
        for h in range(2):
            A("pool", lambda e, h=h: e.memset(Sst[h][0][0][:], 0.0), writes=[Sst[h][0][1]])
        s_par = [0, 0]
        vnew = [sbt([64, 128], f"vnew{i}") for i in range(2)]
        vn_rr = [0]
        o_sb, t_osb = sbt([128, T], "o_sb")
        o_tmp, t_otmp = sbt([128, T], "o_tmp")
        outb = [sbt([128, 2, T], f"outb{i}", BF16) for i in range(2)]
        ov = odnT.rearrange("(h p) t -> p h t", p=128)

        def bank_mm(nparts, items, reads, name=None):
            bk, tb = cx.bank()
            for (c0, ncol, lh, rh) in items:
                A("pe", lambda e, bk=bk, c0=c0, ncol=ncol, lh=lh, rh=rh: e.matmul(
                    bk[0:nparts, c0:c0 + ncol], lhsT=lh, rhs=rh, start=True, stop=True), reads=reads, writes=[tb])
            return bk, tb

        for it in range(ntiles):
            t0 = it * T
            ut, t_u = u_t[it % 2], t_ut[it % 2]
            A("sp", lambda e, ut=ut, t0=t0: e.dma_start(out=ut[:], in_=uT_all[:, t0:t0 + T].rearrange("(k p) t -> p k t", p=128)),
              writes=[t_u], dma_tok=t_u)
            for c in range(8):
                bk, tb = cx.bank()
                for k in range(KC):
                    A("pe", lambda e, bk=bk, c=c, k=k, ut=ut: e.matmul(bk[:, :], lhsT=wres[:, k, c * 128:(c + 1) * 128],
                                                                      rhs=ut[:, k, :], start=(k == 0), stop=(k == KC - 1)),
                      reads=[t_w, t_u], writes=[tb])
                if c < 6:
                    evac(cx, rawbuf[:, c, 3:T + 3], bk[:, :], [tb], [t_raw[c]])
                else:
                    A("act", lambda e, bk=bk, c=c: e.activation(out=siluz[:, c - 6, :], in_=bk[:, :], func=AF.Silu),
                      reads=[tb], writes=[t_sz[c - 6]])
            for c in range(6):
                A("dve", lambda e, c=c: e.tensor_scalar(out=cv[:, c, :], in0=rawbuf[:, c, 0:T],
                                                        scalar1=convw[:, c * 4:c * 4 + 1], scalar2=None, op0=ALU.mult),
                  reads=[t_raw[c], t_cw], writes=[t_cv[c]])
                for j in range(1, 4):
                    A("dve", lambda e, c=c, j=j: e.scalar_tensor_tensor(out=cv[:, c, :], in0=rawbuf[:, c, j:j + T],
                                                                        scalar=convw[:, c * 4 + j:c * 4 + j + 1],
                                                                        in1=cv[:, c, :], op0=ALU.mult, op1=ALU.add),
                      reads=[t_raw[c], t_cw, t_cv[c]], writes=[t_cv[c]])
                A("act", lambda e, c=c: e.copy(out=rawbuf[:, c, 0:3], in_=rawbuf[:, c, T:T + 3]),
                  reads=[t_raw[c]], writes=[t_raw[c]])
                A("act", lambda e, c=c: e.activation(out=cv[:, c, :], in_=cv[:, c, :], func=AF.Silu),
                  reads=[t_cv[c]], writes=[t_cv[c]])
            for idx in range(4):
                A("act", lambda e, idx=idx: e.activation(out=sqb[:], in_=cv[:, idx, :], func=AF.Square),
                  reads=[t_cv[idx]], writes=[t_sqb])
                bk, tb = cx.bank()
                A("pe", lambda e, bk=bk: e.matmul(bk[:, :], lhsT=cx.ones_bf[:], rhs=sqb[:], start=True, stop=True),
                  reads=[t_sqb, tc], writes=[tb])
                A("dve", lambda e, bk=bk: e.tensor_scalar(out=rstd[:], in0=bk[:, :], scalar1=1e-6, scalar2=None,
                                                          op0=ALU.add), reads=[tb], writes=[t_rstd])
                A("act", lambda e: e.activation(out=rstd[:], in_=rstd[:], func=AF.Sqrt), reads=[t_rstd], writes=[t_rstd])
                A("dve", lambda e: e.reciprocal(out=rstd[:], in_=rstd[:]), reads=[t_rstd], writes=[t_rstd])
                sc = float(HD ** -0.5) if idx < 2 else 1.0
                A("dve", lambda e, idx=idx, sc=sc: e.scalar_tensor_tensor(out=qk[:, idx, :], in0=cv[:, idx, :], scalar=sc,
                                                                          in1=rstd[:], op0=ALU.mult, op1=ALU.mult),
                  reads=[t_cv[idx], t_rstd], writes=[t_qk[idx]])
            bk, tb = cx.bank()
            for ck in range(NCHK):
                for k in range(KC):
                    A("pe", lambda e, bk=bk, ck=ck, k=k, ut=ut: e.matmul(bk[0:64, ck * 4:ck * 4 + 4],
                                                                        lhsT=ut[:, k, ck * CH:(ck + 1) * CH],
                                                                        rhs=wbar[:, k, :], start=(k == 0), stop=(k == KC - 1)),
                      reads=[t_wb, t_u], writes=[tb])
            A("dve", lambda e, bk=bk: e.tensor_copy(out=ba[:].rearrange("p c f -> p (c f)"), in_=bk[0:64, 0:NCHK * 4]),
              reads=[tb], writes=[t_ba])
            A("act", lambda e: e.activation(out=beta[:], in_=ba[:, :, 0:2], func=AF.Sigmoid), reads=[t_ba], writes=[t_beta])
            A("dve", lambda e: e.tensor_tensor(out=spx[:], in0=ba[:, :, 2:4], in1=bc_mid(hv[:, 0:2], NCHK), op=ALU.add),
              reads=[t_ba, t_hv], writes=[t_spx])
            A("act", lambda e: e.activation(out=spx[:], in_=spx[:], func=AF.Exp), reads=[t_spx], writes=[t_spx])
            A("dve", lambda e: e.tensor_scalar(out=spx[:], in0=spx[:], scalar1=1.0, scalar2=None, op0=ALU.add),
              reads=[t_spx], writes=[t_spx])
            A("act", lambda e: e.activation(out=spx[:], in_=spx[:], func=AF.Ln), reads=[t_spx], writes=[t_spx])
            A("dve", lambda e: e.scalar_tensor_tensor(out=gt[:], in0=spx[:], scalar=-1.0, in1=bc_mid(expA[:, 0:2], NCHK),
                                                      op0=ALU.mult, op1=ALU.mult), reads=[t_spx, t_hv], writes=[t_g])
            ob, t_ob = outb[it % 2]
            for h in range(2):
                gh = gt[:, :, h]
                bh = beta[:, :, h]
                qT = qk[:, h, :].rearrange("p (c f) -> p c f", f=CH)
                kT = qk[:, 2 + h, :].rearrange("p (c f) -> p c f", f=CH)
                t_q, t_k, t_v = t_qk[h], t_qk[2 + h], t_cv[4 + h]
                bk, tb = bank_mm(64, [(0, NCHK, Um, gh)], [tc, t_g])
                A("dve", lambda e, bk=bk: e.tensor_copy(out=gc_col[:], in_=bk[0:64, 0:NCHK]), reads=[tb], writes=[t_gcc])
                A("dve", lambda e, gh=gh: e.tensor_tensor(out=Ug[:], in0=bc_mid(Um, NCHK), in1=bc_last(gh, 64), op=ALU.mult),
                  reads=[tc, t_g], writes=[t_Ug])
                bk, tb = bank_mm(128, [(c * 64, 64, ones_f[:, :], Ug[:, c, :]) for c in range(NCHK)], [tc, t_Ug])
                A("act", lambda e, bk=bk: e.copy(out=gcrow[:].rearrange("p c f -> p (c f)"), in_=bk[:, :]),
                  reads=[tb], writes=[t_gcr])
                A("dve", lambda e: e.tensor_tensor(out=dl[:], in0=gcrow[0:64, :, :], in1=bc_last(gc_col[:, :], 64),
                                                   op=ALU.subtract), reads=[t_gcr, t_gcc], writes=[t_dl])
                A("dve", lambda e: e.tensor_tensor(out=tmpd[:], in0=dl[:], in1=bc_mid(BUm, NCHK), op=ALU.add),
                  reads=[t_dl, tc], writes=[t_tmpd])
                A("act", lambda e: e.activation(out=decay[:], in_=tmpd[:], func=AF.Exp, scale=-1.0),
                  reads=[t_tmpd], writes=[t_dec])
                A("dve", lambda e: e.tensor_tensor(out=tmpd[:], in0=dl[:], in1=bc_mid(BLm, NCHK), op=ALU.subtract),
                  reads=[t_dl, tc, t_dec], writes=[t_tmpd])
                A("act", lambda e: e.activation(out=decayT[:], in_=tmpd[:], func=AF.Exp), reads=[t_tmpd], writes=[t_decT])
                A("act", lambda e: e.activation(out=egrow[:], in_=gcrow[:], func=AF.Exp), reads=[t_gcr], writes=[t_egr])
                A("act", lambda e: e.activation(out=kgs[:], in_=dl[:, :, 63], func=AF.Exp), reads=[t_dl], writes=[t_kgs])
                A("act", lambda e: e.activation(out=egc[:], in_=gc_col[:], func=AF.Exp), reads=[t_gcc], writes=[t_egc])
                for (src, t_src, dst, t_dst) in ((kT, t_k, Ktok, t_Kt),
                                                 (cv[:, 4 + h, :].rearrange("p (c f) -> p c f", f=CH), t_v, Vtok, t_Vt)):
                    for half in range(2):
                        bk, tb = cx.bank()
                        for c4 in range(4):
                            c = half * 4 + c4
                            A("pe", lambda e, bk=bk, c4=c4, c=c, src=src: e.transpose(out=bk[0:64, c4 * 128:(c4 + 1) * 128],
                                                                                  in_=src[:, c, :], identity=ident),
                              reads=[t_src, tc], writes=[tb])
                        evac(cx, dst[:, half * 4:half * 4 + 4, :].rearrange("p c f -> p (c f)"), bk[0:64, :], [tb], [t_dst])
                bk, tb = bank_mm(64, [(c * 64, 64, kT[:, c, :], kT[:, c, :]) for c in range(NCHK)], [t_k])
                P0, t_P0 = Pb[0]
                Pt0, t_Pt0 = Ptb[0]
                A("dve", lambda e, bk=bk, P0=P0: e.tensor_tensor(out=P0[:].rearrange("p c f -> p (c f)"), in0=bk[0:64, :],
                                                               in1=decay[:].rearrange("p c f -> p (c f)"), op=ALU.mult),
                  reads=[tb, t_dec], writes=[t_P0])
                A("dve", lambda e, P0=P0, bh=bh: e.tensor_tensor(out=P0[:], in0=P0[:], in1=bc_last(bh, 64), op=ALU.mult),
                  reads=[t_P0, t_beta], writes=[t_P0])
                A("dve", lambda e, P0=P0: e.tensor_tensor(out=P0[:], in0=P0[:], in1=bc_mid(SLm, NCHK), op=ALU.mult),
                  reads=[t_P0, tc], writes=[t_P0])
                bk, tb = cx.bank()
                for c in range(NCHK):
                    A("pe", lambda e, bk=bk, c=c, P0=P0: e.transpose(out=bk[0:64, c * 64:(c + 1) * 64], in_=P0[:, c, :],
                                                                    identity=id64), reads=[t_P0, tc], writes=[tb])
                A("act", lambda e, bk=bk, Pt0=Pt0: e.copy(out=Pt0[:].rearrange("p c f -> p (c f)"), in_=bk[0:64, :]),
                  reads=[tb], writes=[t_Pt0])
                A("dve", lambda e, Pt0=Pt0: e.tensor_tensor(out=At[:], in0=bc_mid(id64, NCHK), in1=Pt0[:], op=ALU.subtract),
                  reads=[t_Pt0, tc], writes=[t_At])
                for lv in range(5):
                    Pc, t_Pc = Pb[lv % 2]
                    Ptc, t_Ptc = Ptb[lv % 2]
                    Pn, t_Pn = Pb[(lv + 1) % 2]
                    Ptn, t_Ptn = Ptb[(lv + 1) % 2]
                    bk, tb = bank_mm(64, [(c * 64, 64, Ptc[:, c, :], Pc[:, c, :]) for c in range(NCHK)], [t_Pc, t_Ptc])
                    A("dve", lambda e, bk=bk, Pn=Pn: e.tensor_copy(out=Pn[:].rearrange("p c f -> p (c f)"), in_=bk[0:64, :]),
                      reads=[tb], writes=[t_Pn])
                    bk, tb = bank_mm(64, [(c * 64, 64, Pc[:, c, :], Ptc[:, c, :]) for c in range(NCHK)], [t_Pc, t_Ptc])
                    A("act", lambda e, bk=bk, Ptn=Ptn: e.copy(out=Ptn[:].rearrange("p c f -> p (c f)"), in_=bk[0:64, :]),
                      reads=[tb], writes=[t_Ptn])
                    bk, tb = bank_mm(64, [(c * 64, 64, Pn[:, c, :], At[:, c, :]) for c in range(NCHK)], [t_Pn, t_At])
                    A("dve", lambda e, bk=bk: e.tensor_tensor(out=At[:].rearrange("p c f -> p (c f)"),
                                                              in0=At[:].rearrange("p c f -> p (c f)"), in1=bk[0:64, :],
                                                              op=ALU.add), reads=[tb, t_At], writes=[t_At])
                A("dve", lambda e, bh=bh: e.tensor_tensor(out=Vb[:], in0=Vtok[:], in1=bc_last(bh, 128), op=ALU.mult),
                  reads=[t_Vt, t_beta], writes=[t_Vb])
                A("dve", lambda e, bh=bh: e.tensor_tensor(out=be[:], in0=egc[:], in1=bh, op=ALU.mult),
                  reads=[t_egc, t_beta], writes=[t_be])
                A("dve", lambda e: e.tensor_tensor(out=Kbg[:], in0=Ktok[:], in1=bc_last(be[:, :], 128), op=ALU.mult),
                  reads=[t_Kt, t_be], writes=[t_Kbg])
                for half in range(2):
                    bk, tb = bank_mm(64, [(c4 * 128, 128, At[:, half * 4 + c4, :], Vb[:, half * 4 + c4, :]) for c4 in range(4)],
                                     [t_At, t_Vb])
                    evac(cx, u_sb[:, half * 4:half * 4 + 4, :].rearrange("p c f -> p (c f)"), bk[0:64, :], [tb], [t_usb])
                bk, tb = bank_mm(128, [(c * 64, 64, Kbg[:, c, :], At[:, c, :]) for c in range(NCHK)], [t_At, t_Kbg])
                evac(cx, wT[:].rearrange("p c f -> p (c f)"), bk[:, :], [tb], [t_wT])
                bk, tb = bank_mm(64, [(c * 64, 64, kT[:, c, :], qT[:, c, :]) for c in range(NCHK)], [t_k, t_q])
                A("dve", lambda e, bk=bk: e.tensor_tensor(out=attnT[:].rearrange("p c f -> p (c f)"), in0=bk[0:64, :],
                                                          in1=decayT[:].rearrange("p c f -> p (c f)"), op=ALU.mult),
                  reads=[tb, t_decT], writes=[t_att])
                A("dve", lambda e, qT=qT: e.tensor_tensor(out=qgT[:], in0=qT, in1=egrow[:], op=ALU.mult),
                  reads=[t_q, t_egr], writes=[t_qg])
                A("dve", lambda e: e.tensor_tensor(out=kg[:], in0=Ktok[:], in1=bc_last(kgs[:, :], 128), op=ALU.mult),
                  reads=[t_Kt, t_kgs], writes=[t_kg])
                obk, t_obk = cx.bank()
                cx.reserved.add(id(obk))
                for c in range(NCHK):
                    Sc, t_Sc = Sst[h][s_par[h]]
                    Sn, t_Sn = Sst[h][1 - s_par[h]]
                    s_par[h] = 1 - s_par[h]
                    vn, t_vn = vnew[vn_rr[0] % 2]
                    vn_rr[0] += 1
                    bk, tb = bank_mm(64, [(0, 128, wT[:, c, :], Sc[:, :])], [t_wT, t_Sc])
                    A("dve", lambda e, bk=bk, c=c, vn=vn: e.tensor_tensor(out=vn[:], in0=u_sb[:, c, :], in1=bk[0:64, 0:128],
                                                                         op=ALU.subtract), reads=[tb, t_usb], writes=[t_vn])
                    A("pe", lambda e, c=c, Sc=Sc, obk=obk: e.matmul(obk[:, c * 64:(c + 1) * 64], lhsT=Sc[:, :], rhs=qgT[:, c, :],
                                                                   start=True, stop=False), reads=[t_Sc, t_qg], writes=[t_obk])
                    A("pe", lambda e, c=c, vn=vn, obk=obk: e.matmul(obk[:, c * 64:(c + 1) * 64], lhsT=vn[:, :], rhs=attnT[:, c, :],
                                                                   start=False, stop=True), reads=[t_vn, t_att], writes=[t_obk])
                    bk, tb = bank_mm(128, [(0, 128, kg[:, c, :], vn[:, :])], [t_kg, t_vn])
                    A("dve", lambda e, bk=bk, c=c, Sc=Sc, Sn=Sn: e.scalar_tensor_tensor(
                        out=Sn[:], in0=Sc[:], scalar=egrow[:, c, 63:64], in1=bk[:, 0:128], op0=ALU.mult, op1=ALU.add),
                      reads=[tb, t_Sc, t_egr], writes=[t_Sn])
                A("act", lambda e, obk=obk: e.copy(out=o_sb[:], in_=obk[:, :]), reads=[t_obk], writes=[t_osb])
                cx.reserved.discard(id(obk))
                A("act", lambda e: e.activation(out=sqb[:], in_=o_sb[:], func=AF.Square), reads=[t_osb], writes=[t_sqb])
                bk, tb = cx.bank()
                A("pe", lambda e, bk=bk: e.matmul(bk[:, :], lhsT=cx.ones_bf[:], rhs=sqb[:], start=True, stop=True),
                  reads=[t_sqb, tc], writes=[tb])
                A("dve", lambda e, bk=bk: e.tensor_scalar(out=rstd[:], in0=bk[:, :], scalar1=1.0 / HD, scalar2=EPS,
                                                          op0=ALU.mult, op1=ALU.add), reads=[tb], writes=[t_rstd])
                A("act", lambda e: e.activation(out=rstd[:], in_=rstd[:], func=AF.Sqrt), reads=[t_rstd], writes=[t_rstd])
                A("dve", lambda e: e.reciprocal(out=rstd[:], in_=rstd[:]), reads=[t_rstd], writes=[t_rstd])
                A("dve", lambda e: e.scalar_tensor_tensor(out=o_tmp[:], in0=o_sb[:], scalar=gon[:, 0:1], in1=rstd[:],
                                                          op0=ALU.mult, op1=ALU.mult),
                  reads=[t_osb, t_gon, t_rstd], writes=[t_otmp])
                A("dve", lambda e, h=h, ob=ob: e.tensor_tensor(out=ob[:, h, :], in0=o_tmp[:], in1=siluz[:, h, :], op=ALU.mult),
                  reads=[t_otmp, t_sz[h]], writes=[t_ob])
            A("sp", lambda e, ob=ob, t0=t0: e.dma_start(out=ov[:, :, t0:t0 + T], in_=ob[:]), reads=[t_ob], dma_tok=t_ob)
        R.emit()
    return nc


SB_NM0 = 128
SB_ONE = 128 + 4 * 512


def sb_consts():
    c = np.zeros((128, 128 + 4 * 512 + 1), np.float32)
    c[:, 0:128] = np.eye(128, dtype=np.float32)
    t = np.arange(128)[:, None]
    j = np.arange(128)[None, :]
    tri = (t + j < 128).astype(np.float32)
    for q4 in range(4):
        for b in range(4):
            blk = c[:, SB_NM0 + q4 * 512 + b * 128: SB_NM0 + q4 * 512 + (b + 1) * 128]
            if 3 - b > q4:
                blk[:] = 1.0
            elif 3 - b == q4:
                blk[:] = tri
    c[:, SB_ONE] = 1.0
    return c


def build_stage2_sb(ntiles=SEQ // T, heads=(0, 1)):
    S = ntiles * T
    NB = S // 128
    nc = bass.Bass("TRN2", target_bir_lowering=False)
    uT_all = nc.dram_tensor("uT_all", [D, S], BF16, kind="ExternalInput").ap()
    uT_rev = nc.dram_tensor("uT_rev", [D, S], BF16, kind="ExternalInput").ap()
    wsb = nc.dram_tensor("wsb", [D, 768], F32, kind="ExternalInput").ap()
    consts_d = nc.dram_tensor("consts", [128, SB_ONE + 1], F32, kind="ExternalInput").ap()
    osbT = nc.dram_tensor("osbT", [256, S], BF16, kind="ExternalOutput").ap()
    with contextlib.ExitStack() as stack:
        cx = Ctx(nc, stack)
        R = cx.R
        A = R.add
        tc = cx.t_const
        consts = R.sb([128, SB_ONE + 1], F32, "consts")
        A("sp", lambda e: e.dma_start(out=consts[:], in_=consts_d[:, :]), writes=[tc], dma_tok=tc)
        ones_c = consts[:, SB_ONE:SB_ONE + 1]
        ident_bf = R.sb([128, 128], BF16, "ident_bf")
        A("dve", lambda e: e.tensor_copy(out=ident_bf[:], in_=consts[:, 0:128]), reads=[tc], writes=[tc])
        ones_f = ones_c.to_broadcast([128, T])
        wres = R.sb([128, KC, 768], BF16, "wres")
        t_w = R.tok("wres")
        wv = wsb.rearrange("(k p) n -> p k n", p=128)
        for half in range(2):
            A("pool", lambda e, half=half: e.dma_start(out=wres[:, :, half * 384:(half + 1) * 384],
                                                      in_=wv[:, :, half * 384:(half + 1) * 384]),
              writes=[t_w], dma_tok=t_w)
        u_f = [R.sb([128, KC, T], BF16, f"u_f{i}") for i in range(1)]
        t_uf = [R.tok(f"u_f{i}") for i in range(1)]
        u_r = [R.sb([128, KC, T], BF16, f"u_r{i}") for i in range(1)]
        t_ur = [R.tok(f"u_r{i}") for i in range(1)]
        QT = R.sb([128, S], BF16, "QT")
        KTr = R.sb([128, S], BF16, "KTr")
        Vr = R.sb([128, NB, 128], BF16, "Vr")
        t_Q = [R.tok(f"Q{i}") for i in range(ntiles)]
        t_K = [R.tok(f"K{i}") for i in range(ntiles)]
        t_V = [R.tok(f"V{i}") for i in range(ntiles)]
        G = 4
        smb = [[R.sb([128, T], F32, f"smb{i}_{k}") for k in range(2)] for i in range(G)]
        t_sm = [[R.tok(f"smb{i}_{k}") for k in range(2)] for i in range(G)]
        Pbuf = [[R.sb([128, T + 1], F32, f"Pbuf{i}_{k}") for k in range(2)] for i in range(G)]
        t_P = [[R.tok(f"Pbuf{i}_{k}") for k in range(2)] for i in range(G)]
        Ab = [[R.sb([128, T], BF16, f"Ab{i}_{k}") for k in range(2)] for i in range(G)]
        t_Ab = [[(R.tok(f"Ab{i}_{k}_lo"), R.tok(f"Ab{i}_{k}_hi")) for k in range(2)] for i in range(G)]
        ATs = [R.sb([128, 4, T], BF16, f"ATs{k}") for k in range(1)]
        t_AT = [[R.tok(f"ATs{k}_{hh}") for hh in range(2)] for k in range(1)]
        osb = [R.sb([128, T], BF16, f"osb{i}") for i in range(1)]
        t_os = [R.tok(f"osb{i}") for i in range(1)]
        XS = 256
        scale = float(HD ** -0.5)
        uva = uT_all.rearrange("(k p) t -> p k t", p=128)
        uvr = uT_rev.rearrange("(k p) t -> p k t", p=128)
        step = 0
        for h in heads:
            for it in range(ntiles):
                t0 = it * T
                uf, tuf = u_f[0], t_uf[0]
                ur, tur = u_r[0], t_ur[0]
                A("sp", lambda e, uf=uf, t0=t0: e.dma_start(out=uf[:], in_=uva[:, :, t0:t0 + T]), writes=[tuf], dma_tok=tuf)
                A("sp", lambda e, ur=ur, t0=t0: e.dma_start(out=ur[:], in_=uvr[:, :, t0:t0 + T]), writes=[tur], dma_tok=tur)
                for (src, tsrc, col, dst, tdst) in ((uf, tuf, h * 128, QT, t_Q[it]), (ur, tur, 256 + h * 128, KTr, t_K[it])):
                    bk, tb = cx.bank()
                    for k in range(KC):
                        A("pe", lambda e, bk=bk, k=k, src=src, col=col: e.matmul(bk[:, :], lhsT=wres[:, k, col:col + 128],
                                                                                rhs=src[:, k, :], start=(k == 0), stop=(k == KC - 1)),
                          reads=[t_w, tsrc], writes=[tb])
                    evac(cx, dst[:, t0:t0 + T], bk[:, :], [tb], [tdst])
                bk, tb = cx.bank()
                for b in range(4):
                    for k in range(KC):
                        A("pe", lambda e, bk=bk, k=k, b=b, ur=ur, h=h: e.matmul(bk[:, b * 128:(b + 1) * 128],
                                                                          lhsT=ur[:, k, b * 128:(b + 1) * 128],
                                                                          rhs=wres[:, k, 512 + h * 128:512 + (h + 1) * 128],
                                                                          start=(k == 0), stop=(k == KC - 1)),
                          reads=[t_w, tur], writes=[tb])
                evac(cx, Vr[:, it * 4:it * 4 + 4, :].rearrange("p b d -> p (b d)"), bk[:, :], [tb], [t_V[it]])
            for qg in range(NB // 4):
                obk, t_obk = cx.bank()
                cx.reserved.add(id(obk))
                ntile = qg + 1
                JB0 = NB - 4 - 4 * qg
                for i in range(ntile):
                    JB = JB0 + 4 * i
                    j0 = JB * 128
                    kt = j0 // T
                    par = i % 2
                    zb = []
                    for q4 in range(4):
                        qb = qg * 4 + q4
                        bk, tb = cx.bank()
                        zb.append((bk, tb))
                        A("pe", lambda e, bk=bk, qb=qb, j0=j0: e.matmul(bk[:, :], lhsT=QT[:, qb * 128:(qb + 1) * 128],
                                                                       rhs=KTr[:, j0:j0 + T], start=True, stop=True),
                          reads=[t_Q[qg], t_K[kt]], writes=[tb])
                    for q4 in range(4):
                        bk, tb = zb[q4]
                        sm, tsm = smb[q4][par], t_sm[q4][par]
                        Pc, tPc = Pbuf[q4][par], t_P[q4][par]
                        Pp, tPp = Pbuf[q4][1 - par], t_P[q4][1 - par]
                        A("act", lambda e, bk=bk, sm=sm: e.activation(out=sm[:], in_=bk[:, :], func=AF.Sigmoid, scale=-scale),
                          reads=[tb], writes=[tsm])
                        if i == 0:
                            nm = consts[:, SB_NM0 + q4 * 512:SB_NM0 + (q4 + 1) * 512]
                            A("dve", lambda e, sm=sm, nm=nm: e.tensor_tensor(out=sm[:], in0=sm[:], in1=nm, op=ALU.max),
                              reads=[tsm, tc], writes=[tsm])
                            A("act", lambda e, Pc=Pc: e.copy(out=Pc[:, 0:1], in_=ones_c), reads=[tc], writes=[tPc])
                        else:
                            A("act", lambda e, Pc=Pc, Pp=Pp: e.copy(out=Pc[:, 0:1], in_=Pp[:, T:T + 1]),
                              reads=[tPp], writes=[tPc])
                    for q4 in range(4):
                        sm, tsm = smb[q4][par], t_sm[q4][par]
                        Pc, tPc = Pbuf[q4][par], t_P[q4][par]
                        A("dve", lambda e, sm=sm, Pc=Pc: e.tensor_tensor_scan(out=Pc[:, 1:T + 1], data0=sm[:], data1=ones_f,
                                                                             initial=Pc[:, 0:1], op0=ALU.mult, op1=ALU.mult),
                          reads=[tsm, tPc, tc], writes=[tPc])
                    for q4 in range(4):
                        Pc, tPc = Pbuf[q4][par], t_P[q4][par]
                        ab, (tab_lo, tab_hi) = Ab[q4][par], t_Ab[q4][par]
                        A("pool", lambda e, ab=ab, Pc=Pc: e.tensor_tensor(out=ab[:, XS:T], in0=Pc[:, XS:T], in1=Pc[:, XS + 1:T + 1],
                                                                         op=ALU.subtract), reads=[tPc], writes=[tab_hi])
                        A("dve", lambda e, ab=ab, Pc=Pc: e.tensor_tensor(out=ab[:, 0:XS], in0=Pc[:, 0:XS], in1=Pc[:, 1:XS + 1],
                                                                        op=ALU.subtract), reads=[tPc], writes=[tab_lo])
                    at = ATs[0]
                    for hh in range(2):
                        bk2, tb2 = cx.bank()
                        bkb = bk2[:, :].bitcast(BF16)
                        for bb in range(2):
                            b = hh * 2 + bb
                            for q4 in range(4):
                                ab, tab = Ab[q4][par], t_Ab[q4][par][hh]
                                A("pe", lambda e, bkb=bkb, b=b, bb=bb, q4=q4, ab=ab: e.transpose(
                                    out=bkb[:, bb * 512 + q4 * 128:bb * 512 + (q4 + 1) * 128], in_=ab[:, b * 128:(b + 1) * 128],
                                    identity=ident_bf[:]), reads=[tab, tc], writes=[tb2])
                        A("act", lambda e, bkb=bkb, at=at, hh=hh: e.copy(out=at[:, hh * 2:hh * 2 + 2, :].rearrange("p b q -> p (b q)"),
                                                                        in_=bkb[:, :]), reads=[tb2], writes=[t_AT[0][hh]])
                    for b in range(4):
                        first = (i == 0 and b == 0)
                        last = (i == ntile - 1 and b == 3)
                        A("pe", lambda e, obk=obk, b=b, JB=JB, at=at, first=first, last=last: e.matmul(
                            obk[:, :], lhsT=Vr[:, JB + b, :], rhs=at[:, b, :], start=first, stop=last),
                          reads=[t_AT[0][b // 2], t_V[kt]], writes=[t_obk])
                ob, tob = osb[0], t_os[0]
                evac(cx, ob[:], obk[:, :], [t_obk], [tob])
                cx.reserved.discard(id(obk))
                A("sp", lambda e, ob=ob, qg=qg, h=h: e.dma_start(out=osbT[h * 128:(h + 1) * 128, qg * T:(qg + 1) * T], in_=ob[:]),
                  reads=[tob], dma_tok=tob)
        R.emit()
    return nc


def linear_fm(cx, ws, w_dram, kch, xT, t_x, col0, nout, consume):
    R = cx.R
    for c4 in range(0, nout, 4):
        n = min(4, nout - c4)
        wt, tw = ws.load(w_dram, 0, kch, col0 + c4 * 128, n * 128)
        for j in range(n):
            bk, tb = cx.bank()
            for k in range(kch):
                R.add("pe", lambda e, bk=bk, k=k, j=j, wt=wt: e.matmul(bk[:, :], lhsT=wt[:, k, j * 128:(j + 1) * 128],
                                                                      rhs=xT[:, k, :], start=(k == 0), stop=(k == kch - 1)),
                      reads=[tw, t_x], writes=[tb])
            consume(c4 + j, bk, tb)


def build_stage3(ntiles=TOK // T):
    nc = bass.Bass("TRN2", target_bir_lowering=False)
    NTK = ntiles * T
    h1T = nc.dram_tensor("h1T", [D, NTK], F32, kind="ExternalInput").ap()
    uT_d = nc.dram_tensor("uT", [D, NTK], BF16, kind="ExternalInput").ap()
    odn_d = nc.dram_tensor("odnT", [D, NTK], BF16, kind="ExternalInput").ap()
    osb_d = nc.dram_tensor("osbT", [D, NTK], BF16, kind="ExternalInput").ap()
    pT_d = nc.dram_tensor("pT", [PLE, NTK], F32, kind="ExternalInput").ap()
    wgate = nc.dram_tensor("wgate", [D, 2 * D], F32, kind="ExternalInput").ap()
    wbd = nc.dram_tensor("wbd", [D, D], F32, kind="ExternalInput").ap()
    wbs = nc.dram_tensor("wbs", [D, D], F32, kind="ExternalInput").ap()
    wout = nc.dram_tensor("wout", [D, D], F32, kind="ExternalInput").ap()
    wg = nc.dram_tensor("wg", [D, DFF], F32, kind="ExternalInput").ap()
    wu = nc.dram_tensor("wu", [D, DFF], F32, kind="ExternalInput").ap()
    wd = nc.dram_tensor("wd", [DFF, D], F32, kind="ExternalInput").ap()
    wpg = nc.dram_tensor("wpg", [D, D], F32, kind="ExternalInput").ap()
    wpp = nc.dram_tensor("wpp", [PLE, D], F32, kind="ExternalInput").ap()
    gains = nc.dram_tensor("gains", [128, 5 * KC], F32, kind="ExternalInput").ap()
    ident_d = nc.dram_tensor("ident", [128, 128], F32, kind="ExternalInput").ap()
    out = nc.dram_tensor("out", [NTK, D], F32, kind="ExternalOutput").ap()
    with contextlib.ExitStack() as stack:
        cx = Ctx(nc, stack)
        R = cx.R
        A = R.add
        g_all, t_g = load_vec_fm(cx, gains, "gains_sb")
        ident = R.sb([128, 128], F32, "ident")
        A("sp", lambda e: e.dma_start(out=ident[:], in_=ident_d[:, :]), writes=[cx.t_const], dma_tok=cx.t_const)
        big = R.sb([128, KC * T], F32, "big")
        otok = big[:].rearrange("p (s d) -> p s d", s=4)
        fT = big[:].rearrange("p (c t) -> p c t", c=KC)
        t_f = R.tok("fT")
        t_otok = [t_f] * 4
        hT = R.sb([128, KC, T], F32, "hT")
        t_h = R.tok("hT")
        uT = R.sb([128, KC, T], BF16, "uT")
        t_u = R.tok("uT")
        sq = R.sb([128, KC, T], BF16, "sq")
        t_sq = R.tok("sq")
        rstd = R.sb([128, T], F32, "rstd")
        t_rstd = R.tok("rstd")
        actT = R.sb([128, DFF // 128, T], BF16, "actT")
        t_act = R.tok("actT")
        odn = actT[:, 0:KC, :]
        osb = actT[:, KC:2 * KC, :]
        tmp = [R.sb([128, T], F32, f"tmp{i}") for i in range(2)]
        t_tmp = [R.tok(f"tmp{i}") for i in range(2)]
        pTb = R.sb([128, 2, T], BF16, "pTb")
        t_p = R.tok("pTb")
        ws = WStream(cx, KC, 3, "w")
        hv = h1T.rearrange("(c p) t -> p c t", p=128)
        uv = uT_d.rearrange("(c p) t -> p c t", p=128)
        dnv = odn_d.rearrange("(c p) t -> p c t", p=128)
        sbv = osb_d.rearrange("(c p) t -> p c t", p=128)
        pv = pT_d.rearrange("(c p) t -> p c t", p=128)
        for it in range(ntiles):
            r0 = it * T
            A("sp", lambda e, r0=r0: e.dma_start(out=hT[:], in_=hv[:, :, r0:r0 + T]), writes=[t_h], dma_tok=t_h)
            A("sp", lambda e, r0=r0: e.dma_start(out=uT[:], in_=uv[:, :, r0:r0 + T]), writes=[t_u], dma_tok=t_u)
            t_dn = R.tok("odn_ld")
            A("sp", lambda e, r0=r0: e.dma_start(out=odn, in_=dnv[:, :, r0:r0 + T]), writes=[t_act], dma_tok=t_dn)
            A("sp", lambda e, r0=r0: e.dma_start(out=osb, in_=sbv[:, :, r0:r0 + T]), writes=[t_act], dma_tok=t_dn)
            A("pool", lambda e, r0=r0: e.dma_start(out=pTb[:], in_=pv[:, :, r0:r0 + T]), writes=[t_p], dma_tok=t_p)
            for (gcol, wb, xs, tx, second) in ((0, wbd, odn, t_act, False), (D, wbs, osb, t_act, True)):
                for c4 in range(0, KC, 4):
                    gtile, tgw = ws.load(wgate, 0, KC, gcol + c4 * 128, 512)
                    btile, tbw = ws.load(wb, 0, KC, c4 * 128, 512)
                    for j in range(4):
                        c = c4 + j
                        bg, tbg = cx.bank()
                        for k in range(KC):
                            A("pe", lambda e, bg=bg, k=k, j=j, gtile=gtile: e.matmul(
                                bg[:, :], lhsT=gtile[:, k, j * 128:(j + 1) * 128], rhs=uT[:, k, :],
                                start=(k == 0), stop=(k == KC - 1)), reads=[tgw, t_u], writes=[tbg])
                        bb, tbb = cx.bank()
                        for k in range(KC):
                            A("pe", lambda e, bb=bb, k=k, j=j, btile=btile, xs=xs: e.matmul(
                                bb[:, :], lhsT=btile[:, k, j * 128:(j + 1) * 128], rhs=xs[:, k, :],
                                start=(k == 0), stop=(k == KC - 1)), reads=[tbw, tx], writes=[tbb])
                        tt, tm = t_tmp[c % 2], tmp[c % 2]
                        A("act", lambda e, bg=bg, tm=tm: e.activation(out=tm[:], in_=bg[:, :], func=AF.Sigmoid),
                          reads=[tbg], writes=[tt])
                        if not second:
                            A("dve", lambda e, bb=bb, tm=tm, c=c: e.tensor_tensor(out=fT[:, c, :], in0=tm[:], in1=bb[:, :],
                                                                                 op=ALU.mult), reads=[tt, tbb], writes=[t_f])
                        else:
                            A("dve", lambda e, bb=bb, tm=tm: e.tensor_tensor(out=tm[:], in0=tm[:], in1=bb[:, :], op=ALU.mult),
                              reads=[tt, tbb], writes=[tt])
                            A("dve", lambda e, tm=tm, c=c: e.tensor_tensor(out=sq[:, c, :], in0=tm[:], in1=fT[:, c, :],
                                                                          op=ALU.add), reads=[tt, t_f], writes=[t_sq])
            linear_fm(cx, ws, wout, KC, sq, t_sq, 0, KC,
                      lambda c, bk, tb: evac(cx, fT[:, c, :], bk[:, :], [tb], [t_f]))
            resid_norm_add(cx, fT, t_f, g_all[:, 0:KC], t_g, hT, t_h, sq, t_sq, rstd, t_rstd, 1.0)
            rms_fm(cx, hT, t_h, KC, g_all[:, KC:2 * KC], t_g, uT, t_u, sq, t_sq, rstd, t_rstd, D)
            ffn_fm(cx, uT, t_u, wg, wu, wd, actT, t_act, ws, fT, t_f, tmp, t_tmp)
            resid_norm_add(cx, fT, t_f, g_all[:, 2 * KC:3 * KC], t_g, hT, t_h, sq, t_sq, rstd, t_rstd, 0.5)
            rms_fm(cx, hT, t_h, KC, g_all[:, 3 * KC:4 * KC], t_g, uT, t_u, sq, t_sq, rstd, t_rstd, D)
            for c4 in range(0, KC, 4):
                gtile, tgw = ws.load(wpg, 0, KC, c4 * 128, 512)
                ptile, tpw = ws.load(wpp, 0, 2, c4 * 128, 512)
                for j in range(4):
                    c = c4 + j
                    bg, tbg = cx.bank()
                    for k in range(KC):
                        A("pe", lambda e, bg=bg, k=k, j=j, gtile=gtile: e.matmul(
                            bg[:, :], lhsT=gtile[:, k, j * 128:(j + 1) * 128], rhs=uT[:, k, :],
                            start=(k == 0), stop=(k == KC - 1)), reads=[tgw, t_u], writes=[tbg])
                    bp, tbp = cx.bank()
                    for k in range(2):
                        A("pe", lambda e, bp=bp, k=k, j=j, ptile=ptile: e.matmul(
                            bp[:, :], lhsT=ptile[:, k, j * 128:(j + 1) * 128], rhs=pTb[:, k, :],
                            start=(k == 0), stop=(k == 1)), reads=[tpw, t_p], writes=[tbp])
                    tt, tm = t_tmp[c % 2], tmp[c % 2]
                    A("act", lambda e, bg=bg, tm=tm: e.activation(out=tm[:], in_=bg[:, :], func=AF.Sigmoid),
                      reads=[tbg], writes=[tt])
                    A("dve", lambda e, bp=bp, tm=tm, c=c: e.tensor_tensor(out=fT[:, c, :], in0=tm[:], in1=bp[:, :], op=ALU.mult),
                      reads=[tt, tbp], writes=[t_f])
            resid_norm_add(cx, fT, t_f, g_all[:, 4 * KC:5 * KC], t_g, hT, t_h, sq, t_sq, rstd, t_rstd, 1.0)
            transpose_out(cx, hT, t_h, KC, out, r0, otok, t_otok, ident)
        R.emit()
    return nc


_PROGS = {}


def _prog(name, fn):
    if name not in _PROGS:
        _PROGS[name] = fn()
    return _PROGS[name]


def _run(nc, maps):
    import sys, time
    t0 = time.time()
    res = run_bass_kernel_spmd(nc, maps, core_ids=list(range(NCORE)))
    print(f"[kernel] launch done in {time.time() - t0:.1f}s", file=sys.stderr, flush=True)
    return res.results


DN_W = 2048
O1 = 3 * DN_W
O2 = O1 + DN_W
O3 = O2 + 16
O4 = O3 + 16
O5 = O4 + 3 * 2048
O6 = O5 + D


def kernel(x, p, ffn1_norm_pre, ffn1_w_gate, ffn1_w_up, ffn1_w_down, ffn1_norm_post,
           mix_norm_pre, w_in, dn_conv_w, dn_A_log, dn_dt_bias, dn_out_norm,
           w_branch_dn, w_branch_sb, w_out, mix_norm_post,
           ffn2_norm_pre, ffn2_w_gate, ffn2_w_up, ffn2_w_down, ffn2_norm_post,
           ple_norm_pre, ple_w_gate, ple_w_proj, ple_norm_post):
    f32 = lambda a: np.ascontiguousarray(np.asarray(a, dtype=np.float32))
    x = f32(x)[0]
    p = f32(p)[0, 0]
    w_in = f32(w_in)[0]
    ident = np.eye(128, dtype=np.float32)
    g1 = np.concatenate([fm_vec(ffn1_norm_pre[0]), fm_vec(ffn1_norm_post[0]), fm_vec(mix_norm_pre[0])], 1)
    wg1, wu1, wd1 = f32(ffn1_w_gate)[0], f32(ffn1_w_up)[0], f32(ffn1_w_down)[0]
    maps = [{"x": np.ascontiguousarray(x[c * TOK:(c + 1) * TOK]), "wg": wg1, "wu": wu1, "wd": wd1,
             "gains": g1, "ident": ident} for c in range(NCORE)]
    r1 = _run(_prog("s1", build_stage1), maps)
    h1T = [np.asarray(r["h1T"]) for r in r1]
    uT = [np.asarray(r["uT"]) for r in r1]
    uT_all = np.ascontiguousarray(np.concatenate(uT, axis=1))
    uT_rev = np.ascontiguousarray(uT_all[:, ::-1])
    del r1, maps
    conv = f32(dn_conv_w)[0]
    consts = dn_consts()
    maps = []
    for c in range(NCORE):
        hs = (2 * c, 2 * c + 1)
        cols = [w_in[:, off + h * 128: off + (h + 1) * 128] for off in (0, DN_W, 2 * DN_W, O1) for h in hs]
        wdn = np.ascontiguousarray(np.concatenate(cols, axis=1))
        wba = np.ascontiguousarray(np.stack([w_in[:, O2 + hs[0]], w_in[:, O2 + hs[1]], w_in[:, O3 + hs[0]], w_in[:, O3 + hs[1]]], axis=1))
        cw = np.stack([conv[:, off + h * 128: off + (h + 1) * 128] for off in (0, DN_W, 2 * DN_W) for h in hs], axis=0)
        cw = np.ascontiguousarray(cw.transpose(2, 0, 1).reshape(128, 24))
        hv = np.array([dn_dt_bias[0][hs[0]], dn_dt_bias[0][hs[1]], dn_A_log[0][hs[0]], dn_A_log[0][hs[1]]], np.float32)
        maps.append({"uT_all": uT_all, "wdn": wdn, "wba": wba, "convw": cw,
                     "gon": f32(dn_out_norm)[0][:, None].copy(), "hv": np.ascontiguousarray(np.tile(hv[None, :], (64, 1))),
                     "consts": consts})
    r2a = _run(_prog("s2a", build_stage2_dn), maps)
    odn_full = np.ascontiguousarray(np.concatenate([np.asarray(r["odnT"]) for r in r2a], axis=0))
    del r2a, maps
    sbc = sb_consts()
    maps = []
    for c in range(NCORE):
        hs = (2 * c, 2 * c + 1)
        cols = [w_in[:, O4 + off + h * 128: O4 + off + (h + 1) * 128] for off in (0, 2048, 4096) for h in hs]
        maps.append({"uT_all": uT_all, "uT_rev": uT_rev, "wsb": np.ascontiguousarray(np.concatenate(cols, axis=1)), "consts": sbc})
    r2b = _run(_prog("s2b", build_stage2_sb), maps)
    osb_full = np.ascontiguousarray(np.concatenate([np.asarray(r["osbT"]) for r in r2b], axis=0))
    del r2b, maps, uT_all, uT_rev
    g3 = np.concatenate([fm_vec(mix_norm_post[0]), fm_vec(ffn2_norm_pre[0]), fm_vec(ffn2_norm_post[0]),
                         fm_vec(ple_norm_pre[0]), fm_vec(ple_norm_post[0])], 1)
    wgate = np.ascontiguousarray(w_in[:, O5:O5 + 2 * D])
    shared = {"wgate": wgate, "wbd": f32(w_branch_dn)[0], "wbs": f32(w_branch_sb)[0], "wout": f32(w_out)[0],
              "wg": f32(ffn2_w_gate)[0], "wu": f32(ffn2_w_up)[0], "wd": f32(ffn2_w_down)[0],
              "wpg": f32(ple_w_gate)[0], "wpp": f32(ple_w_proj)[0], "gains": g3, "ident": ident}
    maps = []
    for c in range(NCORE):
        sl = slice(c * TOK, (c + 1) * TOK)
        m = dict(shared)
        m.update({"h1T": h1T[c], "uT": uT[c], "odnT": np.ascontiguousarray(odn_full[:, sl]),
                  "osbT": np.ascontiguousarray(osb_full[:, sl]), "pT": np.ascontiguousarray(p[sl].T)})
        maps.append(m)
    r3 = _run(_prog("s3", build_stage3), maps)
    out = np.concatenate([np.asarray(r["out"]) for r in r3], axis=0)
    return out.reshape(1, SEQ, D).astype(np.float32)
```

```python
import contextlib
import numpy as np
import concourse.bass as bass
import concourse.mybir as mybir
from concourse.bass_utils import run_bass_kernel_spmd

F32 = mybir.dt.float32
BF16 = mybir.dt.bfloat16
AF = mybir.ActivationFunctionType
ALU = mybir.AluOpType

D = 2048
SEQ = 16384
NCORE = 8
TOK = SEQ // NCORE
T = 512
DFF = 5632
PLE = 256
KC = D // 128
EPS = 1e-6


class Tok:
    __slots__ = ("name", "last_w", "readers", "sem", "dma_cnt", "last_dma")

    def __init__(self, name):
        self.name = name
        self.last_w = None
        self.readers = []
        self.sem = None
        self.dma_cnt = 0
        self.last_dma = None


class Op:
    __slots__ = ("eng", "fn", "deps", "dma", "signal", "sem", "count", "tok")

    def __init__(self, eng, fn, dma):
        self.eng = eng
        self.fn = fn
        self.deps = []
        self.dma = dma
        self.signal = dma
        self.sem = None
        self.count = 0
        self.tok = None


ENGS = ("pe", "act", "dve", "pool", "sp")


class Rec:
    def __init__(self, nc, stack):
        self.nc = nc
        self.stack = stack
        self.ops = {e: [] for e in ENGS}
        self.toks = []
        self.dma_toks = []
        self.nsb = 0

    def tok(self, name="t"):
        t = Tok(name)
        self.toks.append(t)
        return t

    def sb(self, shape, dt, name=None):
        self.nsb += 1
        return self.stack.enter_context(self.nc.sbuf_tensor("s_" + (name or f"sb{self.nsb}"), list(shape), dt))

    def ps(self, shape, dt, name=None):
        self.nsb += 1
        return self.stack.enter_context(self.nc.psum_tensor("p_" + (name or f"ps{self.nsb}"), list(shape), dt))

    def add(self, eng, fn, reads=(), writes=(), dma_tok=None):
        op = Op(eng, fn, dma_tok is not None)
        deps = []
        for r in reads:
            if r.last_w is not None:
                deps.append(r.last_w)
        for w in writes:
            if w.last_w is not None:
                deps.append(w.last_w)
            deps.extend(w.readers)
        if dma_tok is not None:
            if dma_tok.sem is None:
                dma_tok.sem = self.stack.enter_context(self.nc.semaphore(f"dq{len(self.dma_toks)}"))
                dma_tok.last_dma = None
                self.dma_toks.append(dma_tok)
            if getattr(dma_tok, "last_dma", None) is not None:
                deps.append(dma_tok.last_dma)
            dma_tok.dma_cnt += 16
            op.sem = dma_tok.sem
            op.count = dma_tok.dma_cnt
            op.tok = dma_tok
            dma_tok.last_dma = op
        seen = set()
        for d in deps:
            if d is op or id(d) in seen:
                continue
            seen.add(id(d))
            if (not d.dma) and d.eng == eng and eng == "pe":
                continue
            op.deps.append(d)
            d.signal = True
        for r in reads:
            r.readers.append(op)
        for w in writes:
            w.last_w = op
            w.readers = []
        self.ops[eng].append(op)
        return op

    def emit(self):
        nc = self.nc
        esem = {}
        for e in ("pe", "act", "dve", "pool"):
            esem[e] = self.stack.enter_context(nc.semaphore(f"es_{e}"))
        for e in ("pe", "act", "dve", "pool", "sp"):
            c = 0
            for op in self.ops[e]:
                if op.dma:
                    continue
                if op.signal:
                    c += 1
                    op.sem = esem.get(e)
                    op.count = c
                    assert e != "sp"
        final = [(t.sem, t.dma_cnt) for t in self.dma_toks]

        def run(e, eng):
            waited = {}
            for op in self.ops[e]:
                for d in op.deps:
                    key = id(d.sem)
                    if waited.get(key, 0) >= d.count:
                        continue
                    eng.wait_ge(d.sem, d.count)
                    waited[key] = d.count
                ins = op.fn(eng)
                if op.signal:
                    ins.then_inc(op.sem, 16 if op.dma else 1)
            if e == "sp":
                for s, c in final:
                    eng.wait_ge(s, c)

        with nc.Block() as block:
            @block.tensor
            def _(eng):
                run("pe", eng)

            @block.scalar
            def _(eng):
                run("act", eng)

            @block.vector
            def _(eng):
                run("dve", eng)

            @block.gpsimd
            def _(eng):
                run("pool", eng)

            @block.sync
            def _(eng):
                run("sp", eng)


class Ctx:
    def __init__(self, nc, stack):
        self.nc = nc
        self.R = Rec(nc, stack)
        R = self.R
        self.banks = [R.ps([128, 512], F32, name=f"bank{i}") for i in range(8)]
        self.bank_tok = [R.tok(f"bank{i}") for i in range(8)]
        self.bank_rr = 0
        self.ones_bf = R.sb([128, 128], BF16, "ones_bf")
        self.t_const = R.tok("const")
        R.add("pool", lambda e: e.memset(self.ones_bf[:], 1.0), writes=[self.t_const])
        self.evac_rr = 0
        self.reserved = set()

    def bank(self):
        while True:
            i = self.bank_rr % 8
            self.bank_rr += 1
            if id(self.banks[i]) not in self.reserved:
                return self.banks[i], self.bank_tok[i]


def load_vec_fm(cx, dram_vec, name):
    R = cx.R
    t = R.sb([128, dram_vec.shape[1]], F32, name)
    tk = R.tok(name)
    R.add("sp", lambda e: e.dma_start(out=t[:], in_=dram_vec[:, :]), writes=[tk], dma_tok=tk)
    return t, tk


def rms_fm(cx, src, t_src, nch, gain, t_gain, dst, t_dst, sq, t_sq, rstd, t_rstd, dim, post=None):
    R = cx.R
    R.add("act", lambda e: e.activation(out=sq[:, 0:nch, :], in_=src[:, 0:nch, :], func=AF.Square),
          reads=[t_src], writes=[t_sq])
    bk, tb = cx.bank()
    for c in range(nch):
        R.add("pe", lambda e, c=c: e.matmul(bk[:, :], lhsT=cx.ones_bf[:], rhs=sq[:, c, :],
                                             start=(c == 0), stop=(c == nch - 1)),
              reads=[t_sq, cx.t_const], writes=[tb])
    R.add("dve", lambda e: e.tensor_scalar(out=rstd[:], in0=bk[:, :], scalar1=1.0 / dim, scalar2=EPS,
                                           op0=ALU.mult, op1=ALU.add), reads=[tb], writes=[t_rstd])
    R.add("act", lambda e: e.activation(out=rstd[:], in_=rstd[:], func=AF.Sqrt), reads=[t_rstd], writes=[t_rstd])
    R.add("dve", lambda e: e.reciprocal(out=rstd[:], in_=rstd[:]), reads=[t_rstd], writes=[t_rstd])
    if post is None:
        for c in range(nch):
            R.add("dve", lambda e, c=c: e.scalar_tensor_tensor(out=dst[:, c, :], in0=src[:, c, :],
                                                                scalar=gain[:, c:c + 1], in1=rstd[:],
                                                                op0=ALU.mult, op1=ALU.mult),
                  reads=[t_src, t_gain, t_rstd], writes=[t_dst])
    else:
        post()


def transpose_in(cx, x_dram, r0, xtok, t_xtok, xT, t_xT, ident, ncol_chunks, alias=()):
    R = cx.R
    W = ncol_chunks * 128
    alias = list(alias)
    for s in range(T // 128):
        R.add("sp", lambda e, s=s: e.dma_start(out=xtok[:, s, 0:W], in_=x_dram[r0 + s * 128:r0 + (s + 1) * 128, :]),
              writes=[t_xtok[s]] + alias, dma_tok=t_xtok[s])
    for c in range(ncol_chunks):
        bk, tb = cx.bank()
        for s in range(T // 128):
            R.add("pe", lambda e, c=c, s=s, bk=bk: e.transpose(out=bk[:, s * 128:(s + 1) * 128],
                                                                in_=xtok[:, s, c * 128:(c + 1) * 128], identity=ident[:]),
                  reads=[t_xtok[s], cx.t_const] + alias, writes=[tb])
        evac(cx, xT[:, c, :], bk[:, :], [tb], [t_xT])


def evac(cx, dst, src, reads, writes):
    R = cx.R
    cx.evac_rr += 1
    if cx.evac_rr % 2 == 0:
        R.add("act", lambda e: e.copy(out=dst, in_=src), reads=reads, writes=writes)
    else:
        R.add("dve", lambda e: e.tensor_copy(out=dst, in_=src), reads=reads, writes=writes)


def transpose_out(cx, srcT, t_src, nch, out_dram, r0, otok, t_otok, ident):
    R = cx.R
    for s in range(T // 128):
        for c4 in range(0, nch, 4):
            bk, tb = cx.bank()
            for j in range(4):
                c = c4 + j
                R.add("pe", lambda e, c=c, s=s, j=j, bk=bk: e.transpose(out=bk[:, j * 128:(j + 1) * 128],
                                                                   in_=srcT[:, c, s * 128:(s + 1) * 128],
                                                                   identity=ident[:]),
                      reads=[t_src, cx.t_const], writes=[tb])
            evac(cx, otok[:, s, c4 * 128:(c4 + 4) * 128], bk[:, :], [tb], [t_otok[s]])
        R.add("sp", lambda e, s=s: e.dma_start(out=out_dram[r0 + s * 128:r0 + (s + 1) * 128, :],
                                               in_=otok[:, s, 0:nch * 128]),
              reads=[t_otok[s]], dma_tok=t_otok[s])


class WStream:
    def __init__(self, cx, kch_max, nbuf, name):
        R = cx.R
        self.cx = cx
        self.bufs = [R.sb([128, kch_max, 512], BF16, f"{name}{i}") for i in range(nbuf)]
        self.toks = [R.tok(f"{name}{i}") for i in range(nbuf)]
        self.rr = 0

    def load(self, w_dram, k0, kch, c0, ncols):
        R = self.cx.R
        i = self.rr % len(self.bufs)
        self.rr += 1
        buf, tk = self.bufs[i], self.toks[i]
        src = w_dram[k0 * 128:(k0 + kch) * 128, c0:c0 + ncols].rearrange("(k p) n -> p k n", p=128)
        R.add("pool", lambda e: e.dma_start(out=buf[:, 0:kch, 0:ncols], in_=src), writes=[tk], dma_tok=tk)
        return buf, tk


def ffn_fm(cx, uT, t_u, wg, wu, wd, actT, t_act, ws, fT, t_f, tmp, t_tmp):
    R = cx.R
    NJ = DFF // 128
    for j4 in range(0, NJ, 4):
        gw, tg = ws.load(wg, 0, KC, j4 * 128, 512)
        uw, tu = ws.load(wu, 0, KC, j4 * 128, 512)
        for j in range(4):
            bg, tbg = cx.bank()
            bu, tbu = cx.bank()
            for k in range(KC):
                R.add("pe", lambda e, k=k, j=j, bg=bg, gw=gw: e.matmul(bg[:, :], lhsT=gw[:, k, j * 128:(j + 1) * 128],
                                                                        rhs=uT[:, k, :], start=(k == 0), stop=(k == KC - 1)),
                      reads=[tg, t_u], writes=[tbg])
            for k in range(KC):
                R.add("pe", lambda e, k=k, j=j, bu=bu, uw=uw: e.matmul(bu[:, :], lhsT=uw[:, k, j * 128:(j + 1) * 128],
                                                                        rhs=uT[:, k, :], start=(k == 0), stop=(k == KC - 1)),
                      reads=[tu, t_u], writes=[tbu])
            jj = j4 + j
            tt = t_tmp[jj % 2]
            tm = tmp[jj % 2]
            R.add("act", lambda e, bg=bg, tm=tm: e.activation(out=tm[:], in_=bg[:, :], func=AF.Silu),
                  reads=[tbg], writes=[tt])
            R.add("dve", lambda e, bu=bu, tm=tm, jj=jj: e.tensor_tensor(out=actT[:, jj, :], in0=tm[:], in1=bu[:, :],
                                                                         op=ALU.mult),
                  reads=[tt, tbu], writes=[t_act])
    for dg in range(4):
        bks = [cx.bank() for _ in range(4)]
        for q in range(4):
            dwt, tdw = ws.load(wd, q * 11, 11, dg * 512, 512)
            for kk in range(11):
                k = q * 11 + kk
                for dd in range(4):
                    bk, tb = bks[dd]
                    R.add("pe", lambda e, kk=kk, k=k, dd=dd, bk=bk, dwt=dwt: e.matmul(
                        bk[:, :], lhsT=dwt[:, kk, dd * 128:(dd + 1) * 128], rhs=actT[:, k, :],
                        start=(k == 0), stop=(k == NJ - 1)), reads=[tdw, t_act], writes=[tb])
        for dd in range(4):
            bk, tb = bks[dd]
            evac(cx, fT[:, dg * 4 + dd, :], bk[:, :], [tb], [t_f])


def resid_norm_add(cx, fT, t_f, gain, t_gain, hT, t_h, sq, t_sq, rstd, t_rstd, coef):
    R = cx.R

    def post():
        for c in range(KC):
            R.add("dve", lambda e, c=c: e.scalar_tensor_tensor(out=fT[:, c, :], in0=fT[:, c, :],
                                                                scalar=gain[:, c:c + 1], in1=rstd[:],
                                                                op0=ALU.mult, op1=ALU.mult),
                  reads=[t_f, t_gain, t_rstd], writes=[t_f])
            R.add("dve", lambda e, c=c: e.scalar_tensor_tensor(out=hT[:, c, :], in0=fT[:, c, :],
                                                                scalar=float(coef), in1=hT[:, c, :],
                                                                op0=ALU.mult, op1=ALU.add),
                  reads=[t_f, t_h], writes=[t_h])

    rms_fm(cx, fT, t_f, KC, gain, t_gain, None, None, sq, t_sq, rstd, t_rstd, D, post=post)


def build_stage1(ntiles=TOK // T):
    nc = bass.Bass("TRN2", target_bir_lowering=False)
    x = nc.dram_tensor("x", [TOK, D], F32, kind="ExternalInput").ap()
    wg = nc.dram_tensor("wg", [D, DFF], F32, kind="ExternalInput").ap()
    wu = nc.dram_tensor("wu", [D, DFF], F32, kind="ExternalInput").ap()
    wd = nc.dram_tensor("wd", [DFF, D], F32, kind="ExternalInput").ap()
    gains = nc.dram_tensor("gains", [128, 3 * KC], F32, kind="ExternalInput").ap()
    ident_d = nc.dram_tensor("ident", [128, 128], F32, kind="ExternalInput").ap()
    h1T = nc.dram_tensor("h1T", [D, TOK], F32, kind="ExternalOutput").ap()
    uT_o = nc.dram_tensor("uT", [D, TOK], BF16, kind="ExternalOutput").ap()
    with contextlib.ExitStack() as stack:
        cx = Ctx(nc, stack)
        R = cx.R
        g_all, t_g = load_vec_fm(cx, gains, "gains_sb")
        ident = R.sb([128, 128], F32, "ident")
        R.add("sp", lambda e: e.dma_start(out=ident[:], in_=ident_d[:, :]), writes=[cx.t_const], dma_tok=cx.t_const)
        big = R.sb([128, KC * T], F32, "big")
        xtok = big[:].rearrange("p (s d) -> p s d", s=4)
        fT = big[:].rearrange("p (c t) -> p c t", c=KC)
        t_xtok = [R.tok(f"xtok{s}") for s in range(4)]
        t_f = R.tok("fT")
        hT = R.sb([128, KC, T], F32, "hT")
        t_h = R.tok("hT")
        uT = R.sb([128, KC, T], BF16, "uT")
        t_u = R.tok("uT")
        sq = R.sb([128, KC, T], BF16, "sq")
        t_sq = R.tok("sq")
        rstd = R.sb([128, T], F32, "rstd")
        t_rstd = R.tok("rstd")
        actT = R.sb([128, DFF // 128, T], BF16, "actT")
        t_act = R.tok("actT")
        tmp = [R.sb([128, T], F32, f"tmp{i}") for i in range(2)]
        t_tmp = [R.tok(f"tmp{i}") for i in range(2)]
        ws = WStream(cx, KC, 3, "w")
        h1v = h1T.rearrange("(c p) t -> p c t", p=128)
        uv = uT_o.rearrange("(c p) t -> p c t", p=128)
        for it in range(ntiles):
            r0 = it * T
            transpose_in(cx, x, r0, xtok, t_xtok, hT, t_h, ident, KC, alias=[t_f])
            rms_fm(cx, hT, t_h, KC, g_all[:, 0:KC], t_g, uT, t_u, sq, t_sq, rstd, t_rstd, D)
            ffn_fm(cx, uT, t_u, wg, wu, wd, actT, t_act, ws, fT, t_f, tmp, t_tmp)
            resid_norm_add(cx, fT, t_f, g_all[:, KC:2 * KC], t_g, hT, t_h, sq, t_sq, rstd, t_rstd, 0.5)
            R.add("sp", lambda e, r0=r0: e.dma_start(out=h1v[:, :, r0:r0 + T], in_=hT[:]), reads=[t_h], dma_tok=t_h)
            rms_fm(cx, hT, t_h, KC, g_all[:, 2 * KC:3 * KC], t_g, uT, t_u, sq, t_sq, rstd, t_rstd, D)
            R.add("sp", lambda e, r0=r0: e.dma_start(out=uv[:, :, r0:r0 + T], in_=uT[:]), reads=[t_u], dma_tok=t_u)
        R.emit()
    return nc


def fm_vec(v):
    return np.ascontiguousarray(np.asarray(v, np.float32).reshape(-1, 128).T)


HD = 128
CH = 64
NCHK = T // CH
C_ID, C_U, C_SL, C_BU, C_BL = 0, 128, 192, 256, 320


def dn_consts():
    c = np.zeros((128, 384), np.float32)
    c[:, 0:128] = np.eye(128, dtype=np.float32)
    p = np.arange(64)[:, None]
    f = np.arange(64)[None, :]
    c[0:64, C_U:C_U + 64] = (p <= f)
    c[0:64, C_SL:C_SL + 64] = (f < p)
    c[0:64, C_BU:C_BU + 64] = 1e4 * (f > p)
    c[0:64, C_BL:C_BL + 64] = 1e4 * (f < p)
    return c


def bc_mid(ap2, n):
    return ap2.unsqueeze(1).to_broadcast([ap2.shape[0], n, ap2.shape[1]])


def bc_last(ap2, n):
    return ap2.unsqueeze(2).to_broadcast([ap2.shape[0], ap2.shape[1], n])


def build_stage2_dn(ntiles=SEQ // T):
    S = ntiles * T
    nc = bass.Bass("TRN2", target_bir_lowering=False)
    uT_all = nc.dram_tensor("uT_all", [D, S], BF16, kind="ExternalInput").ap()
    wdn = nc.dram_tensor("wdn", [D, 1024], F32, kind="ExternalInput").ap()
    wba = nc.dram_tensor("wba", [D, 4], F32, kind="ExternalInput").ap()
    convw_d = nc.dram_tensor("convw", [128, 24], F32, kind="ExternalInput").ap()
    gon_d = nc.dram_tensor("gon", [128, 1], F32, kind="ExternalInput").ap()
    hv_d = nc.dram_tensor("hv", [64, 4], F32, kind="ExternalInput").ap()
    consts_d = nc.dram_tensor("consts", [128, 384], F32, kind="ExternalInput").ap()
    odnT = nc.dram_tensor("odnT", [256, S], BF16, kind="ExternalOutput").ap()
    with contextlib.ExitStack() as stack:
        cx = Ctx(nc, stack)
        R = cx.R
        A = R.add
        tc = cx.t_const
        consts = R.sb([128, 384], F32, "consts")
        A("sp", lambda e: e.dma_start(out=consts[:], in_=consts_d[:, :]), writes=[tc], dma_tok=tc)
        ident = consts[:, 0:128]
        id64 = consts[0:64, 0:64]
        Um = consts[0:64, C_U:C_U + 64]
        SLm = consts[0:64, C_SL:C_SL + 64]
        BUm = consts[0:64, C_BU:C_BU + 64]
        BLm = consts[0:64, C_BL:C_BL + 64]
        convw, t_cw = load_vec_fm(cx, convw_d, "convw_sb")
        gon, t_gon = load_vec_fm(cx, gon_d, "gon_sb")
        hv = R.sb([64, 4], F32, "hv")
        t_hv = R.tok("hv")
        A("sp", lambda e: e.dma_start(out=hv[:], in_=hv_d[:, :]), writes=[t_hv], dma_tok=t_hv)
        ones_f = R.sb([64, 128], F32, "ones_f")
        A("pool", lambda e: e.memset(ones_f[:], 1.0), writes=[tc])
        expA = R.sb([64, 2], F32, "expA")
        A("act", lambda e: e.activation(out=expA[:], in_=hv[:, 2:4], func=AF.Exp), reads=[t_hv], writes=[t_hv])
        wres = R.sb([128, KC, 1024], BF16, "wres")
        t_w = R.tok("wres")
        wv = wdn.rearrange("(k p) n -> p k n", p=128)
        for half in range(2):
            A("pool", lambda e, half=half: e.dma_start(out=wres[:, :, half * 512:(half + 1) * 512],
                                                      in_=wv[:, :, half * 512:(half + 1) * 512]),
              writes=[t_w], dma_tok=t_w)
        wbar = R.sb([128, KC, 4], BF16, "wbar")
        t_wb = R.tok("wbar")
        A("pool", lambda e: e.dma_start(out=wbar[:], in_=wba.rearrange("(k p) n -> p k n", p=128)),
          writes=[t_wb], dma_tok=t_wb)
        u_t = [R.sb([128, KC, T], BF16, f"u_t{i}") for i in range(2)]
        t_ut = [R.tok(f"u_t{i}") for i in range(2)]
        rawbuf = R.sb([128, 6, T + 3], F32, "rawbuf")
        t_raw = [R.tok(f"raw{c}") for c in range(6)]
        A("pool", lambda e: e.memset(rawbuf[:, :, 0:3], 0.0), writes=t_raw)
        cv = R.sb([128, 6, T], F32, "cv")
        t_cv = [R.tok(f"cv{c}") for c in range(6)]
        qk = R.sb([128, 4, T], F32, "qk")
        t_qk = [R.tok(f"qk{c}") for c in range(4)]
        siluz = R.sb([128, 2, T], F32, "siluz")
        t_sz = [R.tok(f"sz{c}") for c in range(2)]
        sqb = R.sb([128, T], BF16, "sqb")
        t_sqb = R.tok("sqb")
        rstd = R.sb([128, T], F32, "rstd")
        t_rstd = R.tok("rstd")
        ba = R.sb([64, NCHK, 4], F32, "ba")
        t_ba = R.tok("ba")
        beta = R.sb([64, NCHK, 2], F32, "beta")
        t_beta = R.tok("beta")
        gt = R.sb([64, NCHK, 2], F32, "gt")
        t_g = R.tok("gt")
        spx = R.sb([64, NCHK, 2], F32, "spx")
        t_spx = R.tok("spx")

        def sbt(shape, name, dt=F32):
            return R.sb(shape, dt, name), R.tok(name)

        gc_col, t_gcc = sbt([64, NCHK], "gc_col")
        egc, t_egc = sbt([64, NCHK], "egc")
        be, t_be = sbt([64, NCHK], "be")
        kgs, t_kgs = sbt([64, NCHK], "kgs")
        Ug, t_Ug = sbt([64, NCHK, 64], "Ug")
        gcrow, t_gcr = sbt([128, NCHK, 64], "gcrow")
        egrow, t_egr = sbt([128, NCHK, 64], "egrow")
        dl, t_dl = sbt([64, NCHK, 64], "dl")
        tmpd, t_tmpd = sbt([64, NCHK, 64], "tmpd")
        decay, t_dec = sbt([64, NCHK, 64], "decay")
        decayT, t_decT = sbt([64, NCHK, 64], "decayT")
        Ktok, t_Kt = sbt([64, NCHK, 128], "Ktok")
        Vtok, t_Vt = sbt([64, NCHK, 128], "Vtok")
        Pb = [sbt([64, NCHK, 64], f"P{i}") for i in range(2)]
        Ptb = [sbt([64, NCHK, 64], f"Pt{i}") for i in range(2)]
        At, t_At = sbt([64, NCHK, 64], "At")
        Vb, t_Vb = sbt([64, NCHK, 128], "Vb")
        Kbg, t_Kbg = sbt([64, NCHK, 128], "Kbg")
        u_sb, t_usb = sbt([64, NCHK, 128], "u_sb")
        wT, t_wT = sbt([128, NCHK, 64], "wT")
        attnT, t_att = sbt([64, NCHK, 64], "attnT")
        qgT, t_qg = sbt([128, NCHK, 64], "qgT")
        kg, t_kg = sbt([64, NCHK, 128], "kg")
        Sst = [[sbt([128, 128], f"S{h}_{i}") for i in range(2)] for h in range(2)]
        for h in range(2):
            A("pool", lambda e, h=h: e.memset(Sst[h][0][0][:], 0.0), writes=[Sst[h][0][1]])
        s_par = [0, 0]
        vnew = [sbt([64, 128], f"vnew{i}") for i in range(2)]
        vn_rr = [0]
        o_sb, t_osb = sbt([128, T], "o_sb")
        o_tmp, t_otmp = sbt([128, T], "o_tmp")
        outb = [sbt([128, 2, T], f"outb{i}", BF16) for i in range(2)]
        ov = odnT.rearrange("(h p) t -> p h t", p=128)

        def bank_mm(nparts, items, reads, name=None):
            bk, tb = cx.bank()
            for (c0, ncol, lh, rh) in items:
                A("pe", lambda e, bk=bk, c0=c0, ncol=ncol, lh=lh, rh=rh: e.matmul(
                    bk[0:nparts, c0:c0 + ncol], lhsT=lh, rhs=rh, start=True, stop=True), reads=reads, writes=[tb])
            return bk, tb

        for it in range(ntiles):
            t0 = it * T
            ut, t_u = u_t[it % 2], t_ut[it % 2]
            A("sp", lambda e, ut=ut, t0=t0: e.dma_start(out=ut[:], in_=uT_all[:, t0:t0 + T].rearrange("(k p) t -> p k t", p=128)),
              writes=[t_u], dma_tok=t_u)
            for c in range(8):
                bk, tb = cx.bank()
                for k in range(KC):
                    A("pe", lambda e, bk=bk, c=c, k=k, ut=ut: e.matmul(bk[:, :], lhsT=wres[:, k, c * 128:(c + 1) * 128],
                                                                      rhs=ut[:, k, :], start=(k == 0), stop=(k == KC - 1)),
                      reads=[t_w, t_u], writes=[tb])
                if c < 6:
                    evac(cx, rawbuf[:, c, 3:T + 3], bk[:, :], [tb], [t_raw[c]])
                else:
                    A("act", lambda e, bk=bk, c=c: e.activation(out=siluz[:, c - 6, :], in_=bk[:, :], func=AF.Silu),
                      reads=[tb], writes=[t_sz[c - 6]])
            for c in range(6):
                A("dve", lambda e, c=c: e.tensor_scalar(out=cv[:, c, :], in0=rawbuf[:, c, 0:T],
                                                        scalar1=convw[:, c * 4:c * 4 + 1], scalar2=None, op0=ALU.mult),
                  reads=[t_raw[c], t_cw], writes=[t_cv[c]])
                for j in range(1, 4):
                    A("dve", lambda e, c=c, j=j: e.scalar_tensor_tensor(out=cv[:, c, :], in0=rawbuf[:, c, j:j + T],
                                                                        scalar=convw[:, c * 4 + j:c * 4 + j + 1],
                                                                        in1=cv[:, c, :], op0=ALU.mult, op1=ALU.add),
                      reads=[t_raw[c], t_cw, t_cv[c]], writes=[t_cv[c]])
                A("act", lambda e, c=c: e.copy(out=rawbuf[:, c, 0:3], in_=rawbuf[:, c, T:T + 3]),
                  reads=[t_raw[c]], writes=[t_raw[c]])
                A("act", lambda e, c=c: e.activation(out=cv[:, c, :], in_=cv[:, c, :], func=AF.Silu),
                  reads=[t_cv[c]], writes=[t_cv[c]])
            for idx in range(4):
                A("act", lambda e, idx=idx: e.activation(out=sqb[:], in_=cv[:, idx, :], func=AF.Square),
                  reads=[t_cv[idx]], writes=[t_sqb])
                bk, tb = cx.bank()
                A("pe", lambda e, bk=bk: e.matmul(bk[:, :], lhsT=cx.ones_bf[:], rhs=sqb[:], start=True, stop=True),
                  reads=[t_sqb, tc], writes=[tb])
                A("dve", lambda e, bk=bk: e.tensor_scalar(out=rstd[:], in0=bk[:, :], scalar1=1e-6, scalar2=None,
                                                          op0=ALU.add), reads=[tb], writes=[t_rstd])
                A("act", lambda e: e.activation(out=rstd[:], in_=rstd[:], func=AF.Sqrt), reads=[t_rstd], writes=[t_rstd])
                A("dve", lambda e: e.reciprocal(out=rstd[:], in_=rstd[:]), reads=[t_rstd], writes=[t_rstd])
                sc = float(HD ** -0.5) if idx < 2 else 1.0
                A("dve", lambda e, idx=idx, sc=sc: e.scalar_tensor_tensor(out=qk[:, idx, :], in0=cv[:, idx, :], scalar=sc,
                                                                          in1=rstd[:], op0=ALU.mult, op1=ALU.mult),
                  reads=[t_cv[idx], t_rstd], writes=[t_qk[idx]])
            bk, tb = cx.bank()
            for ck in range(NCHK):
                for k in range(KC):
                    A("pe", lambda e, bk=bk, ck=ck, k=k, ut=ut: e.matmul(bk[0:64, ck * 4:ck * 4 + 4],
                                                                        lhsT=ut[:, k, ck * CH:(ck + 1) * CH],
                                                                        rhs=wbar[:, k, :], start=(k == 0), stop=(k == KC - 1)),
                      reads=[t_wb, t_u], writes=[tb])
            A("dve", lambda e, bk=bk: e.tensor_copy(out=ba[:].rearrange("p c f -> p (c f)"), in_=bk[0:64, 0:NCHK * 4]),
              reads=[tb], writes=[t_ba])
            A("act", lambda e: e.activation(out=beta[:], in_=ba[:, :, 0:2], func=AF.Sigmoid), reads=[t_ba], writes=[t_beta])
            A("dve", lambda e: e.tensor_tensor(out=spx[:], in0=ba[:, :, 2:4], in1=bc_mid(hv[:, 0:2], NCHK), op=ALU.add),
              reads=[t_ba, t_hv], writes=[t_spx])
            A("act", lambda e: e.activation(out=spx[:], in_=spx[:], func=AF.Exp), reads=[t_spx], writes=[t_spx])
            A("dve", lambda e: e.tensor_scalar(out=spx[:], in0=spx[:], scalar1=1.0, scalar2=None, op0=ALU.add),
              reads=[t_spx], writes=[t_spx])
            A("act", lambda e: e.activation(out=spx[:], in_=spx[:], func=AF.Ln), reads=[t_spx], writes=[t_spx])
            A("dve", lambda e: e.scalar_tensor_tensor(out=gt[:], in0=spx[:], scalar=-1.0, in1=bc_mid(expA[:, 0:2], NCHK),
                                                      op0=ALU.mult, op1=ALU.mult), reads=[t_spx, t_hv], writes=[t_g])
            ob, t_ob = outb[it % 2]
            for h in range(2):
                gh = gt[:, :, h]
                bh = beta[:, :, h]
                qT = qk[:, h, :].rearrange("p (c f) -> p c f", f=CH)
                kT = qk[:, 2 + h, :].rearrange("p (c f) -> p c f", f=CH)
                t_q, t_k, t_v = t_qk[h], t_qk[2 + h], t_cv[4 + h]
                bk, tb = bank_mm(64, [(0, NCHK, Um, gh)], [tc, t_g])
                A("dve", lambda e, bk=bk: e.tensor_copy(out=gc_col[:], in_=bk[0:64, 0:NCHK]), reads=[tb], writes=[t_gcc])
                A("dve", lambda e, gh=gh: e.tensor_tensor(out=Ug[:], in0=bc_mid(Um, NCHK), in1=bc_last(gh, 64), op=ALU.mult),
                  reads=[tc, t_g], writes=[t_Ug])
                bk, tb = bank_mm(128, [(c * 64, 64, ones_f[:, :], Ug[:, c, :]) for c in range(NCHK)], [tc, t_Ug])
                A("act", lambda e, bk=bk: e.copy(out=gcrow[:].rearrange("p c f -> p (c f)"), in_=bk[:, :]),
                  reads=[tb], writes=[t_gcr])
                A("dve", lambda e: e.tensor_tensor(out=dl[:], in0=gcrow[0:64, :, :], in1=bc_last(gc_col[:, :], 64),
                                                   op=ALU.subtract), reads=[t_gcr, t_gcc], writes=[t_dl])
                A("dve", lambda e: e.tensor_tensor(out=tmpd[:], in0=dl[:], in1=bc_mid(BUm, NCHK), op=ALU.add),
                  reads=[t_dl, tc], writes=[t_tmpd])
                A("act", lambda e: e.activation(out=decay[:], in_=tmpd[:], func=AF.Exp, scale=-1.0),
                  reads=[t_tmpd], writes=[t_dec])
                A("dve", lambda e: e.tensor_tensor(out=tmpd[:], in0=dl[:], in1=bc_mid(BLm, NCHK), op=ALU.subtract),
                  reads=[t_dl, tc, t_dec], writes=[t_tmpd])
                A("act", lambda e: e.activation(out=decayT[:], in_=tmpd[:], func=AF.Exp), reads=[t_tmpd], writes=[t_decT])
                A("act", lambda e: e.activation(out=egrow[:], in_=gcrow[:], func=AF.Exp), reads=[t_gcr], writes=[t_egr])
                A("act", lambda e: e.activation(out=kgs[:], in_=dl[:, :, 63], func=AF.Exp), reads=[t_dl], writes=[t_kgs])
                A("act", lambda e: e.activation(out=egc[:], in_=gc_col[:], func=AF.Exp), reads=[t_gcc], writes=[t_egc])
                for (src, t_src, dst, t_dst) in ((kT, t_k, Ktok, t_Kt),
                                                 (cv[:, 4 + h, :].rearrange("p (c f) -> p c f", f=CH), t_v, Vtok, t_Vt)):
                    for half in range(2):
                        bk, tb = cx.bank()
                        for c4 in range(4):
                            c = half * 4 + c4
                            A("pe", lambda e, bk=bk, c4=c4, c=c, src=src: e.transpose(out=bk[0:64, c4 * 128:(c4 + 1) * 128],
                                                                                  in_=src[:, c, :], identity=ident),
                              reads=[t_src, tc], writes=[tb])
                        evac(cx, dst[:, half * 4:half * 4 + 4, :].rearrange("p c f -> p (c f)"), bk[0:64, :], [tb], [t_dst])
                bk, tb = bank_mm(64, [(c * 64, 64, kT[:, c, :], kT[:, c, :]) for c in range(NCHK)], [t_k])
                P0, t_P0 = Pb[0]
                Pt0, t_Pt0 = Ptb[0]
                A("dve", lambda e, bk=bk, P0=P0: e.tensor_tensor(out=P0[:].rearrange("p c f -> p (c f)"), in0=bk[0:64, :],
                                                               in1=decay[:].rearrange("p c f -> p (c f)"), op=ALU.mult),
                  reads=[tb, t_dec], writes=[t_P0])
                A("dve", lambda e, P0=P0, bh=bh: e.tensor_tensor(out=P0[:], in0=P0[:], in1=bc_last(bh, 64), op=ALU.mult),
                  reads=[t_P0, t_beta], writes=[t_P0])
                A("dve", lambda e, P0=P0: e.tensor_tensor(out=P0[:], in0=P0[:], in1=bc_mid(SLm, NCHK), op=ALU.mult),
                  reads=[t_P0, tc], writes=[t_P0])
                bk, tb = cx.bank()
                for c in range(NCHK):
                    A("pe", lambda e, bk=bk, c=c, P0=P0: e.transpose(out=bk[0:64, c * 64:(c + 1) * 64], in_=P0[:, c, :],
                                                                    identity=id64), reads=[t_P0, tc], writes=[tb])
                A("act", lambda e, bk=bk, Pt0=Pt0: e.copy(out=Pt0[:].rearrange("p c f -> p (c f)"), in_=bk[0:64, :]),
                  reads=[tb], writes=[t_Pt0])
                A("dve", lambda e, Pt0=Pt0: e.tensor_tensor(out=At[:], in0=bc_mid(id64, NCHK), in1=Pt0[:], op=ALU.subtract),
                  reads=[t_Pt0, tc], writes=[t_At])
                for lv in range(5):
                    Pc, t_Pc = Pb[lv % 2]
                    Ptc, t_Ptc = Ptb[lv % 2]
                    Pn, t_Pn = Pb[(lv + 1) % 2]
                    Ptn, t_Ptn = Ptb[(lv + 1) % 2]
                    bk, tb = bank_mm(64, [(c * 64, 64, Ptc[:, c, :], Pc[:, c, :]) for c in range(NCHK)], [t_Pc, t_Ptc])
                    A("dve", lambda e, bk=bk, Pn=Pn: e.tensor_copy(out=Pn[:].rearrange("p c f -> p (c f)"), in_=bk[0:64, :]),
                      reads=[tb], writes=[t_Pn])
                    bk, tb = bank_mm(64, [(c * 64, 64, Pc[:, c, :], Ptc[:, c, :]) for c in range(NCHK)], [t_Pc, t_Ptc])
                    A("act", lambda e, bk=bk, Ptn=Ptn: e.copy(out=Ptn[:].rearrange("p c f -> p (c f)"), in_=bk[0:64, :]),
                      reads=[tb], writes=[t_Ptn])
                    bk, tb = bank_mm(64, [(c * 64, 64, Pn[:, c, :], At[:, c, :]) for c in range(NCHK)], [t_Pn, t_At])
                    A("dve", lambda e, bk=bk: e.tensor_tensor(out=At[:].rearrange("p c f -> p (c f)"),
                                                              in0=At[:].rearrange("p c f -> p (c f)"), in1=bk[0:64, :],
                                                              op=ALU.add), reads=[tb, t_At], writes=[t_At])
                A("dve", lambda e, bh=bh: e.tensor_tensor(out=Vb[:], in0=Vtok[:], in1=bc_last(bh, 128), op=ALU.mult),
                  reads=[t_Vt, t_beta], writes=[t_Vb])
                A("dve", lambda e, bh=bh: e.tensor_tensor(out=be[:], in0=egc[:], in1=bh, op=ALU.mult),
                  reads=[t_egc, t_beta], writes=[t_be])
                A("dve", lambda e: e.tensor_tensor(out=Kbg[:], in0=Ktok[:], in1=bc_last(be[:, :], 128), op=ALU.mult),
                  reads=[t_Kt, t_be], writes=[t_Kbg])
                for half in range(2):
                    bk, tb = bank_mm(64, [(c4 * 128, 128, At[:, half * 4 + c4, :], Vb[:, half * 4 + c4, :]) for c4 in range(4)],
                                     [t_At, t_Vb])
                    evac(cx, u_sb[:, half * 4:half * 4 + 4, :].rearrange("p c f -> p (c f)"), bk[0:64, :], [tb], [t_usb])
                bk, tb = bank_mm(128, [(c * 64, 64, Kbg[:, c, :], At[:, c, :]) for c in range(NCHK)], [t_At, t_Kbg])
                evac(cx, wT[:].rearrange("p c f -> p (c f)"), bk[:, :], [tb], [t_wT])
                bk, tb = bank_mm(64, [(c * 64, 64, kT[:, c, :], qT[:, c, :]) for c in range(NCHK)], [t_k, t_q])
                A("dve", lambda e, bk=bk: e.tensor_tensor(out=attnT[:].rearrange("p c f -> p (c f)"), in0=bk[0:64, :],
                                                          in1=decayT[:].rearrange("p c f -> p (c f)"), op=ALU.mult),
                  reads=[tb, t_decT], writes=[t_att])
                A("dve", lambda e, qT=qT: e.tensor_tensor(out=qgT[:], in0=qT, in1=egrow[:], op=ALU.mult),
                  reads=[t_q, t_egr], writes=[t_qg])
                A("dve", lambda e: e.tensor_tensor(out=kg[:], in0=Ktok[:], in1=bc_last(kgs[:, :], 128), op=ALU.mult),
                  reads=[t_Kt, t_kgs], writes=[t_kg])
                obk, t_obk = cx.bank()
                cx.reserved.add(id(obk))
                for c in range(NCHK):
                    Sc, t_Sc = Sst[h][s_par[h]]
                    Sn, t_Sn = Sst[h][1 - s_par[h]]
                    s_par[h] = 1 - s_par[h]
                    vn, t_vn = vnew[vn_rr[0] % 2]
                    vn_rr[0] += 1
                    bk, tb = bank_mm(64, [(0, 128, wT[:, c, :], Sc[:, :])], [t_wT, t_Sc])
                    A("dve", lambda e, bk=bk, c=c, vn=vn: e.tensor_tensor(out=vn[:], in0=u_sb[:, c, :], in1=bk[0:64, 0:128],
                                                                         op=ALU.subtract), reads=[tb, t_usb], writes=[t_vn])
                    A("pe", lambda e, c=c, Sc=Sc, obk=obk: e.matmul(obk[:, c * 64:(c + 1) * 64], lhsT=Sc[:, :], rhs=qgT[:, c, :],
                                                                   start=True, stop=False), reads=[t_Sc, t_qg], writes=[t_obk])
                    A("pe", lambda e, c=c, vn=vn, obk=obk: e.matmul(obk[:, c * 64:(c + 1) * 64], lhsT=vn[:, :], rhs=attnT[:, c, :],
                                                                   start=False, stop=True), reads=[t_vn, t_att], writes=[t_obk])
                    bk, tb = bank_mm(128, [(0, 128, kg[:, c, :], vn[:, :])], [t_kg, t_vn])
                    A("dve", lambda e, bk=bk, c=c, Sc=Sc, Sn=Sn: e.scalar_tensor_tensor(
                        out=Sn[:], in0=Sc[:], scalar=egrow[:, c, 63:64], in1=bk[:, 0:128], op0=ALU.mult, op1=ALU.add),
                      reads=[tb, t_Sc, t_egr], writes=[t_Sn])
                A("act", lambda e, obk=obk: e.copy(out=o_sb[:], in_=obk[:, :]), reads=[t_obk], writes=[t_osb])
                cx.reserved.discard(id(obk))
                A("act", lambda e: e.activation(out=sqb[:], in_=o_sb[:], func=AF.Square), reads=[t_osb], writes=[t_sqb])
                bk, tb = cx.bank()
                A("pe", lambda e, bk=bk: e.matmul(bk[:, :], lhsT=cx.ones_bf[:], rhs=sqb[:], start=True, stop=True),
                  reads=[t_sqb, tc], writes=[tb])
                A("dve", lambda e, bk=bk: e.tensor_scalar(out=rstd[:], in0=bk[:, :], scalar1=1.0 / HD, scalar2=EPS,
                                                          op0=ALU.mult, op1=ALU.add), reads=[tb], writes=[t_rstd])
                A("act", lambda e: e.activation(out=rstd[:], in_=rstd[:], func=AF.Sqrt), reads=[t_rstd], writes=[t_rstd])
                A("dve", lambda e: e.reciprocal(out=rstd[:], in_=rstd[:]), reads=[t_rstd], writes=[t_rstd])
                A("dve", lambda e: e.scalar_tensor_tensor(out=o_tmp[:], in0=o_sb[:], scalar=gon[:, 0:1], in1=rstd[:],
                                                          op0=ALU.mult, op1=ALU.mult),
                  reads=[t_osb, t_gon, t_rstd], writes=[t_otmp])
                A("dve", lambda e, h=h, ob=ob: e.tensor_tensor(out=ob[:, h, :], in0=o_tmp[:], in1=siluz[:, h, :], op=ALU.mult),
                  reads=[t_otmp, t_sz[h]], writes=[t_ob])
            A("sp", lambda e, ob=ob, t0=t0: e.dma_start(out=ov[:, :, t0:t0 + T], in_=ob[:]), reads=[t_ob], dma_tok=t_ob)
        R.emit()
    return nc


SB_ONE = 128
SB_NM0 = 129
SB_VM0 = 129 + 4 * 512
SB_TOT = 129 + 8 * 512


def sb_consts():
    c = np.zeros((128, SB_TOT), np.float32)
    c[:, 0:128] = np.eye(128, dtype=np.float32)
    t = np.arange(128)[:, None]
    j = np.arange(128)[None, :]
    tri = (t + j < 128).astype(np.float32)
    for q4 in range(4):
        for b in range(4):
            blk = c[:, SB_NM0 + q4 * 512 + b * 128: SB_NM0 + q4 * 512 + (b + 1) * 128]
            if 3 - b > q4:
                blk[:] = 1.0
            elif 3 - b == q4:
                blk[:] = tri
    c[:, SB_VM0:SB_VM0 + 4 * 512] = 1.0 - c[:, SB_NM0:SB_NM0 + 4 * 512]
    c[:, SB_ONE] = 1.0
    return c


def build_stage2_sb(ntiles=SEQ // T, heads=(0, 1)):
    S = ntiles * T
    NB = S // 128
    nc = bass.Bass("TRN2", target_bir_lowering=False)
    uT_all = nc.dram_tensor("uT_all", [D, S], BF16, kind="ExternalInput").ap()
    uT_rev = nc.dram_tensor("uT_rev", [D, S], BF16, kind="ExternalInput").ap()
    wsb = nc.dram_tensor("wsb", [D, 768], F32, kind="ExternalInput").ap()
    consts_d = nc.dram_tensor("consts", [128, SB_TOT], F32, kind="ExternalInput").ap()
    osbT = nc.dram_tensor("osbT", [256, S], BF16, kind="ExternalOutput").ap()
    with contextlib.ExitStack() as stack:
        cx = Ctx(nc, stack)
        R = cx.R
        A = R.add
        tc = cx.t_const
        consts = R.sb([128, 129], F32, "consts")
        A("sp", lambda e: e.dma_start(out=consts[:], in_=consts_d[:, 0:129]), writes=[tc], dma_tok=tc)
        ones_c = consts[:, 128:129]
        masks = R.sb([128, 8 * 512], BF16, "masks")
        t_mk = R.tok("masks")
        A("pool", lambda e: e.dma_start(out=masks[:], in_=consts_d[:, SB_NM0:SB_TOT]), writes=[t_mk], dma_tok=t_mk)
        ident_bf = R.sb([128, 128], BF16, "ident_bf")
        A("dve", lambda e: e.tensor_copy(out=ident_bf[:], in_=consts[:, 0:128]), reads=[tc], writes=[tc])
        ones_f = ones_c.to_broadcast([128, T])
        wres = R.sb([128, KC, 768], BF16, "wres")
        t_w = R.tok("wres")
        wv = wsb.rearrange("(k p) n -> p k n", p=128)
        for half in range(2):
            A("pool", lambda e, half=half: e.dma_start(out=wres[:, :, half * 384:(half + 1) * 384],
                                                      in_=wv[:, :, half * 384:(half + 1) * 384]),
              writes=[t_w], dma_tok=t_w)
        uf, tuf = R.sb([128, KC, T], BF16, "u_f"), R.tok("u_f")
        ur, tur = R.sb([128, KC, T], BF16, "u_r"), R.tok("u_r")
        QT = R.sb([128, S], BF16, "QT")
        KTr = R.sb([128, S], BF16, "KTr")
        dVr = R.sb([128, NB, 128], BF16, "dVr")
        t_Q = [R.tok(f"Q{i}") for i in range(ntiles)]
        t_K = [R.tok(f"K{i}") for i in range(ntiles)]
        t_V = [R.tok(f"V{i}") for i in range(ntiles)]
        VTb = [R.sb([128, T + 1], F32, f"VTb{i}") for i in range(2)]
        t_VT = [R.tok(f"VTb{i}") for i in range(2)]
        dVT = R.sb([128, T], BF16, "dVT")
        t_dVT = R.tok("dVT")
        smb = [[R.sb([128, T], F32, f"smb{i}_{k}") for k in range(2)] for i in range(4)]
        t_sm = [[R.tok(f"smb{i}_{k}") for k in range(2)] for i in range(4)]
        Pb = [[R.sb([128, T], BF16, f"Pb{i}_{k}") for k in range(2)] for i in range(4)]
        t_P = [[R.tok(f"Pb{i}_{k}") for k in range(2)] for i in range(4)]
        Pm = [R.sb([128, T], BF16, f"Pm{i}") for i in range(4)]
        t_Pm = [R.tok(f"Pm{i}") for i in range(4)]
        vsh, t_vsh = R.sb([128, T], F32, "vsh"), R.tok("vsh")
        ATs = R.sb([128, 4, T], BF16, "ATs")
        t_AT = [R.tok(f"ATs_{hh}") for hh in range(2)]
        osb, t_os = R.sb([128, T], BF16, "osb"), R.tok("osb")
        scale = float(HD ** -0.5)
        uva = uT_all.rearrange("(k p) t -> p k t", p=128)
        uvr = uT_rev.rearrange("(k p) t -> p k t", p=128)
        gstep = 0
        for h in heads:
            for n, it in enumerate(reversed(range(ntiles))):
                t0 = it * T
                A("sp", lambda e, t0=t0: e.dma_start(out=uf[:], in_=uva[:, :, t0:t0 + T]), writes=[tuf], dma_tok=tuf)
                A("sp", lambda e, t0=t0: e.dma_start(out=ur[:], in_=uvr[:, :, t0:t0 + T]), writes=[tur], dma_tok=tur)
                for (src, tsrc, col, dst, tdst) in ((uf, tuf, h * 128, QT, t_Q[it]), (ur, tur, 256 + h * 128, KTr, t_K[it])):
                    bk, tb = cx.bank()
                    for k in range(KC):
                        A("pe", lambda e, bk=bk, k=k, src=src, col=col: e.matmul(bk[:, :], lhsT=wres[:, k, col:col + 128],
                                                                                rhs=src[:, k, :], start=(k == 0), stop=(k == KC - 1)),
                          reads=[t_w, tsrc], writes=[tb])
                    evac(cx, dst[:, t0:t0 + T], bk[:, :], [tb], [tdst])
                vt, tvt = VTb[n % 2], t_VT[n % 2]
                vtp, tvtp = VTb[(n + 1) % 2], t_VT[(n + 1) % 2]
                bk, tb = cx.bank()
                for k in range(KC):
                    A("pe", lambda e, bk=bk, k=k, h=h: e.matmul(bk[:, :], lhsT=wres[:, k, 512 + h * 128:512 + (h + 1) * 128],
                                                               rhs=ur[:, k, :], start=(k == 0), stop=(k == KC - 1)),
                      reads=[t_w, tur], writes=[tb])
                A("act", lambda e, bk=bk, vt=vt: e.copy(out=vt[:, 0:T], in_=bk[:, :]), reads=[tb], writes=[tvt])
                if n == 0:
                    A("pool", lambda e, vt=vt: e.memset(vt[:, T:T + 1], 0.0), reads=[tvt], writes=[tvt])
                else:
                    A("act", lambda e, vt=vt, vtp=vtp: e.copy(out=vt[:, T:T + 1], in_=vtp[:, 0:1]), reads=[tvtp, tvt], writes=[tvt])
                A("pool", lambda e, vt=vt: e.tensor_tensor(out=dVT[:], in0=vt[:, 1:T + 1], in1=vt[:, 0:T], op=ALU.subtract),
                  reads=[tvt], writes=[t_dVT])
                bk2, tb2 = cx.bank()
                bkb = bk2[:, :].bitcast(BF16)
                for b in range(4):
                    A("pe", lambda e, bkb=bkb, b=b: e.transpose(out=bkb[:, b * 128:(b + 1) * 128], in_=dVT[:, b * 128:(b + 1) * 128],
                                                               identity=ident_bf[:]), reads=[t_dVT, tc], writes=[tb2])
                evac(cx, dVr[:, it * 4:it * 4 + 4, :].rearrange("p b d -> p (b d)"), bkb[:, 0:T], [tb2], [t_V[it]])
            steps = [(qg, i) for qg in range(NB // 4) for i in range(qg + 1)]

            def issue_scores(qg, i, par):
                j0 = (NB - 4 - 4 * qg + 4 * i) * 128
                kt = j0 // T
                zb = []
                for q4 in range(4):
                    qb = qg * 4 + q4
                    bk, tb = cx.bank()
                    zb.append((bk, tb))
                    A("pe", lambda e, bk=bk, qb=qb, j0=j0: e.matmul(bk[:, :], lhsT=QT[:, qb * 128:(qb + 1) * 128],
                                                                   rhs=KTr[:, j0:j0 + T], start=True, stop=True),
                      reads=[t_Q[qg], t_K[kt]], writes=[tb])
                for q4 in range(4):
                    bk, tb = zb[q4]
                    sm, tsm = smb[q4][par], t_sm[q4][par]
                    A("act", lambda e, bk=bk, sm=sm: e.activation(out=sm[:], in_=bk[:, :], func=AF.Sigmoid, scale=-scale),
                      reads=[tb], writes=[tsm])
                    if i == 0:
                        nm = masks[:, q4 * 512:(q4 + 1) * 512]
                        A("dve", lambda e, sm=sm, nm=nm: e.tensor_tensor(out=sm[:], in0=sm[:], in1=nm, op=ALU.max),
                          reads=[tsm, t_mk], writes=[tsm])

            issue_scores(steps[0][0], steps[0][1], gstep % 2)
            obk = t_obk = None
            for n, (qg, i) in enumerate(steps):
                par = gstep % 2
                gstep += 1
                if n + 1 < len(steps):
                    issue_scores(steps[n + 1][0], steps[n + 1][1], gstep % 2)
                ntile = qg + 1
                JB = NB - 4 - 4 * qg + 4 * i
                kt = JB // 4
                if i == 0:
                    obk, t_obk = cx.bank()
                    cx.reserved.add(id(obk))
                    tq0 = qg * T
                    if tq0 == 0:
                        A("pool", lambda e: e.memset(uf[:, :, 0:1], 0.0), writes=[tuf])
                        A("sp", lambda e: e.dma_start(out=uf[:, :, 1:T], in_=uva[:, :, 0:T - 1]), writes=[tuf], dma_tok=tuf)
                    else:
                        A("sp", lambda e, tq0=tq0: e.dma_start(out=uf[:], in_=uva[:, :, tq0 - 1:tq0 + T - 1]), writes=[tuf], dma_tok=tuf)
                for q4 in range(4):
                    sm, tsm = smb[q4][par], t_sm[q4][par]
                    Pc, tPc = Pb[q4][par], t_P[q4][par]
                    Pp, tPp = Pb[q4][1 - par], t_P[q4][1 - par]
                    if i == 0:
                        A("dve", lambda e, sm=sm, Pc=Pc: e.tensor_tensor_scan(out=Pc[:], data0=sm[:], data1=ones_f, initial=ones_c,
                                                                             op0=ALU.mult, op1=ALU.mult),
                          reads=[tsm, tc], writes=[tPc])
                    else:
                        A("dve", lambda e, sm=sm, Pc=Pc, Pp=Pp: e.tensor_tensor_scan(out=Pc[:], data0=sm[:], data1=ones_f,
                                                                                    initial=Pp[:, T - 1:T], op0=ALU.mult, op1=ALU.mult),
                          reads=[tsm, tPp, tc], writes=[tPc])
                if i == 0:
                    for q4 in range(4):
                        vm = masks[:, (4 + q4) * 512:(5 + q4) * 512]
                        A("pool", lambda e, q4=q4, vm=vm, par=par: e.tensor_tensor(out=Pm[q4][:], in0=Pb[q4][par][:], in1=vm, op=ALU.mult),
                          reads=[t_P[q4][par], t_mk], writes=[t_Pm[q4]])
                for hh in range(2):
                    bk2, tb2 = cx.bank()
                    bkb = bk2[:, :].bitcast(BF16)
                    for bb in range(2):
                        b = hh * 2 + bb
                        for q4 in range(4):
                            Pc, tPc = (Pm[q4], t_Pm[q4]) if i == 0 else (Pb[q4][par], t_P[q4][par])
                            A("pe", lambda e, bkb=bkb, b=b, bb=bb, q4=q4, Pc=Pc: e.transpose(
                                out=bkb[:, bb * 512 + q4 * 128:bb * 512 + (q4 + 1) * 128], in_=Pc[:, b * 128:(b + 1) * 128],
                                identity=ident_bf[:]), reads=[tPc, tc], writes=[tb2])
                    A("act", lambda e, bkb=bkb, hh=hh: e.copy(out=ATs[:, hh * 2:hh * 2 + 2, :].rearrange("p b q -> p (b q)"),
                                                             in_=bkb[:, :]), reads=[tb2], writes=[t_AT[hh]])
                for b in range(4):
                    first = (i == 0 and b == 0)
                    last = (i == ntile - 1 and b == 3)
                    A("pe", lambda e, obk=obk, b=b, JB=JB, first=first, last=last: e.matmul(
                        obk[:, :], lhsT=dVr[:, JB + b, :], rhs=ATs[:, b, :], start=first, stop=last),
                      reads=[t_AT[b // 2], t_V[kt]], writes=[t_obk])
                if i == ntile - 1:
                    bkv, tbv = cx.bank()
                    for k in range(KC):
                        A("pe", lambda e, bkv=bkv, k=k, h=h: e.matmul(bkv[:, :], lhsT=wres[:, k, 512 + h * 128:512 + (h + 1) * 128],
                                                                     rhs=uf[:, k, :], start=(k == 0), stop=(k == KC - 1)),
                          reads=[t_w, tuf], writes=[tbv])
                    A("act", lambda e, bkv=bkv: e.copy(out=vsh[:], in_=bkv[:, :]), reads=[tbv], writes=[t_vsh])
                    A("dve", lambda e, obk=obk: e.tensor_tensor(out=osb[:], in0=obk[:, :], in1=vsh[:], op=ALU.add),
                      reads=[t_obk, t_vsh], writes=[t_os])
                    cx.reserved.discard(id(obk))
                    A("sp", lambda e, qg=qg, h=h: e.dma_start(out=osbT[h * 128:(h + 1) * 128, qg * T:(qg + 1) * T], in_=osb[:]),
                      reads=[t_os], dma_tok=t_os)
        R.emit()
    return nc


def linear_fm(cx, ws, w_dram, kch, xT, t_x, col0, nout, consume):
    R = cx.R
    for c4 in range(0, nout, 4):
        n = min(4, nout - c4)
        wt, tw = ws.load(w_dram, 0, kch, col0 + c4 * 128, n * 128)
        for j in range(n):
            bk, tb = cx.bank()
            for k in range(kch):
                R.add("pe", lambda e, bk=bk, k=k, j=j, wt=wt: e.matmul(bk[:, :], lhsT=wt[:, k, j * 128:(j + 1) * 128],
                                                                      rhs=xT[:, k, :], start=(k == 0), stop=(k == kch - 1)),
                      reads=[tw, t_x], writes=[tb])
            consume(c4 + j, bk, tb)


def build_stage3(ntiles=TOK // T):
    nc = bass.Bass("TRN2", target_bir_lowering=False)
    NTK = ntiles * T
    h1T = nc.dram_tensor("h1T", [D, NTK], F32, kind="ExternalInput").ap()
    uT_d = nc.dram_tensor("uT", [D, NTK], BF16, kind="ExternalInput").ap()
    odn_d = nc.dram_tensor("odnT", [D, NTK], BF16, kind="ExternalInput").ap()
    osb_d = nc.dram_tensor("osbT", [D, NTK], BF16, kind="ExternalInput").ap()
    pT_d = nc.dram_tensor("pT", [PLE, NTK], F32, kind="ExternalInput").ap()
    wgate = nc.dram_tensor("wgate", [D, 2 * D], F32, kind="ExternalInput").ap()
    wbd = nc.dram_tensor("wbd", [D, D], F32, kind="ExternalInput").ap()
    wbs = nc.dram_tensor("wbs", [D, D], F32, kind="ExternalInput").ap()
    wout = nc.dram_tensor("wout", [D, D], F32, kind="ExternalInput").ap()
    wg = nc.dram_tensor("wg", [D, DFF], F32, kind="ExternalInput").ap()
    wu = nc.dram_tensor("wu", [D, DFF], F32, kind="ExternalInput").ap()
    wd = nc.dram_tensor("wd", [DFF, D], F32, kind="ExternalInput").ap()
    wpg = nc.dram_tensor("wpg", [D, D], F32, kind="ExternalInput").ap()
    wpp = nc.dram_tensor("wpp", [PLE, D], F32, kind="ExternalInput").ap()
    gains = nc.dram_tensor("gains", [128, 5 * KC], F32, kind="ExternalInput").ap()
    ident_d = nc.dram_tensor("ident", [128, 128], F32, kind="ExternalInput").ap()
    out = nc.dram_tensor("out", [NTK, D], F32, kind="ExternalOutput").ap()
    with contextlib.ExitStack() as stack:
        cx = Ctx(nc, stack)
        R = cx.R
        A = R.add
        g_all, t_g = load_vec_fm(cx, gains, "gains_sb")
        ident = R.sb([128, 128], F32, "ident")
        A("sp", lambda e: e.dma_start(out=ident[:], in_=ident_d[:, :]), writes=[cx.t_const], dma_tok=cx.t_const)
        big = R.sb([128, KC * T], F32, "big")
        otok = big[:].rearrange("p (s d) -> p s d", s=4)
        fT = big[:].rearrange("p (c t) -> p c t", c=KC)
        t_f = R.tok("fT")
        t_otok = [t_f] * 4
        hT = R.sb([128, KC, T], F32, "hT")
        t_h = R.tok("hT")
        uT = R.sb([128, KC, T], BF16, "uT")
        t_u = R.tok("uT")
        sq = R.sb([128, KC, T], BF16, "sq")
        t_sq = R.tok("sq")
        rstd = R.sb([128, T], F32, "rstd")
        t_rstd = R.tok("rstd")
        actT = R.sb([128, DFF // 128, T], BF16, "actT")
        t_act = R.tok("actT")
        odn = actT[:, 0:KC, :]
        osb = actT[:, KC:2 * KC, :]
        tmp = [R.sb([128, T], F32, f"tmp{i}") for i in range(2)]
        t_tmp = [R.tok(f"tmp{i}") for i in range(2)]
        pTb = R.sb([128, 2, T], BF16, "pTb")
        t_p = R.tok("pTb")
        ws = WStream(cx, KC, 3, "w")
        hv = h1T.rearrange("(c p) t -> p c t", p=128)
        uv = uT_d.rearrange("(c p) t -> p c t", p=128)
        dnv = odn_d.rearrange("(c p) t -> p c t", p=128)
        sbv = osb_d.rearrange("(c p) t -> p c t", p=128)
        pv = pT_d.rearrange("(c p) t -> p c t", p=128)
        for it in range(ntiles):
            r0 = it * T
            A("sp", lambda e, r0=r0: e.dma_start(out=hT[:], in_=hv[:, :, r0:r0 + T]), writes=[t_h], dma_tok=t_h)
            A("sp", lambda e, r0=r0: e.dma_start(out=uT[:], in_=uv[:, :, r0:r0 + T]), writes=[t_u], dma_tok=t_u)
            t_dn = R.tok("odn_ld")
            A("sp", lambda e, r0=r0: e.dma_start(out=odn, in_=dnv[:, :, r0:r0 + T]), writes=[t_act], dma_tok=t_dn)
            A("sp", lambda e, r0=r0: e.dma_start(out=osb, in_=sbv[:, :, r0:r0 + T]), writes=[t_act], dma_tok=t_dn)
            A("pool", lambda e, r0=r0: e.dma_start(out=pTb[:], in_=pv[:, :, r0:r0 + T]), writes=[t_p], dma_tok=t_p)
            for (gcol, wb, xs, tx, second) in ((0, wbd, odn, t_act, False), (D, wbs, osb, t_act, True)):
                for c4 in range(0, KC, 4):
                    gtile, tgw = ws.load(wgate, 0, KC, gcol + c4 * 128, 512)
                    btile, tbw = ws.load(wb, 0, KC, c4 * 128, 512)
                    for j in range(4):
                        c = c4 + j
                        bg, tbg = cx.bank()
                        for k in range(KC):
                            A("pe", lambda e, bg=bg, k=k, j=j, gtile=gtile: e.matmul(
                                bg[:, :], lhsT=gtile[:, k, j * 128:(j + 1) * 128], rhs=uT[:, k, :],
                                start=(k == 0), stop=(k == KC - 1)), reads=[tgw, t_u], writes=[tbg])
                        bb, tbb = cx.bank()
                        for k in range(KC):
                            A("pe", lambda e, bb=bb, k=k, j=j, btile=btile, xs=xs: e.matmul(
                                bb[:, :], lhsT=btile[:, k, j * 128:(j + 1) * 128], rhs=xs[:, k, :],
                                start=(k == 0), stop=(k == KC - 1)), reads=[tbw, tx], writes=[tbb])
                        tt, tm = t_tmp[c % 2], tmp[c % 2]
                        A("act", lambda e, bg=bg, tm=tm: e.activation(out=tm[:], in_=bg[:, :], func=AF.Sigmoid),
                          reads=[tbg], writes=[tt])
                        if not second:
                            A("dve", lambda e, bb=bb, tm=tm, c=c: e.tensor_tensor(out=fT[:, c, :], in0=tm[:], in1=bb[:, :],
                                                                                 op=ALU.mult), reads=[tt, tbb], writes=[t_f])
                        else:
                            A("dve", lambda e, bb=bb, tm=tm: e.tensor_tensor(out=tm[:], in0=tm[:], in1=bb[:, :], op=ALU.mult),
                              reads=[tt, tbb], writes=[tt])
                            A("dve", lambda e, tm=tm, c=c: e.tensor_tensor(out=sq[:, c, :], in0=tm[:], in1=fT[:, c, :],
                                                                          op=ALU.add), reads=[tt, t_f], writes=[t_sq])
            linear_fm(cx, ws, wout, KC, sq, t_sq, 0, KC,
                      lambda c, bk, tb: evac(cx, fT[:, c, :], bk[:, :], [tb], [t_f]))
            resid_norm_add(cx, fT, t_f, g_all[:, 0:KC], t_g, hT, t_h, sq, t_sq, rstd, t_rstd, 1.0)
            rms_fm(cx, hT, t_h, KC, g_all[:, KC:2 * KC], t_g, uT, t_u, sq, t_sq, rstd, t_rstd, D)
            ffn_fm(cx, uT, t_u, wg, wu, wd, actT, t_act, ws, fT, t_f, tmp, t_tmp)
            resid_norm_add(cx, fT, t_f, g_all[:, 2 * KC:3 * KC], t_g, hT, t_h, sq, t_sq, rstd, t_rstd, 0.5)
            rms_fm(cx, hT, t_h, KC, g_all[:, 3 * KC:4 * KC], t_g, uT, t_u, sq, t_sq, rstd, t_rstd, D)
            for c4 in range(0, KC, 4):
                gtile, tgw = ws.load(wpg, 0, KC, c4 * 128, 512)
                ptile, tpw = ws.load(wpp, 0, 2, c4 * 128, 512)
                for j in range(4):
                    c = c4 + j
                    bg, tbg = cx.bank()
                    for k in range(KC):
                        A("pe", lambda e, bg=bg, k=k, j=j, gtile=gtile: e.matmul(
                            bg[:, :], lhsT=gtile[:, k, j * 128:(j + 1) * 128], rhs=uT[:, k, :],
                            start=(k == 0), stop=(k == KC - 1)), reads=[tgw, t_u], writes=[tbg])
                    bp, tbp = cx.bank()
                    for k in range(2):
                        A("pe", lambda e, bp=bp, k=k, j=j, ptile=ptile: e.matmul(
                            bp[:, :], lhsT=ptile[:, k, j * 128:(j + 1) * 128], rhs=pTb[:, k, :],
                            start=(k == 0), stop=(k == 1)), reads=[tpw, t_p], writes=[tbp])
                    tt, tm = t_tmp[c % 2], tmp[c % 2]
                    A("act", lambda e, bg=bg, tm=tm: e.activation(out=tm[:], in_=bg[:, :], func=AF.Sigmoid),
                      reads=[tbg], writes=[tt])
                    A("dve", lambda e, bp=bp, tm=tm, c=c: e.tensor_tensor(out=fT[:, c, :], in0=tm[:], in1=bp[:, :], op=ALU.mult),
                      reads=[tt, tbp], writes=[t_f])
            resid_norm_add(cx, fT, t_f, g_all[:, 4 * KC:5 * KC], t_g, hT, t_h, sq, t_sq, rstd, t_rstd, 1.0)
            transpose_out(cx, hT, t_h, KC, out, r0, otok, t_otok, ident)
        R.emit()
    return nc


_PROGS = {}


def _prog(name, fn):
    if name not in _PROGS:
        _PROGS[name] = fn()
    return _PROGS[name]


def _run(nc, maps):
    import sys, time
    t0 = time.time()
    res = run_bass_kernel_spmd(nc, maps, core_ids=list(range(NCORE)))
    print(f"[kernel] launch done in {time.time() - t0:.1f}s", file=sys.stderr, flush=True)
    return res.results


DN_W = 2048
O1 = 3 * DN_W
O2 = O1 + DN_W
O3 = O2 + 16
O4 = O3 + 16
O5 = O4 + 3 * 2048
O6 = O5 + D


def kernel(x, p, ffn1_norm_pre, ffn1_w_gate, ffn1_w_up, ffn1_w_down, ffn1_norm_post,
           mix_norm_pre, w_in, dn_conv_w, dn_A_log, dn_dt_bias, dn_out_norm,
           w_branch_dn, w_branch_sb, w_out, mix_norm_post,
           ffn2_norm_pre, ffn2_w_gate, ffn2_w_up, ffn2_w_down, ffn2_norm_post,
           ple_norm_pre, ple_w_gate, ple_w_proj, ple_norm_post):
    f32 = lambda a: np.ascontiguousarray(np.asarray(a, dtype=np.float32))
    x = f32(x)[0]
    p = f32(p)[0, 0]
    w_in = f32(w_in)[0]
    ident = np.eye(128, dtype=np.float32)
    g1 = np.concatenate([fm_vec(ffn1_norm_pre[0]), fm_vec(ffn1_norm_post[0]), fm_vec(mix_norm_pre[0])], 1)
    wg1, wu1, wd1 = f32(ffn1_w_gate)[0], f32(ffn1_w_up)[0], f32(ffn1_w_down)[0]
    maps = [{"x": np.ascontiguousarray(x[c * TOK:(c + 1) * TOK]), "wg": wg1, "wu": wu1, "wd": wd1,
             "gains": g1, "ident": ident} for c in range(NCORE)]
    r1 = _run(_prog("s1", build_stage1), maps)
    h1T = [np.asarray(r["h1T"]) for r in r1]
    uT = [np.asarray(r["uT"]) for r in r1]
    uT_all = np.ascontiguousarray(np.concatenate(uT, axis=1))
    uT_rev = np.ascontiguousarray(uT_all[:, ::-1])
    del r1, maps
    conv = f32(dn_conv_w)[0]
    consts = dn_consts()
    maps = []
    for c in range(NCORE):
        hs = (2 * c, 2 * c + 1)
        cols = [w_in[:, off + h * 128: off + (h + 1) * 128] for off in (0, DN_W, 2 * DN_W, O1) for h in hs]
        wdn = np.ascontiguousarray(np.concatenate(cols, axis=1))
        wba = np.ascontiguousarray(np.stack([w_in[:, O2 + hs[0]], w_in[:, O2 + hs[1]], w_in[:, O3 + hs[0]], w_in[:, O3 + hs[1]]], axis=1))
        cw = np.stack([conv[:, off + h * 128: off + (h + 1) * 128] for off in (0, DN_W, 2 * DN_W) for h in hs], axis=0)
        cw = np.ascontiguousarray(cw.transpose(2, 0, 1).reshape(128, 24))
        hv = np.array([dn_dt_bias[0][hs[0]], dn_dt_bias[0][hs[1]], dn_A_log[0][hs[0]], dn_A_log[0][hs[1]]], np.float32)
        maps.append({"uT_all": uT_all, "wdn": wdn, "wba": wba, "convw": cw,
                     "gon": f32(dn_out_norm)[0][:, None].copy(), "hv": np.ascontiguousarray(np.tile(hv[None, :], (64, 1))),
                     "consts": consts})
    r2a = _run(_prog("s2a", build_stage2_dn), maps)
    odn_full = np.ascontiguousarray(np.concatenate([np.asarray(r["odnT"]) for r in r2a], axis=0))
    del r2a, maps
    sbc = sb_consts()
    maps = []
    for c in range(NCORE):
        hs = (2 * c, 2 * c + 1)
        cols = [w_in[:, O4 + off + h * 128: O4 + off + (h + 1) * 128] for off in (0, 2048, 4096) for h in hs]
        maps.append({"uT_all": uT_all, "uT_rev": uT_rev, "wsb": np.ascontiguousarray(np.concatenate(cols, axis=1)), "consts": sbc})
    r2b = _run(_prog("s2b", build_stage2_sb), maps)
    osb_full = np.ascontiguousarray(np.concatenate([np.asarray(r["osbT"]) for r in r2b], axis=0))
    del r2b, maps, uT_all, uT_rev
    g3 = np.concatenate([fm_vec(mix_norm_post[0]), fm_vec(ffn2_norm_pre[0]), fm_vec(ffn2_norm_post[0]),
                         fm_vec(ple_norm_pre[0]), fm_vec(ple_norm_post[0])], 1)
    wgate = np.ascontiguousarray(w_in[:, O5:O5 + 2 * D])
    shared = {"wgate": wgate, "wbd": f32(w_branch_dn)[0], "wbs": f32(w_branch_sb)[0], "wout": f32(w_out)[0],
              "wg": f32(ffn2_w_gate)[0], "wu": f32(ffn2_w_up)[0], "wd": f32(ffn2_w_down)[0],
              "wpg": f32(ple_w_gate)[0], "wpp": f32(ple_w_proj)[0], "gains": g3, "ident": ident}
    maps = []
    for c in range(NCORE):
        sl = slice(c * TOK, (c + 1) * TOK)
        m = dict(shared)
        m.update({"h1T": h1T[c], "uT": uT[c], "odnT": np.ascontiguousarray(odn_full[:, sl]),
                  "osbT": np.ascontiguousarray(osb_full[:, sl]), "pT": np.ascontiguousarray(p[sl].T)})
        maps.append(m)
    r3 = _run(_prog("s3", build_stage3), maps)
    out = np.concatenate([np.asarray(r["out"]) for r in r3], axis=0)
    return out.reshape(1, SEQ, D).astype(np.float32)
```

```python
import contextlib
import numpy as np
import concourse.bass as bass
import concourse.mybir as mybir
from concourse.bass_utils import run_bass_kernel_spmd

F32 = mybir.dt.float32
BF16 = mybir.dt.bfloat16
AF = mybir.ActivationFunctionType
ALU = mybir.AluOpType

D = 2048
SEQ = 16384
NCORE = 8
TOK = SEQ // NCORE
T = 512
DFF = 5632
PLE = 256
KC = D // 128
EPS = 1e-6


class Tok:
    __slots__ = ("name", "last_w", "readers", "sem", "dma_cnt", "last_dma")

    def __init__(self, name):
        self.name = name
        self.last_w = None
        self.readers = []
        self.sem = None
        self.dma_cnt = 0
        self.last_dma = None


class Op:
    __slots__ = ("eng", "fn", "deps", "dma", "signal", "sem", "count", "tok")

    def __init__(self, eng, fn, dma):
        self.eng = eng
        self.fn = fn
        self.deps = []
        self.dma = dma
        self.signal = dma
        self.sem = None
        self.count = 0
        self.tok = None


ENGS = ("pe", "act", "dve", "pool", "sp")


class Rec:
    def __init__(self, nc, stack):
        self.nc = nc
        self.stack = stack
        self.ops = {e: [] for e in ENGS}
        self.toks = []
        self.dma_toks = []
        self.nsb = 0

    def tok(self, name="t"):
        t = Tok(name)
        self.toks.append(t)
        return t

    def sb(self, shape, dt, name=None):
        self.nsb += 1
        return self.stack.enter_context(self.nc.sbuf_tensor("s_" + (name or f"sb{self.nsb}"), list(shape), dt))

    def ps(self, shape, dt, name=None):
        self.nsb += 1
        return self.stack.enter_context(self.nc.psum_tensor("p_" + (name or f"ps{self.nsb}"), list(shape), dt))

    def add(self, eng, fn, reads=(), writes=(), dma_tok=None):
        op = Op(eng, fn, dma_tok is not None)
        deps = []
        for r in reads:
            if r.last_w is not None:
                deps.append(r.last_w)
        for w in writes:
            if w.last_w is not None:
                deps.append(w.last_w)
            deps.extend(w.readers)
        if dma_tok is not None:
            if dma_tok.sem is None:
                dma_tok.sem = self.stack.enter_context(self.nc.semaphore(f"dq{len(self.dma_toks)}"))
                dma_tok.last_dma = None
                self.dma_toks.append(dma_tok)
            if getattr(dma_tok, "last_dma", None) is not None:
                deps.append(dma_tok.last_dma)
            dma_tok.dma_cnt += 16
            op.sem = dma_tok.sem
            op.count = dma_tok.dma_cnt
            op.tok = dma_tok
            dma_tok.last_dma = op
        seen = set()
        for d in deps:
            if d is op or id(d) in seen:
                continue
            seen.add(id(d))
            if (not d.dma) and d.eng == eng and eng == "pe":
                continue
            op.deps.append(d)
            d.signal = True
        for r in reads:
            r.readers.append(op)
        for w in writes:
            w.last_w = op
            w.readers = []
        self.ops[eng].append(op)
        return op

    def emit(self):
        nc = self.nc
        esem = {}
        for e in ("pe", "act", "dve", "pool"):
            esem[e] = self.stack.enter_context(nc.semaphore(f"es_{e}"))
        for e in ("pe", "act", "dve", "pool", "sp"):
            c = 0
            for op in self.ops[e]:
                if op.dma:
                    continue
                if op.signal:
                    c += 1
                    op.sem = esem.get(e)
                    op.count = c
                    assert e != "sp"
        final = [(t.sem, t.dma_cnt) for t in self.dma_toks]

        def run(e, eng):
            waited = {}
            for op in self.ops[e]:
                for d in op.deps:
                    key = id(d.sem)
                    if waited.get(key, 0) >= d.count:
                        continue
                    eng.wait_ge(d.sem, d.count)
                    waited[key] = d.count
                ins = op.fn(eng)
                if op.signal:
                    ins.then_inc(op.sem, 16 if op.dma else 1)
            if e == "sp":
                for s, c in final:
                    eng.wait_ge(s, c)

        with nc.Block() as block:
            @block.tensor
            def _(eng):
                run("pe", eng)

            @block.scalar
            def _(eng):
                run("act", eng)

            @block.vector
            def _(eng):
                run("dve", eng)

            @block.gpsimd
            def _(eng):
                run("pool", eng)

            @block.sync
            def _(eng):
                run("sp", eng)


class Ctx:
    def __init__(self, nc, stack):
        self.nc = nc
        self.R = Rec(nc, stack)
        R = self.R
        self.banks = [R.ps([128, 512], F32, name=f"bank{i}") for i in range(8)]
        self.bank_tok = [R.tok(f"bank{i}") for i in range(8)]
        self.bank_rr = 0
        self.ones_bf = R.sb([128, 128], BF16, "ones_bf")
        self.t_const = R.tok("const")
        R.add("pool", lambda e: e.memset(self.ones_bf[:], 1.0), writes=[self.t_const])
        self.evac_rr = 0
        self.reserved = set()

    def bank(self):
        while True:
            i = self.bank_rr % 8
            self.bank_rr += 1
            if id(self.banks[i]) not in self.reserved:
                return self.banks[i], self.bank_tok[i]


def load_vec_fm(cx, dram_vec, name):
    R = cx.R
    t = R.sb([128, dram_vec.shape[1]], F32, name)
    tk = R.tok(name)
    R.add("sp", lambda e: e.dma_start(out=t[:], in_=dram_vec[:, :]), writes=[tk], dma_tok=tk)
    return t, tk


def rms_fm(cx, src, t_src, nch, gain, t_gain, dst, t_dst, sq, t_sq, rstd, t_rstd, dim, post=None):
    R = cx.R
    R.add("act", lambda e: e.activation(out=sq[:, 0:nch, :], in_=src[:, 0:nch, :], func=AF.Square),
          reads=[t_src], writes=[t_sq])
    bk, tb = cx.bank()
    for c in range(nch):
        R.add("pe", lambda e, c=c: e.matmul(bk[:, :], lhsT=cx.ones_bf[:], rhs=sq[:, c, :],
                                             start=(c == 0), stop=(c == nch - 1)),
              reads=[t_sq, cx.t_const], writes=[tb])
    R.add("dve", lambda e: e.tensor_scalar(out=rstd[:], in0=bk[:, :], scalar1=1.0 / dim, scalar2=EPS,
                                           op0=ALU.mult, op1=ALU.add), reads=[tb], writes=[t_rstd])
    R.add("act", lambda e: e.activation(out=rstd[:], in_=rstd[:], func=AF.Sqrt), reads=[t_rstd], writes=[t_rstd])
    R.add("dve", lambda e: e.reciprocal(out=rstd[:], in_=rstd[:]), reads=[t_rstd], writes=[t_rstd])
    if post is None:
        for c in range(nch):
            R.add("dve", lambda e, c=c: e.scalar_tensor_tensor(out=dst[:, c, :], in0=src[:, c, :],
                                                                scalar=gain[:, c:c + 1], in1=rstd[:],
                                                                op0=ALU.mult, op1=ALU.mult),
                  reads=[t_src, t_gain, t_rstd], writes=[t_dst])
    else:
        post()


def transpose_in(cx, x_dram, r0, xtok, t_xtok, xT, t_xT, ident, ncol_chunks, alias=()):
    R = cx.R
    W = ncol_chunks * 128
    alias = list(alias)
    for s in range(T // 128):
        R.add("sp", lambda e, s=s: e.dma_start(out=xtok[:, s, 0:W], in_=x_dram[r0 + s * 128:r0 + (s + 1) * 128, :]),
              writes=[t_xtok[s]] + alias, dma_tok=t_xtok[s])
    for c in range(ncol_chunks):
        bk, tb = cx.bank()
        for s in range(T // 128):
            R.add("pe", lambda e, c=c, s=s, bk=bk: e.transpose(out=bk[:, s * 128:(s + 1) * 128],
                                                                in_=xtok[:, s, c * 128:(c + 1) * 128], identity=ident[:]),
                  reads=[t_xtok[s], cx.t_const] + alias, writes=[tb])
        evac(cx, xT[:, c, :], bk[:, :], [tb], [t_xT])


def evac(cx, dst, src, reads, writes):
    R = cx.R
    cx.evac_rr += 1
    if cx.evac_rr % 2 == 0:
        R.add("act", lambda e: e.copy(out=dst, in_=src), reads=reads, writes=writes)
    else:
        R.add("dve", lambda e: e.tensor_copy(out=dst, in_=src), reads=reads, writes=writes)


def transpose_out(cx, srcT, t_src, nch, out_dram, r0, otok, t_otok, ident):
    R = cx.R
    for s in range(T // 128):
        for c4 in range(0, nch, 4):
            bk, tb = cx.bank()
            for j in range(4):
                c = c4 + j
                R.add("pe", lambda e, c=c, s=s, j=j, bk=bk: e.transpose(out=bk[:, j * 128:(j + 1) * 128],
                                                                   in_=srcT[:, c, s * 128:(s + 1) * 128],
                                                                   identity=ident[:]),
                      reads=[t_src, cx.t_const], writes=[tb])
            evac(cx, otok[:, s, c4 * 128:(c4 + 4) * 128], bk[:, :], [tb], [t_otok[s]])
        R.add("sp", lambda e, s=s: e.dma_start(out=out_dram[r0 + s * 128:r0 + (s + 1) * 128, :],
                                               in_=otok[:, s, 0:nch * 128]),
              reads=[t_otok[s]], dma_tok=t_otok[s])


class WStream:
    def __init__(self, cx, kch_max, nbuf, name):
        R = cx.R
        self.cx = cx
        self.bufs = [R.sb([128, kch_max, 512], BF16, f"{name}{i}") for i in range(nbuf)]
        self.toks = [R.tok(f"{name}{i}") for i in range(nbuf)]
        self.rr = 0

    def load(self, w_dram, k0, kch, c0, ncols):
        R = self.cx.R
        i = self.rr % len(self.bufs)
        self.rr += 1
        buf, tk = self.bufs[i], self.toks[i]
        src = w_dram[k0 * 128:(k0 + kch) * 128, c0:c0 + ncols].rearrange("(k p) n -> p k n", p=128)
        R.add("pool", lambda e: e.dma_start(out=buf[:, 0:kch, 0:ncols], in_=src), writes=[tk], dma_tok=tk)
        return buf, tk


def ffn_fm(cx, uT, t_u, wg, wu, wd, actT, t_act, ws, fT, t_f, tmp, t_tmp):
    R = cx.R
    NJ = DFF // 128
    for j4 in range(0, NJ, 4):
        gw, tg = ws.load(wg, 0, KC, j4 * 128, 512)
        uw, tu = ws.load(wu, 0, KC, j4 * 128, 512)
        for j in range(4):
            bg, tbg = cx.bank()
            bu, tbu = cx.bank()
            for k in range(KC):
                R.add("pe", lambda e, k=k, j=j, bg=bg, gw=gw: e.matmul(bg[:, :], lhsT=gw[:, k, j * 128:(j + 1) * 128],
                                                                        rhs=uT[:, k, :], start=(k == 0), stop=(k == KC - 1)),
                      reads=[tg, t_u], writes=[tbg])
            for k in range(KC):
                R.add("pe", lambda e, k=k, j=j, bu=bu, uw=uw: e.matmul(bu[:, :], lhsT=uw[:, k, j * 128:(j + 1) * 128],
                                                                        rhs=uT[:, k, :], start=(k == 0), stop=(k == KC - 1)),
                      reads=[tu, t_u], writes=[tbu])
            jj = j4 + j
            tt = t_tmp[jj % 2]
            tm = tmp[jj % 2]
            R.add("act", lambda e, bg=bg, tm=tm: e.activation(out=tm[:], in_=bg[:, :], func=AF.Silu),
                  reads=[tbg], writes=[tt])
            R.add("dve", lambda e, bu=bu, tm=tm, jj=jj: e.tensor_tensor(out=actT[:, jj, :], in0=tm[:], in1=bu[:, :],
                                                                         op=ALU.mult),
                  reads=[tt, tbu], writes=[t_act])
    for dg in range(4):
        bks = [cx.bank() for _ in range(4)]
        for q in range(4):
            dwt, tdw = ws.load(wd, q * 11, 11, dg * 512, 512)
            for kk in range(11):
                k = q * 11 + kk
                for dd in range(4):
                    bk, tb = bks[dd]
                    R.add("pe", lambda e, kk=kk, k=k, dd=dd, bk=bk, dwt=dwt: e.matmul(
                        bk[:, :], lhsT=dwt[:, kk, dd * 128:(dd + 1) * 128], rhs=actT[:, k, :],
                        start=(k == 0), stop=(k == NJ - 1)), reads=[tdw, t_act], writes=[tb])
        for dd in range(4):
            bk, tb = bks[dd]
            evac(cx, fT[:, dg * 4 + dd, :], bk[:, :], [tb], [t_f])


def resid_norm_add(cx, fT, t_f, gain, t_gain, hT, t_h, sq, t_sq, rstd, t_rstd, coef):
    R = cx.R

    def post():
        for c in range(KC):
            R.add("dve", lambda e, c=c: e.scalar_tensor_tensor(out=fT[:, c, :], in0=fT[:, c, :],
                                                                scalar=gain[:, c:c + 1], in1=rstd[:],
                                                                op0=ALU.mult, op1=ALU.mult),
                  reads=[t_f, t_gain, t_rstd], writes=[t_f])
            R.add("dve", lambda e, c=c: e.scalar_tensor_tensor(out=hT[:, c, :], in0=fT[:, c, :],
                                                                scalar=float(coef), in1=hT[:, c, :],
                                                                op0=ALU.mult, op1=ALU.add),
                  reads=[t_f, t_h], writes=[t_h])

    rms_fm(cx, fT, t_f, KC, gain, t_gain, None, None, sq, t_sq, rstd, t_rstd, D, post=post)


def build_stage1(ntiles=TOK // T):
    nc = bass.Bass("TRN2", target_bir_lowering=False)
    x = nc.dram_tensor("x", [TOK, D], F32, kind="ExternalInput").ap()
    wg = nc.dram_tensor("wg", [D, DFF], F32, kind="ExternalInput").ap()
    wu = nc.dram_tensor("wu", [D, DFF], F32, kind="ExternalInput").ap()
    wd = nc.dram_tensor("wd", [DFF, D], F32, kind="ExternalInput").ap()
    gains = nc.dram_tensor("gains", [128, 3 * KC], F32, kind="ExternalInput").ap()
    ident_d = nc.dram_tensor("ident", [128, 128], F32, kind="ExternalInput").ap()
    h1T = nc.dram_tensor("h1T", [D, TOK], F32, kind="ExternalOutput").ap()
    uT_o = nc.dram_tensor("uT", [D, TOK], BF16, kind="ExternalOutput").ap()
    with contextlib.ExitStack() as stack:
        cx = Ctx(nc, stack)
        R = cx.R
        g_all, t_g = load_vec_fm(cx, gains, "gains_sb")
        ident = R.sb([128, 128], F32, "ident")
        R.add("sp", lambda e: e.dma_start(out=ident[:], in_=ident_d[:, :]), writes=[cx.t_const], dma_tok=cx.t_const)
        big = R.sb([128, KC * T], F32, "big")
        xtok = big[:].rearrange("p (s d) -> p s d", s=4)
        fT = big[:].rearrange("p (c t) -> p c t", c=KC)
        t_xtok = [R.tok(f"xtok{s}") for s in range(4)]
        t_f = R.tok("fT")
        hT = R.sb([128, KC, T], F32, "hT")
        t_h = R.tok("hT")
        uT = R.sb([128, KC, T], BF16, "uT")
        t_u = R.tok("uT")
        sq = R.sb([128, KC, T], BF16, "sq")
        t_sq = R.tok("sq")
        rstd = R.sb([128, T], F32, "rstd")
        t_rstd = R.tok("rstd")
        actT = R.sb([128, DFF // 128, T], BF16, "actT")
        t_act = R.tok("actT")
        tmp = [R.sb([128, T], F32, f"tmp{i}") for i in range(2)]
        t_tmp = [R.tok(f"tmp{i}") for i in range(2)]
        ws = WStream(cx, KC, 3, "w")
        h1v = h1T.rearrange("(c p) t -> p c t", p=128)
        uv = uT_o.rearrange("(c p) t -> p c t", p=128)
        for it in range(ntiles):
            r0 = it * T
            transpose_in(cx, x, r0, xtok, t_xtok, hT, t_h, ident, KC, alias=[t_f])
            rms_fm(cx, hT, t_h, KC, g_all[:, 0:KC], t_g, uT, t_u, sq, t_sq, rstd, t_rstd, D)
            ffn_fm(cx, uT, t_u, wg, wu, wd, actT, t_act, ws, fT, t_f, tmp, t_tmp)
            resid_norm_add(cx, fT, t_f, g_all[:, KC:2 * KC], t_g, hT, t_h, sq, t_sq, rstd, t_rstd, 0.5)
            R.add("sp", lambda e, r0=r0: e.dma_start(out=h1v[:, :, r0:r0 + T], in_=hT[:]), reads=[t_h], dma_tok=t_h)
            rms_fm(cx, hT, t_h, KC, g_all[:, 2 * KC:3 * KC], t_g, uT, t_u, sq, t_sq, rstd, t_rstd, D)
            R.add("sp", lambda e, r0=r0: e.dma_start(out=uv[:, :, r0:r0 + T], in_=uT[:]), reads=[t_u], dma_tok=t_u)
        R.emit()
    return nc


def fm_vec(v):
    return np.ascontiguousarray(np.asarray(v, np.float32).reshape(-1, 128).T)


HD = 128
CH = 64
NCHK = T // CH
C_ID, C_U, C_SL, C_BU, C_BL = 0, 128, 192, 256, 320


def dn_consts():
    c = np.zeros((128, 384), np.float32)
    c[:, 0:128] = np.eye(128, dtype=np.float32)
    p = np.arange(64)[:, None]
    f = np.arange(64)[None, :]
    c[0:64, C_U:C_U + 64] = (p <= f)
    c[0:64, C_SL:C_SL + 64] = (f < p)
    c[0:64, C_BU:C_BU + 64] = 1e4 * (f > p)
    c[0:64, C_BL:C_BL + 64] = 1e4 * (f < p)
    return c


def bc_mid(ap2, n):
    return ap2.unsqueeze(1).to_broadcast([ap2.shape[0], n, ap2.shape[1]])


def bc_last(ap2, n):
    return ap2.unsqueeze(2).to_broadcast([ap2.shape[0], ap2.shape[1], n])


def build_stage2_dn(ntiles=SEQ // T):
    S = ntiles * T
    nc = bass.Bass("TRN2", target_bir_lowering=False)
    uT_all = nc.dram_tensor("uT_all", [D, S], BF16, kind="ExternalInput").ap()
    wdn = nc.dram_tensor("wdn", [D, 1024], F32, kind="ExternalInput").ap()
    wba = nc.dram_tensor("wba", [D, 4], F32, kind="ExternalInput").ap()
    convw_d = nc.dram_tensor("convw", [128, 24], F32, kind="ExternalInput").ap()
    gon_d = nc.dram_tensor("gon", [128, 1], F32, kind="ExternalInput").ap()
    hv_d = nc.dram_tensor("hv", [64, 4], F32, kind="ExternalInput").ap()
    consts_d = nc.dram_tensor("consts", [128, 384], F32, kind="ExternalInput").ap()
    odnT = nc.dram_tensor("odnT", [256, S], BF16, kind="ExternalOutput").ap()
    with contextlib.ExitStack() as stack:
        cx = Ctx(nc, stack)
        R = cx.R
        A = R.add
        tc = cx.t_const
        consts = R.sb([128, 384], F32, "consts")
        A("sp", lambda e: e.dma_start(out=consts[:], in_=consts_d[:, :]), writes=[tc], dma_tok=tc)
        ident = consts[:, 0:128]
        id64 = consts[0:64, 0:64]
        Um = consts[0:64, C_U:C_U + 64]
        SLm = consts[0:64, C_SL:C_SL + 64]
        BUm = consts[0:64, C_BU:C_BU + 64]
        BLm = consts[0:64, C_BL:C_BL + 64]
        convw, t_cw = load_vec_fm(cx, convw_d, "convw_sb")
        gon, t_gon = load_vec_fm(cx, gon_d, "gon_sb")
        hv = R.sb([64, 4], F32, "hv")
        t_hv = R.tok("hv")
        A("sp", lambda e: e.dma_start(out=hv[:], in_=hv_d[:, :]), writes=[t_hv], dma_tok=t_hv)
        ones_f = R.sb([64, 128], F32, "ones_f")
        A("pool", lambda e: e.memset(ones_f[:], 1.0), writes=[tc])
        expA = R.sb([64, 2], F32, "expA")
        A("act", lambda e: e.activation(out=expA[:], in_=hv[:, 2:4], func=AF.Exp), reads=[t_hv], writes=[t_hv])
        wres = R.sb([128, KC, 1024], BF16, "wres")
        t_w = R.tok("wres")
        wv = wdn.rearrange("(k p) n -> p k n", p=128)
        for half in range(2):
            A("pool", lambda e, half=half: e.dma_start(out=wres[:, :, half * 512:(half + 1) * 512],
                                                      in_=wv[:, :, half * 512:(half + 1) * 512]),
              writes=[t_w], dma_tok=t_w)
        wbar = R.sb([128, KC, 4], BF16, "wbar")
        t_wb = R.tok("wbar")
        A("pool", lambda e: e.dma_start(out=wbar[:], in_=wba.rearrange("(k p) n -> p k n", p=128)),
          writes=[t_wb], dma_tok=t_wb)
        u_t = [R.sb([128, KC, T], BF16, f"u_t{i}") for i in range(1)]
        t_ut = [R.tok(f"u_t{i}") for i in range(1)]
        rawbuf = R.sb([128, 6, T + 3], F32, "rawbuf")
        t_raw = [R.tok(f"raw{c}") for c in range(6)]
        A("pool", lambda e: e.memset(rawbuf[:, :, 0:3], 0.0), writes=t_raw)
        cv = R.sb([128, 6, T], F32, "cv")
        t_cv = [R.tok(f"cv{c}") for c in range(6)]
        qk = R.sb([128, 4, T], F32, "qk")
        t_qk = [R.tok(f"qk{c}") for c in range(4)]
        siluz = R.sb([128, 2, T], F32, "siluz")
        t_sz = [R.tok(f"sz{c}") for c in range(2)]
        sqb = R.sb([128, T], BF16, "sqb")
        t_sqb = R.tok("sqb")
        rstd = R.sb([128, T], F32, "rstd")
        t_rstd = R.tok("rstd")
        ba = R.sb([64, NCHK, 4], F32, "ba")
        t_ba = R.tok("ba")
        beta = R.sb([64, NCHK, 2], F32, "beta")
        t_beta = R.tok("beta")
        gt = R.sb([64, NCHK, 2], F32, "gt")
        t_g = R.tok("gt")
        spx = R.sb([64, NCHK, 2], F32, "spx")
        t_spx = R.tok("spx")

        def sbt(shape, name, dt=F32):
            return R.sb(shape, dt, name), R.tok(name)

        def make_head(h):
            gc_col, t_gcc = sbt([64, NCHK], f"gc_col_h{h}")
            egc, t_egc = sbt([64, NCHK], f"egc_h{h}")
            be, t_be = sbt([64, NCHK], f"be_h{h}")
            kgs, t_kgs = sbt([64, NCHK], f"kgs_h{h}")
            Ug, t_Ug = sbt([64, NCHK, 64], f"Ug_h{h}")
            gcrow, t_gcr = sbt([128, NCHK, 64], f"gcrow_h{h}")
            egrow, t_egr = sbt([128, NCHK, 64], f"egrow_h{h}")
            dl, t_dl = sbt([64, NCHK, 64], f"dl_h{h}")
            decay, t_dec = sbt([64, NCHK, 64], f"decay_h{h}")
            decayT, t_decT = sbt([64, NCHK, 64], f"decayT_h{h}")
            Ktok, t_Kt = sbt([64, NCHK, 128], f"Ktok_h{h}")
            Vtok, t_Vt = sbt([64, NCHK, 128], f"Vtok_h{h}")
            Pb = [sbt([64, NCHK, 64], f"P{i}_h{h}") for i in range(2)]
            Ptb = [sbt([64, NCHK, 64], f"Pt{i}_h{h}") for i in range(2)]
            At, t_At = sbt([64, NCHK, 64], f"At_h{h}")
            u_sb, t_usb = sbt([64, NCHK, 128], f"u_sb_h{h}")
            wT, t_wT = sbt([128, NCHK, 64], f"wT_h{h}")
            attnT, t_att = sbt([64, NCHK, 64], f"attnT_h{h}")
            qgT, t_qg = sbt([128, NCHK, 64], f"qgT_h{h}")
            kg, t_kg = sbt([64, NCHK, 128], f"kg_h{h}")
            Sst = [sbt([128, 128], f"S{i}_h{h}") for i in range(2)]
            A("pool", lambda e: e.memset(Sst[0][0][:], 0.0), writes=[Sst[0][1]])
            s_par = [0]
            vnew = [sbt([64, 128], f"vnew{i}_h{h}") for i in range(2)]
            vn_rr = [0]
            tmpd, t_tmpd = Ug, t_Ug
            Vb, t_Vb = Vtok, t_Vt
            Kbg, t_Kbg = Ktok, t_Kt

            def gen(ob, t_ob):
                    gh = gt[:, :, h]
                    bh = beta[:, :, h]
                    qT = qk[:, h, :].rearrange("p (c f) -> p c f", f=CH)
                    kT = qk[:, 2 + h, :].rearrange("p (c f) -> p c f", f=CH)
                    t_q, t_k, t_v = t_qk[h], t_qk[2 + h], t_cv[4 + h]
                    bk, tb = bank_mm(64, [(0, NCHK, Um, gh)], [tc, t_g])
                    A("dve", lambda e, bk=bk: e.tensor_copy(out=gc_col[:], in_=bk[0:64, 0:NCHK]), reads=[tb], writes=[t_gcc])
                    A("dve", lambda e, gh=gh: e.tensor_tensor(out=Ug[:], in0=bc_mid(Um, NCHK), in1=bc_last(gh, 64), op=ALU.mult),
                      reads=[tc, t_g], writes=[t_Ug])
                    bk, tb = bank_mm(128, [(c * 64, 64, ones_f[:, :], Ug[:, c, :]) for c in range(NCHK)], [tc, t_Ug])
                    A("act", lambda e, bk=bk: e.copy(out=gcrow[:].rearrange("p c f -> p (c f)"), in_=bk[:, :]),
                      reads=[tb], writes=[t_gcr])
                    yield
                    A("dve", lambda e: e.tensor_tensor(out=dl[:], in0=gcrow[0:64, :, :], in1=bc_last(gc_col[:, :], 64),
                                                       op=ALU.subtract), reads=[t_gcr, t_gcc], writes=[t_dl])
                    A("dve", lambda e: e.tensor_tensor(out=tmpd[:], in0=dl[:], in1=bc_mid(BUm, NCHK), op=ALU.add),
                      reads=[t_dl, tc], writes=[t_tmpd])
                    A("act", lambda e: e.activation(out=decay[:], in_=tmpd[:], func=AF.Exp, scale=-1.0),
                      reads=[t_tmpd], writes=[t_dec])
                    A("dve", lambda e: e.tensor_tensor(out=tmpd[:], in0=dl[:], in1=bc_mid(BLm, NCHK), op=ALU.subtract),
                      reads=[t_dl, tc, t_dec], writes=[t_tmpd])
                    A("act", lambda e: e.activation(out=decayT[:], in_=tmpd[:], func=AF.Exp), reads=[t_tmpd], writes=[t_decT])
                    A("act", lambda e: e.activation(out=egrow[:], in_=gcrow[:], func=AF.Exp), reads=[t_gcr], writes=[t_egr])
                    A("act", lambda e: e.activation(out=kgs[:], in_=dl[:, :, 63], func=AF.Exp), reads=[t_dl], writes=[t_kgs])
                    A("act", lambda e: e.activation(out=egc[:], in_=gc_col[:], func=AF.Exp), reads=[t_gcc], writes=[t_egc])
                    yield
                    for (src, t_src, dst, t_dst) in ((kT, t_k, Ktok, t_Kt),
                                                     (cv[:, 4 + h, :].rearrange("p (c f) -> p c f", f=CH), t_v, Vtok, t_Vt)):
                        for half in range(2):
                            bk, tb = cx.bank()
                            for c4 in range(4):
                                c = half * 4 + c4
                                A("pe", lambda e, bk=bk, c4=c4, c=c, src=src: e.transpose(out=bk[0:64, c4 * 128:(c4 + 1) * 128],
                                                                                      in_=src[:, c, :], identity=ident),
                                  reads=[t_src, tc], writes=[tb])
                            evac(cx, dst[:, half * 4:half * 4 + 4, :].rearrange("p c f -> p (c f)"), bk[0:64, :], [tb], [t_dst])
                    bk, tb = bank_mm(64, [(c * 64, 64, kT[:, c, :], kT[:, c, :]) for c in range(NCHK)], [t_k])
                    P0, t_P0 = Pb[0]
                    Pt0, t_Pt0 = Ptb[0]
                    A("dve", lambda e, bk=bk, P0=P0: e.tensor_tensor(out=P0[:].rearrange("p c f -> p (c f)"), in0=bk[0:64, :],
                                                                   in1=decay[:].rearrange("p c f -> p (c f)"), op=ALU.mult),
                      reads=[tb, t_dec], writes=[t_P0])
                    A("dve", lambda e, P0=P0, bh=bh: e.tensor_tensor(out=P0[:], in0=P0[:], in1=bc_last(bh, 64), op=ALU.mult),
                      reads=[t_P0, t_beta], writes=[t_P0])
                    A("dve", lambda e, P0=P0: e.tensor_tensor(out=P0[:], in0=P0[:], in1=bc_mid(SLm, NCHK), op=ALU.mult),
                      reads=[t_P0, tc], writes=[t_P0])
                    yield
                    bk, tb = cx.bank()
                    for c in range(NCHK):
                        A("pe", lambda e, bk=bk, c=c, P0=P0: e.transpose(out=bk[0:64, c * 64:(c + 1) * 64], in_=P0[:, c, :],
                                                                        identity=id64), reads=[t_P0, tc], writes=[tb])
                    A("act", lambda e, bk=bk, Pt0=Pt0: e.copy(out=Pt0[:].rearrange("p c f -> p (c f)"), in_=bk[0:64, :]),
                      reads=[tb], writes=[t_Pt0])
                    A("dve", lambda e, Pt0=Pt0: e.tensor_tensor(out=At[:], in0=bc_mid(id64, NCHK), in1=Pt0[:], op=ALU.subtract),
                      reads=[t_Pt0, tc], writes=[t_At])
                    yield
                    for lv in range(5):
                        Pc, t_Pc = Pb[lv % 2]
                        Ptc, t_Ptc = Ptb[lv % 2]
                        Pn, t_Pn = Pb[(lv + 1) % 2]
                        Ptn, t_Ptn = Ptb[(lv + 1) % 2]
                        bk, tb = bank_mm(64, [(c * 64, 64, Ptc[:, c, :], Pc[:, c, :]) for c in range(NCHK)], [t_Pc, t_Ptc])
                        A("dve", lambda e, bk=bk, Pn=Pn: e.tensor_copy(out=Pn[:].rearrange("p c f -> p (c f)"), in_=bk[0:64, :]),
                          reads=[tb], writes=[t_Pn])
                        bk, tb = bank_mm(64, [(c * 64, 64, Pc[:, c, :], Ptc[:, c, :]) for c in range(NCHK)], [t_Pc, t_Ptc])
                        A("act", lambda e, bk=bk, Ptn=Ptn: e.copy(out=Ptn[:].rearrange("p c f -> p (c f)"), in_=bk[0:64, :]),
                          reads=[tb], writes=[t_Ptn])
                        bk, tb = bank_mm(64, [(c * 64, 64, Pn[:, c, :], At[:, c, :]) for c in range(NCHK)], [t_Pn, t_At])
                        A("dve", lambda e, bk=bk: e.tensor_tensor(out=At[:].rearrange("p c f -> p (c f)"),
                                                                  in0=At[:].rearrange("p c f -> p (c f)"), in1=bk[0:64, :],
                                                                  op=ALU.add), reads=[tb, t_At], writes=[t_At])
                        yield
                    A("dve", lambda e: e.tensor_tensor(out=kg[:], in0=Ktok[:], in1=bc_last(kgs[:, :], 128), op=ALU.mult),
                      reads=[t_Kt, t_kgs], writes=[t_kg])
                    A("dve", lambda e, bh=bh: e.tensor_tensor(out=Vb[:], in0=Vtok[:], in1=bc_last(bh, 128), op=ALU.mult),
                      reads=[t_Vt, t_beta], writes=[t_Vb])
                    A("dve", lambda e, bh=bh: e.tensor_tensor(out=be[:], in0=egc[:], in1=bh, op=ALU.mult),
                      reads=[t_egc, t_beta], writes=[t_be])
                    A("dve", lambda e: e.tensor_tensor(out=Kbg[:], in0=Ktok[:], in1=bc_last(be[:, :], 128), op=ALU.mult),
                      reads=[t_Kt, t_be], writes=[t_Kbg])
                    for half in range(2):
                        bk, tb = bank_mm(64, [(c4 * 128, 128, At[:, half * 4 + c4, :], Vb[:, half * 4 + c4, :]) for c4 in range(4)],
                                         [t_At, t_Vb])
                        evac(cx, u_sb[:, half * 4:half * 4 + 4, :].rearrange("p c f -> p (c f)"), bk[0:64, :], [tb], [t_usb])
                    bk, tb = bank_mm(128, [(c * 64, 64, Kbg[:, c, :], At[:, c, :]) for c in range(NCHK)], [t_At, t_Kbg])
                    evac(cx, wT[:].rearrange("p c f -> p (c f)"), bk[:, :], [tb], [t_wT])
                    yield
                    bk, tb = bank_mm(64, [(c * 64, 64, kT[:, c, :], qT[:, c, :]) for c in range(NCHK)], [t_k, t_q])
                    A("dve", lambda e, bk=bk: e.tensor_tensor(out=attnT[:].rearrange("p c f -> p (c f)"), in0=bk[0:64, :],
                                                              in1=decayT[:].rearrange("p c f -> p (c f)"), op=ALU.mult),
                      reads=[tb, t_decT], writes=[t_att])
                    A("dve", lambda e, qT=qT: e.tensor_tensor(out=qgT[:], in0=qT, in1=egrow[:], op=ALU.mult),
                      reads=[t_q, t_egr], writes=[t_qg])
                    yield
                    obk, t_obk = cx.bank()
                    cx.reserved.add(id(obk))
                    for c in range(NCHK):
                        Sc, t_Sc = Sst[s_par[0]]
                        Sn, t_Sn = Sst[1 - s_par[0]]
                        s_par[0] = 1 - s_par[0]
                        vn, t_vn = vnew[vn_rr[0] % 2]
                        vn_rr[0] += 1
                        bk, tb = bank_mm(64, [(0, 128, wT[:, c, :], Sc[:, :])], [t_wT, t_Sc])
                        A("dve", lambda e, bk=bk, c=c, vn=vn: e.tensor_tensor(out=vn[:], in0=u_sb[:, c, :], in1=bk[0:64, 0:128],
                                                                             op=ALU.subtract), reads=[tb, t_usb], writes=[t_vn])
                        A("pe", lambda e, c=c, Sc=Sc, obk=obk: e.matmul(obk[:, c * 64:(c + 1) * 64], lhsT=Sc[:, :], rhs=qgT[:, c, :],
                                                                       start=True, stop=False), reads=[t_Sc, t_qg], writes=[t_obk])
                        A("pe", lambda e, c=c, vn=vn, obk=obk: e.matmul(obk[:, c * 64:(c + 1) * 64], lhsT=vn[:, :], rhs=attnT[:, c, :],
                                                                       start=False, stop=True), reads=[t_vn, t_att], writes=[t_obk])
                        bk, tb = bank_mm(128, [(0, 128, kg[:, c, :], vn[:, :])], [t_kg, t_vn])
                        A("dve", lambda e, bk=bk, c=c, Sc=Sc, Sn=Sn: e.scalar_tensor_tensor(
                            out=Sn[:], in0=Sc[:], scalar=egrow[:, c, 63:64], in1=bk[:, 0:128], op0=ALU.mult, op1=ALU.add),
                          reads=[tb, t_Sc, t_egr], writes=[t_Sn])
                        yield
                    A("act", lambda e, obk=obk: e.copy(out=o_sb[:], in_=obk[:, :]), reads=[t_obk], writes=[t_osb])
                    cx.reserved.discard(id(obk))
                    A("act", lambda e: e.activation(out=sqb[:], in_=o_sb[:], func=AF.Square), reads=[t_osb], writes=[t_sqb])
                    bk, tb = cx.bank()
                    A("pe", lambda e, bk=bk: e.matmul(bk[:, :], lhsT=cx.ones_bf[:], rhs=sqb[:], start=True, stop=True),
                      reads=[t_sqb, tc], writes=[tb])
                    A("dve", lambda e, bk=bk: e.tensor_scalar(out=rstd[:], in0=bk[:, :], scalar1=1.0 / HD, scalar2=EPS,
                                                              op0=ALU.mult, op1=ALU.add), reads=[tb], writes=[t_rstd])
                    A("act", lambda e: e.activation(out=rstd[:], in_=rstd[:], func=AF.Sqrt), reads=[t_rstd], writes=[t_rstd])
                    A("dve", lambda e: e.reciprocal(out=rstd[:], in_=rstd[:]), reads=[t_rstd], writes=[t_rstd])
                    A("dve", lambda e: e.scalar_tensor_tensor(out=o_tmp[:], in0=o_sb[:], scalar=gon[:, 0:1], in1=rstd[:],
                                                              op0=ALU.mult, op1=ALU.mult),
                      reads=[t_osb, t_gon, t_rstd], writes=[t_otmp])
                    A("dve", lambda e, h=h, ob=ob: e.tensor_tensor(out=ob[:, h, :], in0=o_tmp[:], in1=siluz[:, h, :], op=ALU.mult),
                      reads=[t_otmp, t_sz[h]], writes=[t_ob])

            return gen

        o_sb, t_osb = sbt([128, T], "o_sb")
        o_tmp, t_otmp = sbt([128, T], "o_tmp")
        outb = [sbt([128, 2, T], f"outb{i}", BF16) for i in range(2)]
        ov = odnT.rearrange("(h p) t -> p h t", p=128)
        head_fns = [make_head(0), make_head(1)]

        def bank_mm(nparts, items, reads, name=None):
            bk, tb = cx.bank()
            for (c0, ncol, lh, rh) in items:
                A("pe", lambda e, bk=bk, c0=c0, ncol=ncol, lh=lh, rh=rh: e.matmul(
                    bk[0:nparts, c0:c0 + ncol], lhsT=lh, rhs=rh, start=True, stop=True), reads=reads, writes=[tb])
            return bk, tb

        for it in range(ntiles):
            t0 = it * T
            ut, t_u = u_t[0], t_ut[0]
            A("sp", lambda e, ut=ut, t0=t0: e.dma_start(out=ut[:], in_=uT_all[:, t0:t0 + T].rearrange("(k p) t -> p k t", p=128)),
              writes=[t_u], dma_tok=t_u)
            for c in range(8):
                bk, tb = cx.bank()
                for k in range(KC):
                    A("pe", lambda e, bk=bk, c=c, k=k, ut=ut: e.matmul(bk[:, :], lhsT=wres[:, k, c * 128:(c + 1) * 128],
                                                                      rhs=ut[:, k, :], start=(k == 0), stop=(k == KC - 1)),
                      reads=[t_w, t_u], writes=[tb])
                if c < 6:
                    evac(cx, rawbuf[:, c, 3:T + 3], bk[:, :], [tb], [t_raw[c]])
                else:
                    A("act", lambda e, bk=bk, c=c: e.activation(out=siluz[:, c - 6, :], in_=bk[:, :], func=AF.Silu),
                      reads=[tb], writes=[t_sz[c - 6]])
            for c in range(6):
                A("dve", lambda e, c=c: e.tensor_scalar(out=cv[:, c, :], in0=rawbuf[:, c, 0:T],
                                                        scalar1=convw[:, c * 4:c * 4 + 1], scalar2=None, op0=ALU.mult),
                  reads=[t_raw[c], t_cw], writes=[t_cv[c]])
                for j in range(1, 4):
                    A("dve", lambda e, c=c, j=j: e.scalar_tensor_tensor(out=cv[:, c, :], in0=rawbuf[:, c, j:j + T],
                                                                        scalar=convw[:, c * 4 + j:c * 4 + j + 1],
                                                                        in1=cv[:, c, :], op0=ALU.mult, op1=ALU.add),
                      reads=[t_raw[c], t_cw, t_cv[c]], writes=[t_cv[c]])
                A("act", lambda e, c=c: e.copy(out=rawbuf[:, c, 0:3], in_=rawbuf[:, c, T:T + 3]),
                  reads=[t_raw[c]], writes=[t_raw[c]])
                A("act", lambda e, c=c: e.activation(out=cv[:, c, :], in_=cv[:, c, :], func=AF.Silu),
                  reads=[t_cv[c]], writes=[t_cv[c]])
            for idx in range(4):
                A("act", lambda e, idx=idx: e.activation(out=sqb[:], in_=cv[:, idx, :], func=AF.Square),
                  reads=[t_cv[idx]], writes=[t_sqb])
                bk, tb = cx.bank()
                A("pe", lambda e, bk=bk: e.matmul(bk[:, :], lhsT=cx.ones_bf[:], rhs=sqb[:], start=True, stop=True),
                  reads=[t_sqb, tc], writes=[tb])
                A("dve", lambda e, bk=bk: e.tensor_scalar(out=rstd[:], in0=bk[:, :], scalar1=1e-6, scalar2=None,
                                                          op0=ALU.add), reads=[tb], writes=[t_rstd])
                A("act", lambda e: e.activation(out=rstd[:], in_=rstd[:], func=AF.Sqrt), reads=[t_rstd], writes=[t_rstd])
                A("dve", lambda e: e.reciprocal(out=rstd[:], in_=rstd[:]), reads=[t_rstd], writes=[t_rstd])
                sc = float(HD ** -0.5) if idx < 2 else 1.0
                A("dve", lambda e, idx=idx, sc=sc: e.scalar_tensor_tensor(out=qk[:, idx, :], in0=cv[:, idx, :], scalar=sc,
                                                                          in1=rstd[:], op0=ALU.mult, op1=ALU.mult),
                  reads=[t_cv[idx], t_rstd], writes=[t_qk[idx]])
            bk, tb = cx.bank()
            for ck in range(NCHK):
                for k in range(KC):
                    A("pe", lambda e, bk=bk, ck=ck, k=k, ut=ut: e.matmul(bk[0:64, ck * 4:ck * 4 + 4],
                                                                        lhsT=ut[:, k, ck * CH:(ck + 1) * CH],
                                                                        rhs=wbar[:, k, :], start=(k == 0), stop=(k == KC - 1)),
                      reads=[t_wb, t_u], writes=[tb])
            A("dve", lambda e, bk=bk: e.tensor_copy(out=ba[:].rearrange("p c f -> p (c f)"), in_=bk[0:64, 0:NCHK * 4]),
              reads=[tb], writes=[t_ba])
            A("act", lambda e: e.activation(out=beta[:], in_=ba[:, :, 0:2], func=AF.Sigmoid), reads=[t_ba], writes=[t_beta])
            A("dve", lambda e: e.tensor_tensor(out=spx[:], in0=ba[:, :, 2:4], in1=bc_mid(hv[:, 0:2], NCHK), op=ALU.add),
              reads=[t_ba, t_hv], writes=[t_spx])
            A("act", lambda e: e.activation(out=spx[:], in_=spx[:], func=AF.Exp), reads=[t_spx], writes=[t_spx])
            A("dve", lambda e: e.tensor_scalar(out=spx[:], in0=spx[:], scalar1=1.0, scalar2=None, op0=ALU.add),
              reads=[t_spx], writes=[t_spx])
            A("act", lambda e: e.activation(out=spx[:], in_=spx[:], func=AF.Ln), reads=[t_spx], writes=[t_spx])
            A("dve", lambda e: e.scalar_tensor_tensor(out=gt[:], in0=spx[:], scalar=-1.0, in1=bc_mid(expA[:, 0:2], NCHK),
                                                      op0=ALU.mult, op1=ALU.mult), reads=[t_spx, t_hv], writes=[t_g])
            ob, t_ob = outb[it % 2]
            gens = [head_fns[h](ob, t_ob) for h in range(2)]
            while gens:
                for g in list(gens):
                    try:
                        next(g)
                    except StopIteration:
                        gens.remove(g)
            A("sp", lambda e, ob=ob, t0=t0: e.dma_start(out=ov[:, :, t0:t0 + T], in_=ob[:]), reads=[t_ob], dma_tok=t_ob)
        R.emit()
    return nc


SB_ONE = 128
SB_NM0 = 129
SB_VM0 = 129 + 4 * 512
SB_TOT = 129 + 8 * 512


def sb_consts():
    c = np.zeros((128, SB_TOT), np.float32)
    c[:, 0:128] = np.eye(128, dtype=np.float32)
    t = np.arange(128)[:, None]
    j = np.arange(128)[None, :]
    tri = (t + j < 128).astype(np.float32)
    for q4 in range(4):
        for b in range(4):
            blk = c[:, SB_NM0 + q4 * 512 + b * 128: SB_NM0 + q4 * 512 + (b + 1) * 128]
            if 3 - b > q4:
                blk[:] = 1.0
            elif 3 - b == q4:
                blk[:] = tri
    c[:, SB_VM0:SB_VM0 + 4 * 512] = 1.0 - c[:, SB_NM0:SB_NM0 + 4 * 512]
    c[:, SB_ONE] = 1.0
    return c


def build_stage2_sb(ntiles=SEQ // T, heads=(0, 1)):
    S = ntiles * T
    NB = S // 128
    nc = bass.Bass("TRN2", target_bir_lowering=False)
    uT_all = nc.dram_tensor("uT_all", [D, S], BF16, kind="ExternalInput").ap()
    uT_rev = nc.dram_tensor("uT_rev", [D, S], BF16, kind="ExternalInput").ap()
    wsb = nc.dram_tensor("wsb", [D, 768], F32, kind="ExternalInput").ap()
    consts_d = nc.dram_tensor("consts", [128, SB_TOT], F32, kind="ExternalInput").ap()
    osbT = nc.dram_tensor("osbT", [256, S], BF16, kind="ExternalOutput").ap()
    with contextlib.ExitStack() as stack:
        cx = Ctx(nc, stack)
        R = cx.R
        A = R.add
        tc = cx.t_const
        consts = R.sb([128, 129], F32, "consts")
        A("sp", lambda e: e.dma_start(out=consts[:], in_=consts_d[:, 0:129]), writes=[tc], dma_tok=tc)
        ones_c = consts[:, 128:129]
        masks = R.sb([128, 8 * 512], BF16, "masks")
        t_mk = R.tok("masks")
        A("pool", lambda e: e.dma_start(out=masks[:], in_=consts_d[:, SB_NM0:SB_TOT]), writes=[t_mk], dma_tok=t_mk)
        ident_bf = R.sb([128, 128], BF16, "ident_bf")
        A("dve", lambda e: e.tensor_copy(out=ident_bf[:], in_=consts[:, 0:128]), reads=[tc], writes=[tc])
        ones_f = ones_c.to_broadcast([128, T])
        wres = R.sb([128, KC, 768], BF16, "wres")
        t_w = R.tok("wres")
        wv = wsb.rearrange("(k p) n -> p k n", p=128)
        for half in range(2):
            A("pool", lambda e, half=half: e.dma_start(out=wres[:, :, half * 384:(half + 1) * 384],
                                                      in_=wv[:, :, half * 384:(half + 1) * 384]),
              writes=[t_w], dma_tok=t_w)
        uf, tuf = R.sb([128, KC, T], BF16, "u_f"), R.tok("u_f")
        ur, tur = R.sb([128, KC, T], BF16, "u_r"), R.tok("u_r")
        QT = R.sb([128, S], BF16, "QT")
        KTr = R.sb([128, S], BF16, "KTr")
        dVr = R.sb([128, NB, 128], BF16, "dVr")
        t_Q = [R.tok(f"Q{i}") for i in range(ntiles)]
        t_K = [R.tok(f"K{i}") for i in range(ntiles)]
        t_V = [R.tok(f"V{i}") for i in range(ntiles)]
        VTb = [R.sb([128, T + 1], F32, f"VTb{i}") for i in range(2)]
        t_VT = [R.tok(f"VTb{i}") for i in range(2)]
        dVT = R.sb([128, T], BF16, "dVT")
        t_dVT = R.tok("dVT")
        smb = [[R.sb([128, T], F32, f"smb{i}_{k}") for k in range(2)] for i in range(4)]
        t_sm = [[R.tok(f"smb{i}_{k}") for k in range(2)] for i in range(4)]
        Pb = [[R.sb([128, T], BF16, f"Pb{i}_{k}") for k in range(2)] for i in range(4)]
        t_P = [[R.tok(f"Pb{i}_{k}") for k in range(2)] for i in range(4)]
        Pm = [R.sb([128, T], BF16, f"Pm{i}") for i in range(4)]
        t_Pm = [R.tok(f"Pm{i}") for i in range(4)]
        vsh, t_vsh = R.sb([128, T], F32, "vsh"), R.tok("vsh")
        ATs = R.sb([128, 4, T], BF16, "ATs")
        t_AT = [R.tok(f"ATs_{hh}") for hh in range(2)]
        osb, t_os = R.sb([128, T], BF16, "osb"), R.tok("osb")
        scale = float(HD ** -0.5)
        uva = uT_all.rearrange("(k p) t -> p k t", p=128)
        uvr = uT_rev.rearrange("(k p) t -> p k t", p=128)
        gstep = 0
        for h in heads:
            for n, it in enumerate(reversed(range(ntiles))):
                t0 = it * T
                A("sp", lambda e, t0=t0: e.dma_start(out=uf[:], in_=uva[:, :, t0:t0 + T]), writes=[tuf], dma_tok=tuf)
                A("sp", lambda e, t0=t0: e.dma_start(out=ur[:], in_=uvr[:, :, t0:t0 + T]), writes=[tur], dma_tok=tur)
                for (src, tsrc, col, dst, tdst) in ((uf, tuf, h * 128, QT, t_Q[it]), (ur, tur, 256 + h * 128, KTr, t_K[it])):
                    bk, tb = cx.bank()
                    for k in range(KC):
                        A("pe", lambda e, bk=bk, k=k, src=src, col=col: e.matmul(bk[:, :], lhsT=wres[:, k, col:col + 128],
                                                                                rhs=src[:, k, :], start=(k == 0), stop=(k == KC - 1)),
                          reads=[t_w, tsrc], writes=[tb])
                    evac(cx, dst[:, t0:t0 + T], bk[:, :], [tb], [tdst])
                vt, tvt = VTb[n % 2], t_VT[n % 2]
                vtp, tvtp = VTb[(n + 1) % 2], t_VT[(n + 1) % 2]
                bk, tb = cx.bank()
                for k in range(KC):
                    A("pe", lambda e, bk=bk, k=k, h=h: e.matmul(bk[:, :], lhsT=wres[:, k, 512 + h * 128:512 + (h + 1) * 128],
                                                               rhs=ur[:, k, :], start=(k == 0), stop=(k == KC - 1)),
                      reads=[t_w, tur], writes=[tb])
                A("act", lambda e, bk=bk, vt=vt: e.copy(out=vt[:, 0:T], in_=bk[:, :]), reads=[tb], writes=[tvt])
                if n == 0:
                    A("pool", lambda e, vt=vt: e.memset(vt[:, T:T + 1], 0.0), reads=[tvt], writes=[tvt])
                else:
                    A("act", lambda e, vt=vt, vtp=vtp: e.copy(out=vt[:, T:T + 1], in_=vtp[:, 0:1]), reads=[tvtp, tvt], writes=[tvt])
                A("pool", lambda e, vt=vt: e.tensor_tensor(out=dVT[:], in0=vt[:, 1:T + 1], in1=vt[:, 0:T], op=ALU.subtract),
                  reads=[tvt], writes=[t_dVT])
                bk2, tb2 = cx.bank()
                bkb = bk2[:, :].bitcast(BF16)
                for b in range(4):
                    A("pe", lambda e, bkb=bkb, b=b: e.transpose(out=bkb[:, b * 128:(b + 1) * 128], in_=dVT[:, b * 128:(b + 1) * 128],
                                                               identity=ident_bf[:]), reads=[t_dVT, tc], writes=[tb2])
                evac(cx, dVr[:, it * 4:it * 4 + 4, :].rearrange("p b d -> p (b d)"), bkb[:, 0:T], [tb2], [t_V[it]])
            steps = [(qg, i) for qg in range(NB // 4) for i in range(qg + 1)]

            def issue_scores(qg, i, par):
                j0 = (NB - 4 - 4 * qg + 4 * i) * 128
                kt = j0 // T
                zb = []
                for q4 in range(4):
                    qb = qg * 4 + q4
                    bk, tb = cx.bank()
                    zb.append((bk, tb))
                    A("pe", lambda e, bk=bk, qb=qb, j0=j0: e.matmul(bk[:, :], lhsT=QT[:, qb * 128:(qb + 1) * 128],
                                                                   rhs=KTr[:, j0:j0 + T], start=True, stop=True),
                      reads=[t_Q[qg], t_K[kt]], writes=[tb])
                for q4 in range(4):
                    bk, tb = zb[q4]
                    sm, tsm = smb[q4][par], t_sm[q4][par]
                    A("act", lambda e, bk=bk, sm=sm: e.activation(out=sm[:], in_=bk[:, :], func=AF.Sigmoid, scale=-scale),
                      reads=[tb], writes=[tsm])
                    if i == 0:
                        nm = masks[:, q4 * 512:(q4 + 1) * 512]
                        A("dve", lambda e, sm=sm, nm=nm: e.tensor_tensor(out=sm[:], in0=sm[:], in1=nm, op=ALU.max),
                          reads=[tsm, t_mk], writes=[tsm])

            issue_scores(steps[0][0], steps[0][1], gstep % 2)
            obk = t_obk = None
            for n, (qg, i) in enumerate(steps):
                par = gstep % 2
                gstep += 1
                if n + 1 < len(steps):
                    issue_scores(steps[n + 1][0], steps[n + 1][1], gstep % 2)
                ntile = qg + 1
                JB = NB - 4 - 4 * qg + 4 * i
                kt = JB // 4
                if i == 0:
                    obk, t_obk = cx.bank()
                    cx.reserved.add(id(obk))
                    tq0 = qg * T
                    if tq0 == 0:
                        A("pool", lambda e: e.memset(uf[:, :, 0:1], 0.0), writes=[tuf])
                        A("sp", lambda e: e.dma_start(out=uf[:, :, 1:T], in_=uva[:, :, 0:T - 1]), writes=[tuf], dma_tok=tuf)
                    else:
                        A("sp", lambda e, tq0=tq0: e.dma_start(out=uf[:], in_=uva[:, :, tq0 - 1:tq0 + T - 1]), writes=[tuf], dma_tok=tuf)
                for q4 in range(4):
                    sm, tsm = smb[q4][par], t_sm[q4][par]
                    Pc, tPc = Pb[q4][par], t_P[q4][par]
                    Pp, tPp = Pb[q4][1 - par], t_P[q4][1 - par]
                    if i == 0:
                        A("dve", lambda e, sm=sm, Pc=Pc: e.tensor_tensor_scan(out=Pc[:], data0=sm[:], data1=ones_f, initial=ones_c,
                                                                             op0=ALU.mult, op1=ALU.mult),
                          reads=[tsm, tc], writes=[tPc])
                    else:
                        A("dve", lambda e, sm=sm, Pc=Pc, Pp=Pp: e.tensor_tensor_scan(out=Pc[:], data0=sm[:], data1=ones_f,
                                                                                    initial=Pp[:, T - 1:T], op0=ALU.mult, op1=ALU.mult),
                          reads=[tsm, tPp, tc], writes=[tPc])
                if i == 0:
                    for q4 in range(4):
                        vm = masks[:, (4 + q4) * 512:(5 + q4) * 512]
                        A("pool", lambda e, q4=q4, vm=vm, par=par: e.tensor_tensor(out=Pm[q4][:], in0=Pb[q4][par][:], in1=vm, op=ALU.mult),
                          reads=[t_P[q4][par], t_mk], writes=[t_Pm[q4]])
                for hh in range(2):
                    bk2, tb2 = cx.bank()
                    bkb = bk2[:, :].bitcast(BF16)
                    for bb in range(2):
                        b = hh * 2 + bb
                        for q4 in range(4):
                            Pc, tPc = (Pm[q4], t_Pm[q4]) if i == 0 else (Pb[q4][par], t_P[q4][par])
                            A("pe", lambda e, bkb=bkb, b=b, bb=bb, q4=q4, Pc=Pc: e.transpose(
                                out=bkb[:, bb * 512 + q4 * 128:bb * 512 + (q4 + 1) * 128], in_=Pc[:, b * 128:(b + 1) * 128],
                                identity=ident_bf[:]), reads=[tPc, tc], writes=[tb2])
                    A("act", lambda e, bkb=bkb, hh=hh: e.copy(out=ATs[:, hh * 2:hh * 2 + 2, :].rearrange("p b q -> p (b q)"),
                                                             in_=bkb[:, :]), reads=[tb2], writes=[t_AT[hh]])
                for b in range(4):
                    first = (i == 0 and b == 0)
                    last = (i == ntile - 1 and b == 3)
                    A("pe", lambda e, obk=obk, b=b, JB=JB, first=first, last=last: e.matmul(
                        obk[:, :], lhsT=dVr[:, JB + b, :], rhs=ATs[:, b, :], start=first, stop=last),
                      reads=[t_AT[b // 2], t_V[kt]], writes=[t_obk])
                if i == ntile - 1:
                    bkv, tbv = cx.bank()
                    for k in range(KC):
                        A("pe", lambda e, bkv=bkv, k=k, h=h: e.matmul(bkv[:, :], lhsT=wres[:, k, 512 + h * 128:512 + (h + 1) * 128],
                                                                     rhs=uf[:, k, :], start=(k == 0), stop=(k == KC - 1)),
                          reads=[t_w, tuf], writes=[tbv])
                    A("act", lambda e, bkv=bkv: e.copy(out=vsh[:], in_=bkv[:, :]), reads=[tbv], writes=[t_vsh])
                    A("dve", lambda e, obk=obk: e.tensor_tensor(out=osb[:], in0=obk[:, :], in1=vsh[:], op=ALU.add),
                      reads=[t_obk, t_vsh], writes=[t_os])
                    cx.reserved.discard(id(obk))
                    A("sp", lambda e, qg=qg, h=h: e.dma_start(out=osbT[h * 128:(h + 1) * 128, qg * T:(qg + 1) * T], in_=osb[:]),
                      reads=[t_os], dma_tok=t_os)
        R.emit()
    return nc


def linear_fm(cx, ws, w_dram, kch, xT, t_x, col0, nout, consume):
    R = cx.R
    for c4 in range(0, nout, 4):
        n = min(4, nout - c4)
        wt, tw = ws.load(w_dram, 0, kch, col0 + c4 * 128, n * 128)
        for j in range(n):
            bk, tb = cx.bank()
            for k in range(kch):
                R.add("pe", lambda e, bk=bk, k=k, j=j, wt=wt: e.matmul(bk[:, :], lhsT=wt[:, k, j * 128:(j + 1) * 128],
                                                                      rhs=xT[:, k, :], start=(k == 0), stop=(k == kch - 1)),
                      reads=[tw, t_x], writes=[tb])
            consume(c4 + j, bk, tb)


def build_stage3(ntiles=TOK // T):
    nc = bass.Bass("TRN2", target_bir_lowering=False)
    NTK = ntiles * T
    h1T = nc.dram_tensor("h1T", [D, NTK], F32, kind="ExternalInput").ap()
    uT_d = nc.dram_tensor("uT", [D, NTK], BF16, kind="ExternalInput").ap()
    odn_d = nc.dram_tensor("odnT", [D, NTK], BF16, kind="ExternalInput").ap()
    osb_d = nc.dram_tensor("osbT", [D, NTK], BF16, kind="ExternalInput").ap()
    pT_d = nc.dram_tensor("pT", [PLE, NTK], F32, kind="ExternalInput").ap()
    wgate = nc.dram_tensor("wgate", [D, 2 * D], F32, kind="ExternalInput").ap()
    wbd = nc.dram_tensor("wbd", [D, D], F32, kind="ExternalInput").ap()
    wbs = nc.dram_tensor("wbs", [D, D], F32, kind="ExternalInput").ap()
    wout = nc.dram_tensor("wout", [D, D], F32, kind="ExternalInput").ap()
    wg = nc.dram_tensor("wg", [D, DFF], F32, kind="ExternalInput").ap()
    wu = nc.dram_tensor("wu", [D, DFF], F32, kind="ExternalInput").ap()
    wd = nc.dram_tensor("wd", [DFF, D], F32, kind="ExternalInput").ap()
    wpg = nc.dram_tensor("wpg", [D, D], F32, kind="ExternalInput").ap()
    wpp = nc.dram_tensor("wpp", [PLE, D], F32, kind="ExternalInput").ap()
    gains = nc.dram_tensor("gains", [128, 5 * KC], F32, kind="ExternalInput").ap()
    ident_d = nc.dram_tensor("ident", [128, 128], F32, kind="ExternalInput").ap()
    out = nc.dram_tensor("out", [NTK, D], F32, kind="ExternalOutput").ap()
    with contextlib.ExitStack() as stack:
        cx = Ctx(nc, stack)
        R = cx.R
        A = R.add
        g_all, t_g = load_vec_fm(cx, gains, "gains_sb")
        ident = R.sb([128, 128], F32, "ident")
        A("sp", lambda e: e.dma_start(out=ident[:], in_=ident_d[:, :]), writes=[cx.t_const], dma_tok=cx.t_const)
        big = R.sb([128, KC * T], F32, "big")
        otok = big[:].rearrange("p (s d) -> p s d", s=4)
        fT = big[:].rearrange("p (c t) -> p c t", c=KC)
        t_f = R.tok("fT")
        t_otok = [t_f] * 4
        hT = R.sb([128, KC, T], F32, "hT")
        t_h = R.tok("hT")
        uT = R.sb([128, KC, T], BF16, "uT")
        t_u = R.tok("uT")
        sq = R.sb([128, KC, T], BF16, "sq")
        t_sq = R.tok("sq")
        rstd = R.sb([128, T], F32, "rstd")
        t_rstd = R.tok("rstd")
        actT = R.sb([128, DFF // 128, T], BF16, "actT")
        t_act = R.tok("actT")
        odn = actT[:, 0:KC, :]
        osb = actT[:, KC:2 * KC, :]
        tmp = [R.sb([128, T], F32, f"tmp{i}") for i in range(2)]
        t_tmp = [R.tok(f"tmp{i}") for i in range(2)]
        pTb = R.sb([128, 2, T], BF16, "pTb")
        t_p = R.tok("pTb")
        ws = WStream(cx, KC, 3, "w")
        hv = h1T.rearrange("(c p) t -> p c t", p=128)
        uv = uT_d.rearrange("(c p) t -> p c t", p=128)
        dnv = odn_d.rearrange("(c p) t -> p c t", p=128)
        sbv = osb_d.rearrange("(c p) t -> p c t", p=128)
        pv = pT_d.rearrange("(c p) t -> p c t", p=128)
        for it in range(ntiles):
            r0 = it * T
            A("sp", lambda e, r0=r0: e.dma_start(out=hT[:], in_=hv[:, :, r0:r0 + T]), writes=[t_h], dma_tok=t_h)
            A("sp", lambda e, r0=r0: e.dma_start(out=uT[:], in_=uv[:, :, r0:r0 + T]), writes=[t_u], dma_tok=t_u)
            t_dn = R.tok("odn_ld")
            A("sp", lambda e, r0=r0: e.dma_start(out=odn, in_=dnv[:, :, r0:r0 + T]), writes=[t_act], dma_tok=t_dn)
            A("sp", lambda e, r0=r0: e.dma_start(out=osb, in_=sbv[:, :, r0:r0 + T]), writes=[t_act], dma_tok=t_dn)
            A("pool", lambda e, r0=r0: e.dma_start(out=pTb[:], in_=pv[:, :, r0:r0 + T]), writes=[t_p], dma_tok=t_p)
            for (gcol, wb, xs, tx, second) in ((0, wbd, odn, t_act, False), (D, wbs, osb, t_act, True)):
                for c4 in range(0, KC, 4):
                    gtile, tgw = ws.load(wgate, 0, KC, gcol + c4 * 128, 512)
                    btile, tbw = ws.load(wb, 0, KC, c4 * 128, 512)
                    for j in range(4):
                        c = c4 + j
                        bg, tbg = cx.bank()
                        for k in range(KC):
                            A("pe", lambda e, bg=bg, k=k, j=j, gtile=gtile: e.matmul(
                                bg[:, :], lhsT=gtile[:, k, j * 128:(j + 1) * 128], rhs=uT[:, k, :],
                                start=(k == 0), stop=(k == KC - 1)), reads=[tgw, t_u], writes=[tbg])
                        bb, tbb = cx.bank()
                        for k in range(KC):
                            A("pe", lambda e, bb=bb, k=k, j=j, btile=btile, xs=xs: e.matmul(
                                bb[:, :], lhsT=btile[:, k, j * 128:(j + 1) * 128], rhs=xs[:, k, :],
                                start=(k == 0), stop=(k == KC - 1)), reads=[tbw, tx], writes=[tbb])
                        tt, tm = t_tmp[c % 2], tmp[c % 2]
                        A("act", lambda e, bg=bg, tm=tm: e.activation(out=tm[:], in_=bg[:, :], func=AF.Sigmoid),
                          reads=[tbg], writes=[tt])
                        if not second:
                            A("dve", lambda e, bb=bb, tm=tm, c=c: e.tensor_tensor(out=fT[:, c, :], in0=tm[:], in1=bb[:, :],
                                                                                 op=ALU.mult), reads=[tt, tbb], writes=[t_f])
                        else:
                            A("dve", lambda e, bb=bb, tm=tm: e.tensor_tensor(out=tm[:], in0=tm[:], in1=bb[:, :], op=ALU.mult),
                              reads=[tt, tbb], writes=[tt])
                            A("dve", lambda e, tm=tm, c=c: e.tensor_tensor(out=sq[:, c, :], in0=tm[:], in1=fT[:, c, :],
                                                                          op=ALU.add), reads=[tt, t_f], writes=[t_sq])
            linear_fm(cx, ws, wout, KC, sq, t_sq, 0, KC,
                      lambda c, bk, tb: evac(cx, fT[:, c, :], bk[:, :], [tb], [t_f]))
            resid_norm_add(cx, fT, t_f, g_all[:, 0:KC], t_g, hT, t_h, sq, t_sq, rstd, t_rstd, 1.0)
            rms_fm(cx, hT, t_h, KC, g_all[:, KC:2 * KC], t_g, uT, t_u, sq, t_sq, rstd, t_rstd, D)
            ffn_fm(cx, uT, t_u, wg, wu, wd, actT, t_act, ws, fT, t_f, tmp, t_tmp)
            resid_norm_add(cx, fT, t_f, g_all[:, 2 * KC:3 * KC], t_g, hT, t_h, sq, t_sq, rstd, t_rstd, 0.5)
            rms_fm(cx, hT, t_h, KC, g_all[:, 3 * KC:4 * KC], t_g, uT, t_u, sq, t_sq, rstd, t_rstd, D)
            for c4 in range(0, KC, 4):
                gtile, tgw = ws.load(wpg, 0, KC, c4 * 128, 512)
                ptile, tpw = ws.load(wpp, 0, 2, c4 * 128, 512)
                for j in range(4):
                    c = c4 + j
                    bg, tbg = cx.bank()
                    for k in range(KC):
                        A("pe", lambda e, bg=bg, k=k, j=j, gtile=gtile: e.matmul(
                            bg[:, :], lhsT=gtile[:, k, j * 128:(j + 1) * 128], rhs=uT[:, k, :],
                            start=(k == 0), stop=(k == KC - 1)), reads=[tgw, t_u], writes=[tbg])
                    bp, tbp = cx.bank()
                    for k in range(2):
                        A("pe", lambda e, bp=bp, k=k, j=j, ptile=ptile: e.matmul(
                            bp[:, :], lhsT=ptile[:, k, j * 128:(j + 1) * 128], rhs=pTb[:, k, :],
                            start=(k == 0), stop=(k == 1)), reads=[tpw, t_p], writes=[tbp])
                    tt, tm = t_tmp[c % 2], tmp[c % 2]
                    A("act", lambda e, bg=bg, tm=tm: e.activation(out=tm[:], in_=bg[:, :], func=AF.Sigmoid),
                      reads=[tbg], writes=[tt])
                    A("dve", lambda e, bp=bp, tm=tm, c=c: e.tensor_tensor(out=fT[:, c, :], in0=tm[:], in1=bp[:, :], op=ALU.mult),
                      reads=[tt, tbp], writes=[t_f])
            resid_norm_add(cx, fT, t_f, g_all[:, 4 * KC:5 * KC], t_g, hT, t_h, sq, t_sq, rstd, t_rstd, 1.0)
            transpose_out(cx, hT, t_h, KC, out, r0, otok, t_otok, ident)
        R.emit()
    return nc


_PROGS = {}


def _prog(name, fn):
    if name not in _PROGS:
        _PROGS[name] = fn()
    return _PROGS[name]


def _run(nc, maps):
    import sys, time
    t0 = time.time()
    res = run_bass_kernel_spmd(nc, maps, core_ids=list(range(NCORE)))
    print(f"[kernel] launch done in {time.time() - t0:.1f}s", file=sys.stderr, flush=True)
    return res.results


DN_W = 2048
O1 = 3 * DN_W
O2 = O1 + DN_W
O3 = O2 + 16
O4 = O3 + 16
O5 = O4 + 3 * 2048
O6 = O5 + D


def kernel(x, p, ffn1_norm_pre, ffn1_w_gate, ffn1_w_up, ffn1_w_down, ffn1_norm_post,
           mix_norm_pre, w_in, dn_conv_w, dn_A_log, dn_dt_bias, dn_out_norm,
           w_branch_dn, w_branch_sb, w_out, mix_norm_post,
           ffn2_norm_pre, ffn2_w_gate, ffn2_w_up, ffn2_w_down, ffn2_norm_post,
           ple_norm_pre, ple_w_gate, ple_w_proj, ple_norm_post):
    f32 = lambda a: np.ascontiguousarray(np.asarray(a, dtype=np.float32))
    x = f32(x)[0]
    p = f32(p)[0, 0]
    w_in = f32(w_in)[0]
    ident = np.eye(128, dtype=np.float32)
    g1 = np.concatenate([fm_vec(ffn1_norm_pre[0]), fm_vec(ffn1_norm_post[0]), fm_vec(mix_norm_pre[0])], 1)
    wg1, wu1, wd1 = f32(ffn1_w_gate)[0], f32(ffn1_w_up)[0], f32(ffn1_w_down)[0]
    maps = [{"x": np.ascontiguousarray(x[c * TOK:(c + 1) * TOK]), "wg": wg1, "wu": wu1, "wd": wd1,
             "gains": g1, "ident": ident} for c in range(NCORE)]
    r1 = _run(_prog("s1", build_stage1), maps)
    h1T = [np.asarray(r["h1T"]) for r in r1]
    uT = [np.asarray(r["uT"]) for r in r1]
    uT_all = np.ascontiguousarray(np.concatenate(uT, axis=1))
    uT_rev = np.ascontiguousarray(uT_all[:, ::-1])
    del r1, maps
    conv = f32(dn_conv_w)[0]
    consts = dn_consts()
    maps = []
    for c in range(NCORE):
        hs = (2 * c, 2 * c + 1)
        cols = [w_in[:, off + h * 128: off + (h + 1) * 128] for off in (0, DN_W, 2 * DN_W, O1) for h in hs]
        wdn = np.ascontiguousarray(np.concatenate(cols, axis=1))
        wba = np.ascontiguousarray(np.stack([w_in[:, O2 + hs[0]], w_in[:, O2 + hs[1]], w_in[:, O3 + hs[0]], w_in[:, O3 + hs[1]]], axis=1))
        cw = np.stack([conv[:, off + h * 128: off + (h + 1) * 128] for off in (0, DN_W, 2 * DN_W) for h in hs], axis=0)
        cw = np.ascontiguousarray(cw.transpose(2, 0, 1).reshape(128, 24))
        hv = np.array([dn_dt_bias[0][hs[0]], dn_dt_bias[0][hs[1]], dn_A_log[0][hs[0]], dn_A_log[0][hs[1]]], np.float32)
        maps.append({"uT_all": uT_all, "wdn": wdn, "wba": wba, "convw": cw,
                     "gon": f32(dn_out_norm)[0][:, None].copy(), "hv": np.ascontiguousarray(np.tile(hv[None, :], (64, 1))),
                     "consts": consts})
    r2a = _run(_prog("s2a", build_stage2_dn), maps)
    odn_full = np.ascontiguousarray(np.concatenate([np.asarray(r["odnT"]) for r in r2a], axis=0))
    del r2a, maps
    sbc = sb_consts()
    maps = []
    for c in range(NCORE):
        hs = (2 * c, 2 * c + 1)
        cols = [w_in[:, O4 + off + h * 128: O4 + off + (h + 1) * 128] for off in (0, 2048, 4096) for h in hs]
        maps.append({"uT_all": uT_all, "uT_rev": uT_rev, "wsb": np.ascontiguousarray(np.concatenate(cols, axis=1)), "consts": sbc})
    r2b = _run(_prog("s2b", build_stage2_sb), maps)
    osb_full = np.ascontiguousarray(np.concatenate([np.asarray(r["osbT"]) for r in r2b], axis=0))
    del r2b, maps, uT_all, uT_rev
    g3 = np.concatenate([fm_vec(mix_norm_post[0]), fm_vec(ffn2_norm_pre[0]), fm_vec(ffn2_norm_post[0]),
                         fm_vec(ple_norm_pre[0]), fm_vec(ple_norm_post[0])], 1)
    wgate = np.ascontiguousarray(w_in[:, O5:O5 + 2 * D])
    shared = {"wgate": wgate, "wbd": f32(w_branch_dn)[0], "wbs": f32(w_branch_sb)[0], "wout": f32(w_out)[0],
              "wg": f32(ffn2_w_gate)[0], "wu": f32(ffn2_w_up)[0], "wd": f32(ffn2_w_down)[0],
              "wpg": f32(ple_w_gate)[0], "wpp": f32(ple_w_proj)[0], "gains": g3, "ident": ident}
    maps = []
    for c in range(NCORE):
        sl = slice(c * TOK, (c + 1) * TOK)
        m = dict(shared)
        m.update({"h1T": h1T[c], "uT": uT[c], "odnT": np.ascontiguousarray(odn_full[:, sl]),
                  "osbT": np.ascontiguousarray(osb_full[:, sl]), "pT": np.ascontiguousarray(p[sl].T)})
        maps.append(m)
    r3 = _run(_prog("s3", build_stage3), maps)
    out = np.concatenate([np.asarray(r["out"]) for r in r3], axis=0)
    return out.reshape(1, SEQ, D).astype(np.float32)
```

```python
import contextlib
import numpy as np
import concourse.bass as bass
import concourse.mybir as mybir
from concourse.bass_utils import run_bass_kernel_spmd

F32 = mybir.dt.float32
BF16 = mybir.dt.bfloat16
AF = mybir.ActivationFunctionType
ALU = mybir.AluOpType

D = 2048
SEQ = 16384
NCORE = 8
TOK = SEQ // NCORE
T = 512
DFF = 5632
PLE = 256
KC = D // 128
EPS = 1e-6


class Tok:
    __slots__ = ("name", "last_w", "readers", "sem", "dma_cnt", "last_dma")

    def __init__(self, name):
        self.name = name
        self.last_w = None
        self.readers = []
        self.sem = None
        self.dma_cnt = 0
        self.last_dma = None


class Op:
    __slots__ = ("eng", "fn", "deps", "dma", "signal", "sem", "count", "tok")

    def __init__(self, eng, fn, dma):
        self.eng = eng
        self.fn = fn
        self.deps = []
        self.dma = dma
        self.signal = dma
        self.sem = None
        self.count = 0
        self.tok = None


ENGS = ("pe", "act", "dve", "pool", "sp")


class Rec:
    def __init__(self, nc, stack):
        self.nc = nc
        self.stack = stack
        self.ops = {e: [] for e in ENGS}
        self.toks = []
        self.dma_toks = []
        self.nsb = 0

    def tok(self, name="t"):
        t = Tok(name)
        self.toks.append(t)
        return t

    def sb(self, shape, dt, name=None):
        self.nsb += 1
        return self.stack.enter_context(self.nc.sbuf_tensor("s_" + (name or f"sb{self.nsb}"), list(shape), dt))

    def ps(self, shape, dt, name=None):
        self.nsb += 1
        return self.stack.enter_context(self.nc.psum_tensor("p_" + (name or f"ps{self.nsb}"), list(shape), dt))

    def add(self, eng, fn, reads=(), writes=(), dma_tok=None):
        op = Op(eng, fn, dma_tok is not None)
        deps = []
        for r in reads:
            if r.last_w is not None:
                deps.append(r.last_w)
        for w in writes:
            if w.last_w is not None:
                deps.append(w.last_w)
            deps.extend(w.readers)
        if dma_tok is not None:
            if dma_tok.sem is None:
                dma_tok.sem = self.stack.enter_context(self.nc.semaphore(f"dq{len(self.dma_toks)}"))
                dma_tok.last_dma = None
                self.dma_toks.append(dma_tok)
            if getattr(dma_tok, "last_dma", None) is not None:
                deps.append(dma_tok.last_dma)
            dma_tok.dma_cnt += 16
            op.sem = dma_tok.sem
            op.count = dma_tok.dma_cnt
            op.tok = dma_tok
            dma_tok.last_dma = op
        seen = set()
        for d in deps:
            if d is op or id(d) in seen:
                continue
            seen.add(id(d))
            if (not d.dma) and d.eng == eng and eng == "pe":
                continue
            op.deps.append(d)
            d.signal = True
        for r in reads:
            r.readers.append(op)
        for w in writes:
            w.last_w = op
            w.readers = []
        self.ops[eng].append(op)
        return op

    def emit(self):
        nc = self.nc
        esem = {}
        for e in ("pe", "act", "dve", "pool"):
            esem[e] = self.stack.enter_context(nc.semaphore(f"es_{e}"))
        for e in ("pe", "act", "dve", "pool", "sp"):
            c = 0
            for op in self.ops[e]:
                if op.dma:
                    continue
                if op.signal:
                    c += 1
                    op.sem = esem.get(e)
                    op.count = c
                    assert e != "sp"
        final = [(t.sem, t.dma_cnt) for t in self.dma_toks]

        def run(e, eng):
            waited = {}
            for op in self.ops[e]:
                for d in op.deps:
                    key = id(d.sem)
                    if waited.get(key, 0) >= d.count:
                        continue
                    eng.wait_ge(d.sem, d.count)
                    waited[key] = d.count
                ins = op.fn(eng)
                if op.signal:
                    ins.then_inc(op.sem, 16 if op.dma else 1)
            if e == "sp":
                for s, c in final:
                    eng.wait_ge(s, c)

        with nc.Block() as block:
            @block.tensor
            def _(eng):
                run("pe", eng)

            @block.scalar
            def _(eng):
                run("act", eng)

            @block.vector
            def _(eng):
                run("dve", eng)

            @block.gpsimd
            def _(eng):
                run("pool", eng)

            @block.sync
            def _(eng):
                run("sp", eng)


class Ctx:
    def __init__(self, nc, stack):
        self.nc = nc
        self.R = Rec(nc, stack)
        R = self.R
        self.banks = [R.ps([128, 512], F32, name=f"bank{i}") for i in range(8)]
        self.bank_tok = [R.tok(f"bank{i}") for i in range(8)]
        self.bank_rr = 0
        self.ones_bf = R.sb([128, 128], BF16, "ones_bf")
        self.t_const = R.tok("const")
        R.add("pool", lambda e: e.memset(self.ones_bf[:], 1.0), writes=[self.t_const])
        self.evac_rr = 0
        self.reserved = set()

    def bank(self):
        while True:
            i = self.bank_rr % 8
            self.bank_rr += 1
            if id(self.banks[i]) not in self.reserved:
                return self.banks[i], self.bank_tok[i]


def load_vec_fm(cx, dram_vec, name):
    R = cx.R
    t = R.sb([128, dram_vec.shape[1]], F32, name)
    tk = R.tok(name)
    R.add("sp", lambda e: e.dma_start(out=t[:], in_=dram_vec[:, :]), writes=[tk], dma_tok=tk)
    return t, tk


def rms_fm(cx, src, t_src, nch, gain, t_gain, dst, t_dst, sq, t_sq, rstd, t_rstd, dim, post=None):
    R = cx.R
    R.add("act", lambda e: e.activation(out=sq[:, 0:nch, :], in_=src[:, 0:nch, :], func=AF.Square),
          reads=[t_src], writes=[t_sq])
    bk, tb = cx.bank()
    for c in range(nch):
        R.add("pe", lambda e, c=c: e.matmul(bk[:, :], lhsT=cx.ones_bf[:], rhs=sq[:, c, :],
                                             start=(c == 0), stop=(c == nch - 1)),
              reads=[t_sq, cx.t_const], writes=[tb])
    R.add("act", lambda e: e.activation(out=rstd[:], in_=bk[:, :], func=AF.Ln, bias=EPS, scale=1.0 / dim),
          reads=[tb], writes=[t_rstd])
    R.add("act", lambda e: e.activation(out=rstd[:], in_=rstd[:], func=AF.Exp, scale=-0.5), reads=[t_rstd], writes=[t_rstd])
    if post is None:
        for c in range(nch):
            R.add("dve", lambda e, c=c: e.scalar_tensor_tensor(out=dst[:, c, :], in0=src[:, c, :],
                                                                scalar=gain[:, c:c + 1], in1=rstd[:],
                                                                op0=ALU.mult, op1=ALU.mult),
                  reads=[t_src, t_gain, t_rstd], writes=[t_dst])
    else:
        post()


def transpose_in(cx, x_dram, r0, xtok, t_xtok, xT, t_xT, ident, ncol_chunks, alias=()):
    R = cx.R
    W = ncol_chunks * 128
    alias = list(alias)
    for s in range(T // 128):
        R.add("sp", lambda e, s=s: e.dma_start(out=xtok[:, s, 0:W], in_=x_dram[r0 + s * 128:r0 + (s + 1) * 128, :]),
              writes=[t_xtok[s]] + alias, dma_tok=t_xtok[s])
    for c in range(ncol_chunks):
        bk, tb = cx.bank()
        for s in range(T // 128):
            R.add("pe", lambda e, c=c, s=s, bk=bk: e.transpose(out=bk[:, s * 128:(s + 1) * 128],
                                                                in_=xtok[:, s, c * 128:(c + 1) * 128], identity=ident[:]),
                  reads=[t_xtok[s], cx.t_const] + alias, writes=[tb])
        evac(cx, xT[:, c, :], bk[:, :], [tb], [t_xT])


def evac(cx, dst, src, reads, writes):
    R = cx.R
    cx.evac_rr += 1
    if cx.evac_rr % 2 == 0:
        R.add("act", lambda e: e.copy(out=dst, in_=src), reads=reads, writes=writes)
    else:
        R.add("dve", lambda e: e.tensor_copy(out=dst, in_=src), reads=reads, writes=writes)


def transpose_out(cx, srcT, t_src, nch, out_dram, r0, otok, t_otok, ident):
    R = cx.R
    for s in range(T // 128):
        for c4 in range(0, nch, 4):
            bk, tb = cx.bank()
            for j in range(4):
                c = c4 + j
                R.add("pe", lambda e, c=c, s=s, j=j, bk=bk: e.transpose(out=bk[:, j * 128:(j + 1) * 128],
                                                                   in_=srcT[:, c, s * 128:(s + 1) * 128],
                                                                   identity=ident[:]),
                      reads=[t_src, cx.t_const], writes=[tb])
            evac(cx, otok[:, s, c4 * 128:(c4 + 4) * 128], bk[:, :], [tb], [t_otok[s]])
        R.add("sp", lambda e, s=s: e.dma_start(out=out_dram[r0 + s * 128:r0 + (s + 1) * 128, :],
                                               in_=otok[:, s, 0:nch * 128]),
              reads=[t_otok[s]], dma_tok=t_otok[s])


class WStream:
    def __init__(self, cx, kch_max, nbuf, name):
        R = cx.R
        self.cx = cx
        self.bufs = [R.sb([128, kch_max, 512], BF16, f"{name}{i}") for i in range(nbuf)]
        self.toks = [R.tok(f"{name}{i}") for i in range(nbuf)]
        self.rr = 0

    def load(self, w_dram, k0, kch, c0, ncols):
        R = self.cx.R
        i = self.rr % len(self.bufs)
        self.rr += 1
        buf, tk = self.bufs[i], self.toks[i]
        src = w_dram[k0 * 128:(k0 + kch) * 128, c0:c0 + ncols].rearrange("(k p) n -> p k n", p=128)
        R.add("pool", lambda e: e.dma_start(out=buf[:, 0:kch, 0:ncols], in_=src), writes=[tk], dma_tok=tk)
        return buf, tk


def ffn_fm(cx, uT, t_u, wg, wu, wd, actT, t_act, ws, fT, t_f, tmp, t_tmp):
    R = cx.R
    NJ = DFF // 128
    for j4 in range(0, NJ, 4):
        gw, tg = ws.load(wg, 0, KC, j4 * 128, 512)
        uw, tu = ws.load(wu, 0, KC, j4 * 128, 512)
        for j in range(4):
            bg, tbg = cx.bank()
            bu, tbu = cx.bank()
            for k in range(KC):
                R.add("pe", lambda e, k=k, j=j, bg=bg, gw=gw: e.matmul(bg[:, :], lhsT=gw[:, k, j * 128:(j + 1) * 128],
                                                                        rhs=uT[:, k, :], start=(k == 0), stop=(k == KC - 1)),
                      reads=[tg, t_u], writes=[tbg])
            for k in range(KC):
                R.add("pe", lambda e, k=k, j=j, bu=bu, uw=uw: e.matmul(bu[:, :], lhsT=uw[:, k, j * 128:(j + 1) * 128],
                                                                        rhs=uT[:, k, :], start=(k == 0), stop=(k == KC - 1)),
                      reads=[tu, t_u], writes=[tbu])
            jj = j4 + j
            tt = t_tmp[jj % 2]
            tm = tmp[jj % 2]
            R.add("act", lambda e, bg=bg, tm=tm: e.activation(out=tm[:], in_=bg[:, :], func=AF.Silu),
                  reads=[tbg], writes=[tt])
            R.add("dve", lambda e, bu=bu, tm=tm, jj=jj: e.tensor_tensor(out=actT[:, jj, :], in0=tm[:], in1=bu[:, :],
                                                                         op=ALU.mult),
                  reads=[tt, tbu], writes=[t_act])
    for dg in range(4):
        bks = [cx.bank() for _ in range(4)]
        for q in range(4):
            dwt, tdw = ws.load(wd, q * 11, 11, dg * 512, 512)
            for kk in range(11):
                k = q * 11 + kk
                for dd in range(4):
                    bk, tb = bks[dd]
                    R.add("pe", lambda e, kk=kk, k=k, dd=dd, bk=bk, dwt=dwt: e.matmul(
                        bk[:, :], lhsT=dwt[:, kk, dd * 128:(dd + 1) * 128], rhs=actT[:, k, :],
                        start=(k == 0), stop=(k == NJ - 1)), reads=[tdw, t_act], writes=[tb])
        for dd in range(4):
            bk, tb = bks[dd]
            evac(cx, fT[:, dg * 4 + dd, :], bk[:, :], [tb], [t_f])


def resid_norm_add(cx, fT, t_f, gain, t_gain, hT, t_h, sq, t_sq, rstd, t_rstd, coef):
    R = cx.R

    def post():
        for c in range(KC):
            R.add("dve", lambda e, c=c: e.scalar_tensor_tensor(out=fT[:, c, :], in0=fT[:, c, :],
                                                                scalar=gain[:, c:c + 1], in1=rstd[:],
                                                                op0=ALU.mult, op1=ALU.mult),
                  reads=[t_f, t_gain, t_rstd], writes=[t_f])
            R.add("dve", lambda e, c=c: e.scalar_tensor_tensor(out=hT[:, c, :], in0=fT[:, c, :],
                                                                scalar=float(coef), in1=hT[:, c, :],
                                                                op0=ALU.mult, op1=ALU.add),
                  reads=[t_f, t_h], writes=[t_h])

    rms_fm(cx, fT, t_f, KC, gain, t_gain, None, None, sq, t_sq, rstd, t_rstd, D, post=post)


def build_stage1(ntiles=TOK // T):
    nc = bass.Bass("TRN2", target_bir_lowering=False)
    x = nc.dram_tensor("x", [TOK, D], F32, kind="ExternalInput").ap()
    wg = nc.dram_tensor("wg", [D, DFF], F32, kind="ExternalInput").ap()
    wu = nc.dram_tensor("wu", [D, DFF], F32, kind="ExternalInput").ap()
    wd = nc.dram_tensor("wd", [DFF, D], F32, kind="ExternalInput").ap()
    gains = nc.dram_tensor("gains", [128, 3 * KC], F32, kind="ExternalInput").ap()
    ident_d = nc.dram_tensor("ident", [128, 128], F32, kind="ExternalInput").ap()
    h1T = nc.dram_tensor("h1T", [D, TOK], F32, kind="ExternalOutput").ap()
    uT_o = nc.dram_tensor("uT", [D, TOK], BF16, kind="ExternalOutput").ap()
    with contextlib.ExitStack() as stack:
        cx = Ctx(nc, stack)
        R = cx.R
        g_all, t_g = load_vec_fm(cx, gains, "gains_sb")
        ident = R.sb([128, 128], F32, "ident")
        R.add("sp", lambda e: e.dma_start(out=ident[:], in_=ident_d[:, :]), writes=[cx.t_const], dma_tok=cx.t_const)
        big = R.sb([128, KC * T], F32, "big")
        xtok = big[:].rearrange("p (s d) -> p s d", s=4)
        fT = big[:].rearrange("p (c t) -> p c t", c=KC)
        t_xtok = [R.tok(f"xtok{s}") for s in range(4)]
        t_f = R.tok("fT")
        hT = R.sb([128, KC, T], F32, "hT")
        t_h = R.tok("hT")
        uT = R.sb([128, KC, T], BF16, "uT")
        t_u = R.tok("uT")
        sq = R.sb([128, KC, T], BF16, "sq")
        t_sq = R.tok("sq")
        rstd = R.sb([128, T], F32, "rstd")
        t_rstd = R.tok("rstd")
        actT = R.sb([128, DFF // 128, T], BF16, "actT")
        t_act = R.tok("actT")
        tmp = [R.sb([128, T], F32, f"tmp{i}") for i in range(2)]
        t_tmp = [R.tok(f"tmp{i}") for i in range(2)]
        ws = WStream(cx, KC, 3, "w")
        h1v = h1T.rearrange("(c p) t -> p c t", p=128)
        uv = uT_o.rearrange("(c p) t -> p c t", p=128)
        for it in range(ntiles):
            r0 = it * T
            transpose_in(cx, x, r0, xtok, t_xtok, hT, t_h, ident, KC, alias=[t_f])
            rms_fm(cx, hT, t_h, KC, g_all[:, 0:KC], t_g, uT, t_u, sq, t_sq, rstd, t_rstd, D)
            ffn_fm(cx, uT, t_u, wg, wu, wd, actT, t_act, ws, fT, t_f, tmp, t_tmp)
            resid_norm_add(cx, fT, t_f, g_all[:, KC:2 * KC], t_g, hT, t_h, sq, t_sq, rstd, t_rstd, 0.5)
            R.add("sp", lambda e, r0=r0: e.dma_start(out=h1v[:, :, r0:r0 + T], in_=hT[:]), reads=[t_h], dma_tok=t_h)
            rms_fm(cx, hT, t_h, KC, g_all[:, 2 * KC:3 * KC], t_g, uT, t_u, sq, t_sq, rstd, t_rstd, D)
            R.add("sp", lambda e, r0=r0: e.dma_start(out=uv[:, :, r0:r0 + T], in_=uT[:]), reads=[t_u], dma_tok=t_u)
        R.emit()
    return nc


def fm_vec(v):
    return np.ascontiguousarray(np.asarray(v, np.float32).reshape(-1, 128).T)


HD = 128
CH = 64
NCHK = T // CH
C_ID, C_U, C_SL, C_BU, C_BL = 0, 128, 192, 256, 320


def dn_consts():
    c = np.zeros((128, 384), np.float32)
    c[:, 0:128] = np.eye(128, dtype=np.float32)
    p = np.arange(64)[:, None]
    f = np.arange(64)[None, :]
    c[0:64, C_U:C_U + 64] = (p <= f)
    c[0:64, C_SL:C_SL + 64] = (f < p)
    c[0:64, C_BU:C_BU + 64] = 1e4 * (f > p)
    c[0:64, C_BL:C_BL + 64] = 1e4 * (f < p)
    return c


def bc_mid(ap2, n):
    return ap2.unsqueeze(1).to_broadcast([ap2.shape[0], n, ap2.shape[1]])


def bc_last(ap2, n):
    return ap2.unsqueeze(2).to_broadcast([ap2.shape[0], ap2.shape[1], n])


def build_stage2_dn(ntiles=SEQ // T):
    S = ntiles * T
    nc = bass.Bass("TRN2", target_bir_lowering=False)
    uT_all = nc.dram_tensor("uT_all", [D, S], BF16, kind="ExternalInput").ap()
    wdn = nc.dram_tensor("wdn", [D, 1024], F32, kind="ExternalInput").ap()
    wba = nc.dram_tensor("wba", [D, 4], F32, kind="ExternalInput").ap()
    convw_d = nc.dram_tensor("convw", [128, 24], F32, kind="ExternalInput").ap()
    gon_d = nc.dram_tensor("gon", [128, 1], F32, kind="ExternalInput").ap()
    hv_d = nc.dram_tensor("hv", [64, 4], F32, kind="ExternalInput").ap()
    consts_d = nc.dram_tensor("consts", [128, 384], F32, kind="ExternalInput").ap()
    odnT = nc.dram_tensor("odnT", [256, S], BF16, kind="ExternalOutput").ap()
    with contextlib.ExitStack() as stack:
        cx = Ctx(nc, stack)
        R = cx.R
        A = R.add
        tc = cx.t_const
        consts = R.sb([128, 384], F32, "consts")
        A("sp", lambda e: e.dma_start(out=consts[:], in_=consts_d[:, :]), writes=[tc], dma_tok=tc)
        ident = consts[:, 0:128]
        id64 = consts[0:64, 0:64]
        Um = consts[0:64, C_U:C_U + 64]
        SLm = consts[0:64, C_SL:C_SL + 64]
        BUm = consts[0:64, C_BU:C_BU + 64]
        BLm = consts[0:64, C_BL:C_BL + 64]
        convw, t_cw = load_vec_fm(cx, convw_d, "convw_sb")
        gon, t_gon = load_vec_fm(cx, gon_d, "gon_sb")
        hv = R.sb([64, 4], F32, "hv")
        t_hv = R.tok("hv")
        A("sp", lambda e: e.dma_start(out=hv[:], in_=hv_d[:, :]), writes=[t_hv], dma_tok=t_hv)
        ident_bf = R.sb([128, 128], BF16, "ident_bf")
        A("dve", lambda e: e.tensor_copy(out=ident_bf[:], in_=consts[:, 0:128]), reads=[tc], writes=[tc])
        id64_bf = ident_bf[0:64, 0:64]
        ones_f = R.sb([64, 128], F32, "ones_f")
        A("pool", lambda e: e.memset(ones_f[:], 1.0), writes=[tc])
        expA = R.sb([64, 2], F32, "expA")
        A("act", lambda e: e.activation(out=expA[:], in_=hv[:, 2:4], func=AF.Exp), reads=[t_hv], writes=[t_hv])
        wres = R.sb([128, KC, 1024], BF16, "wres")
        t_w = R.tok("wres")
        wv = wdn.rearrange("(k p) n -> p k n", p=128)
        for half in range(2):
            A("pool", lambda e, half=half: e.dma_start(out=wres[:, :, half * 512:(half + 1) * 512],
                                                      in_=wv[:, :, half * 512:(half + 1) * 512]),
              writes=[t_w], dma_tok=t_w)
        wbar = R.sb([128, KC, 4], BF16, "wbar")
        t_wb = R.tok("wbar")
        A("pool", lambda e: e.dma_start(out=wbar[:], in_=wba.rearrange("(k p) n -> p k n", p=128)),
          writes=[t_wb], dma_tok=t_wb)
        u_t = [R.sb([128, KC, T], BF16, f"u_t{i}") for i in range(1)]
        t_ut = [R.tok(f"u_t{i}") for i in range(1)]
        rawbuf = R.sb([128, 6, T + 3], F32, "rawbuf")
        t_raw = [R.tok(f"raw{c}") for c in range(6)]
        A("pool", lambda e: e.memset(rawbuf[:, :, 0:3], 0.0), writes=t_raw)
        cv = R.sb([128, 6, T], F32, "cv")
        t_cv = [R.tok(f"cv{c}") for c in range(6)]
        qk = R.sb([128, 4, T], BF16, "qk")
        t_qk = [R.tok(f"qk{c}") for c in range(4)]
        siluz = R.sb([128, 2, T], F32, "siluz")
        t_sz = [R.tok(f"sz{c}") for c in range(2)]
        sqb = R.sb([128, T], BF16, "sqb")
        t_sqb = R.tok("sqb")
        rstd = R.sb([128, T], F32, "rstd")
        t_rstd = R.tok("rstd")
        ba = R.sb([64, NCHK, 4], F32, "ba")
        t_ba = R.tok("ba")
        beta = R.sb([64, NCHK, 2], F32, "beta")
        t_beta = R.tok("beta")
        gt = R.sb([64, NCHK, 2], F32, "gt")
        t_g = R.tok("gt")
        spx = R.sb([64, NCHK, 2], F32, "spx")
        t_spx = R.tok("spx")

        def sbt(shape, name, dt=F32):
            return R.sb(shape, dt, name), R.tok(name)

        def make_head(h):
            gc_col, t_gcc = sbt([64, NCHK], f"gc_col_h{h}")
            egc, t_egc = sbt([64, NCHK], f"egc_h{h}")
            be, t_be = sbt([64, NCHK], f"be_h{h}")
            kgs, t_kgs = sbt([64, NCHK], f"kgs_h{h}")
            Ug, t_Ug = sbt([64, NCHK, 64], f"Ug_h{h}")
            gcrow, t_gcr = sbt([128, NCHK, 64], f"gcrow_h{h}")
            egrow, t_egr = sbt([128, NCHK, 64], f"egrow_h{h}")
            dl, t_dl = sbt([64, NCHK, 64], f"dl_h{h}")
            decay, t_dec = sbt([64, NCHK, 64], f"decay_h{h}")
            decayT, t_decT = sbt([64, NCHK, 64], f"decayT_h{h}")
            Ktok, t_Kt = sbt([64, NCHK, 128], f"Ktok_h{h}", BF16)
            Vtok, t_Vt = sbt([64, NCHK, 128], f"Vtok_h{h}")
            Pb = [sbt([64, NCHK, 64], f"P{i}_h{h}", BF16) for i in range(2)]
            Ptb = [sbt([64, NCHK, 64], f"Pt{i}_h{h}", BF16) for i in range(2)]
            At, t_At = sbt([64, NCHK, 64], f"At_h{h}")
            u_sb, t_usb = sbt([64, NCHK, 128], f"u_sb_h{h}")
            wT, t_wT = sbt([128, NCHK, 64], f"wT_h{h}")
            attnT, t_att = sbt([64, NCHK, 64], f"attnT_h{h}")
            qgT, t_qg = sbt([128, NCHK, 64], f"qgT_h{h}")
            kg, t_kg = sbt([64, NCHK, 128], f"kg_h{h}")
            Sst = [sbt([128, 128], f"S{i}_h{h}") for i in range(2)]
            A("pool", lambda e: e.memset(Sst[0][0][:], 0.0), writes=[Sst[0][1]])
            s_par = [0]
            vnew = [sbt([64, 128], f"vnew{i}_h{h}") for i in range(2)]
            vn_rr = [0]
            tmpd, t_tmpd = Ug, t_Ug
            Vb, t_Vb = sbt([64, NCHK, 128], f"Vb_h{h}", BF16)
            Kbg, t_Kbg = sbt([64, NCHK, 128], f"Kbg_h{h}", BF16)
            Atb, t_Atb = sbt([64, NCHK, 64], f"Atb_h{h}", BF16)

            def gen(ob, t_ob):
                    gh = gt[:, :, h]
                    bh = beta[:, :, h]
                    qT = qk[:, h, :].rearrange("p (c f) -> p c f", f=CH)
                    kT = qk[:, 2 + h, :].rearrange("p (c f) -> p c f", f=CH)
                    t_q, t_k, t_v = t_qk[h], t_qk[2 + h], t_cv[4 + h]
                    bk, tb = bank_mm(64, [(0, NCHK, Um, gh)], [tc, t_g])
                    A("dve", lambda e, bk=bk: e.tensor_copy(out=gc_col[:], in_=bk[0:64, 0:NCHK]), reads=[tb], writes=[t_gcc])
                    A("dve", lambda e, gh=gh: e.tensor_tensor(out=Ug[:], in0=bc_mid(Um, NCHK), in1=bc_last(gh, 64), op=ALU.mult),
                      reads=[tc, t_g], writes=[t_Ug])
                    bk, tb = bank_mm(128, [(c * 64, 64, ones_f[:, :], Ug[:, c, :]) for c in range(NCHK)], [tc, t_Ug])
                    A("act", lambda e, bk=bk: e.copy(out=gcrow[:].rearrange("p c f -> p (c f)"), in_=bk[:, :]),
                      reads=[tb], writes=[t_gcr])
                    yield
                    A("dve", lambda e: e.tensor_tensor(out=dl[:], in0=gcrow[0:64, :, :], in1=bc_last(gc_col[:, :], 64),
                                                       op=ALU.subtract), reads=[t_gcr, t_gcc], writes=[t_dl])
                    A("dve", lambda e: e.tensor_tensor(out=tmpd[:], in0=dl[:], in1=bc_mid(BUm, NCHK), op=ALU.add),
                      reads=[t_dl, tc], writes=[t_tmpd])
                    A("act", lambda e: e.activation(out=decay[:], in_=tmpd[:], func=AF.Exp, scale=-1.0),
                      reads=[t_tmpd], writes=[t_dec])
                    A("dve", lambda e: e.tensor_tensor(out=tmpd[:], in0=dl[:], in1=bc_mid(BLm, NCHK), op=ALU.subtract),
                      reads=[t_dl, tc, t_dec], writes=[t_tmpd])
                    A("act", lambda e: e.activation(out=decayT[:], in_=tmpd[:], func=AF.Exp), reads=[t_tmpd], writes=[t_decT])
                    A("act", lambda e: e.activation(out=egrow[:], in_=gcrow[:], func=AF.Exp), reads=[t_gcr], writes=[t_egr])
                    A("act", lambda e: e.activation(out=kgs[:], in_=dl[:, :, 63], func=AF.Exp), reads=[t_dl], writes=[t_kgs])
                    A("act", lambda e: e.activation(out=egc[:], in_=gc_col[:], func=AF.Exp), reads=[t_gcc], writes=[t_egc])
                    yield
                    for (src, t_src, dst, t_dst, isbf) in ((kT, t_k, Ktok, t_Kt, True),
                                                           (cv[:, 4 + h, :].rearrange("p (c f) -> p c f", f=CH), t_v, Vtok, t_Vt, False)):
                        for half in range(2):
                            bk, tb = cx.bank()
                            bkv = bk[:, :].bitcast(BF16) if isbf else bk[:, :]
                            idm = ident_bf[:] if isbf else ident
                            for c4 in range(4):
                                c = half * 4 + c4
                                A("pe", lambda e, bkv=bkv, c4=c4, c=c, src=src, idm=idm: e.transpose(
                                    out=bkv[0:64, c4 * 128:(c4 + 1) * 128], in_=src[:, c, :], identity=idm),
                                  reads=[t_src, tc], writes=[tb])
                            evac(cx, dst[:, half * 4:half * 4 + 4, :].rearrange("p c f -> p (c f)"), bkv[0:64, 0:512], [tb], [t_dst])
                    bk, tb = bank_mm(64, [(c * 64, 64, kT[:, c, :], kT[:, c, :]) for c in range(NCHK)], [t_k])
                    P0, t_P0 = Pb[0]
                    Pt0, t_Pt0 = Ptb[0]
                    A("dve", lambda e, bk=bk: e.tensor_tensor(out=tmpd[:].rearrange("p c f -> p (c f)"), in0=bk[0:64, :],
                                                              in1=decay[:].rearrange("p c f -> p (c f)"), op=ALU.mult),
                      reads=[tb, t_dec], writes=[t_tmpd])
                    A("dve", lambda e, bh=bh: e.tensor_tensor(out=tmpd[:], in0=tmpd[:], in1=bc_last(bh, 64), op=ALU.mult),
                      reads=[t_tmpd, t_beta], writes=[t_tmpd])
                    A("dve", lambda e, P0=P0: e.tensor_tensor(out=P0[:], in0=tmpd[:], in1=bc_mid(SLm, NCHK), op=ALU.mult),
                      reads=[t_tmpd, tc], writes=[t_P0])
                    yield
                    bk, tb = cx.bank()
                    bkv = bk[:, :].bitcast(BF16)
                    for c in range(NCHK):
                        A("pe", lambda e, bkv=bkv, c=c, P0=P0: e.transpose(out=bkv[0:64, c * 64:(c + 1) * 64], in_=P0[:, c, :],
                                                                         identity=id64_bf), reads=[t_P0, tc], writes=[tb])
                    A("act", lambda e, bkv=bkv, Pt0=Pt0: e.copy(out=Pt0[:].rearrange("p c f -> p (c f)"), in_=bkv[0:64, 0:512]),
                      reads=[tb], writes=[t_Pt0])
                    A("dve", lambda e, Pt0=Pt0: e.tensor_tensor(out=At[:], in0=bc_mid(id64, NCHK), in1=Pt0[:], op=ALU.subtract),
                      reads=[t_Pt0, tc], writes=[t_At])
                    A("act", lambda e: e.copy(out=Atb[:], in_=At[:]), reads=[t_At], writes=[t_Atb])
                    yield
                    for lv in range(5):
                        Pc, t_Pc = Pb[lv % 2]
                        Ptc, t_Ptc = Ptb[lv % 2]
                        Pn, t_Pn = Pb[(lv + 1) % 2]
                        Ptn, t_Ptn = Ptb[(lv + 1) % 2]
                        bk, tb = bank_mm(64, [(c * 64, 64, Ptc[:, c, :], Pc[:, c, :]) for c in range(NCHK)], [t_Pc, t_Ptc])
                        A("dve", lambda e, bk=bk, Pn=Pn: e.tensor_copy(out=Pn[:].rearrange("p c f -> p (c f)"), in_=bk[0:64, :]),
                          reads=[tb], writes=[t_Pn])
                        bk, tb = bank_mm(64, [(c * 64, 64, Pc[:, c, :], Ptc[:, c, :]) for c in range(NCHK)], [t_Pc, t_Ptc])
                        A("act", lambda e, bk=bk, Ptn=Ptn: e.copy(out=Ptn[:].rearrange("p c f -> p (c f)"), in_=bk[0:64, :]),
                          reads=[tb], writes=[t_Ptn])
                        bk, tb = bank_mm(64, [(c * 64, 64, Pn[:, c, :], Atb[:, c, :]) for c in range(NCHK)], [t_Pn, t_Atb])
                        A("dve", lambda e, bk=bk: e.tensor_tensor(out=At[:].rearrange("p c f -> p (c f)"),
                                                                  in0=At[:].rearrange("p c f -> p (c f)"), in1=bk[0:64, :],
                                                                  op=ALU.add), reads=[tb, t_At], writes=[t_At])
                        A("act", lambda e: e.copy(out=Atb[:], in_=At[:]), reads=[t_At], writes=[t_Atb])
                        yield
                    A("dve", lambda e: e.tensor_tensor(out=kg[:], in0=Ktok[:], in1=bc_last(kgs[:, :], 128), op=ALU.mult),
                      reads=[t_Kt, t_kgs], writes=[t_kg])
                    A("dve", lambda e, bh=bh: e.tensor_tensor(out=Vb[:], in0=Vtok[:], in1=bc_last(bh, 128), op=ALU.mult),
                      reads=[t_Vt, t_beta], writes=[t_Vb])
                    A("dve", lambda e, bh=bh: e.tensor_tensor(out=be[:], in0=egc[:], in1=bh, op=ALU.mult),
                      reads=[t_egc, t_beta], writes=[t_be])
                    A("dve", lambda e: e.tensor_tensor(out=Kbg[:], in0=Ktok[:], in1=bc_last(be[:, :], 128), op=ALU.mult),
                      reads=[t_Kt, t_be], writes=[t_Kbg])
                    for half in range(2):
                        bk, tb = bank_mm(64, [(c4 * 128, 128, Atb[:, half * 4 + c4, :], Vb[:, half * 4 + c4, :]) for c4 in range(4)],
                                         [t_Atb, t_Vb])
                        evac(cx, u_sb[:, half * 4:half * 4 + 4, :].rearrange("p c f -> p (c f)"), bk[0:64, :], [tb], [t_usb])
                    bk, tb = bank_mm(128, [(c * 64, 64, Kbg[:, c, :], Atb[:, c, :]) for c in range(NCHK)], [t_Atb, t_Kbg])
                    evac(cx, wT[:].rearrange("p c f -> p (c f)"), bk[:, :], [tb], [t_wT])
                    yield
                    bk, tb = bank_mm(64, [(c * 64, 64, kT[:, c, :], qT[:, c, :]) for c in range(NCHK)], [t_k, t_q])
                    A("dve", lambda e, bk=bk: e.tensor_tensor(out=attnT[:].rearrange("p c f -> p (c f)"), in0=bk[0:64, :],
                                                              in1=decayT[:].rearrange("p c f -> p (c f)"), op=ALU.mult),
                      reads=[tb, t_decT], writes=[t_att])
                    A("dve", lambda e, qT=qT: e.tensor_tensor(out=qgT[:], in0=qT, in1=egrow[:], op=ALU.mult),
                      reads=[t_q, t_egr], writes=[t_qg])
                    yield
                    obk, t_obk = cx.bank()
                    cx.reserved.add(id(obk))
                    for c in range(NCHK):
                        Sc, t_Sc = Sst[s_par[0]]
                        Sn, t_Sn = Sst[1 - s_par[0]]
                        s_par[0] = 1 - s_par[0]
                        vn, t_vn = vnew[vn_rr[0] % 2]
                        vn_rr[0] += 1
                        bk, tb = bank_mm(64, [(0, 128, wT[:, c, :], Sc[:, :])], [t_wT, t_Sc])
                        A("dve", lambda e, bk=bk, c=c, vn=vn: e.tensor_tensor(out=vn[:], in0=u_sb[:, c, :], in1=bk[0:64, 0:128],
                                                                             op=ALU.subtract), reads=[tb, t_usb], writes=[t_vn])
                        A("pe", lambda e, c=c, Sc=Sc, obk=obk: e.matmul(obk[:, c * 64:(c + 1) * 64], lhsT=Sc[:, :], rhs=qgT[:, c, :],
                                                                       start=True, stop=False), reads=[t_Sc, t_qg], writes=[t_obk])
                        A("pe", lambda e, c=c, vn=vn, obk=obk: e.matmul(obk[:, c * 64:(c + 1) * 64], lhsT=vn[:, :], rhs=attnT[:, c, :],
                                                                       start=False, stop=True), reads=[t_vn, t_att], writes=[t_obk])
                        bk, tb = bank_mm(128, [(0, 128, kg[:, c, :], vn[:, :])], [t_kg, t_vn])
                        A("dve", lambda e, bk=bk, c=c, Sc=Sc, Sn=Sn: e.scalar_tensor_tensor(
                            out=Sn[:], in0=Sc[:], scalar=egrow[:, c, 63:64], in1=bk[:, 0:128], op0=ALU.mult, op1=ALU.add),
                          reads=[tb, t_Sc, t_egr], writes=[t_Sn])
                        yield
                    A("act", lambda e, obk=obk: e.copy(out=o_sb[:], in_=obk[:, :]), reads=[t_obk], writes=[t_osb])
                    cx.reserved.discard(id(obk))
                    A("act", lambda e: e.activation(out=sqb[:], in_=o_sb[:], func=AF.Square), reads=[t_osb], writes=[t_sqb])
                    bk, tb = cx.bank()
                    A("pe", lambda e, bk=bk: e.matmul(bk[:, :], lhsT=cx.ones_bf[:], rhs=sqb[:], start=True, stop=True),
                      reads=[t_sqb, tc], writes=[tb])
                    A("act", lambda e, bk=bk: e.activation(out=rstd[:], in_=bk[:, :], func=AF.Ln, bias=EPS, scale=1.0 / HD),
                      reads=[tb], writes=[t_rstd])
                    A("act", lambda e: e.activation(out=rstd[:], in_=rstd[:], func=AF.Exp, scale=-0.5), reads=[t_rstd], writes=[t_rstd])
                    A("dve", lambda e: e.scalar_tensor_tensor(out=o_tmp[:], in0=o_sb[:], scalar=gon[:, 0:1], in1=rstd[:],
                                                              op0=ALU.mult, op1=ALU.mult),
                      reads=[t_osb, t_gon, t_rstd], writes=[t_otmp])
                    A("dve", lambda e, h=h, ob=ob: e.tensor_tensor(out=ob[:, h, :], in0=o_tmp[:], in1=siluz[:, h, :], op=ALU.mult),
                      reads=[t_otmp, t_sz[h]], writes=[t_ob])

            return gen

        o_sb, t_osb = sbt([128, T], "o_sb")
        o_tmp, t_otmp = sbt([128, T], "o_tmp")
        outb = [sbt([128, 2, T], f"outb{i}", BF16) for i in range(2)]
        ov = odnT.rearrange("(h p) t -> p h t", p=128)
        head_fns = [make_head(0), make_head(1)]

        def bank_mm(nparts, items, reads, name=None):
            bk, tb = cx.bank()
            for (c0, ncol, lh, rh) in items:
                A("pe", lambda e, bk=bk, c0=c0, ncol=ncol, lh=lh, rh=rh: e.matmul(
                    bk[0:nparts, c0:c0 + ncol], lhsT=lh, rhs=rh, start=True, stop=True), reads=reads, writes=[tb])
            return bk, tb

        for it in range(ntiles):
            t0 = it * T
            ut, t_u = u_t[0], t_ut[0]
            A("sp", lambda e, ut=ut, t0=t0: e.dma_start(out=ut[:], in_=uT_all[:, t0:t0 + T].rearrange("(k p) t -> p k t", p=128)),
              writes=[t_u], dma_tok=t_u)
            for c in range(8):
                bk, tb = cx.bank()
                for k in range(KC):
                    A("pe", lambda e, bk=bk, c=c, k=k, ut=ut: e.matmul(bk[:, :], lhsT=wres[:, k, c * 128:(c + 1) * 128],
                                                                      rhs=ut[:, k, :], start=(k == 0), stop=(k == KC - 1)),
                      reads=[t_w, t_u], writes=[tb])
                if c < 6:
                    evac(cx, rawbuf[:, c, 3:T + 3], bk[:, :], [tb], [t_raw[c]])
                else:
                    A("act", lambda e, bk=bk, c=c: e.activation(out=siluz[:, c - 6, :], in_=bk[:, :], func=AF.Silu),
                      reads=[tb], writes=[t_sz[c - 6]])
            for c in range(6):
                A("dve", lambda e, c=c: e.tensor_scalar(out=cv[:, c, :], in0=rawbuf[:, c, 0:T],
                                                        scalar1=convw[:, c * 4:c * 4 + 1], scalar2=None, op0=ALU.mult),
                  reads=[t_raw[c], t_cw], writes=[t_cv[c]])
                for j in range(1, 4):
                    A("dve", lambda e, c=c, j=j: e.scalar_tensor_tensor(out=cv[:, c, :], in0=rawbuf[:, c, j:j + T],
                                                                        scalar=convw[:, c * 4 + j:c * 4 + j + 1],
                                                                        in1=cv[:, c, :], op0=ALU.mult, op1=ALU.add),
                      reads=[t_raw[c], t_cw, t_cv[c]], writes=[t_cv[c]])
                A("act", lambda e, c=c: e.copy(out=rawbuf[:, c, 0:3], in_=rawbuf[:, c, T:T + 3]),
                  reads=[t_raw[c]], writes=[t_raw[c]])
                A("act", lambda e, c=c: e.activation(out=cv[:, c, :], in_=cv[:, c, :], func=AF.Silu),
                  reads=[t_cv[c]], writes=[t_cv[c]])
            for idx in range(4):
                A("act", lambda e, idx=idx: e.activation(out=sqb[:], in_=cv[:, idx, :], func=AF.Square),
                  reads=[t_cv[idx]], writes=[t_sqb])
                bk, tb = cx.bank()
                A("pe", lambda e, bk=bk: e.matmul(bk[:, :], lhsT=cx.ones_bf[:], rhs=sqb[:], start=True, stop=True),
                  reads=[t_sqb, tc], writes=[tb])
                A("act", lambda e, bk=bk: e.activation(out=rstd[:], in_=bk[:, :], func=AF.Ln, bias=1e-6, scale=1.0),
                  reads=[tb], writes=[t_rstd])
                A("act", lambda e: e.activation(out=rstd[:], in_=rstd[:], func=AF.Exp, scale=-0.5), reads=[t_rstd], writes=[t_rstd])
                sc = float(HD ** -0.5) if idx < 2 else 1.0
                A("dve", lambda e, idx=idx, sc=sc: e.scalar_tensor_tensor(out=qk[:, idx, :], in0=cv[:, idx, :], scalar=sc,
                                                                          in1=rstd[:], op0=ALU.mult, op1=ALU.mult),
                  reads=[t_cv[idx], t_rstd], writes=[t_qk[idx]])
            bk, tb = cx.bank()
            for ck in range(NCHK):
                for k in range(KC):
                    A("pe", lambda e, bk=bk, ck=ck, k=k, ut=ut: e.matmul(bk[0:64, ck * 4:ck * 4 + 4],
                                                                        lhsT=ut[:, k, ck * CH:(ck + 1) * CH],
                                                                        rhs=wbar[:, k, :], start=(k == 0), stop=(k == KC - 1)),
                      reads=[t_wb, t_u], writes=[tb])
            A("dve", lambda e, bk=bk: e.tensor_copy(out=ba[:].rearrange("p c f -> p (c f)"), in_=bk[0:64, 0:NCHK * 4]),
              reads=[tb], writes=[t_ba])
            A("act", lambda e: e.activation(out=beta[:], in_=ba[:, :, 0:2], func=AF.Sigmoid), reads=[t_ba], writes=[t_beta])
            A("dve", lambda e: e.tensor_tensor(out=spx[:], in0=ba[:, :, 2:4], in1=bc_mid(hv[:, 0:2], NCHK), op=ALU.add),
              reads=[t_ba, t_hv], writes=[t_spx])
            A("act", lambda e: e.activation(out=spx[:], in_=spx[:], func=AF.Exp), reads=[t_spx], writes=[t_spx])
            A("dve", lambda e: e.tensor_scalar(out=spx[:], in0=spx[:], scalar1=1.0, scalar2=None, op0=ALU.add),
              reads=[t_spx], writes=[t_spx])
            A("act", lambda e: e.activation(out=spx[:], in_=spx[:], func=AF.Ln), reads=[t_spx], writes=[t_spx])
            A("dve", lambda e: e.scalar_tensor_tensor(out=gt[:], in0=spx[:], scalar=-1.0, in1=bc_mid(expA[:, 0:2], NCHK),
                                                      op0=ALU.mult, op1=ALU.mult), reads=[t_spx, t_hv], writes=[t_g])
            ob, t_ob = outb[it % 2]
            gens = [head_fns[h](ob, t_ob) for h in range(2)]
            while gens:
                for g in list(gens):
                    try:
                        next(g)
                    except StopIteration:
                        gens.remove(g)
            A("sp", lambda e, ob=ob, t0=t0: e.dma_start(out=ov[:, :, t0:t0 + T], in_=ob[:]), reads=[t_ob], dma_tok=t_ob)
        R.emit()
    return nc


SB_ONE = 128
SB_NM0 = 129
SB_VM0 = 129 + 4 * 512
SB_TOT = 129 + 8 * 512


def sb_consts():
    c = np.zeros((128, SB_TOT), np.float32)
    c[:, 0:128] = np.eye(128, dtype=np.float32)
    t = np.arange(128)[:, None]
    j = np.arange(128)[None, :]
    tri = (t + j < 128).astype(np.float32)
    for q4 in range(4):
        for b in range(4):
            blk = c[:, SB_NM0 + q4 * 512 + b * 128: SB_NM0 + q4 * 512 + (b + 1) * 128]
            if 3 - b > q4:
                blk[:] = 1.0
            elif 3 - b == q4:
                blk[:] = tri
    c[:, SB_VM0:SB_VM0 + 4 * 512] = 1.0 - c[:, SB_NM0:SB_NM0 + 4 * 512]
    c[:, SB_ONE] = 1.0
    return c


def build_stage2_sb(ntiles=SEQ // T, heads=(0, 1)):
    S = ntiles * T
    NB = S // 128
    nc = bass.Bass("TRN2", target_bir_lowering=False)
    uT_all = nc.dram_tensor("uT_all", [D, S], BF16, kind="ExternalInput").ap()
    uT_rev = nc.dram_tensor("uT_rev", [D, S], BF16, kind="ExternalInput").ap()
    wsb = nc.dram_tensor("wsb", [D, 768], F32, kind="ExternalInput").ap()
    consts_d = nc.dram_tensor("consts", [128, SB_TOT], F32, kind="ExternalInput").ap()
    osbT = nc.dram_tensor("osbT", [256, S], BF16, kind="ExternalOutput").ap()
    with contextlib.ExitStack() as stack:
        cx = Ctx(nc, stack)
        R = cx.R
        A = R.add
        tc = cx.t_const
        consts = R.sb([128, 129], F32, "consts")
        A("sp", lambda e: e.dma_start(out=consts[:], in_=consts_d[:, 0:129]), writes=[tc], dma_tok=tc)
        ones_c = consts[:, 128:129]
        masks = R.sb([128, 8 * 512], BF16, "masks")
        t_mk = R.tok("masks")
        A("pool", lambda e: e.dma_start(out=masks[:], in_=consts_d[:, SB_NM0:SB_TOT]), writes=[t_mk], dma_tok=t_mk)
        ident_bf = R.sb([128, 128], BF16, "ident_bf")
        A("dve", lambda e: e.tensor_copy(out=ident_bf[:], in_=consts[:, 0:128]), reads=[tc], writes=[tc])
        ones_f = ones_c.to_broadcast([128, T])
        wres = R.sb([128, KC, 768], BF16, "wres")
        t_w = R.tok("wres")
        wv = wsb.rearrange("(k p) n -> p k n", p=128)
        for half in range(2):
            A("pool", lambda e, half=half: e.dma_start(out=wres[:, :, half * 384:(half + 1) * 384],
                                                      in_=wv[:, :, half * 384:(half + 1) * 384]),
              writes=[t_w], dma_tok=t_w)
        uf, tuf = R.sb([128, KC, T], BF16, "u_f"), R.tok("u_f")
        ur, tur = R.sb([128, KC, T], BF16, "u_r"), R.tok("u_r")
        QT = R.sb([128, S], BF16, "QT")
        KTr = R.sb([128, S], BF16, "KTr")
        dVr = R.sb([128, NB, 128], BF16, "dVr")
        t_Q = [R.tok(f"Q{i}") for i in range(ntiles)]
        t_K = [R.tok(f"K{i}") for i in range(ntiles)]
        t_V = [R.tok(f"V{i}") for i in range(ntiles)]
        VTb = [R.sb([128, T + 1], F32, f"VTb{i}") for i in range(2)]
        t_VT = [R.tok(f"VTb{i}") for i in range(2)]
        dVT = R.sb([128, T], BF16, "dVT")
        t_dVT = R.tok("dVT")
        smb = [[R.sb([128, T], F32, f"smb{i}_{k}") for k in range(2)] for i in range(4)]
        t_sm = [[R.tok(f"smb{i}_{k}") for k in range(2)] for i in range(4)]
        Pb = [[R.sb([128, T], BF16, f"Pb{i}_{k}") for k in range(2)] for i in range(4)]
        t_P = [[R.tok(f"Pb{i}_{k}") for k in range(2)] for i in range(4)]
        Pm = [R.sb([128, T], BF16, f"Pm{i}") for i in range(4)]
        t_Pm = [R.tok(f"Pm{i}") for i in range(4)]
        vsh, t_vsh = R.sb([128, T], F32, "vsh"), R.tok("vsh")
        ATs = R.sb([128, 4, T], BF16, "ATs")
        t_AT = [R.tok(f"ATs_{hh}") for hh in range(2)]
        osb, t_os = R.sb([128, T], BF16, "osb"), R.tok("osb")
        scale = float(HD ** -0.5)
        uva = uT_all.rearrange("(k p) t -> p k t", p=128)
        uvr = uT_rev.rearrange("(k p) t -> p k t", p=128)
        gstep = 0
        for h in heads:
            for n, it in enumerate(reversed(range(ntiles))):
                t0 = it * T
                A("sp", lambda e, t0=t0: e.dma_start(out=uf[:], in_=uva[:, :, t0:t0 + T]), writes=[tuf], dma_tok=tuf)
                A("sp", lambda e, t0=t0: e.dma_start(out=ur[:], in_=uvr[:, :, t0:t0 + T]), writes=[tur], dma_tok=tur)
                for (src, tsrc, col, dst, tdst) in ((uf, tuf, h * 128, QT, t_Q[it]), (ur, tur, 256 + h * 128, KTr, t_K[it])):
                    bk, tb = cx.bank()
                    for k in range(KC):
                        A("pe", lambda e, bk=bk, k=k, src=src, col=col: e.matmul(bk[:, :], lhsT=wres[:, k, col:col + 128],
                                                                                rhs=src[:, k, :], start=(k == 0), stop=(k == KC - 1)),
                          reads=[t_w, tsrc], writes=[tb])
                    evac(cx, dst[:, t0:t0 + T], bk[:, :], [tb], [tdst])
                vt, tvt = VTb[n % 2], t_VT[n % 2]
                vtp, tvtp = VTb[(n + 1) % 2], t_VT[(n + 1) % 2]
                bk, tb = cx.bank()
                for k in range(KC):
                    A("pe", lambda e, bk=bk, k=k, h=h: e.matmul(bk[:, :], lhsT=wres[:, k, 512 + h * 128:512 + (h + 1) * 128],
                                                               rhs=ur[:, k, :], start=(k == 0), stop=(k == KC - 1)),
                      reads=[t_w, tur], writes=[tb])
                A("act", lambda e, bk=bk, vt=vt: e.copy(out=vt[:, 0:T], in_=bk[:, :]), reads=[tb], writes=[tvt])
                if n == 0:
                    A("pool", lambda e, vt=vt: e.memset(vt[:, T:T + 1], 0.0), reads=[tvt], writes=[tvt])
                else:
                    A("act", lambda e, vt=vt, vtp=vtp: e.copy(out=vt[:, T:T + 1], in_=vtp[:, 0:1]), reads=[tvtp, tvt], writes=[tvt])
                A("pool", lambda e, vt=vt: e.tensor_tensor(out=dVT[:], in0=vt[:, 1:T + 1], in1=vt[:, 0:T], op=ALU.subtract),
                  reads=[tvt], writes=[t_dVT])
                bk2, tb2 = cx.bank()
                bkb = bk2[:, :].bitcast(BF16)
                for b in range(4):
                    A("pe", lambda e, bkb=bkb, b=b: e.transpose(out=bkb[:, b * 128:(b + 1) * 128], in_=dVT[:, b * 128:(b + 1) * 128],
                                                               identity=ident_bf[:]), reads=[t_dVT, tc], writes=[tb2])
                evac(cx, dVr[:, it * 4:it * 4 + 4, :].rearrange("p b d -> p (b d)"), bkb[:, 0:T], [tb2], [t_V[it]])
            steps = [(qg, i) for qg in range(NB // 4) for i in range(qg + 1)]

            def issue_scores(qg, i, par):
                j0 = (NB - 4 - 4 * qg + 4 * i) * 128
                kt = j0 // T
                zb = []
                for q4 in range(4):
                    qb = qg * 4 + q4
                    bk, tb = cx.bank()
                    zb.append((bk, tb))
                    A("pe", lambda e, bk=bk, qb=qb, j0=j0: e.matmul(bk[:, :], lhsT=QT[:, qb * 128:(qb + 1) * 128],
                                                                   rhs=KTr[:, j0:j0 + T], start=True, stop=True),
                      reads=[t_Q[qg], t_K[kt]], writes=[tb])
                for q4 in range(4):
                    bk, tb = zb[q4]
                    sm, tsm = smb[q4][par], t_sm[q4][par]
                    A("act", lambda e, bk=bk, sm=sm: e.activation(out=sm[:], in_=bk[:, :], func=AF.Sigmoid, scale=-scale),
                      reads=[tb], writes=[tsm])
                    if i == 0:
                        nm = masks[:, q4 * 512:(q4 + 1) * 512]
                        A("dve", lambda e, sm=sm, nm=nm: e.tensor_tensor(out=sm[:], in0=sm[:], in1=nm, op=ALU.max),
                          reads=[tsm, t_mk], writes=[tsm])

            issue_scores(steps[0][0], steps[0][1], gstep % 2)
            obk = t_obk = None
            for n, (qg, i) in enumerate(steps):
                par = gstep % 2
                gstep += 1
                if n + 1 < len(steps):
                    issue_scores(steps[n + 1][0], steps[n + 1][1], gstep % 2)
                ntile = qg + 1
                JB = NB - 4 - 4 * qg + 4 * i
                kt = JB // 4
                if i == 0:
                    obk, t_obk = cx.bank()
                    cx.reserved.add(id(obk))
                    tq0 = qg * T
                    if tq0 == 0:
                        A("pool", lambda e: e.memset(uf[:, :, 0:1], 0.0), writes=[tuf])
                        A("sp", lambda e: e.dma_start(out=uf[:, :, 1:T], in_=uva[:, :, 0:T - 1]), writes=[tuf], dma_tok=tuf)
                    else:
                        A("sp", lambda e, tq0=tq0: e.dma_start(out=uf[:], in_=uva[:, :, tq0 - 1:tq0 + T - 1]), writes=[tuf], dma_tok=tuf)
                for q4 in range(4):
                    sm, tsm = smb[q4][par], t_sm[q4][par]
                    Pc, tPc = Pb[q4][par], t_P[q4][par]
                    Pp, tPp = Pb[q4][1 - par], t_P[q4][1 - par]
                    if i == 0:
                        A("dve", lambda e, sm=sm, Pc=Pc: e.tensor_tensor_scan(out=Pc[:], data0=sm[:], data1=ones_f, initial=ones_c,
                                                                             op0=ALU.mult, op1=ALU.mult),
                          reads=[tsm, tc], writes=[tPc])
                    else:
                        A("dve", lambda e, sm=sm, Pc=Pc, Pp=Pp: e.tensor_tensor_scan(out=Pc[:], data0=sm[:], data1=ones_f,
                                                                                    initial=Pp[:, T - 1:T], op0=ALU.mult, op1=ALU.mult),
                          reads=[tsm, tPp, tc], writes=[tPc])
                if i == 0:
                    for q4 in range(4):
                        vm = masks[:, (4 + q4) * 512:(5 + q4) * 512]
                        A("pool", lambda e, q4=q4, vm=vm, par=par: e.tensor_tensor(out=Pm[q4][:], in0=Pb[q4][par][:], in1=vm, op=ALU.mult),
                          reads=[t_P[q4][par], t_mk], writes=[t_Pm[q4]])
                for hh in range(2):
                    bk2, tb2 = cx.bank()
                    bkb = bk2[:, :].bitcast(BF16)
                    for bb in range(2):
                        b = hh * 2 + bb
                        for q4 in range(4):
                            Pc, tPc = (Pm[q4], t_Pm[q4]) if i == 0 else (Pb[q4][par], t_P[q4][par])
                            A("pe", lambda e, bkb=bkb, b=b, bb=bb, q4=q4, Pc=Pc: e.transpose(
                                out=bkb[:, bb * 512 + q4 * 128:bb * 512 + (q4 + 1) * 128], in_=Pc[:, b * 128:(b + 1) * 128],
                                identity=ident_bf[:]), reads=[tPc, tc], writes=[tb2])
                    A("act", lambda e, bkb=bkb, hh=hh: e.copy(out=ATs[:, hh * 2:hh * 2 + 2, :].rearrange("p b q -> p (b q)"),
                                                             in_=bkb[:, :]), reads=[tb2], writes=[t_AT[hh]])
                for b in range(4):
                    first = (i == 0 and b == 0)
                    last = (i == ntile - 1 and b == 3)
                    A("pe", lambda e, obk=obk, b=b, JB=JB, first=first, last=last: e.matmul(
                        obk[:, :], lhsT=dVr[:, JB + b, :], rhs=ATs[:, b, :], start=first, stop=last),
                      reads=[t_AT[b // 2], t_V[kt]], writes=[t_obk])
                if i == ntile - 1:
                    bkv, tbv = cx.bank()
                    for k in range(KC):
                        A("pe", lambda e, bkv=bkv, k=k, h=h: e.matmul(bkv[:, :], lhsT=wres[:, k, 512 + h * 128:512 + (h + 1) * 128],
                                                                     rhs=uf[:, k, :], start=(k == 0), stop=(k == KC - 1)),
                          reads=[t_w, tuf], writes=[tbv])
                    A("act", lambda e, bkv=bkv: e.copy(out=vsh[:], in_=bkv[:, :]), reads=[tbv], writes=[t_vsh])
                    A("dve", lambda e, obk=obk: e.tensor_tensor(out=osb[:], in0=obk[:, :], in1=vsh[:], op=ALU.add),
                      reads=[t_obk, t_vsh], writes=[t_os])
                    cx.reserved.discard(id(obk))
                    A("sp", lambda e, qg=qg, h=h: e.dma_start(out=osbT[h * 128:(h + 1) * 128, qg * T:(qg + 1) * T], in_=osb[:]),
                      reads=[t_os], dma_tok=t_os)
        R.emit()
    return nc


def linear_fm(cx, ws, w_dram, kch, xT, t_x, col0, nout, consume):
    R = cx.R
    for c4 in range(0, nout, 4):
        n = min(4, nout - c4)
        wt, tw = ws.load(w_dram, 0, kch, col0 + c4 * 128, n * 128)
        for j in range(n):
            bk, tb = cx.bank()
            for k in range(kch):
                R.add("pe", lambda e, bk=bk, k=k, j=j, wt=wt: e.matmul(bk[:, :], lhsT=wt[:, k, j * 128:(j + 1) * 128],
                                                                      rhs=xT[:, k, :], start=(k == 0), stop=(k == kch - 1)),
                      reads=[tw, t_x], writes=[tb])
            consume(c4 + j, bk, tb)


def build_stage3(ntiles=TOK // T):
    nc = bass.Bass("TRN2", target_bir_lowering=False)
    NTK = ntiles * T
    h1T = nc.dram_tensor("h1T", [D, NTK], F32, kind="ExternalInput").ap()
    uT_d = nc.dram_tensor("uT", [D, NTK], BF16, kind="ExternalInput").ap()
    odn_d = nc.dram_tensor("odnT", [D, NTK], BF16, kind="ExternalInput").ap()
    osb_d = nc.dram_tensor("osbT", [D, NTK], BF16, kind="ExternalInput").ap()
    pT_d = nc.dram_tensor("pT", [PLE, NTK], F32, kind="ExternalInput").ap()
    wgate = nc.dram_tensor("wgate", [D, 2 * D], F32, kind="ExternalInput").ap()
    wbd = nc.dram_tensor("wbd", [D, D], F32, kind="ExternalInput").ap()
    wbs = nc.dram_tensor("wbs", [D, D], F32, kind="ExternalInput").ap()
    wout = nc.dram_tensor("wout", [D, D], F32, kind="ExternalInput").ap()
    wg = nc.dram_tensor("wg", [D, DFF], F32, kind="ExternalInput").ap()
    wu = nc.dram_tensor("wu", [D, DFF], F32, kind="ExternalInput").ap()
    wd = nc.dram_tensor("wd", [DFF, D], F32, kind="ExternalInput").ap()
    wpg = nc.dram_tensor("wpg", [D, D], F32, kind="ExternalInput").ap()
    wpp = nc.dram_tensor("wpp", [PLE, D], F32, kind="ExternalInput").ap()
    gains = nc.dram_tensor("gains", [128, 5 * KC], F32, kind="ExternalInput").ap()
    ident_d = nc.dram_tensor("ident", [128, 128], F32, kind="ExternalInput").ap()
    out = nc.dram_tensor("out", [NTK, D], F32, kind="ExternalOutput").ap()
    with contextlib.ExitStack() as stack:
        cx = Ctx(nc, stack)
        R = cx.R
        A = R.add
        g_all, t_g = load_vec_fm(cx, gains, "gains_sb")
        ident = R.sb([128, 128], F32, "ident")
        A("sp", lambda e: e.dma_start(out=ident[:], in_=ident_d[:, :]), writes=[cx.t_const], dma_tok=cx.t_const)
        big = R.sb([128, KC * T], F32, "big")
        otok = big[:].rearrange("p (s d) -> p s d", s=4)
        fT = big[:].rearrange("p (c t) -> p c t", c=KC)
        t_f = R.tok("fT")
        t_otok = [t_f] * 4
        hT = R.sb([128, KC, T], F32, "hT")
        t_h = R.tok("hT")
        uT = R.sb([128, KC, T], BF16, "uT")
        t_u = R.tok("uT")
        sq = R.sb([128, KC, T], BF16, "sq")
        t_sq = R.tok("sq")
        rstd = R.sb([128, T], F32, "rstd")
        t_rstd = R.tok("rstd")
        actT = R.sb([128, DFF // 128, T], BF16, "actT")
        t_act = R.tok("actT")
        odn = actT[:, 0:KC, :]
        osb = actT[:, KC:2 * KC, :]
        tmp = [R.sb([128, T], F32, f"tmp{i}") for i in range(2)]
        t_tmp = [R.tok(f"tmp{i}") for i in range(2)]
        pTb = R.sb([128, 2, T], BF16, "pTb")
        t_p = R.tok("pTb")
        ws = WStream(cx, KC, 3, "w")
        hv = h1T.rearrange("(c p) t -> p c t", p=128)
        uv = uT_d.rearrange("(c p) t -> p c t", p=128)
        dnv = odn_d.rearrange("(c p) t -> p c t", p=128)
        sbv = osb_d.rearrange("(c p) t -> p c t", p=128)
        pv = pT_d.rearrange("(c p) t -> p c t", p=128)
        for it in range(ntiles):
            r0 = it * T
            A("sp", lambda e, r0=r0: e.dma_start(out=hT[:], in_=hv[:, :, r0:r0 + T]), writes=[t_h], dma_tok=t_h)
            A("sp", lambda e, r0=r0: e.dma_start(out=uT[:], in_=uv[:, :, r0:r0 + T]), writes=[t_u], dma_tok=t_u)
            t_dn = R.tok("odn_ld")
            A("sp", lambda e, r0=r0: e.dma_start(out=odn, in_=dnv[:, :, r0:r0 + T]), writes=[t_act], dma_tok=t_dn)
            A("sp", lambda e, r0=r0: e.dma_start(out=osb, in_=sbv[:, :, r0:r0 + T]), writes=[t_act], dma_tok=t_dn)
            A("pool", lambda e, r0=r0: e.dma_start(out=pTb[:], in_=pv[:, :, r0:r0 + T]), writes=[t_p], dma_tok=t_p)
            for (gcol, wb, xs, tx, second) in ((0, wbd, odn, t_act, False), (D, wbs, osb, t_act, True)):
                for c4 in range(0, KC, 4):
                    gtile, tgw = ws.load(wgate, 0, KC, gcol + c4 * 128, 512)
                    btile, tbw = ws.load(wb, 0, KC, c4 * 128, 512)
                    for j in range(4):
                        c = c4 + j
                        bg, tbg = cx.bank()
                        for k in range(KC):
                            A("pe", lambda e, bg=bg, k=k, j=j, gtile=gtile: e.matmul(
                                bg[:, :], lhsT=gtile[:, k, j * 128:(j + 1) * 128], rhs=uT[:, k, :],
                                start=(k == 0), stop=(k == KC - 1)), reads=[tgw, t_u], writes=[tbg])
                        bb, tbb = cx.bank()
                        for k in range(KC):
                            A("pe", lambda e, bb=bb, k=k, j=j, btile=btile, xs=xs: e.matmul(
                                bb[:, :], lhsT=btile[:, k, j * 128:(j + 1) * 128], rhs=xs[:, k, :],
                                start=(k == 0), stop=(k == KC - 1)), reads=[tbw, tx], writes=[tbb])
                        tt, tm = t_tmp[c % 2], tmp[c % 2]
                        A("act", lambda e, bg=bg, tm=tm: e.activation(out=tm[:], in_=bg[:, :], func=AF.Sigmoid),
                          reads=[tbg], writes=[tt])
                        if not second:
                            A("dve", lambda e, bb=bb, tm=tm, c=c: e.tensor_tensor(out=fT[:, c, :], in0=tm[:], in1=bb[:, :],
                                                                                 op=ALU.mult), reads=[tt, tbb], writes=[t_f])
                        else:
                            A("dve", lambda e, bb=bb, tm=tm: e.tensor_tensor(out=tm[:], in0=tm[:], in1=bb[:, :], op=ALU.mult),
                              reads=[tt, tbb], writes=[tt])
                            A("dve", lambda e, tm=tm, c=c: e.tensor_tensor(out=sq[:, c, :], in0=tm[:], in1=fT[:, c, :],
                                                                          op=ALU.add), reads=[tt, t_f], writes=[t_sq])
            linear_fm(cx, ws, wout, KC, sq, t_sq, 0, KC,
                      lambda c, bk, tb: evac(cx, fT[:, c, :], bk[:, :], [tb], [t_f]))
            resid_norm_add(cx, fT, t_f, g_all[:, 0:KC], t_g, hT, t_h, sq, t_sq, rstd, t_rstd, 1.0)
            rms_fm(cx, hT, t_h, KC, g_all[:, KC:2 * KC], t_g, uT, t_u, sq, t_sq, rstd, t_rstd, D)
            ffn_fm(cx, uT, t_u, wg, wu, wd, actT, t_act, ws, fT, t_f, tmp, t_tmp)
            resid_norm_add(cx, fT, t_f, g_all[:, 2 * KC:3 * KC], t_g, hT, t_h, sq, t_sq, rstd, t_rstd, 0.5)
            rms_fm(cx, hT, t_h, KC, g_all[:, 3 * KC:4 * KC], t_g, uT, t_u, sq, t_sq, rstd, t_rstd, D)
            for c4 in range(0, KC, 4):
                gtile, tgw = ws.load(wpg, 0, KC, c4 * 128, 512)
                ptile, tpw = ws.load(wpp, 0, 2, c4 * 128, 512)
                for j in range(4):
                    c = c4 + j
                    bg, tbg = cx.bank()
                    for k in range(KC):
                        A("pe", lambda e, bg=bg, k=k, j=j, gtile=gtile: e.matmul(
                            bg[:, :], lhsT=gtile[:, k, j * 128:(j + 1) * 128], rhs=uT[:, k, :],
                            start=(k == 0), stop=(k == KC - 1)), reads=[tgw, t_u], writes=[tbg])
                    bp, tbp = cx.bank()
                    for k in range(2):
                        A("pe", lambda e, bp=bp, k=k, j=j, ptile=ptile: e.matmul(
                            bp[:, :], lhsT=ptile[:, k, j * 128:(j + 1) * 128], rhs=pTb[:, k, :],
                            start=(k == 0), stop=(k == 1)), reads=[tpw, t_p], writes=[tbp])
                    tt, tm = t_tmp[c % 2], tmp[c % 2]
                    A("act", lambda e, bg=bg, tm=tm: e.activation(out=tm[:], in_=bg[:, :], func=AF.Sigmoid),
                      reads=[tbg], writes=[tt])
                    A("dve", lambda e, bp=bp, tm=tm, c=c: e.tensor_tensor(out=fT[:, c, :], in0=tm[:], in1=bp[:, :], op=ALU.mult),
                      reads=[tt, tbp], writes=[t_f])
            resid_norm_add(cx, fT, t_f, g_all[:, 4 * KC:5 * KC], t_g, hT, t_h, sq, t_sq, rstd, t_rstd, 1.0)
            transpose_out(cx, hT, t_h, KC, out, r0, otok, t_otok, ident)
        R.emit()
    return nc


_PROGS = {}


def _prog(name, fn):
    if name not in _PROGS:
        _PROGS[name] = fn()
    return _PROGS[name]


def _run(nc, maps):
    import sys, time
    t0 = time.time()
    res = run_bass_kernel_spmd(nc, maps, core_ids=list(range(NCORE)))
    print(f"[kernel] launch done in {time.time() - t0:.1f}s", file=sys.stderr, flush=True)
    return res.results


DN_W = 2048
O1 = 3 * DN_W
O2 = O1 + DN_W
O3 = O2 + 16
O4 = O3 + 16
O5 = O4 + 3 * 2048
O6 = O5 + D


def kernel(x, p, ffn1_norm_pre, ffn1_w_gate, ffn1_w_up, ffn1_w_down, ffn1_norm_post,
           mix_norm_pre, w_in, dn_conv_w, dn_A_log, dn_dt_bias, dn_out_norm,
           w_branch_dn, w_branch_sb, w_out, mix_norm_post,
           ffn2_norm_pre, ffn2_w_gate, ffn2_w_up, ffn2_w_down, ffn2_norm_post,
           ple_norm_pre, ple_w_gate, ple_w_proj, ple_norm_post):
    f32 = lambda a: np.ascontiguousarray(np.asarray(a, dtype=np.float32))
    x = f32(x)[0]
    p = f32(p)[0, 0]
    w_in = f32(w_in)[0]
    ident = np.eye(128, dtype=np.float32)
    g1 = np.concatenate([fm_vec(ffn1_norm_pre[0]), fm_vec(ffn1_norm_post[0]), fm_vec(mix_norm_pre[0])], 1)
    wg1, wu1, wd1 = f32(ffn1_w_gate)[0], f32(ffn1_w_up)[0], f32(ffn1_w_down)[0]
    maps = [{"x": np.ascontiguousarray(x[c * TOK:(c + 1) * TOK]), "wg": wg1, "wu": wu1, "wd": wd1,
             "gains": g1, "ident": ident} for c in range(NCORE)]
    r1 = _run(_prog("s1", build_stage1), maps)
    h1T = [np.asarray(r["h1T"]) for r in r1]
    uT = [np.asarray(r["uT"]) for r in r1]
    uT_all = np.ascontiguousarray(np.concatenate(uT, axis=1))
    uT_rev = np.ascontiguousarray(uT_all[:, ::-1])
    del r1, maps
    conv = f32(dn_conv_w)[0]
    consts = dn_consts()
    maps = []
    for c in range(NCORE):
        hs = (2 * c, 2 * c + 1)
        cols = [w_in[:, off + h * 128: off + (h + 1) * 128] for off in (0, DN_W, 2 * DN_W, O1) for h in hs]
        wdn = np.ascontiguousarray(np.concatenate(cols, axis=1))
        wba = np.ascontiguousarray(np.stack([w_in[:, O2 + hs[0]], w_in[:, O2 + hs[1]], w_in[:, O3 + hs[0]], w_in[:, O3 + hs[1]]], axis=1))
        cw = np.stack([conv[:, off + h * 128: off + (h + 1) * 128] for off in (0, DN_W, 2 * DN_W) for h in hs], axis=0)
        cw = np.ascontiguousarray(cw.transpose(2, 0, 1).reshape(128, 24))
        hv = np.array([dn_dt_bias[0][hs[0]], dn_dt_bias[0][hs[1]], dn_A_log[0][hs[0]], dn_A_log[0][hs[1]]], np.float32)
        maps.append({"uT_all": uT_all, "wdn": wdn, "wba": wba, "convw": cw,
                     "gon": f32(dn_out_norm)[0][:, None].copy(), "hv": np.ascontiguousarray(np.tile(hv[None, :], (64, 1))),
                     "consts": consts})
    r2a = _run(_prog("s2a", build_stage2_dn), maps)
    odn_full = np.ascontiguousarray(np.concatenate([np.asarray(r["odnT"]) for r in r2a], axis=0))
    del r2a, maps
    sbc = sb_consts()
    maps = []
    for c in range(NCORE):
        hs = (2 * c, 2 * c + 1)
        cols = [w_in[:, O4 + off + h * 128: O4 + off + (h + 1) * 128] for off in (0, 2048, 4096) for h in hs]
        maps.append({"uT_all": uT_all, "uT_rev": uT_rev, "wsb": np.ascontiguousarray(np.concatenate(cols, axis=1)), "consts": sbc})
    r2b = _run(_prog("s2b", build_stage2_sb), maps)
    osb_full = np.ascontiguousarray(np.concatenate([np.asarray(r["osbT"]) for r in r2b], axis=0))
    del r2b, maps, uT_all, uT_rev
    g3 = np.concatenate([fm_vec(mix_norm_post[0]), fm_vec(ffn2_norm_pre[0]), fm_vec(ffn2_norm_post[0]),
                         fm_vec(ple_norm_pre[0]), fm_vec(ple_norm_post[0])], 1)
    wgate = np.ascontiguousarray(w_in[:, O5:O5 + 2 * D])
    shared = {"wgate": wgate, "wbd": f32(w_branch_dn)[0], "wbs": f32(w_branch_sb)[0], "wout": f32(w_out)[0],
              "wg": f32(ffn2_w_gate)[0], "wu": f32(ffn2_w_up)[0], "wd": f32(ffn2_w_down)[0],
              "wpg": f32(ple_w_gate)[0], "wpp": f32(ple_w_proj)[0], "gains": g3, "ident": ident}
    maps = []
    for c in range(NCORE):
        sl = slice(c * TOK, (c + 1) * TOK)
        m = dict(shared)
        m.update({"h1T": h1T[c], "uT": uT[c], "odnT": np.ascontiguousarray(odn_full[:, sl]),
                  "osbT": np.ascontiguousarray(osb_full[:, sl]), "pT": np.ascontiguousarray(p[sl].T)})
        maps.append(m)
    r3 = _run(_prog("s3", build_stage3), maps)
    out = np.concatenate([np.asarray(r["out"]) for r in r3], axis=0)
    return out.reshape(1, SEQ, D).astype(np.float32)
```

```python
import contextlib
import numpy as np
import concourse.bass as bass
import concourse.mybir as mybir
from concourse.bass_utils import run_bass_kernel_spmd

F32 = mybir.dt.float32
BF16 = mybir.dt.bfloat16
AF = mybir.ActivationFunctionType
ALU = mybir.AluOpType

D = 2048
SEQ = 16384
NCORE = 8
TOK = SEQ // NCORE
T = 512
DFF = 5632
PLE = 256
KC = D // 128
EPS = 1e-6


class Tok:
    __slots__ = ("name", "last_w", "readers", "sem", "dma_cnt", "last_dma")

    def __init__(self, name):
        self.name = name
        self.last_w = None
        self.readers = []
        self.sem = None
        self.dma_cnt = 0
        self.last_dma = None


class Op:
    __slots__ = ("eng", "fn", "deps", "dma", "signal", "sem", "count", "tok")

    def __init__(self, eng, fn, dma):
        self.eng = eng
        self.fn = fn
        self.deps = []
        self.dma = dma
        self.signal = dma
        self.sem = None
        self.count = 0
        self.tok = None


ENGS = ("pe", "act", "dve", "pool", "sp")


class Rec:
    def __init__(self, nc, stack):
        self.nc = nc
        self.stack = stack
        self.ops = {e: [] for e in ENGS}
        self.toks = []
        self.dma_toks = []
        self.nsb = 0

    def tok(self, name="t"):
        t = Tok(name)
        self.toks.append(t)
        return t

    def sb(self, shape, dt, name=None):
        self.nsb += 1
        return self.stack.enter_context(self.nc.sbuf_tensor("s_" + (name or f"sb{self.nsb}"), list(shape), dt))

    def ps(self, shape, dt, name=None):
        self.nsb += 1
        return self.stack.enter_context(self.nc.psum_tensor("p_" + (name or f"ps{self.nsb}"), list(shape), dt))

    def add(self, eng, fn, reads=(), writes=(), dma_tok=None):
        op = Op(eng, fn, dma_tok is not None)
        deps = []
        for r in reads:
            if r.last_w is not None:
                deps.append(r.last_w)
        for w in writes:
            if w.last_w is not None:
                deps.append(w.last_w)
            deps.extend(w.readers)
        if dma_tok is not None:
            if dma_tok.sem is None:
                dma_tok.sem = self.stack.enter_context(self.nc.semaphore(f"dq{len(self.dma_toks)}"))
                dma_tok.last_dma = None
                self.dma_toks.append(dma_tok)
            if getattr(dma_tok, "last_dma", None) is not None:
                deps.append(dma_tok.last_dma)
            dma_tok.dma_cnt += 16
            op.sem = dma_tok.sem
            op.count = dma_tok.dma_cnt
            op.tok = dma_tok
            dma_tok.last_dma = op
        seen = set()
        for d in deps:
            if d is op or id(d) in seen:
                continue
            seen.add(id(d))
            if (not d.dma) and d.eng == eng and eng == "pe":
                continue
            op.deps.append(d)
            d.signal = True
        for r in reads:
            r.readers.append(op)
        for w in writes:
            w.last_w = op
            w.readers = []
        self.ops[eng].append(op)
        return op

    def emit(self):
        nc = self.nc
        esem = {}
        for e in ("pe", "act", "dve", "pool"):
            esem[e] = self.stack.enter_context(nc.semaphore(f"es_{e}"))
        for e in ("pe", "act", "dve", "pool", "sp"):
            c = 0
            for op in self.ops[e]:
                if op.dma:
                    continue
                if op.signal:
                    c += 1
                    op.sem = esem.get(e)
                    op.count = c
                    assert e != "sp"
        final = [(t.sem, t.dma_cnt) for t in self.dma_toks]

        def run(e, eng):
            waited = {}
            for op in self.ops[e]:
                for d in op.deps:
                    key = id(d.sem)
                    if waited.get(key, 0) >= d.count:
                        continue
                    eng.wait_ge(d.sem, d.count)
                    waited[key] = d.count
                ins = op.fn(eng)
                if op.signal:
                    ins.then_inc(op.sem, 16 if op.dma else 1)
            if e == "sp":
                for s, c in final:
                    eng.wait_ge(s, c)

        with nc.Block() as block:
            @block.tensor
            def _(eng):
                run("pe", eng)

            @block.scalar
            def _(eng):
                run("act", eng)

            @block.vector
            def _(eng):
                run("dve", eng)

            @block.gpsimd
            def _(eng):
                run("pool", eng)

            @block.sync
            def _(eng):
                run("sp", eng)


class Ctx:
    def __init__(self, nc, stack):
        self.nc = nc
        self.R = Rec(nc, stack)
        R = self.R
        self.banks = [R.ps([128, 512], F32, name=f"bank{i}") for i in range(8)]
        self.bank_tok = [R.tok(f"bank{i}") for i in range(8)]
        self.bank_rr = 0
        self.ones_bf = R.sb([128, 128], BF16, "ones_bf")
        self.t_const = R.tok("const")
        R.add("pool", lambda e: e.memset(self.ones_bf[:], 1.0), writes=[self.t_const])
        self.evac_rr = 0
        self.reserved = set()

    def bank(self):
        while True:
            i = self.bank_rr % 8
            self.bank_rr += 1
            if id(self.banks[i]) not in self.reserved:
                return self.banks[i], self.bank_tok[i]


def load_vec_fm(cx, dram_vec, name):
    R = cx.R
    t = R.sb([128, dram_vec.shape[1]], F32, name)
    tk = R.tok(name)
    R.add("sp", lambda e: e.dma_start(out=t[:], in_=dram_vec[:, :]), writes=[tk], dma_tok=tk)
    return t, tk


def rms_fm(cx, src, t_src, nch, gain, t_gain, dst, t_dst, sq, t_sq, rstd, t_rstd, dim, post=None):
    R = cx.R
    R.add("act", lambda e: e.activation(out=sq[:, 0:nch, :], in_=src[:, 0:nch, :], func=AF.Square),
          reads=[t_src], writes=[t_sq])
    bk, tb = cx.bank()
    for c in range(nch):
        R.add("pe", lambda e, c=c: e.matmul(bk[:, :], lhsT=cx.ones_bf[:], rhs=sq[:, c, :],
                                             start=(c == 0), stop=(c == nch - 1)),
              reads=[t_sq, cx.t_const], writes=[tb])
    R.add("act", lambda e: e.activation(out=rstd[:], in_=bk[:, :], func=AF.Ln, bias=EPS, scale=1.0 / dim),
          reads=[tb], writes=[t_rstd])
    R.add("act", lambda e: e.activation(out=rstd[:], in_=rstd[:], func=AF.Exp, scale=-0.5), reads=[t_rstd], writes=[t_rstd])
    if post is None:
        for c in range(nch):
            R.add("dve", lambda e, c=c: e.scalar_tensor_tensor(out=dst[:, c, :], in0=src[:, c, :],
                                                                scalar=gain[:, c:c + 1], in1=rstd[:],
                                                                op0=ALU.mult, op1=ALU.mult),
                  reads=[t_src, t_gain, t_rstd], writes=[t_dst])
    else:
        post()


def transpose_in(cx, x_dram, r0, xtok, t_xtok, xT, t_xT, ident, ncol_chunks, alias=()):
    R = cx.R
    W = ncol_chunks * 128
    alias = list(alias)
    for s in range(T // 128):
        R.add("sp", lambda e, s=s: e.dma_start(out=xtok[:, s, 0:W], in_=x_dram[r0 + s * 128:r0 + (s + 1) * 128, :]),
              writes=[t_xtok[s]] + alias, dma_tok=t_xtok[s])
    for c in range(ncol_chunks):
        bk, tb = cx.bank()
        for s in range(T // 128):
            R.add("pe", lambda e, c=c, s=s, bk=bk: e.transpose(out=bk[:, s * 128:(s + 1) * 128],
                                                                in_=xtok[:, s, c * 128:(c + 1) * 128], identity=ident[:]),
                  reads=[t_xtok[s], cx.t_const] + alias, writes=[tb])
        evac(cx, xT[:, c, :], bk[:, :], [tb], [t_xT])


def evac(cx, dst, src, reads, writes):
    R = cx.R
    cx.evac_rr += 1
    if cx.evac_rr % 2 == 0:
        R.add("act", lambda e: e.copy(out=dst, in_=src), reads=reads, writes=writes)
    else:
        R.add("dve", lambda e: e.tensor_copy(out=dst, in_=src), reads=reads, writes=writes)


def transpose_out(cx, srcT, t_src, nch, out_dram, r0, otok, t_otok, ident):
    R = cx.R
    for s in range(T // 128):
        for c4 in range(0, nch, 4):
            bk, tb = cx.bank()
            for j in range(4):
                c = c4 + j
                R.add("pe", lambda e, c=c, s=s, j=j, bk=bk: e.transpose(out=bk[:, j * 128:(j + 1) * 128],
                                                                   in_=srcT[:, c, s * 128:(s + 1) * 128],
                                                                   identity=ident[:]),
                      reads=[t_src, cx.t_const], writes=[tb])
            evac(cx, otok[:, s, c4 * 128:(c4 + 4) * 128], bk[:, :], [tb], [t_otok[s]])
        R.add("sp", lambda e, s=s: e.dma_start(out=out_dram[r0 + s * 128:r0 + (s + 1) * 128, :],
                                               in_=otok[:, s, 0:nch * 128]),
              reads=[t_otok[s]], dma_tok=t_otok[s])


class WStream:
    def __init__(self, cx, kch_max, nbuf, name):
        R = cx.R
        self.cx = cx
        self.bufs = [R.sb([128, kch_max, 512], BF16, f"{name}{i}") for i in range(nbuf)]
        self.toks = [R.tok(f"{name}{i}") for i in range(nbuf)]
        self.st_toks = [R.tok(f"{name}st{i}") for i in range(nbuf)]
        self.sw_toks = [R.tok(f"{name}sw{i}") for i in range(nbuf)]
        self.hw_toks = [R.tok(f"{name}hw{i}") for i in range(nbuf)]
        self.rr = 0
        self.scratch = {}
        self.regions = {}

    def load(self, w_dram, k0, kch, c0, ncols):
        R = self.cx.R
        nc = self.cx.nc
        i = self.rr % len(self.bufs)
        self.rr += 1
        buf, tk = self.bufs[i], self.toks[i]
        wname = w_dram.tensor.name
        if wname not in self.scratch:
            self.scratch[wname] = nc.dram_tensor("bf_" + wname, list(w_dram.shape), BF16).ap()
        scr = self.scratch[wname][k0 * 128:(k0 + kch) * 128, c0:c0 + ncols].rearrange("(k p) n -> p k n", p=128)
        key = (wname, k0, kch, c0, ncols)
        if key not in self.regions:
            t_reg = R.tok("reg")
            self.regions[key] = t_reg
            src = w_dram[k0 * 128:(k0 + kch) * 128, c0:c0 + ncols].rearrange("(k p) n -> p k n", p=128)
            R.add("pool", lambda e: e.dma_start(out=buf[:, 0:kch, 0:ncols], in_=src), writes=[tk], dma_tok=self.sw_toks[i])
            R.add("sp", lambda e: e.dma_start(out=scr, in_=buf[:, 0:kch, 0:ncols]), reads=[tk], writes=[t_reg],
                  dma_tok=self.st_toks[i])
        else:
            t_reg = self.regions[key]
            R.add("sp", lambda e: e.dma_start(out=buf[:, 0:kch, 0:ncols], in_=scr), reads=[t_reg], writes=[tk], dma_tok=self.hw_toks[i])
        return buf, tk


def ffn_fm(cx, uT, t_u, wg, wu, wd, actT, t_act, ws, fT, t_f, tmp, t_tmp):
    R = cx.R
    NJ = DFF // 128
    for j4 in range(0, NJ, 4):
        gw, tg = ws.load(wg, 0, KC, j4 * 128, 512)
        uw, tu = ws.load(wu, 0, KC, j4 * 128, 512)
        for j in range(4):
            bg, tbg = cx.bank()
            bu, tbu = cx.bank()
            for k in range(KC):
                R.add("pe", lambda e, k=k, j=j, bg=bg, gw=gw: e.matmul(bg[:, :], lhsT=gw[:, k, j * 128:(j + 1) * 128],
                                                                        rhs=uT[:, k, :], start=(k == 0), stop=(k == KC - 1)),
                      reads=[tg, t_u], writes=[tbg])
            for k in range(KC):
                R.add("pe", lambda e, k=k, j=j, bu=bu, uw=uw: e.matmul(bu[:, :], lhsT=uw[:, k, j * 128:(j + 1) * 128],
                                                                        rhs=uT[:, k, :], start=(k == 0), stop=(k == KC - 1)),
                      reads=[tu, t_u], writes=[tbu])
            jj = j4 + j
            tt = t_tmp[jj % 2]
            tm = tmp[jj % 2]
            R.add("act", lambda e, bg=bg, tm=tm: e.activation(out=tm[:], in_=bg[:, :], func=AF.Silu),
                  reads=[tbg], writes=[tt])
            R.add("dve", lambda e, bu=bu, tm=tm, jj=jj: e.tensor_tensor(out=actT[:, jj, :], in0=tm[:], in1=bu[:, :],
                                                                         op=ALU.mult),
                  reads=[tt, tbu], writes=[t_act])
    for dg in range(4):
        bks = [cx.bank() for _ in range(4)]
        for q in range(4):
            dwt, tdw = ws.load(wd, q * 11, 11, dg * 512, 512)
            for kk in range(11):
                k = q * 11 + kk
                for dd in range(4):
                    bk, tb = bks[dd]
                    R.add("pe", lambda e, kk=kk, k=k, dd=dd, bk=bk, dwt=dwt: e.matmul(
                        bk[:, :], lhsT=dwt[:, kk, dd * 128:(dd + 1) * 128], rhs=actT[:, k, :],
                        start=(k == 0), stop=(k == NJ - 1)), reads=[tdw, t_act], writes=[tb])
        for dd in range(4):
            bk, tb = bks[dd]
            evac(cx, fT[:, dg * 4 + dd, :], bk[:, :], [tb], [t_f])


def resid_norm_add(cx, fT, t_f, gain, t_gain, hT, t_h, sq, t_sq, rstd, t_rstd, coef):
    R = cx.R

    def post():
        for c in range(KC):
            R.add("dve", lambda e, c=c: e.scalar_tensor_tensor(out=fT[:, c, :], in0=fT[:, c, :],
                                                                scalar=gain[:, c:c + 1], in1=rstd[:],
                                                                op0=ALU.mult, op1=ALU.mult),
                  reads=[t_f, t_gain, t_rstd], writes=[t_f])
            R.add("dve", lambda e, c=c: e.scalar_tensor_tensor(out=hT[:, c, :], in0=fT[:, c, :],
                                                                scalar=float(coef), in1=hT[:, c, :],
                                                                op0=ALU.mult, op1=ALU.add),
                  reads=[t_f, t_h], writes=[t_h])

    rms_fm(cx, fT, t_f, KC, gain, t_gain, None, None, sq, t_sq, rstd, t_rstd, D, post=post)


def build_stage1(ntiles=TOK // T):
    nc = bass.Bass("TRN2", target_bir_lowering=False)
    x = nc.dram_tensor("x", [TOK, D], F32, kind="ExternalInput").ap()
    wg = nc.dram_tensor("wg", [D, DFF], F32, kind="ExternalInput").ap()
    wu = nc.dram_tensor("wu", [D, DFF], F32, kind="ExternalInput").ap()
    wd = nc.dram_tensor("wd", [DFF, D], F32, kind="ExternalInput").ap()
    gains = nc.dram_tensor("gains", [128, 3 * KC], F32, kind="ExternalInput").ap()
    ident_d = nc.dram_tensor("ident", [128, 128], F32, kind="ExternalInput").ap()
    h1T = nc.dram_tensor("h1T", [D, TOK], F32, kind="ExternalOutput").ap()
    uT_o = nc.dram_tensor("uT", [D, TOK], BF16, kind="ExternalOutput").ap()
    with contextlib.ExitStack() as stack:
        cx = Ctx(nc, stack)
        R = cx.R
        g_all, t_g = load_vec_fm(cx, gains, "gains_sb")
        ident = R.sb([128, 128], F32, "ident")
        R.add("sp", lambda e: e.dma_start(out=ident[:], in_=ident_d[:, :]), writes=[cx.t_const], dma_tok=cx.t_const)
        big = R.sb([128, KC * T], F32, "big")
        xtok = big[:].rearrange("p (s d) -> p s d", s=4)
        fT = big[:].rearrange("p (c t) -> p c t", c=KC)
        t_xtok = [R.tok(f"xtok{s}") for s in range(4)]
        t_f = R.tok("fT")
        hT = R.sb([128, KC, T], F32, "hT")
        t_h = R.tok("hT")
        uT = R.sb([128, KC, T], BF16, "uT")
        t_u = R.tok("uT")
        sq = R.sb([128, KC, T], BF16, "sq")
        t_sq = R.tok("sq")
        rstd = R.sb([128, T], F32, "rstd")
        t_rstd = R.tok("rstd")
        actT = R.sb([128, DFF // 128, T], BF16, "actT")
        t_act = R.tok("actT")
        tmp = [R.sb([128, T], F32, f"tmp{i}") for i in range(2)]
        t_tmp = [R.tok(f"tmp{i}") for i in range(2)]
        ws = WStream(cx, KC, 3, "w")
        h1v = h1T.rearrange("(c p) t -> p c t", p=128)
        uv = uT_o.rearrange("(c p) t -> p c t", p=128)
        for it in range(ntiles):
            r0 = it * T
            transpose_in(cx, x, r0, xtok, t_xtok, hT, t_h, ident, KC, alias=[t_f])
            rms_fm(cx, hT, t_h, KC, g_all[:, 0:KC], t_g, uT, t_u, sq, t_sq, rstd, t_rstd, D)
            ffn_fm(cx, uT, t_u, wg, wu, wd, actT, t_act, ws, fT, t_f, tmp, t_tmp)
            resid_norm_add(cx, fT, t_f, g_all[:, KC:2 * KC], t_g, hT, t_h, sq, t_sq, rstd, t_rstd, 0.5)
            R.add("sp", lambda e, r0=r0: e.dma_start(out=h1v[:, :, r0:r0 + T], in_=hT[:]), reads=[t_h], dma_tok=t_h)
            rms_fm(cx, hT, t_h, KC, g_all[:, 2 * KC:3 * KC], t_g, uT, t_u, sq, t_sq, rstd, t_rstd, D)
            R.add("sp", lambda e, r0=r0: e.dma_start(out=uv[:, :, r0:r0 + T], in_=uT[:]), reads=[t_u], dma_tok=t_u)
        R.emit()
    return nc


def fm_vec(v):
    return np.ascontiguousarray(np.asarray(v, np.float32).reshape(-1, 128).T)


HD = 128
CH = 64
NCHK = T // CH
C_ID, C_U, C_SL, C_BU, C_BL = 0, 128, 192, 256, 320


def dn_consts():
    c = np.zeros((128, 384), np.float32)
    c[:, 0:128] = np.eye(128, dtype=np.float32)
    p = np.arange(64)[:, None]
    f = np.arange(64)[None, :]
    c[0:64, C_U:C_U + 64] = (p <= f)
    c[0:64, C_SL:C_SL + 64] = (f < p)
    c[0:64, C_BU:C_BU + 64] = 1e4 * (f > p)
    c[0:64, C_BL:C_BL + 64] = 1e4 * (f < p)
    return c


def bc_mid(ap2, n):
    return ap2.unsqueeze(1).to_broadcast([ap2.shape[0], n, ap2.shape[1]])


def bc_last(ap2, n):
    return ap2.unsqueeze(2).to_broadcast([ap2.shape[0], ap2.shape[1], n])


def build_stage2_dn(ntiles=SEQ // T):
    S = ntiles * T
    nc = bass.Bass("TRN2", target_bir_lowering=False)
    uT_all = nc.dram_tensor("uT_all", [D, S], BF16, kind="ExternalInput").ap()
    wdn = nc.dram_tensor("wdn", [D, 1024], F32, kind="ExternalInput").ap()
    wba = nc.dram_tensor("wba", [D, 4], F32, kind="ExternalInput").ap()
    convw_d = nc.dram_tensor("convw", [128, 24], F32, kind="ExternalInput").ap()
    gon_d = nc.dram_tensor("gon", [128, 1], F32, kind="ExternalInput").ap()
    hv_d = nc.dram_tensor("hv", [64, 4], F32, kind="ExternalInput").ap()
    consts_d = nc.dram_tensor("consts", [128, 384], F32, kind="ExternalInput").ap()
    odnT = nc.dram_tensor("odnT", [256, S], BF16, kind="ExternalOutput").ap()
    with contextlib.ExitStack() as stack:
        cx = Ctx(nc, stack)
        R = cx.R
        A = R.add
        tc = cx.t_const
        consts = R.sb([128, 384], F32, "consts")
        A("sp", lambda e: e.dma_start(out=consts[:], in_=consts_d[:, :]), writes=[tc], dma_tok=tc)
        ident = consts[:, 0:128]
        id64 = consts[0:64, 0:64]
        Um = consts[0:64, C_U:C_U + 64]
        SLm = consts[0:64, C_SL:C_SL + 64]
        BUm = consts[0:64, C_BU:C_BU + 64]
        BLm = consts[0:64, C_BL:C_BL + 64]
        convw, t_cw = load_vec_fm(cx, convw_d, "convw_sb")
        gon, t_gon = load_vec_fm(cx, gon_d, "gon_sb")
        hv = R.sb([64, 4], F32, "hv")
        t_hv = R.tok("hv")
        A("sp", lambda e: e.dma_start(out=hv[:], in_=hv_d[:, :]), writes=[t_hv], dma_tok=t_hv)
        ident_bf = R.sb([128, 128], BF16, "ident_bf")
        A("dve", lambda e: e.tensor_copy(out=ident_bf[:], in_=consts[:, 0:128]), reads=[tc], writes=[tc])
        id64_bf = ident_bf[0:64, 0:64]
        ones_f = R.sb([64, 128], F32, "ones_f")
        A("pool", lambda e: e.memset(ones_f[:], 1.0), writes=[tc])
        expA = R.sb([64, 2], F32, "expA")
        A("act", lambda e: e.activation(out=expA[:], in_=hv[:, 2:4], func=AF.Exp), reads=[t_hv], writes=[t_hv])
        wres = R.sb([128, KC, 1024], BF16, "wres")
        t_w = R.tok("wres")
        wv = wdn.rearrange("(k p) n -> p k n", p=128)
        for half in range(2):
            A("pool", lambda e, half=half: e.dma_start(out=wres[:, :, half * 512:(half + 1) * 512],
                                                      in_=wv[:, :, half * 512:(half + 1) * 512]),
              writes=[t_w], dma_tok=t_w)
        wbar = R.sb([128, KC, 4], BF16, "wbar")
        t_wb = R.tok("wbar")
        A("pool", lambda e: e.dma_start(out=wbar[:], in_=wba.rearrange("(k p) n -> p k n", p=128)),
          writes=[t_wb], dma_tok=t_wb)
        u_t = [R.sb([128, KC, T], BF16, f"u_t{i}") for i in range(1)]
        t_ut = [R.tok(f"u_t{i}") for i in range(1)]
        rawbuf = R.sb([128, 6, T + 3], F32, "rawbuf")
        t_raw = [R.tok(f"raw{c}") for c in range(6)]
        A("pool", lambda e: e.memset(rawbuf[:, :, 0:3], 0.0), writes=t_raw)
        cv = R.sb([128, 6, T], F32, "cv")
        t_cv = [R.tok(f"cv{c}") for c in range(6)]
        qk = R.sb([128, 4, T], BF16, "qk")
        t_qk = [R.tok(f"qk{c}") for c in range(4)]
        siluz = R.sb([128, 2, T], F32, "siluz")
        t_sz = [R.tok(f"sz{c}") for c in range(2)]
        sqb = R.sb([128, T], BF16, "sqb")
        t_sqb = R.tok("sqb")
        rstd = R.sb([128, T], F32, "rstd")
        t_rstd = R.tok("rstd")
        ba = R.sb([64, NCHK, 4], F32, "ba")
        t_ba = R.tok("ba")
        beta = R.sb([64, NCHK, 2], F32, "beta")
        t_beta = R.tok("beta")
        gt = R.sb([64, NCHK, 2], F32, "gt")
        t_g = R.tok("gt")
        spx = R.sb([64, NCHK, 2], F32, "spx")
        t_spx = R.tok("spx")

        def sbt(shape, name, dt=F32):
            return R.sb(shape, dt, name), R.tok(name)

        def make_head(h):
            gc_col, t_gcc = sbt([64, NCHK], f"gc_col_h{h}")
            egc, t_egc = sbt([64, NCHK], f"egc_h{h}")
            be, t_be = sbt([64, NCHK], f"be_h{h}")
            kgs, t_kgs = sbt([64, NCHK], f"kgs_h{h}")
            Ug, t_Ug = sbt([64, NCHK, 64], f"Ug_h{h}")
            gcrow, t_gcr = sbt([128, NCHK, 64], f"gcrow_h{h}")
            egrow, t_egr = sbt([128, NCHK, 64], f"egrow_h{h}")
            dl, t_dl = sbt([64, NCHK, 64], f"dl_h{h}")
            decay, t_dec = sbt([64, NCHK, 64], f"decay_h{h}")
            decayT, t_decT = sbt([64, NCHK, 64], f"decayT_h{h}")
            Ktok, t_Kt = sbt([64, NCHK, 128], f"Ktok_h{h}", BF16)
            Vtok, t_Vt = sbt([64, NCHK, 128], f"Vtok_h{h}")
            Pb = [sbt([64, NCHK, 64], f"P{i}_h{h}", BF16) for i in range(2)]
            Ptb = [sbt([64, NCHK, 64], f"Pt{i}_h{h}", BF16) for i in range(2)]
            At, t_At = sbt([64, NCHK, 64], f"At_h{h}")
            u_sb, t_usb = sbt([64, NCHK, 128], f"u_sb_h{h}")
            wT, t_wT = sbt([128, NCHK, 64], f"wT_h{h}")
            attnT, t_att = sbt([64, NCHK, 64], f"attnT_h{h}")
            qgT, t_qg = sbt([128, NCHK, 64], f"qgT_h{h}")
            kg, t_kg = sbt([64, NCHK, 128], f"kg_h{h}")
            Sst = [sbt([128, 128], f"S{i}_h{h}") for i in range(2)]
            A("pool", lambda e: e.memset(Sst[0][0][:], 0.0), writes=[Sst[0][1]])
            s_par = [0]
            vnew = [sbt([64, 128], f"vnew{i}_h{h}") for i in range(2)]
            vn_rr = [0]
            tmpd, t_tmpd = Ug, t_Ug
            Vb, t_Vb = sbt([64, NCHK, 128], f"Vb_h{h}", BF16)
            Kbg, t_Kbg = sbt([64, NCHK, 128], f"Kbg_h{h}", BF16)
            Atb, t_Atb = sbt([64, NCHK, 64], f"Atb_h{h}", BF16)

            def gen(ob, t_ob):
                    gh = gt[:, :, h]
                    bh = beta[:, :, h]
                    qT = qk[:, h, :].rearrange("p (c f) -> p c f", f=CH)
                    kT = qk[:, 2 + h, :].rearrange("p (c f) -> p c f", f=CH)
                    t_q, t_k, t_v = t_qk[h], t_qk[2 + h], t_cv[4 + h]
                    bk, tb = bank_mm(64, [(0, NCHK, Um, gh)], [tc, t_g])
                    A("dve", lambda e, bk=bk: e.tensor_copy(out=gc_col[:], in_=bk[0:64, 0:NCHK]), reads=[tb], writes=[t_gcc])
                    A("dve", lambda e, gh=gh: e.tensor_tensor(out=Ug[:], in0=bc_mid(Um, NCHK), in1=bc_last(gh, 64), op=ALU.mult),
                      reads=[tc, t_g], writes=[t_Ug])
                    bk, tb = bank_mm(128, [(c * 64, 64, ones_f[:, :], Ug[:, c, :]) for c in range(NCHK)], [tc, t_Ug])
                    A("act", lambda e, bk=bk: e.copy(out=gcrow[:].rearrange("p c f -> p (c f)"), in_=bk[:, :]),
                      reads=[tb], writes=[t_gcr])
                    yield
                    A("dve", lambda e: e.tensor_tensor(out=dl[:], in0=gcrow[0:64, :, :], in1=bc_last(gc_col[:, :], 64),
                                                       op=ALU.subtract), reads=[t_gcr, t_gcc], writes=[t_dl])
                    A("dve", lambda e: e.tensor_tensor(out=tmpd[:], in0=dl[:], in1=bc_mid(BUm, NCHK), op=ALU.add),
                      reads=[t_dl, tc], writes=[t_tmpd])
                    A("act", lambda e: e.activation(out=decay[:], in_=tmpd[:], func=AF.Exp, scale=-1.0),
                      reads=[t_tmpd], writes=[t_dec])
                    A("dve", lambda e: e.tensor_tensor(out=tmpd[:], in0=dl[:], in1=bc_mid(BLm, NCHK), op=ALU.subtract),
                      reads=[t_dl, tc, t_dec], writes=[t_tmpd])
                    A("act", lambda e: e.activation(out=decayT[:], in_=tmpd[:], func=AF.Exp), reads=[t_tmpd], writes=[t_decT])
                    A("act", lambda e: e.activation(out=egrow[:], in_=gcrow[:], func=AF.Exp), reads=[t_gcr], writes=[t_egr])
                    A("act", lambda e: e.activation(out=kgs[:], in_=dl[:, :, 63], func=AF.Exp), reads=[t_dl], writes=[t_kgs])
                    A("act", lambda e: e.activation(out=egc[:], in_=gc_col[:], func=AF.Exp), reads=[t_gcc], writes=[t_egc])
                    yield
                    for (src, t_src, dst, t_dst, isbf) in ((kT, t_k, Ktok, t_Kt, True),
                                                           (cv[:, 4 + h, :].rearrange("p (c f) -> p c f", f=CH), t_v, Vtok, t_Vt, False)):
                        for half in range(2):
                            bk, tb = cx.bank()
                            bkv = bk[:, :].bitcast(BF16) if isbf else bk[:, :]
                            idm = ident_bf[:] if isbf else ident
                            for c4 in range(4):
                                c = half * 4 + c4
                                A("pe", lambda e, bkv=bkv, c4=c4, c=c, src=src, idm=idm: e.transpose(
                                    out=bkv[0:64, c4 * 128:(c4 + 1) * 128], in_=src[:, c, :], identity=idm),
                                  reads=[t_src, tc], writes=[tb])
                            evac(cx, dst[:, half * 4:half * 4 + 4, :].rearrange("p c f -> p (c f)"), bkv[0:64, 0:512], [tb], [t_dst])
                    bk, tb = bank_mm(64, [(c * 64, 64, kT[:, c, :], kT[:, c, :]) for c in range(NCHK)], [t_k])
                    P0, t_P0 = Pb[0]
                    Pt0, t_Pt0 = Ptb[0]
                    A("dve", lambda e, bk=bk: e.tensor_tensor(out=tmpd[:].rearrange("p c f -> p (c f)"), in0=bk[0:64, :],
                                                              in1=decay[:].rearrange("p c f -> p (c f)"), op=ALU.mult),
                      reads=[tb, t_dec], writes=[t_tmpd])
                    A("dve", lambda e, bh=bh: e.tensor_tensor(out=tmpd[:], in0=tmpd[:], in1=bc_last(bh, 64), op=ALU.mult),
                      reads=[t_tmpd, t_beta], writes=[t_tmpd])
                    A("dve", lambda e, P0=P0: e.tensor_tensor(out=P0[:], in0=tmpd[:], in1=bc_mid(SLm, NCHK), op=ALU.mult),
                      reads=[t_tmpd, tc], writes=[t_P0])
                    yield
                    bk, tb = cx.bank()
                    bkv = bk[:, :].bitcast(BF16)
                    for c in range(NCHK):
                        A("pe", lambda e, bkv=bkv, c=c, P0=P0: e.transpose(out=bkv[0:64, c * 64:(c + 1) * 64], in_=P0[:, c, :],
                                                                         identity=id64_bf), reads=[t_P0, tc], writes=[tb])
                    A("act", lambda e, bkv=bkv, Pt0=Pt0: e.copy(out=Pt0[:].rearrange("p c f -> p (c f)"), in_=bkv[0:64, 0:512]),
                      reads=[tb], writes=[t_Pt0])
                    A("dve", lambda e, Pt0=Pt0: e.tensor_tensor(out=At[:], in0=bc_mid(id64, NCHK), in1=Pt0[:], op=ALU.subtract),
                      reads=[t_Pt0, tc], writes=[t_At])
                    A("act", lambda e: e.copy(out=Atb[:], in_=At[:]), reads=[t_At], writes=[t_Atb])
                    yield
                    for lv in range(5):
                        Pc, t_Pc = Pb[lv % 2]
                        Ptc, t_Ptc = Ptb[lv % 2]
                        Pn, t_Pn = Pb[(lv + 1) % 2]
                        Ptn, t_Ptn = Ptb[(lv + 1) % 2]
                        bk, tb = bank_mm(64, [(c * 64, 64, Ptc[:, c, :], Pc[:, c, :]) for c in range(NCHK)], [t_Pc, t_Ptc])
                        A("dve", lambda e, bk=bk, Pn=Pn: e.tensor_copy(out=Pn[:].rearrange("p c f -> p (c f)"), in_=bk[0:64, :]),
                          reads=[tb], writes=[t_Pn])
                        bk, tb = bank_mm(64, [(c * 64, 64, Pc[:, c, :], Ptc[:, c, :]) for c in range(NCHK)], [t_Pc, t_Ptc])
                        A("act", lambda e, bk=bk, Ptn=Ptn: e.copy(out=Ptn[:].rearrange("p c f -> p (c f)"), in_=bk[0:64, :]),
                          reads=[tb], writes=[t_Ptn])
                        bk, tb = bank_mm(64, [(c * 64, 64, Pn[:, c, :], Atb[:, c, :]) for c in range(NCHK)], [t_Pn, t_Atb])
                        A("dve", lambda e, bk=bk: e.tensor_tensor(out=At[:].rearrange("p c f -> p (c f)"),
                                                                  in0=At[:].rearrange("p c f -> p (c f)"), in1=bk[0:64, :],
                                                                  op=ALU.add), reads=[tb, t_At], writes=[t_At])
                        A("act", lambda e: e.copy(out=Atb[:], in_=At[:]), reads=[t_At], writes=[t_Atb])
                        yield
                    A("dve", lambda e: e.tensor_tensor(out=kg[:], in0=Ktok[:], in1=bc_last(kgs[:, :], 128), op=ALU.mult),
                      reads=[t_Kt, t_kgs], writes=[t_kg])
                    A("dve", lambda e, bh=bh: e.tensor_tensor(out=Vb[:], in0=Vtok[:], in1=bc_last(bh, 128), op=ALU.mult),
                      reads=[t_Vt, t_beta], writes=[t_Vb])
                    A("dve", lambda e, bh=bh: e.tensor_tensor(out=be[:], in0=egc[:], in1=bh, op=ALU.mult),
                      reads=[t_egc, t_beta], writes=[t_be])
                    A("dve", lambda e: e.tensor_tensor(out=Kbg[:], in0=Ktok[:], in1=bc_last(be[:, :], 128), op=ALU.mult),
                      reads=[t_Kt, t_be], writes=[t_Kbg])
                    for half in range(2):
                        bk, tb = bank_mm(64, [(c4 * 128, 128, Atb[:, half * 4 + c4, :], Vb[:, half * 4 + c4, :]) for c4 in range(4)],
                                         [t_Atb, t_Vb])
                        evac(cx, u_sb[:, half * 4:half * 4 + 4, :].rearrange("p c f -> p (c f)"), bk[0:64, :], [tb], [t_usb])
                    bk, tb = bank_mm(128, [(c * 64, 64, Kbg[:, c, :], Atb[:, c, :]) for c in range(NCHK)], [t_Atb, t_Kbg])
                    evac(cx, wT[:].rearrange("p c f -> p (c f)"), bk[:, :], [tb], [t_wT])
                    yield
                    bk, tb = bank_mm(64, [(c * 64, 64, kT[:, c, :], qT[:, c, :]) for c in range(NCHK)], [t_k, t_q])
                    A("dve", lambda e, bk=bk: e.tensor_tensor(out=attnT[:].rearrange("p c f -> p (c f)"), in0=bk[0:64, :],
                                                              in1=decayT[:].rearrange("p c f -> p (c f)"), op=ALU.mult),
                      reads=[tb, t_decT], writes=[t_att])
                    A("dve", lambda e, qT=qT: e.tensor_tensor(out=qgT[:], in0=qT, in1=egrow[:], op=ALU.mult),
                      reads=[t_q, t_egr], writes=[t_qg])
                    yield
                    obk, t_obk = cx.bank()
                    cx.reserved.add(id(obk))
                    for c in range(NCHK):
                        Sc, t_Sc = Sst[s_par[0]]
                        Sn, t_Sn = Sst[1 - s_par[0]]
                        s_par[0] = 1 - s_par[0]
                        vn, t_vn = vnew[vn_rr[0] % 2]
                        vn_rr[0] += 1
                        bk, tb = bank_mm(64, [(0, 128, wT[:, c, :], Sc[:, :])], [t_wT, t_Sc])
                        A("dve", lambda e, bk=bk, c=c, vn=vn: e.tensor_tensor(out=vn[:], in0=u_sb[:, c, :], in1=bk[0:64, 0:128],
                                                                             op=ALU.subtract), reads=[tb, t_usb], writes=[t_vn])
                        A("pe", lambda e, c=c, Sc=Sc, obk=obk: e.matmul(obk[:, c * 64:(c + 1) * 64], lhsT=Sc[:, :], rhs=qgT[:, c, :],
                                                                       start=True, stop=False), reads=[t_Sc, t_qg], writes=[t_obk])
                        A("pe", lambda e, c=c, vn=vn, obk=obk: e.matmul(obk[:, c * 64:(c + 1) * 64], lhsT=vn[:, :], rhs=attnT[:, c, :],
                                                                       start=False, stop=True), reads=[t_vn, t_att], writes=[t_obk])
                        bk, tb = bank_mm(128, [(0, 128, kg[:, c, :], vn[:, :])], [t_kg, t_vn])
                        A("dve", lambda e, bk=bk, c=c, Sc=Sc, Sn=Sn: e.scalar_tensor_tensor(
                            out=Sn[:], in0=Sc[:], scalar=egrow[:, c, 63:64], in1=bk[:, 0:128], op0=ALU.mult, op1=ALU.add),
                          reads=[tb, t_Sc, t_egr], writes=[t_Sn])
                        yield
                    A("act", lambda e, obk=obk: e.copy(out=o_sb[:], in_=obk[:, :]), reads=[t_obk], writes=[t_osb])
                    cx.reserved.discard(id(obk))
                    A("act", lambda e: e.activation(out=sqb[:], in_=o_sb[:], func=AF.Square), reads=[t_osb], writes=[t_sqb])
                    bk, tb = cx.bank()
                    A("pe", lambda e, bk=bk: e.matmul(bk[:, :], lhsT=cx.ones_bf[:], rhs=sqb[:], start=True, stop=True),
                      reads=[t_sqb, tc], writes=[tb])
                    A("act", lambda e, bk=bk: e.activation(out=rstd[:], in_=bk[:, :], func=AF.Ln, bias=EPS, scale=1.0 / HD),
                      reads=[tb], writes=[t_rstd])
                    A("act", lambda e: e.activation(out=rstd[:], in_=rstd[:], func=AF.Exp, scale=-0.5), reads=[t_rstd], writes=[t_rstd])
                    A("dve", lambda e: e.scalar_tensor_tensor(out=o_tmp[:], in0=o_sb[:], scalar=gon[:, 0:1], in1=rstd[:],
                                                              op0=ALU.mult, op1=ALU.mult),
                      reads=[t_osb, t_gon, t_rstd], writes=[t_otmp])
                    A("dve", lambda e, h=h, ob=ob: e.tensor_tensor(out=ob[:, h, :], in0=o_tmp[:], in1=siluz[:, h, :], op=ALU.mult),
                      reads=[t_otmp, t_sz[h]], writes=[t_ob])

            return gen

        o_sb, t_osb = sbt([128, T], "o_sb")
        o_tmp, t_otmp = sbt([128, T], "o_tmp")
        outb = [sbt([128, 2, T], f"outb{i}", BF16) for i in range(2)]
        ov = odnT.rearrange("(h p) t -> p h t", p=128)
        head_fns = [make_head(0), make_head(1)]

        def bank_mm(nparts, items, reads, name=None):
            bk, tb = cx.bank()
            for (c0, ncol, lh, rh) in items:
                A("pe", lambda e, bk=bk, c0=c0, ncol=ncol, lh=lh, rh=rh: e.matmul(
                    bk[0:nparts, c0:c0 + ncol], lhsT=lh, rhs=rh, start=True, stop=True), reads=reads, writes=[tb])
            return bk, tb

        for it in range(ntiles):
            t0 = it * T
            ut, t_u = u_t[0], t_ut[0]
            A("sp", lambda e, ut=ut, t0=t0: e.dma_start(out=ut[:], in_=uT_all[:, t0:t0 + T].rearrange("(k p) t -> p k t", p=128)),
              writes=[t_u], dma_tok=t_u)
            for c in range(8):
                bk, tb = cx.bank()
                for k in range(KC):
                    A("pe", lambda e, bk=bk, c=c, k=k, ut=ut: e.matmul(bk[:, :], lhsT=wres[:, k, c * 128:(c + 1) * 128],
                                                                      rhs=ut[:, k, :], start=(k == 0), stop=(k == KC - 1)),
                      reads=[t_w, t_u], writes=[tb])
                if c < 6:
                    evac(cx, rawbuf[:, c, 3:T + 3], bk[:, :], [tb], [t_raw[c]])
                else:
                    A("act", lambda e, bk=bk, c=c: e.activation(out=siluz[:, c - 6, :], in_=bk[:, :], func=AF.Silu),
                      reads=[tb], writes=[t_sz[c - 6]])
            for c in range(6):
                A("dve", lambda e, c=c: e.tensor_scalar(out=cv[:, c, :], in0=rawbuf[:, c, 0:T],
                                                        scalar1=convw[:, c * 4:c * 4 + 1], scalar2=None, op0=ALU.mult),
                  reads=[t_raw[c], t_cw], writes=[t_cv[c]])
                for j in range(1, 4):
                    A("dve", lambda e, c=c, j=j: e.scalar_tensor_tensor(out=cv[:, c, :], in0=rawbuf[:, c, j:j + T],
                                                                        scalar=convw[:, c * 4 + j:c * 4 + j + 1],
                                                                        in1=cv[:, c, :], op0=ALU.mult, op1=ALU.add),
                      reads=[t_raw[c], t_cw, t_cv[c]], writes=[t_cv[c]])
                A("act", lambda e, c=c: e.copy(out=rawbuf[:, c, 0:3], in_=rawbuf[:, c, T:T + 3]),
                  reads=[t_raw[c]], writes=[t_raw[c]])
                A("act", lambda e, c=c: e.activation(out=cv[:, c, :], in_=cv[:, c, :], func=AF.Silu),
                  reads=[t_cv[c]], writes=[t_cv[c]])
            for idx in range(4):
                A("act", lambda e, idx=idx: e.activation(out=sqb[:], in_=cv[:, idx, :], func=AF.Square),
                  reads=[t_cv[idx]], writes=[t_sqb])
                bk, tb = cx.bank()
                A("pe", lambda e, bk=bk: e.matmul(bk[:, :], lhsT=cx.ones_bf[:], rhs=sqb[:], start=True, stop=True),
                  reads=[t_sqb, tc], writes=[tb])
                A("act", lambda e, bk=bk: e.activation(out=rstd[:], in_=bk[:, :], func=AF.Ln, bias=1e-6, scale=1.0),
                  reads=[tb], writes=[t_rstd])
                A("act", lambda e: e.activation(out=rstd[:], in_=rstd[:], func=AF.Exp, scale=-0.5), reads=[t_rstd], writes=[t_rstd])
                sc = float(HD ** -0.5) if idx < 2 else 1.0
                A("dve", lambda e, idx=idx, sc=sc: e.scalar_tensor_tensor(out=qk[:, idx, :], in0=cv[:, idx, :], scalar=sc,
                                                                          in1=rstd[:], op0=ALU.mult, op1=ALU.mult),
                  reads=[t_cv[idx], t_rstd], writes=[t_qk[idx]])
            bk, tb = cx.bank()
            for ck in range(NCHK):
                for k in range(KC):
                    A("pe", lambda e, bk=bk, ck=ck, k=k, ut=ut: e.matmul(bk[0:64, ck * 4:ck * 4 + 4],
                                                                        lhsT=ut[:, k, ck * CH:(ck + 1) * CH],
                                                                        rhs=wbar[:, k, :], start=(k == 0), stop=(k == KC - 1)),
                      reads=[t_wb, t_u], writes=[tb])
            A("dve", lambda e, bk=bk: e.tensor_copy(out=ba[:].rearrange("p c f -> p (c f)"), in_=bk[0:64, 0:NCHK * 4]),
              reads=[tb], writes=[t_ba])
            A("act", lambda e: e.activation(out=beta[:], in_=ba[:, :, 0:2], func=AF.Sigmoid), reads=[t_ba], writes=[t_beta])
            A("dve", lambda e: e.tensor_tensor(out=spx[:], in0=ba[:, :, 2:4], in1=bc_mid(hv[:, 0:2], NCHK), op=ALU.add),
              reads=[t_ba, t_hv], writes=[t_spx])
            A("act", lambda e: e.activation(out=spx[:], in_=spx[:], func=AF.Exp), reads=[t_spx], writes=[t_spx])
            A("dve", lambda e: e.tensor_scalar(out=spx[:], in0=spx[:], scalar1=1.0, scalar2=None, op0=ALU.add),
              reads=[t_spx], writes=[t_spx])
            A("act", lambda e: e.activation(out=spx[:], in_=spx[:], func=AF.Ln), reads=[t_spx], writes=[t_spx])
            A("dve", lambda e: e.scalar_tensor_tensor(out=gt[:], in0=spx[:], scalar=-1.0, in1=bc_mid(expA[:, 0:2], NCHK),
                                                      op0=ALU.mult, op1=ALU.mult), reads=[t_spx, t_hv], writes=[t_g])
            ob, t_ob = outb[it % 2]
            gens = [head_fns[h](ob, t_ob) for h in range(2)]
            while gens:
                for g in list(gens):
                    try:
                        next(g)
                    except StopIteration:
                        gens.remove(g)
            A("sp", lambda e, ob=ob, t0=t0: e.dma_start(out=ov[:, :, t0:t0 + T], in_=ob[:]), reads=[t_ob], dma_tok=t_ob)
        R.emit()
    return nc


SB_ONE = 128
SB_NM0 = 129
SB_VM0 = 129 + 4 * 512
SB_TOT = 129 + 8 * 512


def sb_consts():
    c = np.zeros((128, SB_TOT), np.float32)
    c[:, 0:128] = np.eye(128, dtype=np.float32)
    t = np.arange(128)[:, None]
    j = np.arange(128)[None, :]
    tri = (t + j < 128).astype(np.float32)
    for q4 in range(4):
        for b in range(4):
            blk = c[:, SB_NM0 + q4 * 512 + b * 128: SB_NM0 + q4 * 512 + (b + 1) * 128]
            if 3 - b > q4:
                blk[:] = 1.0
            elif 3 - b == q4:
                blk[:] = tri
    c[:, SB_VM0:SB_VM0 + 4 * 512] = 1.0 - c[:, SB_NM0:SB_NM0 + 4 * 512]
    c[:, SB_ONE] = 1.0
    return c


def build_stage2_sb(ntiles=SEQ // T, heads=(0, 1)):
    S = ntiles * T
    NB = S // 128
    nc = bass.Bass("TRN2", target_bir_lowering=False)
    uT_all = nc.dram_tensor("uT_all", [D, S], BF16, kind="ExternalInput").ap()
    uT_rev = nc.dram_tensor("uT_rev", [D, S], BF16, kind="ExternalInput").ap()
    wsb = nc.dram_tensor("wsb", [D, 768], F32, kind="ExternalInput").ap()
    consts_d = nc.dram_tensor("consts", [128, SB_TOT], F32, kind="ExternalInput").ap()
    osbT = nc.dram_tensor("osbT", [256, S], BF16, kind="ExternalOutput").ap()
    with contextlib.ExitStack() as stack:
        cx = Ctx(nc, stack)
        R = cx.R
        A = R.add
        tc = cx.t_const
        consts = R.sb([128, 129], F32, "consts")
        A("sp", lambda e: e.dma_start(out=consts[:], in_=consts_d[:, 0:129]), writes=[tc], dma_tok=tc)
        ones_c = consts[:, 128:129]
        masks = R.sb([128, 8 * 512], BF16, "masks")
        t_mk = R.tok("masks")
        A("pool", lambda e: e.dma_start(out=masks[:], in_=consts_d[:, SB_NM0:SB_TOT]), writes=[t_mk], dma_tok=t_mk)
        ident_bf = R.sb([128, 128], BF16, "ident_bf")
        A("dve", lambda e: e.tensor_copy(out=ident_bf[:], in_=consts[:, 0:128]), reads=[tc], writes=[tc])
        ones_f = ones_c.to_broadcast([128, T])
        wres = R.sb([128, KC, 768], BF16, "wres")
        t_w = R.tok("wres")
        wv = wsb.rearrange("(k p) n -> p k n", p=128)
        for half in range(2):
            A("pool", lambda e, half=half: e.dma_start(out=wres[:, :, half * 384:(half + 1) * 384],
                                                      in_=wv[:, :, half * 384:(half + 1) * 384]),
              writes=[t_w], dma_tok=t_w)
        uf, tuf = R.sb([128, KC, T], BF16, "u_f"), R.tok("u_f")
        ur, tur = R.sb([128, KC, T], BF16, "u_r"), R.tok("u_r")
        QT = R.sb([128, S], BF16, "QT")
        KTr = R.sb([128, S], BF16, "KTr")
        dVr = R.sb([128, NB, 128], BF16, "dVr")
        t_Q = [R.tok(f"Q{i}") for i in range(ntiles)]
        t_K = [R.tok(f"K{i}") for i in range(ntiles)]
        t_V = [R.tok(f"V{i}") for i in range(ntiles)]
        VTb = [R.sb([128, T + 1], F32, f"VTb{i}") for i in range(2)]
        t_VT = [R.tok(f"VTb{i}") for i in range(2)]
        dVT = R.sb([128, T], BF16, "dVT")
        t_dVT = R.tok("dVT")
        smb = [[R.sb([128, T], F32, f"smb{i}_{k}") for k in range(2)] for i in range(4)]
        t_sm = [[R.tok(f"smb{i}_{k}") for k in range(2)] for i in range(4)]
        Pb = [[R.sb([128, T], BF16, f"Pb{i}_{k}") for k in range(2)] for i in range(4)]
        t_P = [[R.tok(f"Pb{i}_{k}") for k in range(2)] for i in range(4)]
        Pm = [R.sb([128, T], BF16, f"Pm{i}") for i in range(4)]
        t_Pm = [R.tok(f"Pm{i}") for i in range(4)]
        vsh, t_vsh = R.sb([128, T], F32, "vsh"), R.tok("vsh")
        ATs = R.sb([128, 4, T], BF16, "ATs")
        t_AT = [R.tok(f"ATs_{hh}") for hh in range(2)]
        osb, t_os = R.sb([128, T], BF16, "osb"), R.tok("osb")
        scale = float(HD ** -0.5)
        uva = uT_all.rearrange("(k p) t -> p k t", p=128)
        uvr = uT_rev.rearrange("(k p) t -> p k t", p=128)
        gstep = 0
        for h in heads:
            for n, it in enumerate(reversed(range(ntiles))):
                t0 = it * T
                A("sp", lambda e, t0=t0: e.dma_start(out=uf[:], in_=uva[:, :, t0:t0 + T]), writes=[tuf], dma_tok=tuf)
                A("sp", lambda e, t0=t0: e.dma_start(out=ur[:], in_=uvr[:, :, t0:t0 + T]), writes=[tur], dma_tok=tur)
                for (src, tsrc, col, dst, tdst) in ((uf, tuf, h * 128, QT, t_Q[it]), (ur, tur, 256 + h * 128, KTr, t_K[it])):
                    bk, tb = cx.bank()
                    for k in range(KC):
                        A("pe", lambda e, bk=bk, k=k, src=src, col=col: e.matmul(bk[:, :], lhsT=wres[:, k, col:col + 128],
                                                                                rhs=src[:, k, :], start=(k == 0), stop=(k == KC - 1)),
                          reads=[t_w, tsrc], writes=[tb])
                    evac(cx, dst[:, t0:t0 + T], bk[:, :], [tb], [tdst])
                vt, tvt = VTb[n % 2], t_VT[n % 2]
                vtp, tvtp = VTb[(n + 1) % 2], t_VT[(n + 1) % 2]
                bk, tb = cx.bank()
                for k in range(KC):
                    A("pe", lambda e, bk=bk, k=k, h=h: e.matmul(bk[:, :], lhsT=wres[:, k, 512 + h * 128:512 + (h + 1) * 128],
                                                               rhs=ur[:, k, :], start=(k == 0), stop=(k == KC - 1)),
                      reads=[t_w, tur], writes=[tb])
                A("act", lambda e, bk=bk, vt=vt: e.copy(out=vt[:, 0:T], in_=bk[:, :]), reads=[tb], writes=[tvt])
                if n == 0:
                    A("pool", lambda e, vt=vt: e.memset(vt[:, T:T + 1], 0.0), reads=[tvt], writes=[tvt])
                else:
                    A("act", lambda e, vt=vt, vtp=vtp: e.copy(out=vt[:, T:T + 1], in_=vtp[:, 0:1]), reads=[tvtp, tvt], writes=[tvt])
                A("pool", lambda e, vt=vt: e.tensor_tensor(out=dVT[:], in0=vt[:, 1:T + 1], in1=vt[:, 0:T], op=ALU.subtract),
                  reads=[tvt], writes=[t_dVT])
                bk2, tb2 = cx.bank()
                bkb = bk2[:, :].bitcast(BF16)
                for b in range(4):
                    A("pe", lambda e, bkb=bkb, b=b: e.transpose(out=bkb[:, b * 128:(b + 1) * 128], in_=dVT[:, b * 128:(b + 1) * 128],
                                                               identity=ident_bf[:]), reads=[t_dVT, tc], writes=[tb2])
                evac(cx, dVr[:, it * 4:it * 4 + 4, :].rearrange("p b d -> p (b d)"), bkb[:, 0:T], [tb2], [t_V[it]])
            steps = [(qg, i) for qg in range(NB // 4) for i in range(qg + 1)]

            def issue_scores(qg, i, par):
                j0 = (NB - 4 - 4 * qg + 4 * i) * 128
                kt = j0 // T
                zb = []
                for q4 in range(4):
                    qb = qg * 4 + q4
                    bk, tb = cx.bank()
                    zb.append((bk, tb))
                    A("pe", lambda e, bk=bk, qb=qb, j0=j0: e.matmul(bk[:, :], lhsT=QT[:, qb * 128:(qb + 1) * 128],
                                                                   rhs=KTr[:, j0:j0 + T], start=True, stop=True),
                      reads=[t_Q[qg], t_K[kt]], writes=[tb])
                for q4 in range(4):
                    bk, tb = zb[q4]
                    sm, tsm = smb[q4][par], t_sm[q4][par]
                    A("act", lambda e, bk=bk, sm=sm: e.activation(out=sm[:], in_=bk[:, :], func=AF.Sigmoid, scale=-scale),
                      reads=[tb], writes=[tsm])
                    if i == 0:
                        nm = masks[:, q4 * 512:(q4 + 1) * 512]
                        A("dve", lambda e, sm=sm, nm=nm: e.tensor_tensor(out=sm[:], in0=sm[:], in1=nm, op=ALU.max),
                          reads=[tsm, t_mk], writes=[tsm])

            issue_scores(steps[0][0], steps[0][1], gstep % 2)
            obk = t_obk = None
            for n, (qg, i) in enumerate(steps):
                par = gstep % 2
                gstep += 1
                if n + 1 < len(steps):
                    issue_scores(steps[n + 1][0], steps[n + 1][1], gstep % 2)
                ntile = qg + 1
                JB = NB - 4 - 4 * qg + 4 * i
                kt = JB // 4
                if i == 0:
                    obk, t_obk = cx.bank()
                    cx.reserved.add(id(obk))
                    tq0 = qg * T
                    if tq0 == 0:
                        A("pool", lambda e: e.memset(uf[:, :, 0:1], 0.0), writes=[tuf])
                        A("sp", lambda e: e.dma_start(out=uf[:, :, 1:T], in_=uva[:, :, 0:T - 1]), writes=[tuf], dma_tok=tuf)
                    else:
                        A("sp", lambda e, tq0=tq0: e.dma_start(out=uf[:], in_=uva[:, :, tq0 - 1:tq0 + T - 1]), writes=[tuf], dma_tok=tuf)
                for q4 in range(4):
                    sm, tsm = smb[q4][par], t_sm[q4][par]
                    Pc, tPc = Pb[q4][par], t_P[q4][par]
                    Pp, tPp = Pb[q4][1 - par], t_P[q4][1 - par]
                    if i == 0:
                        A("dve", lambda e, sm=sm, Pc=Pc: e.tensor_tensor_scan(out=Pc[:], data0=sm[:], data1=ones_f, initial=ones_c,
                                                                             op0=ALU.mult, op1=ALU.mult),
                          reads=[tsm, tc], writes=[tPc])
                    else:
                        A("dve", lambda e, sm=sm, Pc=Pc, Pp=Pp: e.tensor_tensor_scan(out=Pc[:], data0=sm[:], data1=ones_f,
                                                                                    initial=Pp[:, T - 1:T], op0=ALU.mult, op1=ALU.mult),
                          reads=[tsm, tPp, tc], writes=[tPc])
                if i == 0:
                    for q4 in range(4):
                        vm = masks[:, (4 + q4) * 512:(5 + q4) * 512]
                        A("pool", lambda e, q4=q4, vm=vm, par=par: e.tensor_tensor(out=Pm[q4][:], in0=Pb[q4][par][:], in1=vm, op=ALU.mult),
                          reads=[t_P[q4][par], t_mk], writes=[t_Pm[q4]])
                for hh in range(2):
                    bk2, tb2 = cx.bank()
                    bkb = bk2[:, :].bitcast(BF16)
                    for bb in range(2):
                        b = hh * 2 + bb
                        for q4 in range(4):
                            Pc, tPc = (Pm[q4], t_Pm[q4]) if i == 0 else (Pb[q4][par], t_P[q4][par])
                            A("pe", lambda e, bkb=bkb, b=b, bb=bb, q4=q4, Pc=Pc: e.transpose(
                                out=bkb[:, bb * 512 + q4 * 128:bb * 512 + (q4 + 1) * 128], in_=Pc[:, b * 128:(b + 1) * 128],
                                identity=ident_bf[:]), reads=[tPc, tc], writes=[tb2])
                    A("act", lambda e, bkb=bkb, hh=hh: e.copy(out=ATs[:, hh * 2:hh * 2 + 2, :].rearrange("p b q -> p (b q)"),
                                                             in_=bkb[:, :]), reads=[tb2], writes=[t_AT[hh]])
                for b in range(4):
                    first = (i == 0 and b == 0)
                    last = (i == ntile - 1 and b == 3)
                    A("pe", lambda e, obk=obk, b=b, JB=JB, first=first, last=last: e.matmul(
                        obk[:, :], lhsT=dVr[:, JB + b, :], rhs=ATs[:, b, :], start=first, stop=last),
                      reads=[t_AT[b // 2], t_V[kt]], writes=[t_obk])
                if i == ntile - 1:
                    bkv, tbv = cx.bank()
                    for k in range(KC):
                        A("pe", lambda e, bkv=bkv, k=k, h=h: e.matmul(bkv[:, :], lhsT=wres[:, k, 512 + h * 128:512 + (h + 1) * 128],
                                                                     rhs=uf[:, k, :], start=(k == 0), stop=(k == KC - 1)),
                          reads=[t_w, tuf], writes=[tbv])
                    A("act", lambda e, bkv=bkv: e.copy(out=vsh[:], in_=bkv[:, :]), reads=[tbv], writes=[t_vsh])
                    A("dve", lambda e, obk=obk: e.tensor_tensor(out=osb[:], in0=obk[:, :], in1=vsh[:], op=ALU.add),
                      reads=[t_obk, t_vsh], writes=[t_os])
                    cx.reserved.discard(id(obk))
                    A("sp", lambda e, qg=qg, h=h: e.dma_start(out=osbT[h * 128:(h + 1) * 128, qg * T:(qg + 1) * T], in_=osb[:]),
                      reads=[t_os], dma_tok=t_os)
        R.emit()
    return nc


def linear_fm(cx, ws, w_dram, kch, xT, t_x, col0, nout, consume):
    R = cx.R
    for c4 in range(0, nout, 4):
        n = min(4, nout - c4)
        wt, tw = ws.load(w_dram, 0, kch, col0 + c4 * 128, n * 128)
        for j in range(n):
            bk, tb = cx.bank()
            for k in range(kch):
                R.add("pe", lambda e, bk=bk, k=k, j=j, wt=wt: e.matmul(bk[:, :], lhsT=wt[:, k, j * 128:(j + 1) * 128],
                                                                      rhs=xT[:, k, :], start=(k == 0), stop=(k == kch - 1)),
                      reads=[tw, t_x], writes=[tb])
            consume(c4 + j, bk, tb)


def build_stage3(ntiles=TOK // T):
    nc = bass.Bass("TRN2", target_bir_lowering=False)
    NTK = ntiles * T
    h1T = nc.dram_tensor("h1T", [D, NTK], F32, kind="ExternalInput").ap()
    uT_d = nc.dram_tensor("uT", [D, NTK], BF16, kind="ExternalInput").ap()
    odn_d = nc.dram_tensor("odnT", [D, NTK], BF16, kind="ExternalInput").ap()
    osb_d = nc.dram_tensor("osbT", [D, NTK], BF16, kind="ExternalInput").ap()
    pT_d = nc.dram_tensor("pT", [PLE, NTK], F32, kind="ExternalInput").ap()
    wgate = nc.dram_tensor("wgate", [D, 2 * D], F32, kind="ExternalInput").ap()
    wbd = nc.dram_tensor("wbd", [D, D], F32, kind="ExternalInput").ap()
    wbs = nc.dram_tensor("wbs", [D, D], F32, kind="ExternalInput").ap()
    wout = nc.dram_tensor("wout", [D, D], F32, kind="ExternalInput").ap()
    wg = nc.dram_tensor("wg", [D, DFF], F32, kind="ExternalInput").ap()
    wu = nc.dram_tensor("wu", [D, DFF], F32, kind="ExternalInput").ap()
    wd = nc.dram_tensor("wd", [DFF, D], F32, kind="ExternalInput").ap()
    wpg = nc.dram_tensor("wpg", [D, D], F32, kind="ExternalInput").ap()
    wpp = nc.dram_tensor("wpp", [PLE, D], F32, kind="ExternalInput").ap()
    gains = nc.dram_tensor("gains", [128, 5 * KC], F32, kind="ExternalInput").ap()
    ident_d = nc.dram_tensor("ident", [128, 128], F32, kind="ExternalInput").ap()
    out = nc.dram_tensor("out", [NTK, D], F32, kind="ExternalOutput").ap()
    with contextlib.ExitStack() as stack:
        cx = Ctx(nc, stack)
        R = cx.R
        A = R.add
        g_all, t_g = load_vec_fm(cx, gains, "gains_sb")
        ident = R.sb([128, 128], F32, "ident")
        A("sp", lambda e: e.dma_start(out=ident[:], in_=ident_d[:, :]), writes=[cx.t_const], dma_tok=cx.t_const)
        big = R.sb([128, KC * T], F32, "big")
        otok = big[:].rearrange("p (s d) -> p s d", s=4)
        fT = big[:].rearrange("p (c t) -> p c t", c=KC)
        t_f = R.tok("fT")
        t_otok = [t_f] * 4
        hT = R.sb([128, KC, T], F32, "hT")
        t_h = R.tok("hT")
        uT = R.sb([128, KC, T], BF16, "uT")
        t_u = R.tok("uT")
        sq = R.sb([128, KC, T], BF16, "sq")
        t_sq = R.tok("sq")
        rstd = R.sb([128, T], F32, "rstd")
        t_rstd = R.tok("rstd")
        actT = R.sb([128, DFF // 128, T], BF16, "actT")
        t_act = R.tok("actT")
        odn = actT[:, 0:KC, :]
        osb = actT[:, KC:2 * KC, :]
        tmp = [R.sb([128, T], F32, f"tmp{i}") for i in range(2)]
        t_tmp = [R.tok(f"tmp{i}") for i in range(2)]
        pTb = R.sb([128, 2, T], BF16, "pTb")
        t_p = R.tok("pTb")
        ws = WStream(cx, KC, 3, "w")
        hv = h1T.rearrange("(c p) t -> p c t", p=128)
        uv = uT_d.rearrange("(c p) t -> p c t", p=128)
        dnv = odn_d.rearrange("(c p) t -> p c t", p=128)
        sbv = osb_d.rearrange("(c p) t -> p c t", p=128)
        pv = pT_d.rearrange("(c p) t -> p c t", p=128)
        for it in range(ntiles):
            r0 = it * T
            A("sp", lambda e, r0=r0: e.dma_start(out=hT[:], in_=hv[:, :, r0:r0 + T]), writes=[t_h], dma_tok=t_h)
            A("sp", lambda e, r0=r0: e.dma_start(out=uT[:], in_=uv[:, :, r0:r0 + T]), writes=[t_u], dma_tok=t_u)
            t_dn = R.tok("odn_ld")
            A("sp", lambda e, r0=r0: e.dma_start(out=odn, in_=dnv[:, :, r0:r0 + T]), writes=[t_act], dma_tok=t_dn)
            A("sp", lambda e, r0=r0: e.dma_start(out=osb, in_=sbv[:, :, r0:r0 + T]), writes=[t_act], dma_tok=t_dn)
            A("pool", lambda e, r0=r0: e.dma_start(out=pTb[:], in_=pv[:, :, r0:r0 + T]), writes=[t_p], dma_tok=t_p)
            for (gcol, wb, xs, tx, second) in ((0, wbd, odn, t_act, False), (D, wbs, osb, t_act, True)):
                for c4 in range(0, KC, 4):
                    gtile, tgw = ws.load(wgate, 0, KC, gcol + c4 * 128, 512)
                    btile, tbw = ws.load(wb, 0, KC, c4 * 128, 512)
                    for j in range(4):
                        c = c4 + j
                        bg, tbg = cx.bank()
                        for k in range(KC):
                            A("pe", lambda e, bg=bg, k=k, j=j, gtile=gtile: e.matmul(
                                bg[:, :], lhsT=gtile[:, k, j * 128:(j + 1) * 128], rhs=uT[:, k, :],
                                start=(k == 0), stop=(k == KC - 1)), reads=[tgw, t_u], writes=[tbg])
                        bb, tbb = cx.bank()
                        for k in range(KC):
                            A("pe", lambda e, bb=bb, k=k, j=j, btile=btile, xs=xs: e.matmul(
                                bb[:, :], lhsT=btile[:, k, j * 128:(j + 1) * 128], rhs=xs[:, k, :],
                                start=(k == 0), stop=(k == KC - 1)), reads=[tbw, tx], writes=[tbb])
                        tt, tm = t_tmp[c % 2], tmp[c % 2]
                        A("act", lambda e, bg=bg, tm=tm: e.activation(out=tm[:], in_=bg[:, :], func=AF.Sigmoid),
                          reads=[tbg], writes=[tt])
                        if not second:
                            A("dve", lambda e, bb=bb, tm=tm, c=c: e.tensor_tensor(out=fT[:, c, :], in0=tm[:], in1=bb[:, :],
                                                                                 op=ALU.mult), reads=[tt, tbb], writes=[t_f])
                        else:
                            A("dve", lambda e, bb=bb, tm=tm: e.tensor_tensor(out=tm[:], in0=tm[:], in1=bb[:, :], op=ALU.mult),
                              reads=[tt, tbb], writes=[tt])
                            A("dve", lambda e, tm=tm, c=c: e.tensor_tensor(out=sq[:, c, :], in0=tm[:], in1=fT[:, c, :],
                                                                          op=ALU.add), reads=[tt, t_f], writes=[t_sq])
            linear_fm(cx, ws, wout, KC, sq, t_sq, 0, KC,
                      lambda c, bk, tb: evac(cx, fT[:, c, :], bk[:, :], [tb], [t_f]))
            resid_norm_add(cx, fT, t_f, g_all[:, 0:KC], t_g, hT, t_h, sq, t_sq, rstd, t_rstd, 1.0)
            rms_fm(cx, hT, t_h, KC, g_all[:, KC:2 * KC], t_g, uT, t_u, sq, t_sq, rstd, t_rstd, D)
            ffn_fm(cx, uT, t_u, wg, wu, wd, actT, t_act, ws, fT, t_f, tmp, t_tmp)
            resid_norm_add(cx, fT, t_f, g_all[:, 2 * KC:3 * KC], t_g, hT, t_h, sq, t_sq, rstd, t_rstd, 0.5)
            rms_fm(cx, hT, t_h, KC, g_all[:, 3 * KC:4 * KC], t_g, uT, t_u, sq, t_sq, rstd, t_rstd, D)
            for c4 in range(0, KC, 4):
                gtile, tgw = ws.load(wpg, 0, KC, c4 * 128, 512)
                ptile, tpw = ws.load(wpp, 0, 2, c4 * 128, 512)
                for j in range(4):
                    c = c4 + j
                    bg, tbg = cx.bank()
                    for k in range(KC):
                        A("pe", lambda e, bg=bg, k=k, j=j, gtile=gtile: e.matmul(
                            bg[:, :], lhsT=gtile[:, k, j * 128:(j + 1) * 128], rhs=uT[:, k, :],
                            start=(k == 0), stop=(k == KC - 1)), reads=[tgw, t_u], writes=[tbg])
                    bp, tbp = cx.bank()
                    for k in range(2):
                        A("pe", lambda e, bp=bp, k=k, j=j, ptile=ptile: e.matmul(
                            bp[:, :], lhsT=ptile[:, k, j * 128:(j + 1) * 128], rhs=pTb[:, k, :],
                            start=(k == 0), stop=(k == 1)), reads=[tpw, t_p], writes=[tbp])
                    tt, tm = t_tmp[c % 2], tmp[c % 2]
                    A("act", lambda e, bg=bg, tm=tm: e.activation(out=tm[:], in_=bg[:, :], func=AF.Sigmoid),
                      reads=[tbg], writes=[tt])
                    A("dve", lambda e, bp=bp, tm=tm, c=c: e.tensor_tensor(out=fT[:, c, :], in0=tm[:], in1=bp[:, :], op=ALU.mult),
                      reads=[tt, tbp], writes=[t_f])
            resid_norm_add(cx, fT, t_f, g_all[:, 4 * KC:5 * KC], t_g, hT, t_h, sq, t_sq, rstd, t_rstd, 1.0)
            transpose_out(cx, hT, t_h, KC, out, r0, otok, t_otok, ident)
        R.emit()
    return nc


_PROGS = {}


def _prog(name, fn):
    if name not in _PROGS:
        _PROGS[name] = fn()
    return _PROGS[name]


def _run(nc, maps):
    import sys, time
    t0 = time.time()
    res = run_bass_kernel_spmd(nc, maps, core_ids=list(range(NCORE)))
    print(f"[kernel] launch done in {time.time() - t0:.1f}s", file=sys.stderr, flush=True)
    return res.results


DN_W = 2048
O1 = 3 * DN_W
O2 = O1 + DN_W
O3 = O2 + 16
O4 = O3 + 16
O5 = O4 + 3 * 2048
O6 = O5 + D


def kernel(x, p, ffn1_norm_pre, ffn1_w_gate, ffn1_w_up, ffn1_w_down, ffn1_norm_post,
           mix_norm_pre, w_in, dn_conv_w, dn_A_log, dn_dt_bias, dn_out_norm,
           w_branch_dn, w_branch_sb, w_out, mix_norm_post,
           ffn2_norm_pre, ffn2_w_gate, ffn2_w_up, ffn2_w_down, ffn2_norm_post,
           ple_norm_pre, ple_w_gate, ple_w_proj, ple_norm_post):
    f32 = lambda a: np.ascontiguousarray(np.asarray(a, dtype=np.float32))
    x = f32(x)[0]
    p = f32(p)[0, 0]
    w_in = f32(w_in)[0]
    ident = np.eye(128, dtype=np.float32)
    g1 = np.concatenate([fm_vec(ffn1_norm_pre[0]), fm_vec(ffn1_norm_post[0]), fm_vec(mix_norm_pre[0])], 1)
    wg1, wu1, wd1 = f32(ffn1_w_gate)[0], f32(ffn1_w_up)[0], f32(ffn1_w_down)[0]
    maps = [{"x": np.ascontiguousarray(x[c * TOK:(c + 1) * TOK]), "wg": wg1, "wu": wu1, "wd": wd1,
             "gains": g1, "ident": ident} for c in range(NCORE)]
    r1 = _run(_prog("s1", build_stage1), maps)
    h1T = [np.asarray(r["h1T"]) for r in r1]
    uT = [np.asarray(r["uT"]) for r in r1]
    uT_all = np.ascontiguousarray(np.concatenate(uT, axis=1))
    uT_rev = np.ascontiguousarray(uT_all[:, ::-1])
    del r1, maps
    conv = f32(dn_conv_w)[0]
    consts = dn_consts()
    maps = []
    for c in range(NCORE):
        hs = (2 * c, 2 * c + 1)
        cols = [w_in[:, off + h * 128: off + (h + 1) * 128] for off in (0, DN_W, 2 * DN_W, O1) for h in hs]
        wdn = np.ascontiguousarray(np.concatenate(cols, axis=1))
        wba = np.ascontiguousarray(np.stack([w_in[:, O2 + hs[0]], w_in[:, O2 + hs[1]], w_in[:, O3 + hs[0]], w_in[:, O3 + hs[1]]], axis=1))
        cw = np.stack([conv[:, off + h * 128: off + (h + 1) * 128] for off in (0, DN_W, 2 * DN_W) for h in hs], axis=0)
        cw = np.ascontiguousarray(cw.transpose(2, 0, 1).reshape(128, 24))
        hv = np.array([dn_dt_bias[0][hs[0]], dn_dt_bias[0][hs[1]], dn_A_log[0][hs[0]], dn_A_log[0][hs[1]]], np.float32)
        maps.append({"uT_all": uT_all, "wdn": wdn, "wba": wba, "convw": cw,
                     "gon": f32(dn_out_norm)[0][:, None].copy(), "hv": np.ascontiguousarray(np.tile(hv[None, :], (64, 1))),
                     "consts": consts})
    r2a = _run(_prog("s2a", build_stage2_dn), maps)
    odn_full = np.ascontiguousarray(np.concatenate([np.asarray(r["odnT"]) for r in r2a], axis=0))
    del r2a, maps
    sbc = sb_consts()
    maps = []
    for c in range(NCORE):
        hs = (2 * c, 2 * c + 1)
        cols = [w_in[:, O4 + off + h * 128: O4 + off + (h + 1) * 128] for off in (0, 2048, 4096) for h in hs]
        maps.append({"uT_all": uT_all, "uT_rev": uT_rev, "wsb": np.ascontiguousarray(np.concatenate(cols, axis=1)), "consts": sbc})
    r2b = _run(_prog("s2b", build_stage2_sb), maps)
    osb_full = np.ascontiguousarray(np.concatenate([np.asarray(r["osbT"]) for r in r2b], axis=0))
    del r2b, maps, uT_all, uT_rev
    g3 = np.concatenate([fm_vec(mix_norm_post[0]), fm_vec(ffn2_norm_pre[0]), fm_vec(ffn2_norm_post[0]),
                         fm_vec(ple_norm_pre[0]), fm_vec(ple_norm_post[0])], 1)
    wgate = np.ascontiguousarray(w_in[:, O5:O5 + 2 * D])
    shared = {"wgate": wgate, "wbd": f32(w_branch_dn)[0], "wbs": f32(w_branch_sb)[0], "wout": f32(w_out)[0],
              "wg": f32(ffn2_w_gate)[0], "wu": f32(ffn2_w_up)[0], "wd": f32(ffn2_w_down)[0],
              "wpg": f32(ple_w_gate)[0], "wpp": f32(ple_w_proj)[0], "gains": g3, "ident": ident}
    maps = []
    for c in range(NCORE):
        sl = slice(c * TOK, (c + 1) * TOK)
        m = dict(shared)
        m.update({"h1T": h1T[c], "uT": uT[c], "odnT": np.ascontiguousarray(odn_full[:, sl]),
                  "osbT": np.ascontiguousarray(osb_full[:, sl]), "pT": np.ascontiguousarray(p[sl].T)})
        maps.append(m)
    r3 = _run(_prog("s3", build_stage3), maps)
    out = np.concatenate([np.asarray(r["out"]) for r in r3], axis=0)
    return out.reshape(1, SEQ, D).astype(np.float32)
```

```python
import contextlib
import numpy as np
import concourse.bass as bass
import concourse.mybir as mybir
from concourse.bass_utils import run_bass_kernel_spmd

F32 = mybir.dt.float32
BF16 = mybir.dt.bfloat16
AF = mybir.ActivationFunctionType
ALU = mybir.AluOpType

D = 2048
SEQ = 16384
NCORE = 8
TOK = SEQ // NCORE
T = 512
DFF = 5632
PLE = 256
KC = D // 128
EPS = 1e-6


class Tok:
    __slots__ = ("name", "last_w", "readers", "sem", "dma_cnt", "last_dma")

    def __init__(self, name):
        self.name = name
        self.last_w = None
        self.readers = []
        self.sem = None
        self.dma_cnt = 0
        self.last_dma = None


class Op:
    __slots__ = ("eng", "fn", "deps", "dma", "signal", "sem", "count", "tok")

    def __init__(self, eng, fn, dma):
        self.eng = eng
        self.fn = fn
        self.deps = []
        self.dma = dma
        self.signal = dma
        self.sem = None
        self.count = 0
        self.tok = None


ENGS = ("pe", "act", "dve", "pool", "sp")


class Rec:
    def __init__(self, nc, stack):
        self.nc = nc
        self.stack = stack
        self.ops = {e: [] for e in ENGS}
        self.toks = []
        self.dma_toks = []
        self.nsb = 0

    def tok(self, name="t"):
        t = Tok(name)
        self.toks.append(t)
        return t

    def sb(self, shape, dt, name=None):
        self.nsb += 1
        return self.stack.enter_context(self.nc.sbuf_tensor("s_" + (name or f"sb{self.nsb}"), list(shape), dt))

    def ps(self, shape, dt, name=None):
        self.nsb += 1
        return self.stack.enter_context(self.nc.psum_tensor("p_" + (name or f"ps{self.nsb}"), list(shape), dt))

    def add(self, eng, fn, reads=(), writes=(), dma_tok=None):
        op = Op(eng, fn, dma_tok is not None)
        deps = []
        for r in reads:
            if r.last_w is not None:
                deps.append(r.last_w)
        for w in writes:
            if w.last_w is not None:
                deps.append(w.last_w)
            deps.extend(w.readers)
        if dma_tok is not None:
            if dma_tok.sem is None:
                dma_tok.sem = self.stack.enter_context(self.nc.semaphore(f"dq{len(self.dma_toks)}"))
                dma_tok.last_dma = None
                self.dma_toks.append(dma_tok)
            if getattr(dma_tok, "last_dma", None) is not None:
                deps.append(dma_tok.last_dma)
            dma_tok.dma_cnt += 16
            op.sem = dma_tok.sem
            op.count = dma_tok.dma_cnt
            op.tok = dma_tok
            dma_tok.last_dma = op
        seen = set()
        for d in deps:
            if d is op or id(d) in seen:
                continue
            seen.add(id(d))
            if (not d.dma) and d.eng == eng and eng == "pe":
                continue
            op.deps.append(d)
            d.signal = True
        for r in reads:
            r.readers.append(op)
        for w in writes:
            w.last_w = op
            w.readers = []
        self.ops[eng].append(op)
        return op

    def emit(self):
        nc = self.nc
        esem = {}
        for e in ("pe", "act", "dve", "pool"):
            esem[e] = self.stack.enter_context(nc.semaphore(f"es_{e}"))
        for e in ("pe", "act", "dve", "pool", "sp"):
            c = 0
            for op in self.ops[e]:
                if op.dma:
                    continue
                if op.signal:
                    c += 1
                    op.sem = esem.get(e)
                    op.count = c
                    assert e != "sp"
        final = [(t.sem, t.dma_cnt) for t in self.dma_toks]

        def run(e, eng):
            waited = {}
            for op in self.ops[e]:
                for d in op.deps:
                    key = id(d.sem)
                    if waited.get(key, 0) >= d.count:
                        continue
                    eng.wait_ge(d.sem, d.count)
                    waited[key] = d.count
                ins = op.fn(eng)
                if op.signal:
                    ins.then_inc(op.sem, 16 if op.dma else 1)
            if e == "sp":
                for s, c in final:
                    eng.wait_ge(s, c)

        with nc.Block() as block:
            @block.tensor
            def _(eng):
                run("pe", eng)

            @block.scalar
            def _(eng):
                run("act", eng)

            @block.vector
            def _(eng):
                run("dve", eng)

            @block.gpsimd
            def _(eng):
                run("pool", eng)

            @block.sync
            def _(eng):
                run("sp", eng)


class Ctx:
    def __init__(self, nc, stack):
        self.nc = nc
        self.R = Rec(nc, stack)
        R = self.R
        self.banks = [R.ps([128, 512], F32, name=f"bank{i}") for i in range(8)]
        self.bank_tok = [R.tok(f"bank{i}") for i in range(8)]
        self.bank_rr = 0
        self.ones_bf = R.sb([128, 128], BF16, "ones_bf")
        self.t_const = R.tok("const")
        R.add("pool", lambda e: e.memset(self.ones_bf[:], 1.0), writes=[self.t_const])
        self.evac_rr = 0
        self.reserved = set()

    def bank(self):
        while True:
            i = self.bank_rr % 8
            self.bank_rr += 1
            if id(self.banks[i]) not in self.reserved:
                return self.banks[i], self.bank_tok[i]


def load_vec_fm(cx, dram_vec, name):
    R = cx.R
    t = R.sb([128, dram_vec.shape[1]], F32, name)
    tk = R.tok(name)
    R.add("sp", lambda e: e.dma_start(out=t[:], in_=dram_vec[:, :]), writes=[tk], dma_tok=tk)
    return t, tk


def rms_fm(cx, src, t_src, nch, gain, t_gain, dst, t_dst, sq, t_sq, rstd, t_rstd, dim, post=None):
    R = cx.R
    R.add("act", lambda e: e.activation(out=sq[:, 0:nch, :], in_=src[:, 0:nch, :], func=AF.Square),
          reads=[t_src], writes=[t_sq])
    bk, tb = cx.bank()
    for c in range(nch):
        R.add("pe", lambda e, c=c: e.matmul(bk[:, :], lhsT=cx.ones_bf[:], rhs=sq[:, c, :],
                                             start=(c == 0), stop=(c == nch - 1)),
              reads=[t_sq, cx.t_const], writes=[tb])
    R.add("act", lambda e: e.activation(out=rstd[:], in_=bk[:, :], func=AF.Ln, bias=EPS, scale=1.0 / dim),
          reads=[tb], writes=[t_rstd])
    R.add("act", lambda e: e.activation(out=rstd[:], in_=rstd[:], func=AF.Exp, scale=-0.5), reads=[t_rstd], writes=[t_rstd])
    if post is None:
        for c in range(nch):
            R.add("dve", lambda e, c=c: e.scalar_tensor_tensor(out=dst[:, c, :], in0=src[:, c, :],
                                                                scalar=gain[:, c:c + 1], in1=rstd[:],
                                                                op0=ALU.mult, op1=ALU.mult),
                  reads=[t_src, t_gain, t_rstd], writes=[t_dst])
    else:
        post()


def transpose_in(cx, x_dram, r0, xtok, t_xtok, xT, t_xT, ident, ncol_chunks, alias=()):
    R = cx.R
    W = ncol_chunks * 128
    alias = list(alias)
    for s in range(T // 128):
        R.add("sp", lambda e, s=s: e.dma_start(out=xtok[:, s, 0:W], in_=x_dram[r0 + s * 128:r0 + (s + 1) * 128, :]),
              writes=[t_xtok[s]] + alias, dma_tok=t_xtok[s])
    for c in range(ncol_chunks):
        bk, tb = cx.bank()
        for s in range(T // 128):
            R.add("pe", lambda e, c=c, s=s, bk=bk: e.transpose(out=bk[:, s * 128:(s + 1) * 128],
                                                                in_=xtok[:, s, c * 128:(c + 1) * 128], identity=ident[:]),
                  reads=[t_xtok[s], cx.t_const] + alias, writes=[tb])
        evac(cx, xT[:, c, :], bk[:, :], [tb], [t_xT])


def evac(cx, dst, src, reads, writes):
    R = cx.R
    cx.evac_rr += 1
    if cx.evac_rr % 2 == 0:
        R.add("act", lambda e: e.copy(out=dst, in_=src), reads=reads, writes=writes)
    else:
        R.add("dve", lambda e: e.tensor_copy(out=dst, in_=src), reads=reads, writes=writes)


def transpose_out(cx, srcT, t_src, nch, out_dram, r0, otok, t_otok, ident):
    R = cx.R
    for s in range(T // 128):
        for c4 in range(0, nch, 4):
            bk, tb = cx.bank()
            for j in range(4):
                c = c4 + j
                R.add("pe", lambda e, c=c, s=s, j=j, bk=bk: e.transpose(out=bk[:, j * 128:(j + 1) * 128],
                                                                   in_=srcT[:, c, s * 128:(s + 1) * 128],
                                                                   identity=ident[:]),
                      reads=[t_src, cx.t_const], writes=[tb])
            evac(cx, otok[:, s, c4 * 128:(c4 + 4) * 128], bk[:, :], [tb], [t_otok[s]])
        R.add("sp", lambda e, s=s: e.dma_start(out=out_dram[r0 + s * 128:r0 + (s + 1) * 128, :],
                                               in_=otok[:, s, 0:nch * 128]),
              reads=[t_otok[s]], dma_tok=t_otok[s])


class WStream:
    def __init__(self, cx, kch_max, nbuf, name):
        R = cx.R
        self.cx = cx
        self.bufs = [R.sb([128, kch_max, 512], BF16, f"{name}{i}") for i in range(nbuf)]
        self.toks = [R.tok(f"{name}{i}") for i in range(nbuf)]
        self.st_toks = [R.tok(f"{name}st{i}") for i in range(nbuf)]
        self.sw_toks = [R.tok(f"{name}sw{i}") for i in range(nbuf)]
        self.hw_toks = [R.tok(f"{name}hw{i}") for i in range(nbuf)]
        self.rr = 0
        self.scratch = {}
        self.regions = {}

    def load(self, w_dram, k0, kch, c0, ncols):
        R = self.cx.R
        nc = self.cx.nc
        i = self.rr % len(self.bufs)
        self.rr += 1
        buf, tk = self.bufs[i], self.toks[i]
        wname = w_dram.tensor.name
        if wname not in self.scratch:
            self.scratch[wname] = nc.dram_tensor("bf_" + wname, list(w_dram.shape), BF16).ap()
        scr = self.scratch[wname][k0 * 128:(k0 + kch) * 128, c0:c0 + ncols].rearrange("(k p) n -> p k n", p=128)
        key = (wname, k0, kch, c0, ncols)
        if key not in self.regions:
            t_reg = R.tok("reg")
            self.regions[key] = t_reg
            src = w_dram[k0 * 128:(k0 + kch) * 128, c0:c0 + ncols].rearrange("(k p) n -> p k n", p=128)
            R.add("pool", lambda e: e.dma_start(out=buf[:, 0:kch, 0:ncols], in_=src), writes=[tk], dma_tok=self.sw_toks[i])
            R.add("sp", lambda e: e.dma_start(out=scr, in_=buf[:, 0:kch, 0:ncols]), reads=[tk], writes=[t_reg],
                  dma_tok=self.st_toks[i])
        else:
            t_reg = self.regions[key]
            R.add("sp", lambda e: e.dma_start(out=buf[:, 0:kch, 0:ncols], in_=scr), reads=[t_reg], writes=[tk], dma_tok=self.hw_toks[i])
        return buf, tk


def ffn_fm(cx, uT, t_u, wg, wu, wd, actT, t_act, ws, fT, t_f, tmp, t_tmp):
    R = cx.R
    NJ = DFF // 128
    for j4 in range(0, NJ, 4):
        gw, tg = ws.load(wg, 0, KC, j4 * 128, 512)
        uw, tu = ws.load(wu, 0, KC, j4 * 128, 512)
        for j in range(4):
            bg, tbg = cx.bank()
            bu, tbu = cx.bank()
            for k in range(KC):
                R.add("pe", lambda e, k=k, j=j, bg=bg, gw=gw: e.matmul(bg[:, :], lhsT=gw[:, k, j * 128:(j + 1) * 128],
                                                                        rhs=uT[:, k, :], start=(k == 0), stop=(k == KC - 1)),
                      reads=[tg, t_u], writes=[tbg])
            for k in range(KC):
                R.add("pe", lambda e, k=k, j=j, bu=bu, uw=uw: e.matmul(bu[:, :], lhsT=uw[:, k, j * 128:(j + 1) * 128],
                                                                        rhs=uT[:, k, :], start=(k == 0), stop=(k == KC - 1)),
                      reads=[tu, t_u], writes=[tbu])
            jj = j4 + j
            tt = t_tmp[jj % 2]
            tm = tmp[jj % 2]
            R.add("act", lambda e, bg=bg, tm=tm: e.activation(out=tm[:], in_=bg[:, :], func=AF.Silu),
                  reads=[tbg], writes=[tt])
            R.add("dve", lambda e, bu=bu, tm=tm, jj=jj: e.tensor_tensor(out=actT[:, jj, :], in0=tm[:], in1=bu[:, :],
                                                                         op=ALU.mult),
                  reads=[tt, tbu], writes=[t_act])
    for dg in range(4):
        bks = [cx.bank() for _ in range(4)]
        for q in range(4):
            dwt, tdw = ws.load(wd, q * 11, 11, dg * 512, 512)
            for kk in range(11):
                k = q * 11 + kk
                for dd in range(4):
                    bk, tb = bks[dd]
                    R.add("pe", lambda e, kk=kk, k=k, dd=dd, bk=bk, dwt=dwt: e.matmul(
                        bk[:, :], lhsT=dwt[:, kk, dd * 128:(dd + 1) * 128], rhs=actT[:, k, :],
                        start=(k == 0), stop=(k == NJ - 1)), reads=[tdw, t_act], writes=[tb])
        for dd in range(4):
            bk, tb = bks[dd]
            evac(cx, fT[:, dg * 4 + dd, :], bk[:, :], [tb], [t_f])


def resid_norm_add(cx, fT, t_f, gain, t_gain, hT, t_h, sq, t_sq, rstd, t_rstd, coef):
    R = cx.R

    def post():
        for c in range(KC):
            R.add("dve", lambda e, c=c: e.scalar_tensor_tensor(out=fT[:, c, :], in0=fT[:, c, :],
                                                                scalar=gain[:, c:c + 1], in1=rstd[:],
                                                                op0=ALU.mult, op1=ALU.mult),
                  reads=[t_f, t_gain, t_rstd], writes=[t_f])
            R.add("dve", lambda e, c=c: e.scalar_tensor_tensor(out=hT[:, c, :], in0=fT[:, c, :],
                                                                scalar=float(coef), in1=hT[:, c, :],
                                                                op0=ALU.mult, op1=ALU.add),
                  reads=[t_f, t_h], writes=[t_h])

    rms_fm(cx, fT, t_f, KC, gain, t_gain, None, None, sq, t_sq, rstd, t_rstd, D, post=post)


def build_stage1(ntiles=TOK // T):
    nc = bass.Bass("TRN2", target_bir_lowering=False)
    x = nc.dram_tensor("x", [TOK, D], F32, kind="ExternalInput").ap()
    wg = nc.dram_tensor("wg", [D, DFF], F32, kind="ExternalInput").ap()
    wu = nc.dram_tensor("wu", [D, DFF], F32, kind="ExternalInput").ap()
    wd = nc.dram_tensor("wd", [DFF, D], F32, kind="ExternalInput").ap()
    gains = nc.dram_tensor("gains", [128, 3 * KC], F32, kind="ExternalInput").ap()
    ident_d = nc.dram_tensor("ident", [128, 128], F32, kind="ExternalInput").ap()
    h1T = nc.dram_tensor("h1T", [D, TOK], F32, kind="ExternalOutput").ap()
    uT_o = nc.dram_tensor("uT", [D, TOK], BF16, kind="ExternalOutput").ap()
    with contextlib.ExitStack() as stack:
        cx = Ctx(nc, stack)
        R = cx.R
        g_all, t_g = load_vec_fm(cx, gains, "gains_sb")
        ident = R.sb([128, 128], F32, "ident")
        R.add("sp", lambda e: e.dma_start(out=ident[:], in_=ident_d[:, :]), writes=[cx.t_const], dma_tok=cx.t_const)
        big = R.sb([128, KC * T], F32, "big")
        xtok = big[:].rearrange("p (s d) -> p s d", s=4)
        fT = big[:].rearrange("p (c t) -> p c t", c=KC)
        t_xtok = [R.tok(f"xtok{s}") for s in range(4)]
        t_f = R.tok("fT")
        hT = R.sb([128, KC, T], F32, "hT")
        t_h = R.tok("hT")
        uT = R.sb([128, KC, T], BF16, "uT")
        t_u = R.tok("uT")
        sq = R.sb([128, KC, T], BF16, "sq")
        t_sq = R.tok("sq")
        rstd = R.sb([128, T], F32, "rstd")
        t_rstd = R.tok("rstd")
        actT = R.sb([128, DFF // 128, T], BF16, "actT")
        t_act = R.tok("actT")
        tmp = [R.sb([128, T], F32, f"tmp{i}") for i in range(2)]
        t_tmp = [R.tok(f"tmp{i}") for i in range(2)]
        ws = WStream(cx, KC, 3, "w")
        h1v = h1T.rearrange("(c p) t -> p c t", p=128)
        uv = uT_o.rearrange("(c p) t -> p c t", p=128)
        for it in range(ntiles):
            r0 = it * T
            transpose_in(cx, x, r0, xtok, t_xtok, hT, t_h, ident, KC, alias=[t_f])
            rms_fm(cx, hT, t_h, KC, g_all[:, 0:KC], t_g, uT, t_u, sq, t_sq, rstd, t_rstd, D)
            ffn_fm(cx, uT, t_u, wg, wu, wd, actT, t_act, ws, fT, t_f, tmp, t_tmp)
            resid_norm_add(cx, fT, t_f, g_all[:, KC:2 * KC], t_g, hT, t_h, sq, t_sq, rstd, t_rstd, 0.5)
            R.add("sp", lambda e, r0=r0: e.dma_start(out=h1v[:, :, r0:r0 + T], in_=hT[:]), reads=[t_h], dma_tok=t_h)
            rms_fm(cx, hT, t_h, KC, g_all[:, 2 * KC:3 * KC], t_g, uT, t_u, sq, t_sq, rstd, t_rstd, D)
            R.add("sp", lambda e, r0=r0: e.dma_start(out=uv[:, :, r0:r0 + T], in_=uT[:]), reads=[t_u], dma_tok=t_u)
        R.emit()
    return nc


def fm_vec(v):
    return np.ascontiguousarray(np.asarray(v, np.float32).reshape(-1, 128).T)


HD = 128
CH = 64
NCHK = T // CH
C_ID, C_U, C_SL, C_BU, C_BL = 0, 128, 192, 256, 320


def dn_consts():
    c = np.zeros((128, 384), np.float32)
    c[:, 0:128] = np.eye(128, dtype=np.float32)
    p = np.arange(64)[:, None]
    f = np.arange(64)[None, :]
    c[0:64, C_U:C_U + 64] = (p <= f)
    c[0:64, C_SL:C_SL + 64] = (f < p)
    c[0:64, C_BU:C_BU + 64] = 1e4 * (f > p)
    c[0:64, C_BL:C_BL + 64] = 1e4 * (f < p)
    return c


def bc_mid(ap2, n):
    return ap2.unsqueeze(1).to_broadcast([ap2.shape[0], n, ap2.shape[1]])


def bc_last(ap2, n):
    return ap2.unsqueeze(2).to_broadcast([ap2.shape[0], ap2.shape[1], n])


def build_stage2_dn(ntiles=SEQ // T):
    S = ntiles * T
    nc = bass.Bass("TRN2", target_bir_lowering=False)
    uT_all = nc.dram_tensor("uT_all", [D, S], BF16, kind="ExternalInput").ap()
    wdn = nc.dram_tensor("wdn", [D, 1024], F32, kind="ExternalInput").ap()
    wba = nc.dram_tensor("wba", [D, 4], F32, kind="ExternalInput").ap()
    convw_d = nc.dram_tensor("convw", [128, 24], F32, kind="ExternalInput").ap()
    gon_d = nc.dram_tensor("gon", [128, 1], F32, kind="ExternalInput").ap()
    hv_d = nc.dram_tensor("hv", [64, 4], F32, kind="ExternalInput").ap()
    consts_d = nc.dram_tensor("consts", [128, 384], F32, kind="ExternalInput").ap()
    odnT = nc.dram_tensor("odnT", [256, S], BF16, kind="ExternalOutput").ap()
    with contextlib.ExitStack() as stack:
        cx = Ctx(nc, stack)
        R = cx.R
        A = R.add
        tc = cx.t_const
        consts = R.sb([128, 384], F32, "consts")
        A("sp", lambda e: e.dma_start(out=consts[:], in_=consts_d[:, :]), writes=[tc], dma_tok=tc)
        ident = consts[:, 0:128]
        id64 = consts[0:64, 0:64]
        Um = consts[0:64, C_U:C_U + 64]
        SLm = consts[0:64, C_SL:C_SL + 64]
        BUm = consts[0:64, C_BU:C_BU + 64]
        BLm = consts[0:64, C_BL:C_BL + 64]
        convw, t_cw = load_vec_fm(cx, convw_d, "convw_sb")
        gon, t_gon = load_vec_fm(cx, gon_d, "gon_sb")
        hv = R.sb([64, 4], F32, "hv")
        t_hv = R.tok("hv")
        A("sp", lambda e: e.dma_start(out=hv[:], in_=hv_d[:, :]), writes=[t_hv], dma_tok=t_hv)
        ident_bf = R.sb([128, 128], BF16, "ident_bf")
        A("dve", lambda e: e.tensor_copy(out=ident_bf[:], in_=consts[:, 0:128]), reads=[tc], writes=[tc])
        id64_bf = ident_bf[0:64, 0:64]
        ones_f = R.sb([64, 128], F32, "ones_f")
        A("pool", lambda e: e.memset(ones_f[:], 1.0), writes=[tc])
        expA = R.sb([64, 2], F32, "expA")
        A("act", lambda e: e.activation(out=expA[:], in_=hv[:, 2:4], func=AF.Exp), reads=[t_hv], writes=[t_hv])
        wres = R.sb([128, KC, 1024], BF16, "wres")
        t_w = R.tok("wres")
        wv = wdn.rearrange("(k p) n -> p k n", p=128)
        for half in range(2):
            A("pool", lambda e, half=half: e.dma_start(out=wres[:, :, half * 512:(half + 1) * 512],
                                                      in_=wv[:, :, half * 512:(half + 1) * 512]),
              writes=[t_w], dma_tok=t_w)
        wbar = R.sb([128, KC, 4], BF16, "wbar")
        t_wb = R.tok("wbar")
        A("pool", lambda e: e.dma_start(out=wbar[:], in_=wba.rearrange("(k p) n -> p k n", p=128)),
          writes=[t_wb], dma_tok=t_wb)
        u_t = [R.sb([128, KC, T], BF16, f"u_t{i}") for i in range(1)]
        t_ut = [R.tok(f"u_t{i}") for i in range(1)]
        rawbuf = R.sb([128, 6, T + 3], F32, "rawbuf")
        t_raw = [R.tok(f"raw{c}") for c in range(6)]
        A("pool", lambda e: e.memset(rawbuf[:, :, 0:3], 0.0), writes=t_raw)
        cv = R.sb([128, 6, T], F32, "cv")
        t_cv = [R.tok(f"cv{c}") for c in range(6)]
        qk = R.sb([128, 4, T], BF16, "qk")
        t_qk = [R.tok(f"qk{c}") for c in range(4)]
        siluz = R.sb([128, 2, T], F32, "siluz")
        t_sz = [R.tok(f"sz{c}") for c in range(2)]
        sqb = R.sb([128, T], BF16, "sqb")
        t_sqb = R.tok("sqb")
        rstd = R.sb([128, T], F32, "rstd")
        t_rstd = R.tok("rstd")
        ba = R.sb([64, NCHK, 4], F32, "ba")
        t_ba = R.tok("ba")
        beta = R.sb([64, NCHK, 2], F32, "beta")
        t_beta = R.tok("beta")
        gt = R.sb([64, NCHK, 2], F32, "gt")
        t_g = R.tok("gt")
        spx = R.sb([64, NCHK, 2], F32, "spx")
        t_spx = R.tok("spx")

        def sbt(shape, name, dt=F32):
            return R.sb(shape, dt, name), R.tok(name)

        def make_head(h):
            gc_col, t_gcc = sbt([64, NCHK], f"gc_col_h{h}")
            egc, t_egc = sbt([64, NCHK], f"egc_h{h}")
            be, t_be = sbt([64, NCHK], f"be_h{h}")
            kgs, t_kgs = sbt([64, NCHK], f"kgs_h{h}")
            Ug, t_Ug = sbt([64, NCHK, 64], f"Ug_h{h}")
            gcrow, t_gcr = sbt([128, NCHK, 64], f"gcrow_h{h}")
            egrow, t_egr = sbt([128, NCHK, 64], f"egrow_h{h}")
            dl, t_dl = sbt([64, NCHK, 64], f"dl_h{h}")
            decay, t_dec = sbt([64, NCHK, 64], f"decay_h{h}")
            decayT, t_decT = sbt([64, NCHK, 64], f"decayT_h{h}")
            Ktok, t_Kt = sbt([64, NCHK, 128], f"Ktok_h{h}", BF16)
            Vtok, t_Vt = sbt([64, NCHK, 128], f"Vtok_h{h}")
            Pb = [sbt([64, NCHK, 64], f"P{i}_h{h}", BF16) for i in range(2)]
            Ptb = [sbt([64, NCHK, 64], f"Pt{i}_h{h}", BF16) for i in range(2)]
            At, t_At = sbt([64, NCHK, 64], f"At_h{h}")
            u_sb, t_usb = sbt([64, NCHK, 128], f"u_sb_h{h}")
            wT, t_wT = sbt([128, NCHK, 64], f"wT_h{h}", BF16)
            attnT, t_att = sbt([64, NCHK, 64], f"attnT_h{h}", BF16)
            qgT, t_qg = sbt([128, NCHK, 64], f"qgT_h{h}", BF16)
            kg, t_kg = sbt([64, NCHK, 128], f"kg_h{h}", BF16)
            Sst = [sbt([128, 128], f"S{i}_h{h}") for i in range(2)]
            Ssb = [sbt([128, 128], f"Sb{i}_h{h}", BF16) for i in range(2)]
            A("pool", lambda e: e.memset(Sst[0][0][:], 0.0), writes=[Sst[0][1]])
            A("pool", lambda e: e.memset(Ssb[0][0][:], 0.0), writes=[Ssb[0][1]])
            s_par = [0]
            vnew = [sbt([64, 128], f"vnew{i}_h{h}", BF16) for i in range(2)]
            vn_rr = [0]
            tmpd, t_tmpd = Ug, t_Ug
            Vb, t_Vb = sbt([64, NCHK, 128], f"Vb_h{h}", BF16)
            Kbg, t_Kbg = sbt([64, NCHK, 128], f"Kbg_h{h}", BF16)
            Atb, t_Atb = sbt([64, NCHK, 64], f"Atb_h{h}", BF16)

            def gen(ob, t_ob):
                    gh = gt[:, :, h]
                    bh = beta[:, :, h]
                    qT = qk[:, h, :].rearrange("p (c f) -> p c f", f=CH)
                    kT = qk[:, 2 + h, :].rearrange("p (c f) -> p c f", f=CH)
                    t_q, t_k, t_v = t_qk[h], t_qk[2 + h], t_cv[4 + h]
                    bk, tb = bank_mm(64, [(0, NCHK, Um, gh)], [tc, t_g])
                    A("dve", lambda e, bk=bk: e.tensor_copy(out=gc_col[:], in_=bk[0:64, 0:NCHK]), reads=[tb], writes=[t_gcc])
                    A("dve", lambda e, gh=gh: e.tensor_tensor(out=Ug[:], in0=bc_mid(Um, NCHK), in1=bc_last(gh, 64), op=ALU.mult),
                      reads=[tc, t_g], writes=[t_Ug])
                    bk, tb = bank_mm(128, [(c * 64, 64, ones_f[:, :], Ug[:, c, :]) for c in range(NCHK)], [tc, t_Ug])
                    A("act", lambda e, bk=bk: e.copy(out=gcrow[:].rearrange("p c f -> p (c f)"), in_=bk[:, :]),
                      reads=[tb], writes=[t_gcr])
                    yield
                    A("dve", lambda e: e.tensor_tensor(out=dl[:], in0=gcrow[0:64, :, :], in1=bc_last(gc_col[:, :], 64),
                                                       op=ALU.subtract), reads=[t_gcr, t_gcc], writes=[t_dl])
                    A("dve", lambda e: e.tensor_tensor(out=tmpd[:], in0=dl[:], in1=bc_mid(BUm, NCHK), op=ALU.add),
                      reads=[t_dl, tc], writes=[t_tmpd])
                    A("act", lambda e: e.activation(out=decay[:], in_=tmpd[:], func=AF.Exp, scale=-1.0),
                      reads=[t_tmpd], writes=[t_dec])
                    A("dve", lambda e: e.tensor_tensor(out=tmpd[:], in0=dl[:], in1=bc_mid(BLm, NCHK), op=ALU.subtract),
                      reads=[t_dl, tc, t_dec], writes=[t_tmpd])
                    A("act", lambda e: e.activation(out=decayT[:], in_=tmpd[:], func=AF.Exp), reads=[t_tmpd], writes=[t_decT])
                    A("act", lambda e: e.activation(out=egrow[:], in_=gcrow[:], func=AF.Exp), reads=[t_gcr], writes=[t_egr])
                    A("act", lambda e: e.activation(out=kgs[:], in_=dl[:, :, 63], func=AF.Exp), reads=[t_dl], writes=[t_kgs])
                    A("act", lambda e: e.activation(out=egc[:], in_=gc_col[:], func=AF.Exp), reads=[t_gcc], writes=[t_egc])
                    yield
                    for (src, t_src, dst, t_dst, isbf) in ((kT, t_k, Ktok, t_Kt, True),
                                                           (cv[:, 4 + h, :].rearrange("p (c f) -> p c f", f=CH), t_v, Vtok, t_Vt, False)):
                        for half in range(2):
                            bk, tb = cx.bank()
                            bkv = bk[:, :].bitcast(BF16) if isbf else bk[:, :]
                            idm = ident_bf[:] if isbf else ident
                            for c4 in range(4):
                                c = half * 4 + c4
                                A("pe", lambda e, bkv=bkv, c4=c4, c=c, src=src, idm=idm: e.transpose(
                                    out=bkv[0:64, c4 * 128:(c4 + 1) * 128], in_=src[:, c, :], identity=idm),
                                  reads=[t_src, tc], writes=[tb])
                            evac(cx, dst[:, half * 4:half * 4 + 4, :].rearrange("p c f -> p (c f)"), bkv[0:64, 0:512], [tb], [t_dst])
                    bk, tb = bank_mm(64, [(c * 64, 64, kT[:, c, :], kT[:, c, :]) for c in range(NCHK)], [t_k])
                    P0, t_P0 = Pb[0]
                    Pt0, t_Pt0 = Ptb[0]
                    A("dve", lambda e, bk=bk: e.tensor_tensor(out=tmpd[:].rearrange("p c f -> p (c f)"), in0=bk[0:64, :],
                                                              in1=decay[:].rearrange("p c f -> p (c f)"), op=ALU.mult),
                      reads=[tb, t_dec], writes=[t_tmpd])
                    A("dve", lambda e, bh=bh: e.tensor_tensor(out=tmpd[:], in0=tmpd[:], in1=bc_last(bh, 64), op=ALU.mult),
                      reads=[t_tmpd, t_beta], writes=[t_tmpd])
                    A("dve", lambda e, P0=P0: e.tensor_tensor(out=P0[:], in0=tmpd[:], in1=bc_mid(SLm, NCHK), op=ALU.mult),
                      reads=[t_tmpd, tc], writes=[t_P0])
                    yield
                    bk, tb = cx.bank()
                    bkv = bk[:, :].bitcast(BF16)
                    for c in range(NCHK):
                        A("pe", lambda e, bkv=bkv, c=c, P0=P0: e.transpose(out=bkv[0:64, c * 64:(c + 1) * 64], in_=P0[:, c, :],
                                                                         identity=id64_bf), reads=[t_P0, tc], writes=[tb])
                    A("act", lambda e, bkv=bkv, Pt0=Pt0: e.copy(out=Pt0[:].rearrange("p c f -> p (c f)"), in_=bkv[0:64, 0:512]),
                      reads=[tb], writes=[t_Pt0])
                    A("dve", lambda e, Pt0=Pt0: e.tensor_tensor(out=At[:], in0=bc_mid(id64, NCHK), in1=Pt0[:], op=ALU.subtract),
                      reads=[t_Pt0, tc], writes=[t_At])
                    A("act", lambda e: e.copy(out=Atb[:], in_=At[:]), reads=[t_At], writes=[t_Atb])
                    yield
                    for lv in range(5):
                        Pc, t_Pc = Pb[lv % 2]
                        Ptc, t_Ptc = Ptb[lv % 2]
                        Pn, t_Pn = Pb[(lv + 1) % 2]
                        Ptn, t_Ptn = Ptb[(lv + 1) % 2]
                        bk, tb = bank_mm(64, [(c * 64, 64, Ptc[:, c, :], Pc[:, c, :]) for c in range(NCHK)], [t_Pc, t_Ptc])
                        A("dve", lambda e, bk=bk, Pn=Pn: e.tensor_copy(out=Pn[:].rearrange("p c f -> p (c f)"), in_=bk[0:64, :]),
                          reads=[tb], writes=[t_Pn])
                        bk, tb = bank_mm(64, [(c * 64, 64, Pc[:, c, :], Ptc[:, c, :]) for c in range(NCHK)], [t_Pc, t_Ptc])
                        A("act", lambda e, bk=bk, Ptn=Ptn: e.copy(out=Ptn[:].rearrange("p c f -> p (c f)"), in_=bk[0:64, :]),
                          reads=[tb], writes=[t_Ptn])
                        bk, tb = bank_mm(64, [(c * 64, 64, Pn[:, c, :], Atb[:, c, :]) for c in range(NCHK)], [t_Pn, t_Atb])
                        A("dve", lambda e, bk=bk: e.tensor_tensor(out=At[:].rearrange("p c f -> p (c f)"),
                                                                  in0=At[:].rearrange("p c f -> p (c f)"), in1=bk[0:64, :],
                                                                  op=ALU.add), reads=[tb, t_At], writes=[t_At])
                        A("act", lambda e: e.copy(out=Atb[:], in_=At[:]), reads=[t_At], writes=[t_Atb])
                        yield
                    A("dve", lambda e: e.tensor_tensor(out=kg[:], in0=Ktok[:], in1=bc_last(kgs[:, :], 128), op=ALU.mult),
                      reads=[t_Kt, t_kgs], writes=[t_kg])
                    A("dve", lambda e, bh=bh: e.tensor_tensor(out=Vb[:], in0=Vtok[:], in1=bc_last(bh, 128), op=ALU.mult),
                      reads=[t_Vt, t_beta], writes=[t_Vb])
                    A("dve", lambda e, bh=bh: e.tensor_tensor(out=be[:], in0=egc[:], in1=bh, op=ALU.mult),
                      reads=[t_egc, t_beta], writes=[t_be])
                    A("dve", lambda e: e.tensor_tensor(out=Kbg[:], in0=Ktok[:], in1=bc_last(be[:, :], 128), op=ALU.mult),
                      reads=[t_Kt, t_be], writes=[t_Kbg])
                    for half in range(2):
                        bk, tb = bank_mm(64, [(c4 * 128, 128, Atb[:, half * 4 + c4, :], Vb[:, half * 4 + c4, :]) for c4 in range(4)],
                                         [t_Atb, t_Vb])
                        evac(cx, u_sb[:, half * 4:half * 4 + 4, :].rearrange("p c f -> p (c f)"), bk[0:64, :], [tb], [t_usb])
                    bk, tb = bank_mm(128, [(c * 64, 64, Kbg[:, c, :], Atb[:, c, :]) for c in range(NCHK)], [t_Atb, t_Kbg])
                    evac(cx, wT[:].rearrange("p c f -> p (c f)"), bk[:, :], [tb], [t_wT])
                    yield
                    bk, tb = bank_mm(64, [(c * 64, 64, kT[:, c, :], qT[:, c, :]) for c in range(NCHK)], [t_k, t_q])
                    A("dve", lambda e, bk=bk: e.tensor_tensor(out=attnT[:].rearrange("p c f -> p (c f)"), in0=bk[0:64, :],
                                                              in1=decayT[:].rearrange("p c f -> p (c f)"), op=ALU.mult),
                      reads=[tb, t_decT], writes=[t_att])
                    A("dve", lambda e, qT=qT: e.tensor_tensor(out=qgT[:], in0=qT, in1=egrow[:], op=ALU.mult),
                      reads=[t_q, t_egr], writes=[t_qg])
                    yield
                    obk, t_obk = cx.bank()
                    cx.reserved.add(id(obk))
                    for c in range(NCHK):
                        Sc, t_Sc = Sst[s_par[0]]
                        Sn, t_Sn = Sst[1 - s_par[0]]
                        Sbc, t_Sbc = Ssb[s_par[0]]
                        Sbn, t_Sbn = Ssb[1 - s_par[0]]
                        s_par[0] = 1 - s_par[0]
                        vn, t_vn = vnew[vn_rr[0] % 2]
                        vn_rr[0] += 1
                        bk, tb = bank_mm(64, [(0, 128, wT[:, c, :], Sbc[:, :])], [t_wT, t_Sbc])
                        A("dve", lambda e, bk=bk, c=c, vn=vn: e.tensor_tensor(out=vn[:], in0=u_sb[:, c, :], in1=bk[0:64, 0:128],
                                                                             op=ALU.subtract), reads=[tb, t_usb], writes=[t_vn])
                        A("pe", lambda e, c=c, Sbc=Sbc, obk=obk: e.matmul(obk[:, c * 64:(c + 1) * 64], lhsT=Sbc[:, :], rhs=qgT[:, c, :],
                                                                         start=True, stop=False), reads=[t_Sbc, t_qg], writes=[t_obk])
                        A("pe", lambda e, c=c, vn=vn, obk=obk: e.matmul(obk[:, c * 64:(c + 1) * 64], lhsT=vn[:, :], rhs=attnT[:, c, :],
                                                                       start=False, stop=True), reads=[t_vn, t_att], writes=[t_obk])
                        bk, tb = bank_mm(128, [(0, 128, kg[:, c, :], vn[:, :])], [t_kg, t_vn])
                        A("dve", lambda e, bk=bk, c=c, Sc=Sc, Sn=Sn: e.scalar_tensor_tensor(
                            out=Sn[:], in0=Sc[:], scalar=egrow[:, c, 63:64], in1=bk[:, 0:128], op0=ALU.mult, op1=ALU.add),
                          reads=[tb, t_Sc, t_egr], writes=[t_Sn])
                        A("act", lambda e, Sn=Sn, Sbn=Sbn: e.copy(out=Sbn[:], in_=Sn[:]), reads=[t_Sn], writes=[t_Sbn])
                        yield
                    A("act", lambda e, obk=obk: e.copy(out=o_sb[:], in_=obk[:, :]), reads=[t_obk], writes=[t_osb])
                    cx.reserved.discard(id(obk))
                    A("act", lambda e: e.activation(out=sqb[:], in_=o_sb[:], func=AF.Square), reads=[t_osb], writes=[t_sqb])
                    bk, tb = cx.bank()
                    A("pe", lambda e, bk=bk: e.matmul(bk[:, :], lhsT=cx.ones_bf[:], rhs=sqb[:], start=True, stop=True),
                      reads=[t_sqb, tc], writes=[tb])
                    A("act", lambda e, bk=bk: e.activation(out=rstd[:], in_=bk[:, :], func=AF.Ln, bias=EPS, scale=1.0 / HD),
                      reads=[tb], writes=[t_rstd])
                    A("act", lambda e: e.activation(out=rstd[:], in_=rstd[:], func=AF.Exp, scale=-0.5), reads=[t_rstd], writes=[t_rstd])
                    A("dve", lambda e: e.scalar_tensor_tensor(out=o_tmp[:], in0=o_sb[:], scalar=gon[:, 0:1], in1=rstd[:],
                                                              op0=ALU.mult, op1=ALU.mult),
                      reads=[t_osb, t_gon, t_rstd], writes=[t_otmp])
                    A("dve", lambda e, h=h, ob=ob: e.tensor_tensor(out=ob[:, h, :], in0=o_tmp[:], in1=siluz[:, h, :], op=ALU.mult),
                      reads=[t_otmp, t_sz[h]], writes=[t_ob])

            return gen

        o_sb, t_osb = sbt([128, T], "o_sb")
        o_tmp, t_otmp = sbt([128, T], "o_tmp")
        outb = [sbt([128, 2, T], f"outb{i}", BF16) for i in range(2)]
        ov = odnT.rearrange("(h p) t -> p h t", p=128)
        head_fns = [make_head(0), make_head(1)]

        def bank_mm(nparts, items, reads, name=None):
            bk, tb = cx.bank()
            for (c0, ncol, lh, rh) in items:
                A("pe", lambda e, bk=bk, c0=c0, ncol=ncol, lh=lh, rh=rh: e.matmul(
                    bk[0:nparts, c0:c0 + ncol], lhsT=lh, rhs=rh, start=True, stop=True), reads=reads, writes=[tb])
            return bk, tb

        for it in range(ntiles):
            t0 = it * T
            ut, t_u = u_t[0], t_ut[0]
            A("sp", lambda e, ut=ut, t0=t0: e.dma_start(out=ut[:], in_=uT_all[:, t0:t0 + T].rearrange("(k p) t -> p k t", p=128)),
              writes=[t_u], dma_tok=t_u)
            for c in range(8):
                bk, tb = cx.bank()
                for k in range(KC):
                    A("pe", lambda e, bk=bk, c=c, k=k, ut=ut: e.matmul(bk[:, :], lhsT=wres[:, k, c * 128:(c + 1) * 128],
                                                                      rhs=ut[:, k, :], start=(k == 0), stop=(k == KC - 1)),
                      reads=[t_w, t_u], writes=[tb])
                if c < 6:
                    evac(cx, rawbuf[:, c, 3:T + 3], bk[:, :], [tb], [t_raw[c]])
                else:
                    A("act", lambda e, bk=bk, c=c: e.activation(out=siluz[:, c - 6, :], in_=bk[:, :], func=AF.Silu),
                      reads=[tb], writes=[t_sz[c - 6]])
            for c in range(6):
                A("dve", lambda e, c=c: e.tensor_scalar(out=cv[:, c, :], in0=rawbuf[:, c, 0:T],
                                                        scalar1=convw[:, c * 4:c * 4 + 1], scalar2=None, op0=ALU.mult),
                  reads=[t_raw[c], t_cw], writes=[t_cv[c]])
                for j in range(1, 4):
                    A("dve", lambda e, c=c, j=j: e.scalar_tensor_tensor(out=cv[:, c, :], in0=rawbuf[:, c, j:j + T],
                                                                        scalar=convw[:, c * 4 + j:c * 4 + j + 1],
                                                                        in1=cv[:, c, :], op0=ALU.mult, op1=ALU.add),
                      reads=[t_raw[c], t_cw, t_cv[c]], writes=[t_cv[c]])
                A("act", lambda e, c=c: e.copy(out=rawbuf[:, c, 0:3], in_=rawbuf[:, c, T:T + 3]),
                  reads=[t_raw[c]], writes=[t_raw[c]])
                A("act", lambda e, c=c: e.activation(out=cv[:, c, :], in_=cv[:, c, :], func=AF.Silu),
                  reads=[t_cv[c]], writes=[t_cv[c]])
            for idx in range(4):
                A("act", lambda e, idx=idx: e.activation(out=sqb[:], in_=cv[:, idx, :], func=AF.Square),
                  reads=[t_cv[idx]], writes=[t_sqb])
                bk, tb = cx.bank()
                A("pe", lambda e, bk=bk: e.matmul(bk[:, :], lhsT=cx.ones_bf[:], rhs=sqb[:], start=True, stop=True),
                  reads=[t_sqb, tc], writes=[tb])
                A("act", lambda e, bk=bk: e.activation(out=rstd[:], in_=bk[:, :], func=AF.Ln, bias=1e-6, scale=1.0),
                  reads=[tb], writes=[t_rstd])
                A("act", lambda e: e.activation(out=rstd[:], in_=rstd[:], func=AF.Exp, scale=-0.5), reads=[t_rstd], writes=[t_rstd])
                sc = float(HD ** -0.5) if idx < 2 else 1.0
                A("dve", lambda e, idx=idx, sc=sc: e.scalar_tensor_tensor(out=qk[:, idx, :], in0=cv[:, idx, :], scalar=sc,
                                                                          in1=rstd[:], op0=ALU.mult, op1=ALU.mult),
                  reads=[t_cv[idx], t_rstd], writes=[t_qk[idx]])
            bk, tb = cx.bank()
            for ck in range(NCHK):
                for k in range(KC):
                    A("pe", lambda e, bk=bk, ck=ck, k=k, ut=ut: e.matmul(bk[0:64, ck * 4:ck * 4 + 4],
                                                                        lhsT=ut[:, k, ck * CH:(ck + 1) * CH],
                                                                        rhs=wbar[:, k, :], start=(k == 0), stop=(k == KC - 1)),
                      reads=[t_wb, t_u], writes=[tb])
            A("dve", lambda e, bk=bk: e.tensor_copy(out=ba[:].rearrange("p c f -> p (c f)"), in_=bk[0:64, 0:NCHK * 4]),
              reads=[tb], writes=[t_ba])
            A("act", lambda e: e.activation(out=beta[:], in_=ba[:, :, 0:2], func=AF.Sigmoid), reads=[t_ba], writes=[t_beta])
            A("dve", lambda e: e.tensor_tensor(out=spx[:], in0=ba[:, :, 2:4], in1=bc_mid(hv[:, 0:2], NCHK), op=ALU.add),
              reads=[t_ba, t_hv], writes=[t_spx])
            A("act", lambda e: e.activation(out=spx[:], in_=spx[:], func=AF.Exp), reads=[t_spx], writes=[t_spx])
            A("dve", lambda e: e.tensor_scalar(out=spx[:], in0=spx[:], scalar1=1.0, scalar2=None, op0=ALU.add),
              reads=[t_spx], writes=[t_spx])
            A("act", lambda e: e.activation(out=spx[:], in_=spx[:], func=AF.Ln), reads=[t_spx], writes=[t_spx])
            A("dve", lambda e: e.scalar_tensor_tensor(out=gt[:], in0=spx[:], scalar=-1.0, in1=bc_mid(expA[:, 0:2], NCHK),
                                                      op0=ALU.mult, op1=ALU.mult), reads=[t_spx, t_hv], writes=[t_g])
            ob, t_ob = outb[it % 2]
            gens = [head_fns[h](ob, t_ob) for h in range(2)]
            while gens:
                for g in list(gens):
                    try:
                        next(g)
                    except StopIteration:
                        gens.remove(g)
            A("sp", lambda e, ob=ob, t0=t0: e.dma_start(out=ov[:, :, t0:t0 + T], in_=ob[:]), reads=[t_ob], dma_tok=t_ob)
        R.emit()
    return nc


SB_ONE = 128
SB_NM0 = 129
SB_VM0 = 129 + 4 * 512
SB_TOT = 129 + 8 * 512


def sb_consts():
    c = np.zeros((128, SB_TOT), np.float32)
    c[:, 0:128] = np.eye(128, dtype=np.float32)
    t = np.arange(128)[:, None]
    j = np.arange(128)[None, :]
    tri = (t + j < 128).astype(np.float32)
    for q4 in range(4):
        for b in range(4):
            blk = c[:, SB_NM0 + q4 * 512 + b * 128: SB_NM0 + q4 * 512 + (b + 1) * 128]
            if 3 - b > q4:
                blk[:] = 1.0
            elif 3 - b == q4:
                blk[:] = tri
    c[:, SB_VM0:SB_VM0 + 4 * 512] = 1.0 - c[:, SB_NM0:SB_NM0 + 4 * 512]
    c[:, SB_ONE] = 1.0
    return c


def build_stage2_sb(ntiles=SEQ // T, heads=(0, 1)):
    S = ntiles * T
    NB = S // 128
    nc = bass.Bass("TRN2", target_bir_lowering=False)
    uT_all = nc.dram_tensor("uT_all", [D, S], BF16, kind="ExternalInput").ap()
    uT_rev = nc.dram_tensor("uT_rev", [D, S], BF16, kind="ExternalInput").ap()
    wsb = nc.dram_tensor("wsb", [D, 768], F32, kind="ExternalInput").ap()
    consts_d = nc.dram_tensor("consts", [128, SB_TOT], F32, kind="ExternalInput").ap()
    osbT = nc.dram_tensor("osbT", [256, S], BF16, kind="ExternalOutput").ap()
    with contextlib.ExitStack() as stack:
        cx = Ctx(nc, stack)
        R = cx.R
        A = R.add
        tc = cx.t_const
        consts = R.sb([128, 129], F32, "consts")
        A("sp", lambda e: e.dma_start(out=consts[:], in_=consts_d[:, 0:129]), writes=[tc], dma_tok=tc)
        ones_c = consts[:, 128:129]
        masks = R.sb([128, 8 * 512], BF16, "masks")
        t_mk = R.tok("masks")
        A("pool", lambda e: e.dma_start(out=masks[:], in_=consts_d[:, SB_NM0:SB_TOT]), writes=[t_mk], dma_tok=t_mk)
        ident_bf = R.sb([128, 128], BF16, "ident_bf")
        A("dve", lambda e: e.tensor_copy(out=ident_bf[:], in_=consts[:, 0:128]), reads=[tc], writes=[tc])
        ones_f = ones_c.to_broadcast([128, T])
        wres = R.sb([128, KC, 768], BF16, "wres")
        t_w = R.tok("wres")
        wv = wsb.rearrange("(k p) n -> p k n", p=128)
        for half in range(2):
            A("pool", lambda e, half=half: e.dma_start(out=wres[:, :, half * 384:(half + 1) * 384],
                                                      in_=wv[:, :, half * 384:(half + 1) * 384]),
              writes=[t_w], dma_tok=t_w)
        uf, tuf = R.sb([128, KC, T], BF16, "u_f"), R.tok("u_f")
        ur, tur = R.sb([128, KC, T], BF16, "u_r"), R.tok("u_r")
        QT = R.sb([128, S], BF16, "QT")
        KTr = R.sb([128, S], BF16, "KTr")
        dVr = R.sb([128, NB, 128], BF16, "dVr")
        t_Q = [R.tok(f"Q{i}") for i in range(ntiles)]
        t_K = [R.tok(f"K{i}") for i in range(ntiles)]
        t_V = [R.tok(f"V{i}") for i in range(ntiles)]
        VTb = [R.sb([128, T + 1], F32, f"VTb{i}") for i in range(2)]
        t_VT = [R.tok(f"VTb{i}") for i in range(2)]
        dVT = R.sb([128, T], BF16, "dVT")
        t_dVT = R.tok("dVT")
        smb = [[R.sb([128, T], F32, f"smb{i}_{k}") for k in range(2)] for i in range(4)]
        t_sm = [[R.tok(f"smb{i}_{k}") for k in range(2)] for i in range(4)]
        Pb = [[R.sb([128, T], BF16, f"Pb{i}_{k}") for k in range(2)] for i in range(4)]
        t_P = [[R.tok(f"Pb{i}_{k}") for k in range(2)] for i in range(4)]
        Pm = [R.sb([128, T], BF16, f"Pm{i}") for i in range(4)]
        t_Pm = [R.tok(f"Pm{i}") for i in range(4)]
        vsh, t_vsh = R.sb([128, T], F32, "vsh"), R.tok("vsh")
        ATs = R.sb([128, 4, T], BF16, "ATs")
        t_AT = [R.tok(f"ATs_{hh}") for hh in range(2)]
        osb, t_os = R.sb([128, T], BF16, "osb"), R.tok("osb")
        scale = float(HD ** -0.5)
        uva = uT_all.rearrange("(k p) t -> p k t", p=128)
        uvr = uT_rev.rearrange("(k p) t -> p k t", p=128)
        gstep = 0
        for h in heads:
            for n, it in enumerate(reversed(range(ntiles))):
                t0 = it * T
                A("sp", lambda e, t0=t0: e.dma_start(out=uf[:], in_=uva[:, :, t0:t0 + T]), writes=[tuf], dma_tok=tuf)
                A("sp", lambda e, t0=t0: e.dma_start(out=ur[:], in_=uvr[:, :, t0:t0 + T]), writes=[tur], dma_tok=tur)
                for (src, tsrc, col, dst, tdst) in ((uf, tuf, h * 128, QT, t_Q[it]), (ur, tur, 256 + h * 128, KTr, t_K[it])):
                    bk, tb = cx.bank()
                    for k in range(KC):
                        A("pe", lambda e, bk=bk, k=k, src=src, col=col: e.matmul(bk[:, :], lhsT=wres[:, k, col:col + 128],
                                                                                rhs=src[:, k, :], start=(k == 0), stop=(k == KC - 1)),
                          reads=[t_w, tsrc], writes=[tb])
                    evac(cx, dst[:, t0:t0 + T], bk[:, :], [tb], [tdst])
                vt, tvt = VTb[n % 2], t_VT[n % 2]
                vtp, tvtp = VTb[(n + 1) % 2], t_VT[(n + 1) % 2]
                bk, tb = cx.bank()
                for k in range(KC):
                    A("pe", lambda e, bk=bk, k=k, h=h: e.matmul(bk[:, :], lhsT=wres[:, k, 512 + h * 128:512 + (h + 1) * 128],
                                                               rhs=ur[:, k, :], start=(k == 0), stop=(k == KC - 1)),
                      reads=[t_w, tur], writes=[tb])
                A("act", lambda e, bk=bk, vt=vt: e.copy(out=vt[:, 0:T], in_=bk[:, :]), reads=[tb], writes=[tvt])
                if n == 0:
                    A("pool", lambda e, vt=vt: e.memset(vt[:, T:T + 1], 0.0), reads=[tvt], writes=[tvt])
                else:
                    A("act", lambda e, vt=vt, vtp=vtp: e.copy(out=vt[:, T:T + 1], in_=vtp[:, 0:1]), reads=[tvtp, tvt], writes=[tvt])
                A("pool", lambda e, vt=vt: e.tensor_tensor(out=dVT[:], in0=vt[:, 1:T + 1], in1=vt[:, 0:T], op=ALU.subtract),
                  reads=[tvt], writes=[t_dVT])
                bk2, tb2 = cx.bank()
                bkb = bk2[:, :].bitcast(BF16)
                for b in range(4):
                    A("pe", lambda e, bkb=bkb, b=b: e.transpose(out=bkb[:, b * 128:(b + 1) * 128], in_=dVT[:, b * 128:(b + 1) * 128],
                                                               identity=ident_bf[:]), reads=[t_dVT, tc], writes=[tb2])
                evac(cx, dVr[:, it * 4:it * 4 + 4, :].rearrange("p b d -> p (b d)"), bkb[:, 0:T], [tb2], [t_V[it]])
            steps = [(qg, i) for qg in range(NB // 4) for i in range(qg + 1)]

            def issue_scores(qg, i, par):
                j0 = (NB - 4 - 4 * qg + 4 * i) * 128
                kt = j0 // T
                zb = []
                for q4 in range(4):
                    qb = qg * 4 + q4
                    bk, tb = cx.bank()
                    zb.append((bk, tb))
                    A("pe", lambda e, bk=bk, qb=qb, j0=j0: e.matmul(bk[:, :], lhsT=QT[:, qb * 128:(qb + 1) * 128],
                                                                   rhs=KTr[:, j0:j0 + T], start=True, stop=True),
                      reads=[t_Q[qg], t_K[kt]], writes=[tb])
                for q4 in range(4):
                    bk, tb = zb[q4]
                    sm, tsm = smb[q4][par], t_sm[q4][par]
                    A("act", lambda e, bk=bk, sm=sm: e.activation(out=sm[:], in_=bk[:, :], func=AF.Sigmoid, scale=-scale),
                      reads=[tb], writes=[tsm])
                    if i == 0:
                        nm = masks[:, q4 * 512:(q4 + 1) * 512]
                        A("dve", lambda e, sm=sm, nm=nm: e.tensor_tensor(out=sm[:], in0=sm[:], in1=nm, op=ALU.max),
                          reads=[tsm, t_mk], writes=[tsm])

            issue_scores(steps[0][0], steps[0][1], gstep % 2)
            obk = t_obk = None
            for n, (qg, i) in enumerate(steps):
                par = gstep % 2
                gstep += 1
                if n + 1 < len(steps):
                    issue_scores(steps[n + 1][0], steps[n + 1][1], gstep % 2)
                ntile = qg + 1
                JB = NB - 4 - 4 * qg + 4 * i
                kt = JB // 4
                if i == 0:
                    obk, t_obk = cx.bank()
                    cx.reserved.add(id(obk))
                    tq0 = qg * T
                    if tq0 == 0:
                        A("pool", lambda e: e.memset(uf[:, :, 0:1], 0.0), writes=[tuf])
                        A("sp", lambda e: e.dma_start(out=uf[:, :, 1:T], in_=uva[:, :, 0:T - 1]), writes=[tuf], dma_tok=tuf)
                    else:
                        A("sp", lambda e, tq0=tq0: e.dma_start(out=uf[:], in_=uva[:, :, tq0 - 1:tq0 + T - 1]), writes=[tuf], dma_tok=tuf)
                for q4 in range(4):
                    sm, tsm = smb[q4][par], t_sm[q4][par]
                    Pc, tPc = Pb[q4][par], t_P[q4][par]
                    Pp, tPp = Pb[q4][1 - par], t_P[q4][1 - par]
                    if i == 0:
                        A("dve", lambda e, sm=sm, Pc=Pc: e.tensor_tensor_scan(out=Pc[:], data0=sm[:], data1=ones_f, initial=ones_c,
                                                                             op0=ALU.mult, op1=ALU.mult),
                          reads=[tsm, tc], writes=[tPc])
                    else:
                        A("dve", lambda e, sm=sm, Pc=Pc, Pp=Pp: e.tensor_tensor_scan(out=Pc[:], data0=sm[:], data1=ones_f,
                                                                                    initial=Pp[:, T - 1:T], op0=ALU.mult, op1=ALU.mult),
                          reads=[tsm, tPp, tc], writes=[tPc])
                if i == 0:
                    for q4 in range(4):
                        vm = masks[:, (4 + q4) * 512:(5 + q4) * 512]
                        A("pool", lambda e, q4=q4, vm=vm, par=par: e.tensor_tensor(out=Pm[q4][:], in0=Pb[q4][par][:], in1=vm, op=ALU.mult),
                          reads=[t_P[q4][par], t_mk], writes=[t_Pm[q4]])
                for hh in range(2):
                    bk2, tb2 = cx.bank()
                    bkb = bk2[:, :].bitcast(BF16)
                    for bb in range(2):
                        b = hh * 2 + bb
                        for q4 in range(4):
                            Pc, tPc = (Pm[q4], t_Pm[q4]) if i == 0 else (Pb[q4][par], t_P[q4][par])
                            A("pe", lambda e, bkb=bkb, b=b, bb=bb, q4=q4, Pc=Pc: e.transpose(
                                out=bkb[:, bb * 512 + q4 * 128:bb * 512 + (q4 + 1) * 128], in_=Pc[:, b * 128:(b + 1) * 128],
                                identity=ident_bf[:]), reads=[tPc, tc], writes=[tb2])
                    A("act", lambda e, bkb=bkb, hh=hh: e.copy(out=ATs[:, hh * 2:hh * 2 + 2, :].rearrange("p b q -> p (b q)"),
                                                             in_=bkb[:, :]), reads=[tb2], writes=[t_AT[hh]])
                for b in range(4):
                    first = (i == 0 and b == 0)
                    last = (i == ntile - 1 and b == 3)
                    A("pe", lambda e, obk=obk, b=b, JB=JB, first=first, last=last: e.matmul(
                        obk[:, :], lhsT=dVr[:, JB + b, :], rhs=ATs[:, b, :], start=first, stop=last),
                      reads=[t_AT[b // 2], t_V[kt]], writes=[t_obk])
                if i == ntile - 1:
                    bkv, tbv = cx.bank()
                    for k in range(KC):
                        A("pe", lambda e, bkv=bkv, k=k, h=h: e.matmul(bkv[:, :], lhsT=wres[:, k, 512 + h * 128:512 + (h + 1) * 128],
                                                                     rhs=uf[:, k, :], start=(k == 0), stop=(k == KC - 1)),
                          reads=[t_w, tuf], writes=[tbv])
                    A("act", lambda e, bkv=bkv: e.copy(out=vsh[:], in_=bkv[:, :]), reads=[tbv], writes=[t_vsh])
                    A("dve", lambda e, obk=obk: e.tensor_tensor(out=osb[:], in0=obk[:, :], in1=vsh[:], op=ALU.add),
                      reads=[t_obk, t_vsh], writes=[t_os])
                    cx.reserved.discard(id(obk))
                    A("sp", lambda e, qg=qg, h=h: e.dma_start(out=osbT[h * 128:(h + 1) * 128, qg * T:(qg + 1) * T], in_=osb[:]),
                      reads=[t_os], dma_tok=t_os)
        R.emit()
    return nc


def linear_fm(cx, ws, w_dram, kch, xT, t_x, col0, nout, consume):
    R = cx.R
    for c4 in range(0, nout, 4):
        n = min(4, nout - c4)
        wt, tw = ws.load(w_dram, 0, kch, col0 + c4 * 128, n * 128)
        for j in range(n):
            bk, tb = cx.bank()
            for k in range(kch):
                R.add("pe", lambda e, bk=bk, k=k, j=j, wt=wt: e.matmul(bk[:, :], lhsT=wt[:, k, j * 128:(j + 1) * 128],
                                                                      rhs=xT[:, k, :], start=(k == 0), stop=(k == kch - 1)),
                      reads=[tw, t_x], writes=[tb])
            consume(c4 + j, bk, tb)


def build_stage3(ntiles=TOK // T):
    nc = bass.Bass("TRN2", target_bir_lowering=False)
    NTK = ntiles * T
    h1T = nc.dram_tensor("h1T", [D, NTK], F32, kind="ExternalInput").ap()
    uT_d = nc.dram_tensor("uT", [D, NTK], BF16, kind="ExternalInput").ap()
    odn_d = nc.dram_tensor("odnT", [D, NTK], BF16, kind="ExternalInput").ap()
    osb_d = nc.dram_tensor("osbT", [D, NTK], BF16, kind="ExternalInput").ap()
    pT_d = nc.dram_tensor("pT", [PLE, NTK], F32, kind="ExternalInput").ap()
    wgate = nc.dram_tensor("wgate", [D, 2 * D], F32, kind="ExternalInput").ap()
    wbd = nc.dram_tensor("wbd", [D, D], F32, kind="ExternalInput").ap()
    wbs = nc.dram_tensor("wbs", [D, D], F32, kind="ExternalInput").ap()
    wout = nc.dram_tensor("wout", [D, D], F32, kind="ExternalInput").ap()
    wg = nc.dram_tensor("wg", [D, DFF], F32, kind="ExternalInput").ap()
    wu = nc.dram_tensor("wu", [D, DFF], F32, kind="ExternalInput").ap()
    wd = nc.dram_tensor("wd", [DFF, D], F32, kind="ExternalInput").ap()
    wpg = nc.dram_tensor("wpg", [D, D], F32, kind="ExternalInput").ap()
    wpp = nc.dram_tensor("wpp", [PLE, D], F32, kind="ExternalInput").ap()
    gains = nc.dram_tensor("gains", [128, 5 * KC], F32, kind="ExternalInput").ap()
    ident_d = nc.dram_tensor("ident", [128, 128], F32, kind="ExternalInput").ap()
    out = nc.dram_tensor("out", [NTK, D], F32, kind="ExternalOutput").ap()
    with contextlib.ExitStack() as stack:
        cx = Ctx(nc, stack)
        R = cx.R
        A = R.add
        g_all, t_g = load_vec_fm(cx, gains, "gains_sb")
        ident = R.sb([128, 128], F32, "ident")
        A("sp", lambda e: e.dma_start(out=ident[:], in_=ident_d[:, :]), writes=[cx.t_const], dma_tok=cx.t_const)
        big = R.sb([128, KC * T], F32, "big")
        otok = big[:].rearrange("p (s d) -> p s d", s=4)
        fT = big[:].rearrange("p (c t) -> p c t", c=KC)
        t_f = R.tok("fT")
        t_otok = [t_f] * 4
        hT = R.sb([128, KC, T], F32, "hT")
        t_h = R.tok("hT")
        uT = R.sb([128, KC, T], BF16, "uT")
        t_u = R.tok("uT")
        sq = R.sb([128, KC, T], BF16, "sq")
        t_sq = R.tok("sq")
        rstd = R.sb([128, T], F32, "rstd")
        t_rstd = R.tok("rstd")
        actT = R.sb([128, DFF // 128, T], BF16, "actT")
        t_act = R.tok("actT")
        odn = actT[:, 0:KC, :]
        osb = actT[:, KC:2 * KC, :]
        tmp = [R.sb([128, T], F32, f"tmp{i}") for i in range(2)]
        t_tmp = [R.tok(f"tmp{i}") for i in range(2)]
        pTb = R.sb([128, 2, T], BF16, "pTb")
        t_p = R.tok("pTb")
        ws = WStream(cx, KC, 3, "w")
        hv = h1T.rearrange("(c p) t -> p c t", p=128)
        uv = uT_d.rearrange("(c p) t -> p c t", p=128)
        dnv = odn_d.rearrange("(c p) t -> p c t", p=128)
        sbv = osb_d.rearrange("(c p) t -> p c t", p=128)
        pv = pT_d.rearrange("(c p) t -> p c t", p=128)
        for it in range(ntiles):
            r0 = it * T
            A("sp", lambda e, r0=r0: e.dma_start(out=hT[:], in_=hv[:, :, r0:r0 + T]), writes=[t_h], dma_tok=t_h)
            A("sp", lambda e, r0=r0: e.dma_start(out=uT[:], in_=uv[:, :, r0:r0 + T]), writes=[t_u], dma_tok=t_u)
            t_dn = R.tok("odn_ld")
            A("sp", lambda e, r0=r0: e.dma_start(out=odn, in_=dnv[:, :, r0:r0 + T]), writes=[t_act], dma_tok=t_dn)
            A("sp", lambda e, r0=r0: e.dma_start(out=osb, in_=sbv[:, :, r0:r0 + T]), writes=[t_act], dma_tok=t_dn)
            A("pool", lambda e, r0=r0: e.dma_start(out=pTb[:], in_=pv[:, :, r0:r0 + T]), writes=[t_p], dma_tok=t_p)
            for (gcol, wb, xs, tx, second) in ((0, wbd, odn, t_act, False), (D, wbs, osb, t_act, True)):
                for c4 in range(0, KC, 4):
                    gtile, tgw = ws.load(wgate, 0, KC, gcol + c4 * 128, 512)
                    btile, tbw = ws.load(wb, 0, KC, c4 * 128, 512)
                    for j in range(4):
                        c = c4 + j
                        bg, tbg = cx.bank()
                        for k in range(KC):
                            A("pe", lambda e, bg=bg, k=k, j=j, gtile=gtile: e.matmul(
                                bg[:, :], lhsT=gtile[:, k, j * 128:(j + 1) * 128], rhs=uT[:, k, :],
                                start=(k == 0), stop=(k == KC - 1)), reads=[tgw, t_u], writes=[tbg])
                        bb, tbb = cx.bank()
                        for k in range(KC):
                            A("pe", lambda e, bb=bb, k=k, j=j, btile=btile, xs=xs: e.matmul(
                                bb[:, :], lhsT=btile[:, k, j * 128:(j + 1) * 128], rhs=xs[:, k, :],
                                start=(k == 0), stop=(k == KC - 1)), reads=[tbw, tx], writes=[tbb])
                        tt, tm = t_tmp[c % 2], tmp[c % 2]
                        A("act", lambda e, bg=bg, tm=tm: e.activation(out=tm[:], in_=bg[:, :], func=AF.Sigmoid),
                          reads=[tbg], writes=[tt])
                        if not second:
                            A("dve", lambda e, bb=bb, tm=tm, c=c: e.tensor_tensor(out=fT[:, c, :], in0=tm[:], in1=bb[:, :],
                                                                                 op=ALU.mult), reads=[tt, tbb], writes=[t_f])
                        else:
                            A("dve", lambda e, bb=bb, tm=tm: e.tensor_tensor(out=tm[:], in0=tm[:], in1=bb[:, :], op=ALU.mult),
                              reads=[tt, tbb], writes=[tt])
                            A("dve", lambda e, tm=tm, c=c: e.tensor_tensor(out=sq[:, c, :], in0=tm[:], in1=fT[:, c, :],
                                                                          op=ALU.add), reads=[tt, t_f], writes=[t_sq])
            linear_fm(cx, ws, wout, KC, sq, t_sq, 0, KC,
                      lambda c, bk, tb: evac(cx, fT[:, c, :], bk[:, :], [tb], [t_f]))
            resid_norm_add(cx, fT, t_f, g_all[:, 0:KC], t_g, hT, t_h, sq, t_sq, rstd, t_rstd, 1.0)
            rms_fm(cx, hT, t_h, KC, g_all[:, KC:2 * KC], t_g, uT, t_u, sq, t_sq, rstd, t_rstd, D)
            ffn_fm(cx, uT, t_u, wg, wu, wd, actT, t_act, ws, fT, t_f, tmp, t_tmp)
            resid_norm_add(cx, fT, t_f, g_all[:, 2 * KC:3 * KC], t_g, hT, t_h, sq, t_sq, rstd, t_rstd, 0.5)
            rms_fm(cx, hT, t_h, KC, g_all[:, 3 * KC:4 * KC], t_g, uT, t_u, sq, t_sq, rstd, t_rstd, D)
            for c4 in range(0, KC, 4):
                gtile, tgw = ws.load(wpg, 0, KC, c4 * 128, 512)
                ptile, tpw = ws.load(wpp, 0, 2, c4 * 128, 512)
                for j in range(4):
                    c = c4 + j
                    bg, tbg = cx.bank()
                    for k in range(KC):
                        A("pe", lambda e, bg=bg, k=k, j=j, gtile=gtile: e.matmul(
                            bg[:, :], lhsT=gtile[:, k, j * 128:(j + 1) * 128], rhs=uT[:, k, :],
                            start=(k == 0), stop=(k == KC - 1)), reads=[tgw, t_u], writes=[tbg])
                    bp, tbp = cx.bank()
                    for k in range(2):
                        A("pe", lambda e, bp=bp, k=k, j=j, ptile=ptile: e.matmul(
                            bp[:, :], lhsT=ptile[:, k, j * 128:(j + 1) * 128], rhs=pTb[:, k, :],
                            start=(k == 0), stop=(k == 1)), reads=[tpw, t_p], writes=[tbp])
                    tt, tm = t_tmp[c % 2], tmp[c % 2]
                    A("act", lambda e, bg=bg, tm=tm: e.activation(out=tm[:], in_=bg[:, :], func=AF.Sigmoid),
                      reads=[tbg], writes=[tt])
                    A("dve", lambda e, bp=bp, tm=tm, c=c: e.tensor_tensor(out=fT[:, c, :], in0=tm[:], in1=bp[:, :], op=ALU.mult),
                      reads=[tt, tbp], writes=[t_f])
            resid_norm_add(cx, fT, t_f, g_all[:, 4 * KC:5 * KC], t_g, hT, t_h, sq, t_sq, rstd, t_rstd, 1.0)
            transpose_out(cx, hT, t_h, KC, out, r0, otok, t_otok, ident)
        R.emit()
    return nc


_PROGS = {}


def _prog(name, fn):
    if name not in _PROGS:
        _PROGS[name] = fn()
    return _PROGS[name]


def _run(nc, maps):
    import sys, time
    t0 = time.time()
    res = run_bass_kernel_spmd(nc, maps, core_ids=list(range(NCORE)))
    print(f"[kernel] launch done in {time.time() - t0:.1f}s", file=sys.stderr, flush=True)
    return res.results


DN_W = 2048
O1 = 3 * DN_W
O2 = O1 + DN_W
O3 = O2 + 16
O4 = O3 + 16
O5 = O4 + 3 * 2048
O6 = O5 + D


def kernel(x, p, ffn1_norm_pre, ffn1_w_gate, ffn1_w_up, ffn1_w_down, ffn1_norm_post,
           mix_norm_pre, w_in, dn_conv_w, dn_A_log, dn_dt_bias, dn_out_norm,
           w_branch_dn, w_branch_sb, w_out, mix_norm_post,
           ffn2_norm_pre, ffn2_w_gate, ffn2_w_up, ffn2_w_down, ffn2_norm_post,
           ple_norm_pre, ple_w_gate, ple_w_proj, ple_norm_post):
    f32 = lambda a: np.ascontiguousarray(np.asarray(a, dtype=np.float32))
    x = f32(x)[0]
    p = f32(p)[0, 0]
    w_in = f32(w_in)[0]
    ident = np.eye(128, dtype=np.float32)
    g1 = np.concatenate([fm_vec(ffn1_norm_pre[0]), fm_vec(ffn1_norm_post[0]), fm_vec(mix_norm_pre[0])], 1)
    wg1, wu1, wd1 = f32(ffn1_w_gate)[0], f32(ffn1_w_up)[0], f32(ffn1_w_down)[0]
    maps = [{"x": np.ascontiguousarray(x[c * TOK:(c + 1) * TOK]), "wg": wg1, "wu": wu1, "wd": wd1,
             "gains": g1, "ident": ident} for c in range(NCORE)]
    r1 = _run(_prog("s1", build_stage1), maps)
    h1T = [np.asarray(r["h1T"]) for r in r1]
    uT = [np.asarray(r["uT"]) for r in r1]
    uT_all = np.ascontiguousarray(np.concatenate(uT, axis=1))
    uT_rev = np.ascontiguousarray(uT_all[:, ::-1])
    del r1, maps
    conv = f32(dn_conv_w)[0]
    consts = dn_consts()
    maps = []
    for c in range(NCORE):
        hs = (2 * c, 2 * c + 1)
        cols = [w_in[:, off + h * 128: off + (h + 1) * 128] for off in (0, DN_W, 2 * DN_W, O1) for h in hs]
        wdn = np.ascontiguousarray(np.concatenate(cols, axis=1))
        wba = np.ascontiguousarray(np.stack([w_in[:, O2 + hs[0]], w_in[:, O2 + hs[1]], w_in[:, O3 + hs[0]], w_in[:, O3 + hs[1]]], axis=1))
        cw = np.stack([conv[:, off + h * 128: off + (h + 1) * 128] for off in (0, DN_W, 2 * DN_W) for h in hs], axis=0)
        cw = np.ascontiguousarray(cw.transpose(2, 0, 1).reshape(128, 24))
        hv = np.array([dn_dt_bias[0][hs[0]], dn_dt_bias[0][hs[1]], dn_A_log[0][hs[0]], dn_A_log[0][hs[1]]], np.float32)
        maps.append({"uT_all": uT_all, "wdn": wdn, "wba": wba, "convw": cw,
                     "gon": f32(dn_out_norm)[0][:, None].copy(), "hv": np.ascontiguousarray(np.tile(hv[None, :], (64, 1))),
                     "consts": consts})
    r2a = _run(_prog("s2a", build_stage2_dn), maps)
    odn_full = np.ascontiguousarray(np.concatenate([np.asarray(r["odnT"]) for r in r2a], axis=0))
    del r2a, maps
    sbc = sb_consts()
    maps = []
    for c in range(NCORE):
        hs = (2 * c, 2 * c + 1)
        cols = [w_in[:, O4 + off + h * 128: O4 + off + (h + 1) * 128] for off in (0, 2048, 4096) for h in hs]
        maps.append({"uT_all": uT_all, "uT_rev": uT_rev, "wsb": np.ascontiguousarray(np.concatenate(cols, axis=1)), "consts": sbc})
    r2b = _run(_prog("s2b", build_stage2_sb), maps)
    osb_full = np.ascontiguousarray(np.concatenate([np.asarray(r["osbT"]) for r in r2b], axis=0))
    del r2b, maps, uT_all, uT_rev
    g3 = np.concatenate([fm_vec(mix_norm_post[0]), fm_vec(ffn2_norm_pre[0]), fm_vec(ffn2_norm_post[0]),
                         fm_vec(ple_norm_pre[0]), fm_vec(ple_norm_post[0])], 1)
    wgate = np.ascontiguousarray(w_in[:, O5:O5 + 2 * D])
    shared = {"wgate": wgate, "wbd": f32(w_branch_dn)[0], "wbs": f32(w_branch_sb)[0], "wout": f32(w_out)[0],
              "wg": f32(ffn2_w_gate)[0], "wu": f32(ffn2_w_up)[0], "wd": f32(ffn2_w_down)[0],
              "wpg": f32(ple_w_gate)[0], "wpp": f32(ple_w_proj)[0], "gains": g3, "ident": ident}
    maps = []
    for c in range(NCORE):
        sl = slice(c * TOK, (c + 1) * TOK)
        m = dict(shared)
        m.update({"h1T": h1T[c], "uT": uT[c], "odnT": np.ascontiguousarray(odn_full[:, sl]),
                  "osbT": np.ascontiguousarray(osb_full[:, sl]), "pT": np.ascontiguousarray(p[sl].T)})
        maps.append(m)
    r3 = _run(_prog("s3", build_stage3), maps)
    out = np.concatenate([np.asarray(r["out"]) for r in r3], axis=0)
    return out.reshape(1, SEQ, D).astype(np.float32)
```

```python
import contextlib
import numpy as np
import concourse.bass as bass
import concourse.mybir as mybir
from concourse.bass_utils import run_bass_kernel_spmd

F32 = mybir.dt.float32
BF16 = mybir.dt.bfloat16
AF = mybir.ActivationFunctionType
ALU = mybir.AluOpType

D = 2048
SEQ = 16384
NCORE = 8
TOK = SEQ // NCORE
T = 512
DFF = 5632
PLE = 256
KC = D // 128
EPS = 1e-6


class Tok:
    __slots__ = ("name", "last_w", "readers", "sem", "dma_cnt", "last_dma")

    def __init__(self, name):
        self.name = name
        self.last_w = None
        self.readers = []
        self.sem = None
        self.dma_cnt = 0
        self.last_dma = None


class Op:
    __slots__ = ("eng", "fn", "deps", "dma", "signal", "sem", "count", "tok")

    def __init__(self, eng, fn, dma):
        self.eng = eng
        self.fn = fn
        self.deps = []
        self.dma = dma
        self.signal = dma
        self.sem = None
        self.count = 0
        self.tok = None


ENGS = ("pe", "act", "dve", "pool", "sp")


class Rec:
    def __init__(self, nc, stack):
        self.nc = nc
        self.stack = stack
        self.ops = {e: [] for e in ENGS}
        self.toks = []
        self.dma_toks = []
        self.nsb = 0

    def tok(self, name="t"):
        t = Tok(name)
        self.toks.append(t)
        return t

    def sb(self, shape, dt, name=None):
        self.nsb += 1
        return self.stack.enter_context(self.nc.sbuf_tensor("s_" + (name or f"sb{self.nsb}"), list(shape), dt))

    def ps(self, shape, dt, name=None):
        self.nsb += 1
        return self.stack.enter_context(self.nc.psum_tensor("p_" + (name or f"ps{self.nsb}"), list(shape), dt))

    def add(self, eng, fn, reads=(), writes=(), dma_tok=None):
        op = Op(eng, fn, dma_tok is not None)
        deps = []
        for r in reads:
            if r.last_w is not None:
                deps.append(r.last_w)
        for w in writes:
            if w.last_w is not None:
                deps.append(w.last_w)
            deps.extend(w.readers)
        if dma_tok is not None:
            if dma_tok.sem is None:
                dma_tok.sem = self.stack.enter_context(self.nc.semaphore(f"dq{len(self.dma_toks)}"))
                dma_tok.last_dma = None
                self.dma_toks.append(dma_tok)
            if getattr(dma_tok, "last_dma", None) is not None:
                deps.append(dma_tok.last_dma)
            dma_tok.dma_cnt += 16
            op.sem = dma_tok.sem
            op.count = dma_tok.dma_cnt
            op.tok = dma_tok
            dma_tok.last_dma = op
        seen = set()
        for d in deps:
            if d is op or id(d) in seen:
                continue
            seen.add(id(d))
            if (not d.dma) and d.eng == eng and eng == "pe":
                continue
            op.deps.append(d)
            d.signal = True
        for r in reads:
            r.readers.append(op)
        for w in writes:
            w.last_w = op
            w.readers = []
        self.ops[eng].append(op)
        return op

    def emit(self):
        nc = self.nc
        esem = {}
        for e in ("pe", "act", "dve", "pool"):
            esem[e] = self.stack.enter_context(nc.semaphore(f"es_{e}"))
        for e in ("pe", "act", "dve", "pool", "sp"):
            c = 0
            for op in self.ops[e]:
                if op.dma:
                    continue
                if op.signal:
                    c += 1
                    op.sem = esem.get(e)
                    op.count = c
                    assert e != "sp"
        final = [(t.sem, t.dma_cnt) for t in self.dma_toks]

        def run(e, eng):
            waited = {}
            for op in self.ops[e]:
                for d in op.deps:
                    key = id(d.sem)
                    if waited.get(key, 0) >= d.count:
                        continue
                    eng.wait_ge(d.sem, d.count)
                    waited[key] = d.count
                ins = op.fn(eng)
                if op.signal:
                    ins.then_inc(op.sem, 16 if op.dma else 1)
            if e == "sp":
                for s, c in final:
                    eng.wait_ge(s, c)

        with nc.Block() as block:
            @block.tensor
            def _(eng):
                run("pe", eng)

            @block.scalar
            def _(eng):
                run("act", eng)

            @block.vector
            def _(eng):
                run("dve", eng)

            @block.gpsimd
            def _(eng):
                run("pool", eng)

            @block.sync
            def _(eng):
                run("sp", eng)


class Ctx:
    def __init__(self, nc, stack):
        self.nc = nc
        self.R = Rec(nc, stack)
        R = self.R
        self.banks = [R.ps([128, 512], F32, name=f"bank{i}") for i in range(8)]
        self.bank_tok = [R.tok(f"bank{i}") for i in range(8)]
        self.bank_rr = 0
        self.ones_bf = R.sb([128, 128], BF16, "ones_bf")
        self.t_const = R.tok("const")
        R.add("pool", lambda e: e.memset(self.ones_bf[:], 1.0), writes=[self.t_const])
        self.evac_rr = 0
        self.reserved = set()

    def bank(self):
        while True:
            i = self.bank_rr % 8
            self.bank_rr += 1
            if id(self.banks[i]) not in self.reserved:
                return self.banks[i], self.bank_tok[i]


def load_vec_fm(cx, dram_vec, name):
    R = cx.R
    t = R.sb([128, dram_vec.shape[1]], F32, name)
    tk = R.tok(name)
    R.add("sp", lambda e: e.dma_start(out=t[:], in_=dram_vec[:, :]), writes=[tk], dma_tok=tk)
    return t, tk


def rms_fm(cx, src, t_src, nch, gain, t_gain, dst, t_dst, sq, t_sq, rstd, t_rstd, dim, post=None):
    R = cx.R
    R.add("act", lambda e: e.activation(out=sq[:, 0:nch, :], in_=src[:, 0:nch, :], func=AF.Square),
          reads=[t_src], writes=[t_sq])
    bk, tb = cx.bank()
    for c in range(nch):
        R.add("pe", lambda e, c=c: e.matmul(bk[:, :], lhsT=cx.ones_bf[:], rhs=sq[:, c, :],
                                             start=(c == 0), stop=(c == nch - 1)),
              reads=[t_sq, cx.t_const], writes=[tb])
    R.add("act", lambda e: e.activation(out=rstd[:], in_=bk[:, :], func=AF.Ln, bias=EPS, scale=1.0 / dim),
          reads=[tb], writes=[t_rstd])
    R.add("act", lambda e: e.activation(out=rstd[:], in_=rstd[:], func=AF.Exp, scale=-0.5), reads=[t_rstd], writes=[t_rstd])
    if post is None:
        for c in range(nch):
            R.add("dve", lambda e, c=c: e.scalar_tensor_tensor(out=dst[:, c, :], in0=src[:, c, :],
                                                                scalar=gain[:, c:c + 1], in1=rstd[:],
                                                                op0=ALU.mult, op1=ALU.mult),
                  reads=[t_src, t_gain, t_rstd], writes=[t_dst])
    else:
        post()


def transpose_in(cx, x_dram, r0, xtok, t_xtok, xT, t_xT, ident, ncol_chunks, alias=()):
    R = cx.R
    W = ncol_chunks * 128
    alias = list(alias)
    for s in range(T // 128):
        R.add("sp", lambda e, s=s: e.dma_start(out=xtok[:, s, 0:W], in_=x_dram[r0 + s * 128:r0 + (s + 1) * 128, :]),
              writes=[t_xtok[s]] + alias, dma_tok=t_xtok[s])
    for c in range(ncol_chunks):
        bk, tb = cx.bank()
        for s in range(T // 128):
            R.add("pe", lambda e, c=c, s=s, bk=bk: e.transpose(out=bk[:, s * 128:(s + 1) * 128],
                                                                in_=xtok[:, s, c * 128:(c + 1) * 128], identity=ident[:]),
                  reads=[t_xtok[s], cx.t_const] + alias, writes=[tb])
        evac(cx, xT[:, c, :], bk[:, :], [tb], [t_xT])


def evac(cx, dst, src, reads, writes):
    R = cx.R
    cx.evac_rr += 1
    if cx.evac_rr % 2 == 0:
        R.add("act", lambda e: e.copy(out=dst, in_=src), reads=reads, writes=writes)
    else:
        R.add("dve", lambda e: e.tensor_copy(out=dst, in_=src), reads=reads, writes=writes)


def transpose_out(cx, srcT, t_src, nch, out_dram, r0, otok, t_otok, ident):
    R = cx.R
    for s in range(T // 128):
        for c4 in range(0, nch, 4):
            bk, tb = cx.bank()
            for j in range(4):
                c = c4 + j
                R.add("pe", lambda e, c=c, s=s, j=j, bk=bk: e.transpose(out=bk[:, j * 128:(j + 1) * 128],
                                                                   in_=srcT[:, c, s * 128:(s + 1) * 128],
                                                                   identity=ident[:]),
                      reads=[t_src, cx.t_const], writes=[tb])
            evac(cx, otok[:, s, c4 * 128:(c4 + 4) * 128], bk[:, :], [tb], [t_otok[s]])
        R.add("sp", lambda e, s=s: e.dma_start(out=out_dram[r0 + s * 128:r0 + (s + 1) * 128, :],
                                               in_=otok[:, s, 0:nch * 128]),
              reads=[t_otok[s]], dma_tok=t_otok[s])


class WStream:
    def __init__(self, cx, kch_max, nbuf, name):
        R = cx.R
        self.cx = cx
        self.bufs = [R.sb([128, kch_max, 512], BF16, f"{name}{i}") for i in range(nbuf)]
        self.toks = [R.tok(f"{name}{i}") for i in range(nbuf)]
        self.st_toks = [R.tok(f"{name}st{i}") for i in range(nbuf)]
        self.sw_toks = [R.tok(f"{name}sw{i}") for i in range(nbuf)]
        self.hw_toks = [R.tok(f"{name}hw{i}") for i in range(nbuf)]
        self.rr = 0
        self.scratch = {}
        self.regions = {}

    def load(self, w_dram, k0, kch, c0, ncols):
        R = self.cx.R
        nc = self.cx.nc
        i = self.rr % len(self.bufs)
        self.rr += 1
        buf, tk = self.bufs[i], self.toks[i]
        wname = w_dram.tensor.name
        if wname not in self.scratch:
            self.scratch[wname] = nc.dram_tensor("bf_" + wname, list(w_dram.shape), BF16).ap()
        scr = self.scratch[wname][k0 * 128:(k0 + kch) * 128, c0:c0 + ncols].rearrange("(k p) n -> p k n", p=128)
        key = (wname, k0, kch, c0, ncols)
        if key not in self.regions:
            t_reg = R.tok("reg")
            self.regions[key] = t_reg
            src = w_dram[k0 * 128:(k0 + kch) * 128, c0:c0 + ncols].rearrange("(k p) n -> p k n", p=128)
            R.add("pool", lambda e: e.dma_start(out=buf[:, 0:kch, 0:ncols], in_=src), writes=[tk], dma_tok=self.sw_toks[i])
            R.add("sp", lambda e: e.dma_start(out=scr, in_=buf[:, 0:kch, 0:ncols]), reads=[tk], writes=[t_reg],
                  dma_tok=self.st_toks[i])
        else:
            t_reg = self.regions[key]
            R.add("sp", lambda e: e.dma_start(out=buf[:, 0:kch, 0:ncols], in_=scr), reads=[t_reg], writes=[tk], dma_tok=self.hw_toks[i])
        return buf, tk


def ffn_fm(cx, uT, t_u, wg, wu, wd, actT, t_act, ws, fT, t_f, tmp, t_tmp):
    R = cx.R
    NJ = DFF // 128
    for j4 in range(0, NJ, 4):
        gw, tg = ws.load(wg, 0, KC, j4 * 128, 512)
        uw, tu = ws.load(wu, 0, KC, j4 * 128, 512)
        for j in range(4):
            bg, tbg = cx.bank()
            bu, tbu = cx.bank()
            for k in range(KC):
                R.add("pe", lambda e, k=k, j=j, bg=bg, gw=gw: e.matmul(bg[:, :], lhsT=gw[:, k, j * 128:(j + 1) * 128],
                                                                        rhs=uT[:, k, :], start=(k == 0), stop=(k == KC - 1)),
                      reads=[tg, t_u], writes=[tbg])
            for k in range(KC):
                R.add("pe", lambda e, k=k, j=j, bu=bu, uw=uw: e.matmul(bu[:, :], lhsT=uw[:, k, j * 128:(j + 1) * 128],
                                                                        rhs=uT[:, k, :], start=(k == 0), stop=(k == KC - 1)),
                      reads=[tu, t_u], writes=[tbu])
            jj = j4 + j
            tt = t_tmp[jj % 2]
            tm = tmp[jj % 2]
            R.add("act", lambda e, bg=bg, tm=tm: e.activation(out=tm[:], in_=bg[:, :], func=AF.Silu),
                  reads=[tbg], writes=[tt])
            R.add("dve", lambda e, bu=bu, tm=tm, jj=jj: e.tensor_tensor(out=actT[:, jj, :], in0=tm[:], in1=bu[:, :],
                                                                         op=ALU.mult),
                  reads=[tt, tbu], writes=[t_act])
    for dg in range(4):
        bks = [cx.bank() for _ in range(4)]
        for q in range(4):
            dwt, tdw = ws.load(wd, q * 11, 11, dg * 512, 512)
            for kk in range(11):
                k = q * 11 + kk
                for dd in range(4):
                    bk, tb = bks[dd]
                    R.add("pe", lambda e, kk=kk, k=k, dd=dd, bk=bk, dwt=dwt: e.matmul(
                        bk[:, :], lhsT=dwt[:, kk, dd * 128:(dd + 1) * 128], rhs=actT[:, k, :],
                        start=(k == 0), stop=(k == NJ - 1)), reads=[tdw, t_act], writes=[tb])
        for dd in range(4):
            bk, tb = bks[dd]
            evac(cx, fT[:, dg * 4 + dd, :], bk[:, :], [tb], [t_f])


def resid_norm_add(cx, fT, t_f, gain, t_gain, hT, t_h, sq, t_sq, rstd, t_rstd, coef):
    R = cx.R

    def post():
        for c in range(KC):
            R.add("dve", lambda e, c=c: e.scalar_tensor_tensor(out=fT[:, c, :], in0=fT[:, c, :],
                                                                scalar=gain[:, c:c + 1], in1=rstd[:],
                                                                op0=ALU.mult, op1=ALU.mult),
                  reads=[t_f, t_gain, t_rstd], writes=[t_f])
            R.add("dve", lambda e, c=c: e.scalar_tensor_tensor(out=hT[:, c, :], in0=fT[:, c, :],
                                                                scalar=float(coef), in1=hT[:, c, :],
                                                                op0=ALU.mult, op1=ALU.add),
                  reads=[t_f, t_h], writes=[t_h])

    rms_fm(cx, fT, t_f, KC, gain, t_gain, None, None, sq, t_sq, rstd, t_rstd, D, post=post)


def build_stage1(ntiles=TOK // T):
    nc = bass.Bass("TRN2", target_bir_lowering=False)
    x = nc.dram_tensor("x", [TOK, D], F32, kind="ExternalInput").ap()
    wg = nc.dram_tensor("wg", [D, DFF], F32, kind="ExternalInput").ap()
    wu = nc.dram_tensor("wu", [D, DFF], F32, kind="ExternalInput").ap()
    wd = nc.dram_tensor("wd", [DFF, D], F32, kind="ExternalInput").ap()
    gains = nc.dram_tensor("gains", [128, 3 * KC], F32, kind="ExternalInput").ap()
    ident_d = nc.dram_tensor("ident", [128, 128], F32, kind="ExternalInput").ap()
    h1T = nc.dram_tensor("h1T", [D, TOK], F32, kind="ExternalOutput").ap()
    uT_o = nc.dram_tensor("uT", [D, TOK], BF16, kind="ExternalOutput").ap()
    with contextlib.ExitStack() as stack:
        cx = Ctx(nc, stack)
        R = cx.R
        g_all, t_g = load_vec_fm(cx, gains, "gains_sb")
        ident = R.sb([128, 128], F32, "ident")
        R.add("sp", lambda e: e.dma_start(out=ident[:], in_=ident_d[:, :]), writes=[cx.t_const], dma_tok=cx.t_const)
        big = R.sb([128, KC * T], F32, "big")
        xtok = big[:].rearrange("p (s d) -> p s d", s=4)
        fT = big[:].rearrange("p (c t) -> p c t", c=KC)
        t_xtok = [R.tok(f"xtok{s}") for s in range(4)]
        t_f = R.tok("fT")
        hT = R.sb([128, KC, T], F32, "hT")
        t_h = R.tok("hT")
        uT = R.sb([128, KC, T], BF16, "uT")
        t_u = R.tok("uT")
        sq = R.sb([128, KC, T], BF16, "sq")
        t_sq = R.tok("sq")
        rstd = R.sb([128, T], F32, "rstd")
        t_rstd = R.tok("rstd")
        actT = R.sb([128, DFF // 128, T], BF16, "actT")
        t_act = R.tok("actT")
        tmp = [R.sb([128, T], F32, f"tmp{i}") for i in range(2)]
        t_tmp = [R.tok(f"tmp{i}") for i in range(2)]
        ws = WStream(cx, KC, 3, "w")
        h1v = h1T.rearrange("(c p) t -> p c t", p=128)
        uv = uT_o.rearrange("(c p) t -> p c t", p=128)
        for it in range(ntiles):
            r0 = it * T
            transpose_in(cx, x, r0, xtok, t_xtok, hT, t_h, ident, KC, alias=[t_f])
            rms_fm(cx, hT, t_h, KC, g_all[:, 0:KC], t_g, uT, t_u, sq, t_sq, rstd, t_rstd, D)
            ffn_fm(cx, uT, t_u, wg, wu, wd, actT, t_act, ws, fT, t_f, tmp, t_tmp)
            resid_norm_add(cx, fT, t_f, g_all[:, KC:2 * KC], t_g, hT, t_h, sq, t_sq, rstd, t_rstd, 0.5)
            R.add("sp", lambda e, r0=r0: e.dma_start(out=h1v[:, :, r0:r0 + T], in_=hT[:]), reads=[t_h], dma_tok=t_h)
            rms_fm(cx, hT, t_h, KC, g_all[:, 2 * KC:3 * KC], t_g, uT, t_u, sq, t_sq, rstd, t_rstd, D)
            R.add("sp", lambda e, r0=r0: e.dma_start(out=uv[:, :, r0:r0 + T], in_=uT[:]), reads=[t_u], dma_tok=t_u)
        R.emit()
    return nc


def fm_vec(v):
    return np.ascontiguousarray(np.asarray(v, np.float32).reshape(-1, 128).T)


HD = 128
CH = 64
NCHK = T // CH
C_ID, C_U, C_SL, C_BU, C_BL = 0, 128, 192, 256, 320


def dn_consts():
    c = np.zeros((128, 384), np.float32)
    c[:, 0:128] = np.eye(128, dtype=np.float32)
    p = np.arange(64)[:, None]
    f = np.arange(64)[None, :]
    c[0:64, C_U:C_U + 64] = (p <= f)
    c[0:64, C_SL:C_SL + 64] = (f < p)
    c[0:64, C_BU:C_BU + 64] = 1e4 * (f > p)
    c[0:64, C_BL:C_BL + 64] = 1e4 * (f < p)
    return c


def bc_mid(ap2, n):
    return ap2.unsqueeze(1).to_broadcast([ap2.shape[0], n, ap2.shape[1]])


def bc_last(ap2, n):
    return ap2.unsqueeze(2).to_broadcast([ap2.shape[0], ap2.shape[1], n])


def build_stage2_dn(ntiles=SEQ // T):
    S = ntiles * T
    nc = bass.Bass("TRN2", target_bir_lowering=False)
    uT_all = nc.dram_tensor("uT_all", [D, S], BF16, kind="ExternalInput").ap()
    wdn = nc.dram_tensor("wdn", [D, 1024], F32, kind="ExternalInput").ap()
    wba = nc.dram_tensor("wba", [D, 4], F32, kind="ExternalInput").ap()
    convw_d = nc.dram_tensor("convw", [128, 24], F32, kind="ExternalInput").ap()
    gon_d = nc.dram_tensor("gon", [128, 1], F32, kind="ExternalInput").ap()
    hv_d = nc.dram_tensor("hv", [64, 4], F32, kind="ExternalInput").ap()
    consts_d = nc.dram_tensor("consts", [128, 384], F32, kind="ExternalInput").ap()
    odnT = nc.dram_tensor("odnT", [256, S], BF16, kind="ExternalOutput").ap()
    with contextlib.ExitStack() as stack:
        cx = Ctx(nc, stack)
        R = cx.R
        A = R.add
        tc = cx.t_const
        consts = R.sb([128, 384], F32, "consts")
        A("sp", lambda e: e.dma_start(out=consts[:], in_=consts_d[:, :]), writes=[tc], dma_tok=tc)
        ident = consts[:, 0:128]
        id64 = consts[0:64, 0:64]
        Um = consts[0:64, C_U:C_U + 64]
        SLm = consts[0:64, C_SL:C_SL + 64]
        BUm = consts[0:64, C_BU:C_BU + 64]
        BLm = consts[0:64, C_BL:C_BL + 64]
        convw, t_cw = load_vec_fm(cx, convw_d, "convw_sb")
        gon, t_gon = load_vec_fm(cx, gon_d, "gon_sb")
        hv = R.sb([64, 4], F32, "hv")
        t_hv = R.tok("hv")
        A("sp", lambda e: e.dma_start(out=hv[:], in_=hv_d[:, :]), writes=[t_hv], dma_tok=t_hv)
        ident_bf = R.sb([128, 128], BF16, "ident_bf")
        A("dve", lambda e: e.tensor_copy(out=ident_bf[:], in_=consts[:, 0:128]), reads=[tc], writes=[tc])
        id64_bf = ident_bf[0:64, 0:64]
        ones_f = R.sb([64, 128], F32, "ones_f")
        A("pool", lambda e: e.memset(ones_f[:], 1.0), writes=[tc])
        expA = R.sb([64, 2], F32, "expA")
        A("act", lambda e: e.activation(out=expA[:], in_=hv[:, 2:4], func=AF.Exp), reads=[t_hv], writes=[t_hv])
        wres = R.sb([128, KC, 1024], BF16, "wres")
        t_w = R.tok("wres")
        wv = wdn.rearrange("(k p) n -> p k n", p=128)
        for half in range(2):
            A("pool", lambda e, half=half: e.dma_start(out=wres[:, :, half * 512:(half + 1) * 512],
                                                      in_=wv[:, :, half * 512:(half + 1) * 512]),
              writes=[t_w], dma_tok=t_w)
        wbar = R.sb([128, KC, 4], BF16, "wbar")
        t_wb = R.tok("wbar")
        A("pool", lambda e: e.dma_start(out=wbar[:], in_=wba.rearrange("(k p) n -> p k n", p=128)),
          writes=[t_wb], dma_tok=t_wb)
        u_t = [R.sb([128, KC, T], BF16, f"u_t{i}") for i in range(1)]
        t_ut = [R.tok(f"u_t{i}") for i in range(1)]
        rawbuf = R.sb([128, 6, T + 3], F32, "rawbuf")
        t_raw = [R.tok(f"raw{c}") for c in range(6)]
        A("pool", lambda e: e.memset(rawbuf[:, :, 0:3], 0.0), writes=t_raw)
        cv = R.sb([128, 6, T], F32, "cv")
        t_cv = [R.tok(f"cv{c}") for c in range(6)]
        qk = R.sb([128, 4, T], BF16, "qk")
        t_qk = [R.tok(f"qk{c}") for c in range(4)]
        siluz = R.sb([128, 2, T], F32, "siluz")
        t_sz = [R.tok(f"sz{c}") for c in range(2)]
        sqb = R.sb([128, T], BF16, "sqb")
        t_sqb = R.tok("sqb")
        rstd = R.sb([128, T], F32, "rstd")
        t_rstd = R.tok("rstd")
        ba = R.sb([64, NCHK, 4], F32, "ba")
        t_ba = R.tok("ba")
        beta = R.sb([64, NCHK, 2], F32, "beta")
        t_beta = R.tok("beta")
        gt = R.sb([64, NCHK, 2], F32, "gt")
        t_g = R.tok("gt")
        spx = R.sb([64, NCHK, 2], F32, "spx")
        t_spx = R.tok("spx")

        def sbt(shape, name, dt=F32):
            return R.sb(shape, dt, name), R.tok(name)

        def make_head(h):
            gc_col, t_gcc = sbt([64, NCHK], f"gc_col_h{h}")
            egc, t_egc = sbt([64, NCHK], f"egc_h{h}")
            be, t_be = sbt([64, NCHK], f"be_h{h}")
            kgs, t_kgs = sbt([64, NCHK], f"kgs_h{h}")
            Ug, t_Ug = sbt([64, NCHK, 64], f"Ug_h{h}")
            gcrow, t_gcr = sbt([128, NCHK, 64], f"gcrow_h{h}")
            egrow, t_egr = sbt([128, NCHK, 64], f"egrow_h{h}")
            dl, t_dl = sbt([64, NCHK, 64], f"dl_h{h}")
            decay, t_dec = sbt([64, NCHK, 64], f"decay_h{h}")
            decayT, t_decT = sbt([64, NCHK, 64], f"decayT_h{h}")
            Ktok, t_Kt = sbt([64, NCHK, 128], f"Ktok_h{h}", BF16)
            Vtok, t_Vt = sbt([64, NCHK, 128], f"Vtok_h{h}")
            Pb = [sbt([64, NCHK, 64], f"P{i}_h{h}", BF16) for i in range(2)]
            Ptb = [sbt([64, NCHK, 64], f"Pt{i}_h{h}", BF16) for i in range(2)]
            At, t_At = sbt([64, NCHK, 64], f"At_h{h}")
            u_sb, t_usb = sbt([64, NCHK, 128], f"u_sb_h{h}")
            wT, t_wT = sbt([128, NCHK, 64], f"wT_h{h}", BF16)
            attnT, t_att = sbt([64, NCHK, 64], f"attnT_h{h}", BF16)
            qgT, t_qg = sbt([128, NCHK, 64], f"qgT_h{h}", BF16)
            kg, t_kg = sbt([64, NCHK, 128], f"kg_h{h}", BF16)
            Sst = [sbt([128, 128], f"S{i}_h{h}") for i in range(2)]
            Ssb = [sbt([128, 128], f"Sb{i}_h{h}", BF16) for i in range(2)]
            A("pool", lambda e: e.memset(Sst[0][0][:], 0.0), writes=[Sst[0][1]])
            A("pool", lambda e: e.memset(Ssb[0][0][:], 0.0), writes=[Ssb[0][1]])
            s_par = [0]
            vnew = [sbt([64, 128], f"vnew{i}_h{h}", BF16) for i in range(2)]
            vn_rr = [0]
            tmpd, t_tmpd = Ug, t_Ug
            Vb, t_Vb = sbt([64, NCHK, 128], f"Vb_h{h}", BF16)
            Kbg, t_Kbg = sbt([64, NCHK, 128], f"Kbg_h{h}", BF16)
            Atb, t_Atb = sbt([64, NCHK, 64], f"Atb_h{h}", BF16)

            def gen(ob, t_ob):
                    gh = gt[:, :, h]
                    bh = beta[:, :, h]
                    qT = qk[:, h, :].rearrange("p (c f) -> p c f", f=CH)
                    kT = qk[:, 2 + h, :].rearrange("p (c f) -> p c f", f=CH)
                    t_q, t_k, t_v = t_qk[h], t_qk[2 + h], t_cv[4 + h]
                    bk, tb = bank_mm(64, [(0, NCHK, Um, gh)], [tc, t_g])
                    A("dve", lambda e, bk=bk: e.tensor_copy(out=gc_col[:], in_=bk[0:64, 0:NCHK]), reads=[tb], writes=[t_gcc])
                    A("dve", lambda e, gh=gh: e.tensor_tensor(out=Ug[:], in0=bc_mid(Um, NCHK), in1=bc_last(gh, 64), op=ALU.mult),
                      reads=[tc, t_g], writes=[t_Ug])
                    bk, tb = bank_mm(128, [(c * 64, 64, ones_f[:, :], Ug[:, c, :]) for c in range(NCHK)], [tc, t_Ug])
                    A("act", lambda e, bk=bk: e.copy(out=gcrow[:].rearrange("p c f -> p (c f)"), in_=bk[:, :]),
                      reads=[tb], writes=[t_gcr])
                    yield
                    A("dve", lambda e: e.tensor_tensor(out=dl[:], in0=gcrow[0:64, :, :], in1=bc_last(gc_col[:, :], 64),
                                                       op=ALU.subtract), reads=[t_gcr, t_gcc], writes=[t_dl])
                    A("dve", lambda e: e.tensor_tensor(out=tmpd[:], in0=dl[:], in1=bc_mid(BUm, NCHK), op=ALU.add),
                      reads=[t_dl, tc], writes=[t_tmpd])
                    A("act", lambda e: e.activation(out=decay[:], in_=tmpd[:], func=AF.Exp, scale=-1.0),
                      reads=[t_tmpd], writes=[t_dec])
                    A("dve", lambda e: e.tensor_tensor(out=tmpd[:], in0=dl[:], in1=bc_mid(BLm, NCHK), op=ALU.subtract),
                      reads=[t_dl, tc, t_dec], writes=[t_tmpd])
                    A("act", lambda e: e.activation(out=decayT[:], in_=tmpd[:], func=AF.Exp), reads=[t_tmpd], writes=[t_decT])
                    A("act", lambda e: e.activation(out=egrow[:], in_=gcrow[:], func=AF.Exp), reads=[t_gcr], writes=[t_egr])
                    A("act", lambda e: e.activation(out=kgs[:], in_=dl[:, :, 63], func=AF.Exp), reads=[t_dl], writes=[t_kgs])
                    A("act", lambda e: e.activation(out=egc[:], in_=gc_col[:], func=AF.Exp), reads=[t_gcc], writes=[t_egc])
                    yield
                    for (src, t_src, dst, t_dst, isbf) in ((kT, t_k, Ktok, t_Kt, True),
                                                           (cv[:, 4 + h, :].rearrange("p (c f) -> p c f", f=CH), t_v, Vtok, t_Vt, False)):
                        for half in range(2):
                            bk, tb = cx.bank()
                            bkv = bk[:, :].bitcast(BF16) if isbf else bk[:, :]
                            idm = ident_bf[:] if isbf else ident
                            for c4 in range(4):
                                c = half * 4 + c4
                                A("pe", lambda e, bkv=bkv, c4=c4, c=c, src=src, idm=idm: e.transpose(
                                    out=bkv[0:64, c4 * 128:(c4 + 1) * 128], in_=src[:, c, :], identity=idm),
                                  reads=[t_src, tc], writes=[tb])
                            evac(cx, dst[:, half * 4:half * 4 + 4, :].rearrange("p c f -> p (c f)"), bkv[0:64, 0:512], [tb], [t_dst])
                    bk, tb = bank_mm(64, [(c * 64, 64, kT[:, c, :], kT[:, c, :]) for c in range(NCHK)], [t_k])
                    P0, t_P0 = Pb[0]
                    Pt0, t_Pt0 = Ptb[0]
                    A("dve", lambda e, bk=bk: e.tensor_tensor(out=tmpd[:].rearrange("p c f -> p (c f)"), in0=bk[0:64, :],
                                                              in1=decay[:].rearrange("p c f -> p (c f)"), op=ALU.mult),
                      reads=[tb, t_dec], writes=[t_tmpd])
                    A("dve", lambda e, bh=bh: e.tensor_tensor(out=tmpd[:], in0=tmpd[:], in1=bc_last(bh, 64), op=ALU.mult),
                      reads=[t_tmpd, t_beta], writes=[t_tmpd])
                    A("dve", lambda e, P0=P0: e.tensor_tensor(out=P0[:], in0=tmpd[:], in1=bc_mid(SLm, NCHK), op=ALU.mult),
                      reads=[t_tmpd, tc], writes=[t_P0])
                    yield
                    bk, tb = cx.bank()
                    bkv = bk[:, :].bitcast(BF16)
                    for c in range(NCHK):
                        A("pe", lambda e, bkv=bkv, c=c, P0=P0: e.transpose(out=bkv[0:64, c * 64:(c + 1) * 64], in_=P0[:, c, :],
                                                                         identity=id64_bf), reads=[t_P0, tc], writes=[tb])
                    A("act", lambda e, bkv=bkv, Pt0=Pt0: e.copy(out=Pt0[:].rearrange("p c f -> p (c f)"), in_=bkv[0:64, 0:512]),
                      reads=[tb], writes=[t_Pt0])
                    A("dve", lambda e, Pt0=Pt0: e.tensor_tensor(out=At[:], in0=bc_mid(id64, NCHK), in1=Pt0[:], op=ALU.subtract),
                      reads=[t_Pt0, tc], writes=[t_At])
                    A("act", lambda e: e.copy(out=Atb[:], in_=At[:]), reads=[t_At], writes=[t_Atb])
                    yield
                    for lv in range(5):
                        Pc, t_Pc = Pb[lv % 2]
                        Ptc, t_Ptc = Ptb[lv % 2]
                        Pn, t_Pn = Pb[(lv + 1) % 2]
                        Ptn, t_Ptn = Ptb[(lv + 1) % 2]
                        bk, tb = bank_mm(64, [(c * 64, 64, Ptc[:, c, :], Pc[:, c, :]) for c in range(NCHK)], [t_Pc, t_Ptc])
                        A("dve", lambda e, bk=bk, Pn=Pn: e.tensor_copy(out=Pn[:].rearrange("p c f -> p (c f)"), in_=bk[0:64, :]),
                          reads=[tb], writes=[t_Pn])
                        bk, tb = bank_mm(64, [(c * 64, 64, Pc[:, c, :], Ptc[:, c, :]) for c in range(NCHK)], [t_Pc, t_Ptc])
                        A("act", lambda e, bk=bk, Ptn=Ptn: e.copy(out=Ptn[:].rearrange("p c f -> p (c f)"), in_=bk[0:64, :]),
                          reads=[tb], writes=[t_Ptn])
                        bk, tb = bank_mm(64, [(c * 64, 64, Pn[:, c, :], Atb[:, c, :]) for c in range(NCHK)], [t_Pn, t_Atb])
                        A("dve", lambda e, bk=bk: e.tensor_tensor(out=At[:].rearrange("p c f -> p (c f)"),
                                                                  in0=At[:].rearrange("p c f -> p (c f)"), in1=bk[0:64, :],
                                                                  op=ALU.add), reads=[tb, t_At], writes=[t_At])
                        A("act", lambda e: e.copy(out=Atb[:], in_=At[:]), reads=[t_At], writes=[t_Atb])
                        yield
                    A("dve", lambda e: e.tensor_tensor(out=kg[:], in0=Ktok[:], in1=bc_last(kgs[:, :], 128), op=ALU.mult),
                      reads=[t_Kt, t_kgs], writes=[t_kg])
                    A("dve", lambda e, bh=bh: e.tensor_tensor(out=Vb[:], in0=Vtok[:], in1=bc_last(bh, 128), op=ALU.mult),
                      reads=[t_Vt, t_beta], writes=[t_Vb])
                    A("dve", lambda e, bh=bh: e.tensor_tensor(out=be[:], in0=egc[:], in1=bh, op=ALU.mult),
                      reads=[t_egc, t_beta], writes=[t_be])
                    A("dve", lambda e: e.tensor_tensor(out=Kbg[:], in0=Ktok[:], in1=bc_last(be[:, :], 128), op=ALU.mult),
                      reads=[t_Kt, t_be], writes=[t_Kbg])
                    for half in range(2):
                        bk, tb = bank_mm(64, [(c4 * 128, 128, Atb[:, half * 4 + c4, :], Vb[:, half * 4 + c4, :]) for c4 in range(4)],
                                         [t_Atb, t_Vb])
                        evac(cx, u_sb[:, half * 4:half * 4 + 4, :].rearrange("p c f -> p (c f)"), bk[0:64, :], [tb], [t_usb])
                    bk, tb = bank_mm(128, [(c * 64, 64, Kbg[:, c, :], Atb[:, c, :]) for c in range(NCHK)], [t_Atb, t_Kbg])
                    evac(cx, wT[:].rearrange("p c f -> p (c f)"), bk[:, :], [tb], [t_wT])
                    yield
                    bk, tb = bank_mm(64, [(c * 64, 64, kT[:, c, :], qT[:, c, :]) for c in range(NCHK)], [t_k, t_q])
                    A("dve", lambda e, bk=bk: e.tensor_tensor(out=attnT[:].rearrange("p c f -> p (c f)"), in0=bk[0:64, :],
                                                              in1=decayT[:].rearrange("p c f -> p (c f)"), op=ALU.mult),
                      reads=[tb, t_decT], writes=[t_att])
                    A("dve", lambda e, qT=qT: e.tensor_tensor(out=qgT[:], in0=qT, in1=egrow[:], op=ALU.mult),
                      reads=[t_q, t_egr], writes=[t_qg])
                    yield
                    obk, t_obk = cx.bank()
                    cx.reserved.add(id(obk))
                    for c in range(NCHK):
                        Sc, t_Sc = Sst[s_par[0]]
                        Sn, t_Sn = Sst[1 - s_par[0]]
                        Sbc, t_Sbc = Ssb[s_par[0]]
                        Sbn, t_Sbn = Ssb[1 - s_par[0]]
                        s_par[0] = 1 - s_par[0]
                        vn, t_vn = vnew[vn_rr[0] % 2]
                        vn_rr[0] += 1
                        bk, tb = bank_mm(64, [(0, 128, wT[:, c, :], Sbc[:, :])], [t_wT, t_Sbc])
                        A("dve", lambda e, bk=bk, c=c, vn=vn: e.tensor_tensor(out=vn[:], in0=u_sb[:, c, :], in1=bk[0:64, 0:128],
                                                                             op=ALU.subtract), reads=[tb, t_usb], writes=[t_vn])
                        A("pe", lambda e, c=c, Sbc=Sbc, obk=obk: e.matmul(obk[:, c * 64:(c + 1) * 64], lhsT=Sbc[:, :], rhs=qgT[:, c, :],
                                                                         start=True, stop=False), reads=[t_Sbc, t_qg], writes=[t_obk])
                        A("pe", lambda e, c=c, vn=vn, obk=obk: e.matmul(obk[:, c * 64:(c + 1) * 64], lhsT=vn[:, :], rhs=attnT[:, c, :],
                                                                       start=False, stop=True), reads=[t_vn, t_att], writes=[t_obk])
                        bk, tb = bank_mm(128, [(0, 128, kg[:, c, :], vn[:, :])], [t_kg, t_vn])
                        A("dve", lambda e, bk=bk, c=c, Sc=Sc, Sn=Sn: e.scalar_tensor_tensor(
                            out=Sn[:], in0=Sc[:], scalar=egrow[:, c, 63:64], in1=bk[:, 0:128], op0=ALU.mult, op1=ALU.add),
                          reads=[tb, t_Sc, t_egr], writes=[t_Sn])
                        A("act", lambda e, Sn=Sn, Sbn=Sbn: e.copy(out=Sbn[:], in_=Sn[:]), reads=[t_Sn], writes=[t_Sbn])
                        yield
                    A("act", lambda e, obk=obk: e.copy(out=o_sb[:], in_=obk[:, :]), reads=[t_obk], writes=[t_osb])
                    cx.reserved.discard(id(obk))
                    A("act", lambda e: e.activation(out=sqb[:], in_=o_sb[:], func=AF.Square), reads=[t_osb], writes=[t_sqb])
                    bk, tb = cx.bank()
                    A("pe", lambda e, bk=bk: e.matmul(bk[:, :], lhsT=cx.ones_bf[:], rhs=sqb[:], start=True, stop=True),
                      reads=[t_sqb, tc], writes=[tb])
                    A("act", lambda e, bk=bk: e.activation(out=rstd[:], in_=bk[:, :], func=AF.Ln, bias=EPS, scale=1.0 / HD),
                      reads=[tb], writes=[t_rstd])
                    A("act", lambda e: e.activation(out=rstd[:], in_=rstd[:], func=AF.Exp, scale=-0.5), reads=[t_rstd], writes=[t_rstd])
                    A("dve", lambda e: e.scalar_tensor_tensor(out=o_tmp[:], in0=o_sb[:], scalar=gon[:, 0:1], in1=rstd[:],
                                                              op0=ALU.mult, op1=ALU.mult),
                      reads=[t_osb, t_gon, t_rstd], writes=[t_otmp])
                    A("dve", lambda e, h=h, ob=ob: e.tensor_tensor(out=ob[:, h, :], in0=o_tmp[:], in1=siluz[:, h, :], op=ALU.mult),
                      reads=[t_otmp, t_sz[h]], writes=[t_ob])

            return gen

        o_sb, t_osb = sbt([128, T], "o_sb")
        o_tmp, t_otmp = sbt([128, T], "o_tmp")
        outb = [sbt([128, 2, T], f"outb{i}", BF16) for i in range(2)]
        ov = odnT.rearrange("(h p) t -> p h t", p=128)
        head_fns = [make_head(0), make_head(1)]

        def bank_mm(nparts, items, reads, name=None):
            bk, tb = cx.bank()
            for (c0, ncol, lh, rh) in items:
                A("pe", lambda e, bk=bk, c0=c0, ncol=ncol, lh=lh, rh=rh: e.matmul(
                    bk[0:nparts, c0:c0 + ncol], lhsT=lh, rhs=rh, start=True, stop=True), reads=reads, writes=[tb])
            return bk, tb

        for it in range(ntiles):
            t0 = it * T
            ut, t_u = u_t[0], t_ut[0]
            A("sp", lambda e, ut=ut, t0=t0: e.dma_start(out=ut[:], in_=uT_all[:, t0:t0 + T].rearrange("(k p) t -> p k t", p=128)),
              writes=[t_u], dma_tok=t_u)
            for c in range(8):
                bk, tb = cx.bank()
                for k in range(KC):
                    A("pe", lambda e, bk=bk, c=c, k=k, ut=ut: e.matmul(bk[:, :], lhsT=wres[:, k, c * 128:(c + 1) * 128],
                                                                      rhs=ut[:, k, :], start=(k == 0), stop=(k == KC - 1)),
                      reads=[t_w, t_u], writes=[tb])
                if c < 6:
                    evac(cx, rawbuf[:, c, 3:T + 3], bk[:, :], [tb], [t_raw[c]])
                else:
                    A("act", lambda e, bk=bk, c=c: e.activation(out=siluz[:, c - 6, :], in_=bk[:, :], func=AF.Silu),
                      reads=[tb], writes=[t_sz[c - 6]])
            for c in range(6):
                A("dve", lambda e, c=c: e.tensor_scalar(out=cv[:, c, :], in0=rawbuf[:, c, 0:T],
                                                        scalar1=convw[:, c * 4:c * 4 + 1], scalar2=None, op0=ALU.mult),
                  reads=[t_raw[c], t_cw], writes=[t_cv[c]])
                for j in range(1, 4):
                    A("dve", lambda e, c=c, j=j: e.scalar_tensor_tensor(out=cv[:, c, :], in0=rawbuf[:, c, j:j + T],
                                                                        scalar=convw[:, c * 4 + j:c * 4 + j + 1],
                                                                        in1=cv[:, c, :], op0=ALU.mult, op1=ALU.add),
                      reads=[t_raw[c], t_cw, t_cv[c]], writes=[t_cv[c]])
                A("act", lambda e, c=c: e.copy(out=rawbuf[:, c, 0:3], in_=rawbuf[:, c, T:T + 3]),
                  reads=[t_raw[c]], writes=[t_raw[c]])
                A("act", lambda e, c=c: e.activation(out=cv[:, c, :], in_=cv[:, c, :], func=AF.Silu),
                  reads=[t_cv[c]], writes=[t_cv[c]])
            for idx in range(4):
                A("act", lambda e, idx=idx: e.activation(out=sqb[:], in_=cv[:, idx, :], func=AF.Square),
                  reads=[t_cv[idx]], writes=[t_sqb])
                bk, tb = cx.bank()
                A("pe", lambda e, bk=bk: e.matmul(bk[:, :], lhsT=cx.ones_bf[:], rhs=sqb[:], start=True, stop=True),
                  reads=[t_sqb, tc], writes=[tb])
                A("act", lambda e, bk=bk: e.activation(out=rstd[:], in_=bk[:, :], func=AF.Ln, bias=1e-6, scale=1.0),
                  reads=[tb], writes=[t_rstd])
                A("act", lambda e: e.activation(out=rstd[:], in_=rstd[:], func=AF.Exp, scale=-0.5), reads=[t_rstd], writes=[t_rstd])
                sc = float(HD ** -0.5) if idx < 2 else 1.0
                A("dve", lambda e, idx=idx, sc=sc: e.scalar_tensor_tensor(out=qk[:, idx, :], in0=cv[:, idx, :], scalar=sc,
                                                                          in1=rstd[:], op0=ALU.mult, op1=ALU.mult),
                  reads=[t_cv[idx], t_rstd], writes=[t_qk[idx]])
            bk, tb = cx.bank()
            for ck in range(NCHK):
                for k in range(KC):
                    A("pe", lambda e, bk=bk, ck=ck, k=k, ut=ut: e.matmul(bk[0:64, ck * 4:ck * 4 + 4],
                                                                        lhsT=ut[:, k, ck * CH:(ck + 1) * CH],
                                                                        rhs=wbar[:, k, :], start=(k == 0), stop=(k == KC - 1)),
                      reads=[t_wb, t_u], writes=[tb])
            A("dve", lambda e, bk=bk: e.tensor_copy(out=ba[:].rearrange("p c f -> p (c f)"), in_=bk[0:64, 0:NCHK * 4]),
              reads=[tb], writes=[t_ba])
            A("act", lambda e: e.activation(out=beta[:], in_=ba[:, :, 0:2], func=AF.Sigmoid), reads=[t_ba], writes=[t_beta])
            A("dve", lambda e: e.tensor_tensor(out=spx[:], in0=ba[:, :, 2:4], in1=bc_mid(hv[:, 0:2], NCHK), op=ALU.add),
              reads=[t_ba, t_hv], writes=[t_spx])
            A("act", lambda e: e.activation(out=spx[:], in_=spx[:], func=AF.Exp), reads=[t_spx], writes=[t_spx])
            A("dve", lambda e: e.tensor_scalar(out=spx[:], in0=spx[:], scalar1=1.0, scalar2=None, op0=ALU.add),
              reads=[t_spx], writes=[t_spx])
            A("act", lambda e: e.activation(out=spx[:], in_=spx[:], func=AF.Ln), reads=[t_spx], writes=[t_spx])
            A("dve", lambda e: e.scalar_tensor_tensor(out=gt[:], in0=spx[:], scalar=-1.0, in1=bc_mid(expA[:, 0:2], NCHK),
                                                      op0=ALU.mult, op1=ALU.mult), reads=[t_spx, t_hv], writes=[t_g])
            ob, t_ob = outb[it % 2]
            gens = [head_fns[h](ob, t_ob) for h in range(2)]
            while gens:
                for g in list(gens):
                    try:
                        next(g)
                    except StopIteration:
                        gens.remove(g)
            A("sp", lambda e, ob=ob, t0=t0: e.dma_start(out=ov[:, :, t0:t0 + T], in_=ob[:]), reads=[t_ob], dma_tok=t_ob)
        R.emit()
    return nc


SB_ONE = 128
SB_NM0 = 129
SB_VM0 = 129 + 4 * 512
SB_TOT = 129 + 8 * 512


def sb_consts():
    c = np.zeros((128, SB_TOT), np.float32)
    c[:, 0:128] = np.eye(128, dtype=np.float32)
    t = np.arange(128)[:, None]
    j = np.arange(128)[None, :]
    tri = (t + j < 128).astype(np.float32)
    for q4 in range(4):
        for b in range(4):
            blk = c[:, SB_NM0 + q4 * 512 + b * 128: SB_NM0 + q4 * 512 + (b + 1) * 128]
            if 3 - b > q4:
                blk[:] = 1.0
            elif 3 - b == q4:
                blk[:] = tri
    c[:, SB_VM0:SB_VM0 + 4 * 512] = 1.0 - c[:, SB_NM0:SB_NM0 + 4 * 512]
    c[:, SB_ONE] = 1.0
    return c


def build_stage2_sb(ntiles=SEQ // T, heads=(0, 1)):
    S = ntiles * T
    NB = S // 128
    nc = bass.Bass("TRN2", target_bir_lowering=False)
    uT_all = nc.dram_tensor("uT_all", [D, S], BF16, kind="ExternalInput").ap()
    uT_rev = nc.dram_tensor("uT_rev", [D, S], BF16, kind="ExternalInput").ap()
    wsb = nc.dram_tensor("wsb", [D, 768], F32, kind="ExternalInput").ap()
    consts_d = nc.dram_tensor("consts", [128, SB_TOT], F32, kind="ExternalInput").ap()
    osbT = nc.dram_tensor("osbT", [256, S], BF16, kind="ExternalOutput").ap()
    with contextlib.ExitStack() as stack:
        cx = Ctx(nc, stack)
        R = cx.R
        A = R.add
        tc = cx.t_const
        consts = R.sb([128, 129], F32, "consts")
        A("sp", lambda e: e.dma_start(out=consts[:], in_=consts_d[:, 0:129]), writes=[tc], dma_tok=tc)
        ones_c = consts[:, 128:129]
        masks = R.sb([128, 8 * 512], BF16, "masks")
        t_mk = R.tok("masks")
        A("pool", lambda e: e.dma_start(out=masks[:], in_=consts_d[:, SB_NM0:SB_TOT]), writes=[t_mk], dma_tok=t_mk)
        mneg = R.sb([128, 4 * 512], BF16, "mneg")
        t_mneg = R.tok("mneg")
        A("pool", lambda e: e.tensor_scalar(out=mneg[:], in0=masks[:, 0:4 * 512], scalar1=-1e4, scalar2=None, op0=ALU.mult),
          reads=[t_mk], writes=[t_mneg])
        ident_bf = R.sb([128, 128], BF16, "ident_bf")
        A("dve", lambda e: e.tensor_copy(out=ident_bf[:], in_=consts[:, 0:128]), reads=[tc], writes=[tc])
        ones_f = ones_c.to_broadcast([128, T])
        wres = R.sb([128, KC, 768], BF16, "wres")
        t_w = R.tok("wres")
        wv = wsb.rearrange("(k p) n -> p k n", p=128)
        for half in range(2):
            A("pool", lambda e, half=half: e.dma_start(out=wres[:, :, half * 384:(half + 1) * 384],
                                                      in_=wv[:, :, half * 384:(half + 1) * 384]),
              writes=[t_w], dma_tok=t_w)
        uf, tuf = R.sb([128, KC, T], BF16, "u_f"), R.tok("u_f")
        ur, tur = R.sb([128, KC, T], BF16, "u_r"), R.tok("u_r")
        QT = R.sb([128, S], BF16, "QT")
        KTr = R.sb([128, S], BF16, "KTr")
        dVr = R.sb([128, NB, 128], BF16, "dVr")
        t_Q = [R.tok(f"Q{i}") for i in range(ntiles)]
        t_K = [R.tok(f"K{i}") for i in range(ntiles)]
        t_V = [R.tok(f"V{i}") for i in range(ntiles)]
        VTb = [R.sb([128, T + 1], F32, f"VTb{i}") for i in range(2)]
        t_VT = [R.tok(f"VTb{i}") for i in range(2)]
        dVT = R.sb([128, T], BF16, "dVT")
        t_dVT = R.tok("dVT")
        smb = [[R.sb([128, T], F32, f"smb{i}_{k}") for k in range(2)] for i in range(4)]
        t_sm = [[R.tok(f"smb{i}_{k}") for k in range(2)] for i in range(4)]
        Pb = [[R.sb([128, T], BF16, f"Pb{i}_{k}") for k in range(2)] for i in range(4)]
        t_P = [[R.tok(f"Pb{i}_{k}") for k in range(2)] for i in range(4)]
        Pm = [R.sb([128, T], BF16, f"Pm{i}") for i in range(4)]
        t_Pm = [R.tok(f"Pm{i}") for i in range(4)]
        vsh, t_vsh = R.sb([128, T], F32, "vsh"), R.tok("vsh")
        ATs = R.sb([128, 4, T], BF16, "ATs")
        t_AT = [R.tok(f"ATs_{hh}") for hh in range(2)]
        osb, t_os = R.sb([128, T], BF16, "osb"), R.tok("osb")
        scale = float(HD ** -0.5)
        uva = uT_all.rearrange("(k p) t -> p k t", p=128)
        uvr = uT_rev.rearrange("(k p) t -> p k t", p=128)
        gstep = 0
        for h in heads:
            for n, it in enumerate(reversed(range(ntiles))):
                t0 = it * T
                A("sp", lambda e, t0=t0: e.dma_start(out=uf[:], in_=uva[:, :, t0:t0 + T]), writes=[tuf], dma_tok=tuf)
                A("sp", lambda e, t0=t0: e.dma_start(out=ur[:], in_=uvr[:, :, t0:t0 + T]), writes=[tur], dma_tok=tur)
                for (src, tsrc, col, dst, tdst) in ((uf, tuf, h * 128, QT, t_Q[it]), (ur, tur, 256 + h * 128, KTr, t_K[it])):
                    bk, tb = cx.bank()
                    for k in range(KC):
                        A("pe", lambda e, bk=bk, k=k, src=src, col=col: e.matmul(bk[:, :], lhsT=wres[:, k, col:col + 128],
                                                                                rhs=src[:, k, :], start=(k == 0), stop=(k == KC - 1)),
                          reads=[t_w, tsrc], writes=[tb])
                    evac(cx, dst[:, t0:t0 + T], bk[:, :], [tb], [tdst])
                vt, tvt = VTb[n % 2], t_VT[n % 2]
                vtp, tvtp = VTb[(n + 1) % 2], t_VT[(n + 1) % 2]
                bk, tb = cx.bank()
                for k in range(KC):
                    A("pe", lambda e, bk=bk, k=k, h=h: e.matmul(bk[:, :], lhsT=wres[:, k, 512 + h * 128:512 + (h + 1) * 128],
                                                               rhs=ur[:, k, :], start=(k == 0), stop=(k == KC - 1)),
                      reads=[t_w, tur], writes=[tb])
                A("act", lambda e, bk=bk, vt=vt: e.copy(out=vt[:, 0:T], in_=bk[:, :]), reads=[tb], writes=[tvt])
                if n == 0:
                    A("pool", lambda e, vt=vt: e.memset(vt[:, T:T + 1], 0.0), reads=[tvt], writes=[tvt])
                else:
                    A("act", lambda e, vt=vt, vtp=vtp: e.copy(out=vt[:, T:T + 1], in_=vtp[:, 0:1]), reads=[tvtp, tvt], writes=[tvt])
                A("pool", lambda e, vt=vt: e.tensor_tensor(out=dVT[:], in0=vt[:, 1:T + 1], in1=vt[:, 0:T], op=ALU.subtract),
                  reads=[tvt], writes=[t_dVT])
                bk2, tb2 = cx.bank()
                bkb = bk2[:, :].bitcast(BF16)
                for b in range(4):
                    A("pe", lambda e, bkb=bkb, b=b: e.transpose(out=bkb[:, b * 128:(b + 1) * 128], in_=dVT[:, b * 128:(b + 1) * 128],
                                                               identity=ident_bf[:]), reads=[t_dVT, tc], writes=[tb2])
                evac(cx, dVr[:, it * 4:it * 4 + 4, :].rearrange("p b d -> p (b d)"), bkb[:, 0:T], [tb2], [t_V[it]])
            steps = [(qg, i) for qg in range(NB // 4) for i in range(qg + 1)]

            def issue_scores(qg, i, par):
                j0 = (NB - 4 - 4 * qg + 4 * i) * 128
                kt = j0 // T
                zb = []
                for q4 in range(4):
                    qb = qg * 4 + q4
                    bk, tb = cx.bank()
                    zb.append((bk, tb))
                    A("pe", lambda e, bk=bk, qb=qb, j0=j0, i=i: e.matmul(bk[:, :], lhsT=QT[:, qb * 128:(qb + 1) * 128],
                                                                        rhs=KTr[:, j0:j0 + T], start=True, stop=(i != 0)),
                      reads=[t_Q[qg], t_K[kt]], writes=[tb])
                    if i == 0:
                        A("pe", lambda e, bk=bk, q4=q4: e.matmul(bk[:, :], lhsT=ident_bf[:], rhs=mneg[:, q4 * 512:(q4 + 1) * 512],
                                                                start=False, stop=True), reads=[t_mneg, tc], writes=[tb])
                for q4 in range(4):
                    bk, tb = zb[q4]
                    sm, tsm = smb[q4][par], t_sm[q4][par]
                    A("act", lambda e, bk=bk, sm=sm: e.activation(out=sm[:], in_=bk[:, :], func=AF.Sigmoid, scale=-scale),
                      reads=[tb], writes=[tsm])

            issue_scores(steps[0][0], steps[0][1], gstep % 2)
            obk = t_obk = None
            for n, (qg, i) in enumerate(steps):
                par = gstep % 2
                gstep += 1
                if n + 1 < len(steps):
                    issue_scores(steps[n + 1][0], steps[n + 1][1], gstep % 2)
                ntile = qg + 1
                JB = NB - 4 - 4 * qg + 4 * i
                kt = JB // 4
                if i == 0:
                    obk, t_obk = cx.bank()
                    cx.reserved.add(id(obk))
                    tq0 = qg * T
                    if tq0 == 0:
                        A("pool", lambda e: e.memset(uf[:, :, 0:1], 0.0), writes=[tuf])
                        A("sp", lambda e: e.dma_start(out=uf[:, :, 1:T], in_=uva[:, :, 0:T - 1]), writes=[tuf], dma_tok=tuf)
                    else:
                        A("sp", lambda e, tq0=tq0: e.dma_start(out=uf[:], in_=uva[:, :, tq0 - 1:tq0 + T - 1]), writes=[tuf], dma_tok=tuf)
                for q4 in range(4):
                    sm, tsm = smb[q4][par], t_sm[q4][par]
                    Pc, tPc = Pb[q4][par], t_P[q4][par]
                    Pp, tPp = Pb[q4][1 - par], t_P[q4][1 - par]
                    if i == 0:
                        A("dve", lambda e, sm=sm, Pc=Pc: e.tensor_tensor_scan(out=Pc[:], data0=sm[:], data1=ones_f, initial=ones_c,
                                                                             op0=ALU.mult, op1=ALU.mult),
                          reads=[tsm, tc], writes=[tPc])
                    else:
                        A("dve", lambda e, sm=sm, Pc=Pc, Pp=Pp: e.tensor_tensor_scan(out=Pc[:], data0=sm[:], data1=ones_f,
                                                                                    initial=Pp[:, T - 1:T], op0=ALU.mult, op1=ALU.mult),
                          reads=[tsm, tPp, tc], writes=[tPc])
                if i == 0:
                    for q4 in range(4):
                        vm = masks[:, (4 + q4) * 512:(5 + q4) * 512]
                        A("pool", lambda e, q4=q4, vm=vm, par=par: e.tensor_tensor(out=Pm[q4][:], in0=Pb[q4][par][:], in1=vm, op=ALU.mult),
                          reads=[t_P[q4][par], t_mk], writes=[t_Pm[q4]])
                for hh in range(2):
                    bk2, tb2 = cx.bank()
                    bkb = bk2[:, :].bitcast(BF16)
                    for bb in range(2):
                        b = hh * 2 + bb
                        for q4 in range(4):
                            Pc, tPc = (Pm[q4], t_Pm[q4]) if i == 0 else (Pb[q4][par], t_P[q4][par])
                            A("pe", lambda e, bkb=bkb, b=b, bb=bb, q4=q4, Pc=Pc: e.transpose(
                                out=bkb[:, bb * 512 + q4 * 128:bb * 512 + (q4 + 1) * 128], in_=Pc[:, b * 128:(b + 1) * 128],
                                identity=ident_bf[:]), reads=[tPc, tc], writes=[tb2])
                    A("act", lambda e, bkb=bkb, hh=hh: e.copy(out=ATs[:, hh * 2:hh * 2 + 2, :].rearrange("p b q -> p (b q)"),
                                                             in_=bkb[:, :]), reads=[tb2], writes=[t_AT[hh]])
                for b in range(4):
                    first = (i == 0 and b == 0)
                    last = (i == ntile - 1 and b == 3)
                    A("pe", lambda e, obk=obk, b=b, JB=JB, first=first, last=last: e.matmul(
                        obk[:, :], lhsT=dVr[:, JB + b, :], rhs=ATs[:, b, :], start=first, stop=last),
                      reads=[t_AT[b // 2], t_V[kt]], writes=[t_obk])
                if i == ntile - 1:
                    bkv, tbv = cx.bank()
                    for k in range(KC):
                        A("pe", lambda e, bkv=bkv, k=k, h=h: e.matmul(bkv[:, :], lhsT=wres[:, k, 512 + h * 128:512 + (h + 1) * 128],
                                                                     rhs=uf[:, k, :], start=(k == 0), stop=(k == KC - 1)),
                          reads=[t_w, tuf], writes=[tbv])
                    A("act", lambda e, bkv=bkv: e.copy(out=vsh[:], in_=bkv[:, :]), reads=[tbv], writes=[t_vsh])
                    A("dve", lambda e, obk=obk: e.tensor_tensor(out=osb[:], in0=obk[:, :], in1=vsh[:], op=ALU.add),
                      reads=[t_obk, t_vsh], writes=[t_os])
                    cx.reserved.discard(id(obk))
                    A("sp", lambda e, qg=qg, h=h: e.dma_start(out=osbT[h * 128:(h + 1) * 128, qg * T:(qg + 1) * T], in_=osb[:]),
                      reads=[t_os], dma_tok=t_os)
        R.emit()
    return nc


def linear_fm(cx, ws, w_dram, kch, xT, t_x, col0, nout, consume):
    R = cx.R
    for c4 in range(0, nout, 4):
        n = min(4, nout - c4)
        wt, tw = ws.load(w_dram, 0, kch, col0 + c4 * 128, n * 128)
        for j in range(n):
            bk, tb = cx.bank()
            for k in range(kch):
                R.add("pe", lambda e, bk=bk, k=k, j=j, wt=wt: e.matmul(bk[:, :], lhsT=wt[:, k, j * 128:(j + 1) * 128],
                                                                      rhs=xT[:, k, :], start=(k == 0), stop=(k == kch - 1)),
                      reads=[tw, t_x], writes=[tb])
            consume(c4 + j, bk, tb)


def build_stage3(ntiles=TOK // T):
    nc = bass.Bass("TRN2", target_bir_lowering=False)
    NTK = ntiles * T
    h1T = nc.dram_tensor("h1T", [D, NTK], F32, kind="ExternalInput").ap()
    uT_d = nc.dram_tensor("uT", [D, NTK], BF16, kind="ExternalInput").ap()
    odn_d = nc.dram_tensor("odnT", [D, NTK], BF16, kind="ExternalInput").ap()
    osb_d = nc.dram_tensor("osbT", [D, NTK], BF16, kind="ExternalInput").ap()
    pT_d = nc.dram_tensor("pT", [PLE, NTK], F32, kind="ExternalInput").ap()
    wgate = nc.dram_tensor("wgate", [D, 2 * D], F32, kind="ExternalInput").ap()
    wbd = nc.dram_tensor("wbd", [D, D], F32, kind="ExternalInput").ap()
    wbs = nc.dram_tensor("wbs", [D, D], F32, kind="ExternalInput").ap()
    wout = nc.dram_tensor("wout", [D, D], F32, kind="ExternalInput").ap()
    wg = nc.dram_tensor("wg", [D, DFF], F32, kind="ExternalInput").ap()
    wu = nc.dram_tensor("wu", [D, DFF], F32, kind="ExternalInput").ap()
    wd = nc.dram_tensor("wd", [DFF, D], F32, kind="ExternalInput").ap()
    wpg = nc.dram_tensor("wpg", [D, D], F32, kind="ExternalInput").ap()
    wpp = nc.dram_tensor("wpp", [PLE, D], F32, kind="ExternalInput").ap()
    gains = nc.dram_tensor("gains", [128, 5 * KC], F32, kind="ExternalInput").ap()
    ident_d = nc.dram_tensor("ident", [128, 128], F32, kind="ExternalInput").ap()
    out = nc.dram_tensor("out", [NTK, D], F32, kind="ExternalOutput").ap()
    with contextlib.ExitStack() as stack:
        cx = Ctx(nc, stack)
        R = cx.R
        A = R.add
        g_all, t_g = load_vec_fm(cx, gains, "gains_sb")
        ident = R.sb([128, 128], F32, "ident")
        A("sp", lambda e: e.dma_start(out=ident[:], in_=ident_d[:, :]), writes=[cx.t_const], dma_tok=cx.t_const)
        big = R.sb([128, KC * T], F32, "big")
        otok = big[:].rearrange("p (s d) -> p s d", s=4)
        fT = big[:].rearrange("p (c t) -> p c t", c=KC)
        t_f = R.tok("fT")
        t_otok = [t_f] * 4
        hT = R.sb([128, KC, T], F32, "hT")
        t_h = R.tok("hT")
        uT = R.sb([128, KC, T], BF16, "uT")
        t_u = R.tok("uT")
        sq = R.sb([128, KC, T], BF16, "sq")
        t_sq = R.tok("sq")
        rstd = R.sb([128, T], F32, "rstd")
        t_rstd = R.tok("rstd")
        actT = R.sb([128, DFF // 128, T], BF16, "actT")
        t_act = R.tok("actT")
        odn = actT[:, 0:KC, :]
        osb = actT[:, KC:2 * KC, :]
        tmp = [R.sb([128, T], F32, f"tmp{i}") for i in range(2)]
        t_tmp = [R.tok(f"tmp{i}") for i in range(2)]
        pTb = R.sb([128, 2, T], BF16, "pTb")
        t_p = R.tok("pTb")
        ws = WStream(cx, KC, 3, "w")
        hv = h1T.rearrange("(c p) t -> p c t", p=128)
        uv = uT_d.rearrange("(c p) t -> p c t", p=128)
        dnv = odn_d.rearrange("(c p) t -> p c t", p=128)
        sbv = osb_d.rearrange("(c p) t -> p c t", p=128)
        pv = pT_d.rearrange("(c p) t -> p c t", p=128)
        for it in range(ntiles):
            r0 = it * T
            A("sp", lambda e, r0=r0: e.dma_start(out=hT[:], in_=hv[:, :, r0:r0 + T]), writes=[t_h], dma_tok=t_h)
            A("sp", lambda e, r0=r0: e.dma_start(out=uT[:], in_=uv[:, :, r0:r0 + T]), writes=[t_u], dma_tok=t_u)
            t_dn = R.tok("odn_ld")
            A("sp", lambda e, r0=r0: e.dma_start(out=odn, in_=dnv[:, :, r0:r0 + T]), writes=[t_act], dma_tok=t_dn)
            A("sp", lambda e, r0=r0: e.dma_start(out=osb, in_=sbv[:, :, r0:r0 + T]), writes=[t_act], dma_tok=t_dn)
            A("pool", lambda e, r0=r0: e.dma_start(out=pTb[:], in_=pv[:, :, r0:r0 + T]), writes=[t_p], dma_tok=t_p)
            for (gcol, wb, xs, tx, second) in ((0, wbd, odn, t_act, False), (D, wbs, osb, t_act, True)):
                for c4 in range(0, KC, 4):
                    gtile, tgw = ws.load(wgate, 0, KC, gcol + c4 * 128, 512)
                    btile, tbw = ws.load(wb, 0, KC, c4 * 128, 512)
                    for j in range(4):
                        c = c4 + j
                        bg, tbg = cx.bank()
                        for k in range(KC):
                            A("pe", lambda e, bg=bg, k=k, j=j, gtile=gtile: e.matmul(
                                bg[:, :], lhsT=gtile[:, k, j * 128:(j + 1) * 128], rhs=uT[:, k, :],
                                start=(k == 0), stop=(k == KC - 1)), reads=[tgw, t_u], writes=[tbg])
                        bb, tbb = cx.bank()
                        for k in range(KC):
                            A("pe", lambda e, bb=bb, k=k, j=j, btile=btile, xs=xs: e.matmul(
                                bb[:, :], lhsT=btile[:, k, j * 128:(j + 1) * 128], rhs=xs[:, k, :],
                                start=(k == 0), stop=(k == KC - 1)), reads=[tbw, tx], writes=[tbb])
                        tt, tm = t_tmp[c % 2], tmp[c % 2]
                        A("act", lambda e, bg=bg, tm=tm: e.activation(out=tm[:], in_=bg[:, :], func=AF.Sigmoid),
                          reads=[tbg], writes=[tt])
                        if not second:
                            A("dve", lambda e, bb=bb, tm=tm, c=c: e.tensor_tensor(out=fT[:, c, :], in0=tm[:], in1=bb[:, :],
                                                                                 op=ALU.mult), reads=[tt, tbb], writes=[t_f])
                        else:
                            A("dve", lambda e, bb=bb, tm=tm: e.tensor_tensor(out=tm[:], in0=tm[:], in1=bb[:, :], op=ALU.mult),
                              reads=[tt, tbb], writes=[tt])
                            A("dve", lambda e, tm=tm, c=c: e.tensor_tensor(out=sq[:, c, :], in0=tm[:], in1=fT[:, c, :],
                                                                          op=ALU.add), reads=[tt, t_f], writes=[t_sq])
            linear_fm(cx, ws, wout, KC, sq, t_sq, 0, KC,
                      lambda c, bk, tb: evac(cx, fT[:, c, :], bk[:, :], [tb], [t_f]))
            resid_norm_add(cx, fT, t_f, g_all[:, 0:KC], t_g, hT, t_h, sq, t_sq, rstd, t_rstd, 1.0)
            rms_fm(cx, hT, t_h, KC, g_all[:, KC:2 * KC], t_g, uT, t_u, sq, t_sq, rstd, t_rstd, D)
            ffn_fm(cx, uT, t_u, wg, wu, wd, actT, t_act, ws, fT, t_f, tmp, t_tmp)
            resid_norm_add(cx, fT, t_f, g_all[:, 2 * KC:3 * KC], t_g, hT, t_h, sq, t_sq, rstd, t_rstd, 0.5)
            rms_fm(cx, hT, t_h, KC, g_all[:, 3 * KC:4 * KC], t_g, uT, t_u, sq, t_sq, rstd, t_rstd, D)
            for c4 in range(0, KC, 4):
                gtile, tgw = ws.load(wpg, 0, KC, c4 * 128, 512)
                ptile, tpw = ws.load(wpp, 0, 2, c4 * 128, 512)
                for j in range(4):
                    c = c4 + j
                    bg, tbg = cx.bank()
                    for k in range(KC):
                        A("pe", lambda e, bg=bg, k=k, j=j, gtile=gtile: e.matmul(
                            bg[:, :], lhsT=gtile[:, k, j * 128:(j + 1) * 128], rhs=uT[:, k, :],
                            start=(k == 0), stop=(k == KC - 1)), reads=[tgw, t_u], writes=[tbg])
                    bp, tbp = cx.bank()
                    for k in range(2):
                        A("pe", lambda e, bp=bp, k=k, j=j, ptile=ptile: e.matmul(
                            bp[:, :], lhsT=ptile[:, k, j * 128:(j + 1) * 128], rhs=pTb[:, k, :],
                            start=(k == 0), stop=(k == 1)), reads=[tpw, t_p], writes=[tbp])
                    tt, tm = t_tmp[c % 2], tmp[c % 2]
                    A("act", lambda e, bg=bg, tm=tm: e.activation(out=tm[:], in_=bg[:, :], func=AF.Sigmoid),
                      reads=[tbg], writes=[tt])
                    A("dve", lambda e, bp=bp, tm=tm, c=c: e.tensor_tensor(out=fT[:, c, :], in0=tm[:], in1=bp[:, :], op=ALU.mult),
                      reads=[tt, tbp], writes=[t_f])
            resid_norm_add(cx, fT, t_f, g_all[:, 4 * KC:5 * KC], t_g, hT, t_h, sq, t_sq, rstd, t_rstd, 1.0)
            transpose_out(cx, hT, t_h, KC, out, r0, otok, t_otok, ident)
        R.emit()
    return nc


_PROGS = {}


def _prog(name, fn):
    if name not in _PROGS:
        _PROGS[name] = fn()
    return _PROGS[name]


def _run(nc, maps):
    import sys, time
    t0 = time.time()
    res = run_bass_kernel_spmd(nc, maps, core_ids=list(range(NCORE)))
    print(f"[kernel] launch done in {time.time() - t0:.1f}s", file=sys.stderr, flush=True)
    return res.results


DN_W = 2048
O1 = 3 * DN_W
O2 = O1 + DN_W
O3 = O2 + 16
O4 = O3 + 16
O5 = O4 + 3 * 2048
O6 = O5 + D


def kernel(x, p, ffn1_norm_pre, ffn1_w_gate, ffn1_w_up, ffn1_w_down, ffn1_norm_post,
           mix_norm_pre, w_in, dn_conv_w, dn_A_log, dn_dt_bias, dn_out_norm,
           w_branch_dn, w_branch_sb, w_out, mix_norm_post,
           ffn2_norm_pre, ffn2_w_gate, ffn2_w_up, ffn2_w_down, ffn2_norm_post,
           ple_norm_pre, ple_w_gate, ple_w_proj, ple_norm_post):
    f32 = lambda a: np.ascontiguousarray(np.asarray(a, dtype=np.float32))
    x = f32(x)[0]
    p = f32(p)[0, 0]
    w_in = f32(w_in)[0]
    ident = np.eye(128, dtype=np.float32)
    g1 = np.concatenate([fm_vec(ffn1_norm_pre[0]), fm_vec(ffn1_norm_post[0]), fm_vec(mix_norm_pre[0])], 1)
    wg1, wu1, wd1 = f32(ffn1_w_gate)[0], f32(ffn1_w_up)[0], f32(ffn1_w_down)[0]
    maps = [{"x": np.ascontiguousarray(x[c * TOK:(c + 1) * TOK]), "wg": wg1, "wu": wu1, "wd": wd1,
             "gains": g1, "ident": ident} for c in range(NCORE)]
    r1 = _run(_prog("s1", build_stage1), maps)
    h1T = [np.asarray(r["h1T"]) for r in r1]
    uT = [np.asarray(r["uT"]) for r in r1]
    uT_all = np.ascontiguousarray(np.concatenate(uT, axis=1))
    uT_rev = np.ascontiguousarray(uT_all[:, ::-1])
    del r1, maps
    conv = f32(dn_conv_w)[0]
    consts = dn_consts()
    maps = []
    for c in range(NCORE):
        hs = (2 * c, 2 * c + 1)
        cols = [w_in[:, off + h * 128: off + (h + 1) * 128] for off in (0, DN_W, 2 * DN_W, O1) for h in hs]
        wdn = np.ascontiguousarray(np.concatenate(cols, axis=1))
        wba = np.ascontiguousarray(np.stack([w_in[:, O2 + hs[0]], w_in[:, O2 + hs[1]], w_in[:, O3 + hs[0]], w_in[:, O3 + hs[1]]], axis=1))
        cw = np.stack([conv[:, off + h * 128: off + (h + 1) * 128] for off in (0, DN_W, 2 * DN_W) for h in hs], axis=0)
        cw = np.ascontiguousarray(cw.transpose(2, 0, 1).reshape(128, 24))
        hv = np.array([dn_dt_bias[0][hs[0]], dn_dt_bias[0][hs[1]], dn_A_log[0][hs[0]], dn_A_log[0][hs[1]]], np.float32)
        maps.append({"uT_all": uT_all, "wdn": wdn, "wba": wba, "convw": cw,
                     "gon": f32(dn_out_norm)[0][:, None].copy(), "hv": np.ascontiguousarray(np.tile(hv[None, :], (64, 1))),
                     "consts": consts})
    r2a = _run(_prog("s2a", build_stage2_dn), maps)
    odn_full = np.ascontiguousarray(np.concatenate([np.asarray(r["odnT"]) for r in r2a], axis=0))
    del r2a, maps
    sbc = sb_consts()
    maps = []
    for c in range(NCORE):
        hs = (2 * c, 2 * c + 1)
        cols = [w_in[:, O4 + off + h * 128: O4 + off + (h + 1) * 128] for off in (0, 2048, 4096) for h in hs]
        maps.append({"uT_all": uT_all, "uT_rev": uT_rev, "wsb": np.ascontiguousarray(np.concatenate(cols, axis=1)), "consts": sbc})
    r2b = _run(_prog("s2b", build_stage2_sb), maps)
    osb_full = np.ascontiguousarray(np.concatenate([np.asarray(r["osbT"]) for r in r2b], axis=0))
    del r2b, maps, uT_all, uT_rev
    g3 = np.concatenate([fm_vec(mix_norm_post[0]), fm_vec(ffn2_norm_pre[0]), fm_vec(ffn2_norm_post[0]),
                         fm_vec(ple_norm_pre[0]), fm_vec(ple_norm_post[0])], 1)
    wgate = np.ascontiguousarray(w_in[:, O5:O5 + 2 * D])
    shared = {"wgate": wgate, "wbd": f32(w_branch_dn)[0], "wbs": f32(w_branch_sb)[0], "wout": f32(w_out)[0],
              "wg": f32(ffn2_w_gate)[0], "wu": f32(ffn2_w_up)[0], "wd": f32(ffn2_w_down)[0],
              "wpg": f32(ple_w_gate)[0], "wpp": f32(ple_w_proj)[0], "gains": g3, "ident": ident}
    maps = []
    for c in range(NCORE):
        sl = slice(c * TOK, (c + 1) * TOK)
        m = dict(shared)
        m.update({"h1T": h1T[c], "uT": uT[c], "odnT": np.ascontiguousarray(odn_full[:, sl]),
                  "osbT": np.ascontiguousarray(osb_full[:, sl]), "pT": np.ascontiguousarray(p[sl].T)})
        maps.append(m)
    r3 = _run(_prog("s3", build_stage3), maps)
    out = np.concatenate([np.asarray(r["out"]) for r in r3], axis=0)
    return out.reshape(1, SEQ, D).astype(np.float32)
```

```python
import contextlib
import numpy as np
import concourse.bass as bass
import concourse.mybir as mybir
from concourse.bass_utils import run_bass_kernel_spmd

F32 = mybir.dt.float32
BF16 = mybir.dt.bfloat16
AF = mybir.ActivationFunctionType
ALU = mybir.AluOpType

D = 2048
SEQ = 16384
NCORE = 8
TOK = SEQ // NCORE
T = 512
DFF = 5632
PLE = 256
KC = D // 128
EPS = 1e-6


class Tok:
    __slots__ = ("name", "last_w", "readers", "sem", "dma_cnt", "last_dma")

    def __init__(self, name):
        self.name = name
        self.last_w = None
        self.readers = []
        self.sem = None
        self.dma_cnt = 0
        self.last_dma = None


class Op:
    __slots__ = ("eng", "fn", "deps", "dma", "signal", "sem", "count", "tok")

    def __init__(self, eng, fn, dma):
        self.eng = eng
        self.fn = fn
        self.deps = []
        self.dma = dma
        self.signal = dma
        self.sem = None
        self.count = 0
        self.tok = None


ENGS = ("pe", "act", "dve", "pool", "sp")


class Rec:
    def __init__(self, nc, stack, sem_stack=None, prefix=""):
        self.nc = nc
        self.stack = stack
        self.sem_stack = sem_stack if sem_stack is not None else stack
        self.prefix = prefix
        self.ops = {e: [] for e in ENGS}
        self.toks = []
        self.dma_toks = []
        self.nsb = 0

    def tok(self, name="t"):
        t = Tok(name)
        self.toks.append(t)
        return t

    def sb(self, shape, dt, name=None):
        self.nsb += 1
        return self.stack.enter_context(self.nc.sbuf_tensor("s_" + self.prefix + (name or f"sb{self.nsb}"), list(shape), dt))

    def ps(self, shape, dt, name=None):
        self.nsb += 1
        return self.stack.enter_context(self.nc.psum_tensor("p_" + self.prefix + (name or f"ps{self.nsb}"), list(shape), dt))

    def add(self, eng, fn, reads=(), writes=(), dma_tok=None):
        op = Op(eng, fn, dma_tok is not None)
        deps = []
        for r in reads:
            if r.last_w is not None:
                deps.append(r.last_w)
        for w in writes:
            if w.last_w is not None:
                deps.append(w.last_w)
            deps.extend(w.readers)
        if dma_tok is not None:
            if dma_tok.sem is None:
                dma_tok.sem = self.sem_stack.enter_context(self.nc.semaphore(f"{self.prefix}dq{len(self.dma_toks)}"))
                dma_tok.last_dma = None
                self.dma_toks.append(dma_tok)
            if getattr(dma_tok, "last_dma", None) is not None:
                deps.append(dma_tok.last_dma)
            dma_tok.dma_cnt += 16
            op.sem = dma_tok.sem
            op.count = dma_tok.dma_cnt
            op.tok = dma_tok
            dma_tok.last_dma = op
        seen = set()
        for d in deps:
            if d is op or id(d) in seen:
                continue
            seen.add(id(d))
            if (not d.dma) and d.eng == eng and eng == "pe":
                continue
            op.deps.append(d)
            d.signal = True
        for r in reads:
            r.readers.append(op)
        for w in writes:
            w.last_w = op
            w.readers = []
        self.ops[eng].append(op)
        return op

    def emit(self):
        nc = self.nc
        esem = {}
        for e in ("pe", "act", "dve", "pool"):
            esem[e] = self.sem_stack.enter_context(nc.semaphore(f"{self.prefix}es_{e}"))
        for e in ("pe", "act", "dve", "pool", "sp"):
            c = 0
            for op in self.ops[e]:
                if op.dma:
                    continue
                if op.signal:
                    c += 1
                    op.sem = esem.get(e)
                    op.count = c
                    assert e != "sp"
        final = [(t.sem, t.dma_cnt) for t in self.dma_toks]

        def run(e, eng):
            waited = {}
            for op in self.ops[e]:
                for d in op.deps:
                    key = id(d.sem)
                    if waited.get(key, 0) >= d.count:
                        continue
                    eng.wait_ge(d.sem, d.count)
                    waited[key] = d.count
                ins = op.fn(eng)
                if op.signal:
                    ins.then_inc(op.sem, 16 if op.dma else 1)
            if e == "sp":
                for s, c in final:
                    eng.wait_ge(s, c)

        with nc.Block() as block:
            @block.tensor
            def _(eng):
                run("pe", eng)

            @block.scalar
            def _(eng):
                run("act", eng)

            @block.vector
            def _(eng):
                run("dve", eng)

            @block.gpsimd
            def _(eng):
                run("pool", eng)

            @block.sync
            def _(eng):
                run("sp", eng)


class Ctx:
    def __init__(self, nc, stack, sem_stack=None, prefix=""):
        self.nc = nc
        self.R = Rec(nc, stack, sem_stack, prefix)
        R = self.R
        self.banks = [R.ps([128, 512], F32, name=f"bank{i}") for i in range(8)]
        self.bank_tok = [R.tok(f"bank{i}") for i in range(8)]
        self.bank_rr = 0
        self.ones_bf = R.sb([128, 128], BF16, "ones_bf")
        self.t_const = R.tok("const")
        R.add("pool", lambda e: e.memset(self.ones_bf[:], 1.0), writes=[self.t_const])
        self.evac_rr = 0
        self.reserved = set()

    def bank(self):
        while True:
            i = self.bank_rr % 8
            self.bank_rr += 1
            if id(self.banks[i]) not in self.reserved:
                return self.banks[i], self.bank_tok[i]


def load_vec_fm(cx, dram_vec, name):
    R = cx.R
    t = R.sb([128, dram_vec.shape[1]], F32, name)
    tk = R.tok(name)
    R.add("sp", lambda e: e.dma_start(out=t[:], in_=dram_vec[:, :]), writes=[tk], dma_tok=tk)
    return t, tk


def rms_fm(cx, src, t_src, nch, gain, t_gain, dst, t_dst, sq, t_sq, rstd, t_rstd, dim, post=None):
    R = cx.R
    R.add("act", lambda e: e.activation(out=sq[:, 0:nch, :], in_=src[:, 0:nch, :], func=AF.Square),
          reads=[t_src], writes=[t_sq])
    bk, tb = cx.bank()
    for c in range(nch):
        R.add("pe", lambda e, c=c: e.matmul(bk[:, :], lhsT=cx.ones_bf[:], rhs=sq[:, c, :],
                                             start=(c == 0), stop=(c == nch - 1)),
              reads=[t_sq, cx.t_const], writes=[tb])
    R.add("act", lambda e: e.activation(out=rstd[:], in_=bk[:, :], func=AF.Ln, bias=EPS, scale=1.0 / dim),
          reads=[tb], writes=[t_rstd])
    R.add("act", lambda e: e.activation(out=rstd[:], in_=rstd[:], func=AF.Exp, scale=-0.5), reads=[t_rstd], writes=[t_rstd])
    if post is None:
        for c in range(nch):
            R.add("dve", lambda e, c=c: e.scalar_tensor_tensor(out=dst[:, c, :], in0=src[:, c, :],
                                                                scalar=gain[:, c:c + 1], in1=rstd[:],
                                                                op0=ALU.mult, op1=ALU.mult),
                  reads=[t_src, t_gain, t_rstd], writes=[t_dst])
    else:
        post()


def transpose_in(cx, x_dram, r0, xtok, t_xtok, xT, t_xT, ident, ncol_chunks, alias=()):
    R = cx.R
    W = ncol_chunks * 128
    alias = list(alias)
    for s in range(T // 128):
        R.add("sp", lambda e, s=s: e.dma_start(out=xtok[:, s, 0:W], in_=x_dram[r0 + s * 128:r0 + (s + 1) * 128, :]),
              writes=[t_xtok[s]] + alias, dma_tok=t_xtok[s])
    for c in range(ncol_chunks):
        bk, tb = cx.bank()
        for s in range(T // 128):
            R.add("pe", lambda e, c=c, s=s, bk=bk: e.transpose(out=bk[:, s * 128:(s + 1) * 128],
                                                                in_=xtok[:, s, c * 128:(c + 1) * 128], identity=ident[:]),
                  reads=[t_xtok[s], cx.t_const] + alias, writes=[tb])
        evac(cx, xT[:, c, :], bk[:, :], [tb], [t_xT])


def evac(cx, dst, src, reads, writes):
    R = cx.R
    cx.evac_rr += 1
    if cx.evac_rr % 2 == 0:
        R.add("act", lambda e: e.copy(out=dst, in_=src), reads=reads, writes=writes)
    else:
        R.add("dve", lambda e: e.tensor_copy(out=dst, in_=src), reads=reads, writes=writes)


def transpose_out(cx, srcT, t_src, nch, out_dram, r0, otok, t_otok, ident):
    R = cx.R
    for s in range(T // 128):
        for c4 in range(0, nch, 4):
            bk, tb = cx.bank()
            for j in range(4):
                c = c4 + j
                R.add("pe", lambda e, c=c, s=s, j=j, bk=bk: e.transpose(out=bk[:, j * 128:(j + 1) * 128],
                                                                   in_=srcT[:, c, s * 128:(s + 1) * 128],
                                                                   identity=ident[:]),
                      reads=[t_src, cx.t_const], writes=[tb])
            evac(cx, otok[:, s, c4 * 128:(c4 + 4) * 128], bk[:, :], [tb], [t_otok[s]])
        R.add("sp", lambda e, s=s: e.dma_start(out=out_dram[r0 + s * 128:r0 + (s + 1) * 128, :],
                                               in_=otok[:, s, 0:nch * 128]),
              reads=[t_otok[s]], dma_tok=t_otok[s])


class WStream:
    def __init__(self, cx, kch_max, nbuf, name):
        R = cx.R
        self.cx = cx
        self.bufs = [R.sb([128, kch_max, 512], BF16, f"{name}{i}") for i in range(nbuf)]
        self.toks = [R.tok(f"{name}{i}") for i in range(nbuf)]
        self.st_toks = [R.tok(f"{name}st{i}") for i in range(nbuf)]
        self.sw_toks = [R.tok(f"{name}sw{i}") for i in range(nbuf)]
        self.hw_toks = [R.tok(f"{name}hw{i}") for i in range(nbuf)]
        self.rr = 0
        self.scratch = {}
        self.regions = {}

    def load(self, w_dram, k0, kch, c0, ncols):
        R = self.cx.R
        nc = self.cx.nc
        i = self.rr % len(self.bufs)
        self.rr += 1
        buf, tk = self.bufs[i], self.toks[i]
        wname = w_dram.tensor.name
        if wname not in self.scratch:
            self.scratch[wname] = nc.dram_tensor("bf_" + wname, list(w_dram.shape), BF16).ap()
        scr = self.scratch[wname][k0 * 128:(k0 + kch) * 128, c0:c0 + ncols].rearrange("(k p) n -> p k n", p=128)
        key = (wname, k0, kch, c0, ncols)
        if key not in self.regions:
            t_reg = R.tok("reg")
            self.regions[key] = t_reg
            src = w_dram[k0 * 128:(k0 + kch) * 128, c0:c0 + ncols].rearrange("(k p) n -> p k n", p=128)
            R.add("pool", lambda e: e.dma_start(out=buf[:, 0:kch, 0:ncols], in_=src), writes=[tk], dma_tok=self.sw_toks[i])
            R.add("sp", lambda e: e.dma_start(out=scr, in_=buf[:, 0:kch, 0:ncols]), reads=[tk], writes=[t_reg],
                  dma_tok=self.st_toks[i])
        else:
            t_reg = self.regions[key]
            R.add("sp", lambda e: e.dma_start(out=buf[:, 0:kch, 0:ncols], in_=scr), reads=[t_reg], writes=[tk], dma_tok=self.hw_toks[i])
        return buf, tk


def ffn_fm(cx, uT, t_u, wg, wu, wd, actT, t_act, ws, fT, t_f, tmp, t_tmp):
    R = cx.R
    NJ = DFF // 128
    for j4 in range(0, NJ, 4):
        gw, tg = ws.load(wg, 0, KC, j4 * 128, 512)
        uw, tu = ws.load(wu, 0, KC, j4 * 128, 512)
        for j in range(4):
            bg, tbg = cx.bank()
            bu, tbu = cx.bank()
            for k in range(KC):
                R.add("pe", lambda e, k=k, j=j, bg=bg, gw=gw: e.matmul(bg[:, :], lhsT=gw[:, k, j * 128:(j + 1) * 128],
                                                                        rhs=uT[:, k, :], start=(k == 0), stop=(k == KC - 1)),
                      reads=[tg, t_u], writes=[tbg])
            for k in range(KC):
                R.add("pe", lambda e, k=k, j=j, bu=bu, uw=uw: e.matmul(bu[:, :], lhsT=uw[:, k, j * 128:(j + 1) * 128],
                                                                        rhs=uT[:, k, :], start=(k == 0), stop=(k == KC - 1)),
                      reads=[tu, t_u], writes=[tbu])
            jj = j4 + j
            tt = t_tmp[jj % 2]
            tm = tmp[jj % 2]
            R.add("act", lambda e, bg=bg, tm=tm: e.activation(out=tm[:], in_=bg[:, :], func=AF.Silu),
                  reads=[tbg], writes=[tt])
            R.add("dve", lambda e, bu=bu, tm=tm, jj=jj: e.tensor_tensor(out=actT[:, jj, :], in0=tm[:], in1=bu[:, :],
                                                                         op=ALU.mult),
                  reads=[tt, tbu], writes=[t_act])
    for dg in range(4):
        bks = [cx.bank() for _ in range(4)]
        for q in range(4):
            dwt, tdw = ws.load(wd, q * 11, 11, dg * 512, 512)
            for kk in range(11):
                k = q * 11 + kk
                for dd in range(4):
                    bk, tb = bks[dd]
                    R.add("pe", lambda e, kk=kk, k=k, dd=dd, bk=bk, dwt=dwt: e.matmul(
                        bk[:, :], lhsT=dwt[:, kk, dd * 128:(dd + 1) * 128], rhs=actT[:, k, :],
                        start=(k == 0), stop=(k == NJ - 1)), reads=[tdw, t_act], writes=[tb])
        for dd in range(4):
            bk, tb = bks[dd]
            evac(cx, fT[:, dg * 4 + dd, :], bk[:, :], [tb], [t_f])


def resid_norm_add(cx, fT, t_f, gain, t_gain, hT, t_h, sq, t_sq, rstd, t_rstd, coef):
    R = cx.R

    def post():
        for c in range(KC):
            R.add("dve", lambda e, c=c: e.scalar_tensor_tensor(out=fT[:, c, :], in0=fT[:, c, :],
                                                                scalar=gain[:, c:c + 1], in1=rstd[:],
                                                                op0=ALU.mult, op1=ALU.mult),
                  reads=[t_f, t_gain, t_rstd], writes=[t_f])
            R.add("dve", lambda e, c=c: e.scalar_tensor_tensor(out=hT[:, c, :], in0=fT[:, c, :],
                                                                scalar=float(coef), in1=hT[:, c, :],
                                                                op0=ALU.mult, op1=ALU.add),
                  reads=[t_f, t_h], writes=[t_h])

    rms_fm(cx, fT, t_f, KC, gain, t_gain, None, None, sq, t_sq, rstd, t_rstd, D, post=post)


def build_stage1(ntiles=TOK // T):
    nc = bass.Bass("TRN2", target_bir_lowering=False)
    x = nc.dram_tensor("x", [TOK, D], F32, kind="ExternalInput").ap()
    wg = nc.dram_tensor("wg", [D, DFF], F32, kind="ExternalInput").ap()
    wu = nc.dram_tensor("wu", [D, DFF], F32, kind="ExternalInput").ap()
    wd = nc.dram_tensor("wd", [DFF, D], F32, kind="ExternalInput").ap()
    gains = nc.dram_tensor("gains", [128, 3 * KC], F32, kind="ExternalInput").ap()
    ident_d = nc.dram_tensor("ident", [128, 128], F32, kind="ExternalInput").ap()
    h1T = nc.dram_tensor("h1T", [D, TOK], F32, kind="ExternalOutput").ap()
    uT_o = nc.dram_tensor("uT", [D, TOK], BF16, kind="ExternalOutput").ap()
    with contextlib.ExitStack() as stack:
        cx = Ctx(nc, stack)
        R = cx.R
        g_all, t_g = load_vec_fm(cx, gains, "gains_sb")
        ident = R.sb([128, 128], F32, "ident")
        R.add("sp", lambda e: e.dma_start(out=ident[:], in_=ident_d[:, :]), writes=[cx.t_const], dma_tok=cx.t_const)
        big = R.sb([128, KC * T], F32, "big")
        xtok = big[:].rearrange("p (s d) -> p s d", s=4)
        fT = big[:].rearrange("p (c t) -> p c t", c=KC)
        t_xtok = [R.tok(f"xtok{s}") for s in range(4)]
        t_f = R.tok("fT")
        hT = R.sb([128, KC, T], F32, "hT")
        t_h = R.tok("hT")
        uT = R.sb([128, KC, T], BF16, "uT")
        t_u = R.tok("uT")
        sq = R.sb([128, KC, T], BF16, "sq")
        t_sq = R.tok("sq")
        rstd = R.sb([128, T], F32, "rstd")
        t_rstd = R.tok("rstd")
        actT = R.sb([128, DFF // 128, T], BF16, "actT")
        t_act = R.tok("actT")
        tmp = [R.sb([128, T], F32, f"tmp{i}") for i in range(2)]
        t_tmp = [R.tok(f"tmp{i}") for i in range(2)]
        ws = WStream(cx, KC, 3, "w")
        h1v = h1T.rearrange("(c p) t -> p c t", p=128)
        uv = uT_o.rearrange("(c p) t -> p c t", p=128)
        for it in range(ntiles):
            r0 = it * T
            transpose_in(cx, x, r0, xtok, t_xtok, hT, t_h, ident, KC, alias=[t_f])
            rms_fm(cx, hT, t_h, KC, g_all[:, 0:KC], t_g, uT, t_u, sq, t_sq, rstd, t_rstd, D)
            ffn_fm(cx, uT, t_u, wg, wu, wd, actT, t_act, ws, fT, t_f, tmp, t_tmp)
            resid_norm_add(cx, fT, t_f, g_all[:, KC:2 * KC], t_g, hT, t_h, sq, t_sq, rstd, t_rstd, 0.5)
            R.add("sp", lambda e, r0=r0: e.dma_start(out=h1v[:, :, r0:r0 + T], in_=hT[:]), reads=[t_h], dma_tok=t_h)
            rms_fm(cx, hT, t_h, KC, g_all[:, 2 * KC:3 * KC], t_g, uT, t_u, sq, t_sq, rstd, t_rstd, D)
            R.add("sp", lambda e, r0=r0: e.dma_start(out=uv[:, :, r0:r0 + T], in_=uT[:]), reads=[t_u], dma_tok=t_u)
        R.emit()
    return nc


def fm_vec(v):
    return np.ascontiguousarray(np.asarray(v, np.float32).reshape(-1, 128).T)


HD = 128
CH = 64
NCHK = T // CH
C_ID, C_U, C_SL, C_BU, C_BL = 0, 128, 192, 256, 320


def dn_consts():
    c = np.zeros((128, 384), np.float32)
    c[:, 0:128] = np.eye(128, dtype=np.float32)
    p = np.arange(64)[:, None]
    f = np.arange(64)[None, :]
    c[0:64, C_U:C_U + 64] = (p <= f)
    c[0:64, C_SL:C_SL + 64] = (f < p)
    c[0:64, C_BU:C_BU + 64] = 1e4 * (f > p)
    c[0:64, C_BL:C_BL + 64] = 1e4 * (f < p)
    return c


def bc_mid(ap2, n):
    return ap2.unsqueeze(1).to_broadcast([ap2.shape[0], n, ap2.shape[1]])


def bc_last(ap2, n):
    return ap2.unsqueeze(2).to_broadcast([ap2.shape[0], ap2.shape[1], n])


def build_stage2_dn(ntiles=SEQ // T, shared=None):
    S = ntiles * T
    nc = shared["nc"] if shared else bass.Bass("TRN2", target_bir_lowering=False)
    uT_all = shared["uT_all"] if shared else nc.dram_tensor("uT_all", [D, S], BF16, kind="ExternalInput").ap()
    wdn = nc.dram_tensor("wdn", [D, 1024], F32, kind="ExternalInput").ap()
    wba = nc.dram_tensor("wba", [D, 4], F32, kind="ExternalInput").ap()
    convw_d = nc.dram_tensor("convw", [128, 24], F32, kind="ExternalInput").ap()
    gon_d = nc.dram_tensor("gon", [128, 1], F32, kind="ExternalInput").ap()
    hv_d = nc.dram_tensor("hv", [64, 4], F32, kind="ExternalInput").ap()
    consts_d = nc.dram_tensor("consts_dn", [128, 384], F32, kind="ExternalInput").ap()
    odnT = nc.dram_tensor("odnT", [256, S], BF16, kind="ExternalOutput").ap()
    with contextlib.ExitStack() as stack:
        cx = Ctx(nc, stack, shared["sems"] if shared else None, "a_" if shared else "")
        R = cx.R
        A = R.add
        tc = cx.t_const
        consts = R.sb([128, 384], F32, "consts")
        A("sp", lambda e: e.dma_start(out=consts[:], in_=consts_d[:, :]), writes=[tc], dma_tok=tc)
        ident = consts[:, 0:128]
        id64 = consts[0:64, 0:64]
        Um = consts[0:64, C_U:C_U + 64]
        SLm = consts[0:64, C_SL:C_SL + 64]
        BUm = consts[0:64, C_BU:C_BU + 64]
        BLm = consts[0:64, C_BL:C_BL + 64]
        convw, t_cw = load_vec_fm(cx, convw_d, "convw_sb")
        gon, t_gon = load_vec_fm(cx, gon_d, "gon_sb")
        hv = R.sb([64, 4], F32, "hv")
        t_hv = R.tok("hv")
        A("sp", lambda e: e.dma_start(out=hv[:], in_=hv_d[:, :]), writes=[t_hv], dma_tok=t_hv)
        ident_bf = R.sb([128, 128], BF16, "ident_bf")
        A("dve", lambda e: e.tensor_copy(out=ident_bf[:], in_=consts[:, 0:128]), reads=[tc], writes=[tc])
        id64_bf = ident_bf[0:64, 0:64]
        ones_f = R.sb([64, 128], F32, "ones_f")
        A("pool", lambda e: e.memset(ones_f[:], 1.0), writes=[tc])
        expA = R.sb([64, 2], F32, "expA")
        A("act", lambda e: e.activation(out=expA[:], in_=hv[:, 2:4], func=AF.Exp), reads=[t_hv], writes=[t_hv])
        wres = R.sb([128, KC, 1024], BF16, "wres")
        t_w = R.tok("wres")
        wv = wdn.rearrange("(k p) n -> p k n", p=128)
        for half in range(2):
            A("pool", lambda e, half=half: e.dma_start(out=wres[:, :, half * 512:(half + 1) * 512],
                                                      in_=wv[:, :, half * 512:(half + 1) * 512]),
              writes=[t_w], dma_tok=t_w)
        wbar = R.sb([128, KC, 4], BF16, "wbar")
        t_wb = R.tok("wbar")
        A("pool", lambda e: e.dma_start(out=wbar[:], in_=wba.rearrange("(k p) n -> p k n", p=128)),
          writes=[t_wb], dma_tok=t_wb)
        u_t = [R.sb([128, KC, T], BF16, f"u_t{i}") for i in range(1)]
        t_ut = [R.tok(f"u_t{i}") for i in range(1)]
        rawbuf = R.sb([128, 6, T + 3], F32, "rawbuf")
        t_raw = [R.tok(f"raw{c}") for c in range(6)]
        A("pool", lambda e: e.memset(rawbuf[:, :, 0:3], 0.0), writes=t_raw)
        cv = R.sb([128, 6, T], F32, "cv")
        t_cv = [R.tok(f"cv{c}") for c in range(6)]
        qk = R.sb([128, 4, T], BF16, "qk")
        t_qk = [R.tok(f"qk{c}") for c in range(4)]
        siluz = R.sb([128, 2, T], F32, "siluz")
        t_sz = [R.tok(f"sz{c}") for c in range(2)]
        sqb = R.sb([128, T], BF16, "sqb")
        t_sqb = R.tok("sqb")
        rstd = R.sb([128, T], F32, "rstd")
        t_rstd = R.tok("rstd")
        ba = R.sb([64, NCHK, 4], F32, "ba")
        t_ba = R.tok("ba")
        beta = R.sb([64, NCHK, 2], F32, "beta")
        t_beta = R.tok("beta")
        gt = R.sb([64, NCHK, 2], F32, "gt")
        t_g = R.tok("gt")
        spx = R.sb([64, NCHK, 2], F32, "spx")
        t_spx = R.tok("spx")

        def sbt(shape, name, dt=F32):
            return R.sb(shape, dt, name), R.tok(name)

        def make_head(h):
            gc_col, t_gcc = sbt([64, NCHK], f"gc_col_h{h}")
            egc, t_egc = sbt([64, NCHK], f"egc_h{h}")
            be, t_be = sbt([64, NCHK], f"be_h{h}")
            kgs, t_kgs = sbt([64, NCHK], f"kgs_h{h}")
            Ug, t_Ug = sbt([64, NCHK, 64], f"Ug_h{h}")
            gcrow, t_gcr = sbt([128, NCHK, 64], f"gcrow_h{h}")
            egrow, t_egr = sbt([128, NCHK, 64], f"egrow_h{h}")
            dl, t_dl = sbt([64, NCHK, 64], f"dl_h{h}")
            decay, t_dec = sbt([64, NCHK, 64], f"decay_h{h}")
            decayT, t_decT = sbt([64, NCHK, 64], f"decayT_h{h}")
            Ktok, t_Kt = sbt([64, NCHK, 128], f"Ktok_h{h}", BF16)
            Vtok, t_Vt = sbt([64, NCHK, 128], f"Vtok_h{h}")
            Pb = [sbt([64, NCHK, 64], f"P{i}_h{h}", BF16) for i in range(2)]
            Ptb = [sbt([64, NCHK, 64], f"Pt{i}_h{h}", BF16) for i in range(2)]
            At, t_At = sbt([64, NCHK, 64], f"At_h{h}")
            u_sb, t_usb = sbt([64, NCHK, 128], f"u_sb_h{h}")
            wT, t_wT = sbt([128, NCHK, 64], f"wT_h{h}", BF16)
            attnT, t_att = sbt([64, NCHK, 64], f"attnT_h{h}", BF16)
            qgT, t_qg = sbt([128, NCHK, 64], f"qgT_h{h}", BF16)
            kg, t_kg = sbt([64, NCHK, 128], f"kg_h{h}", BF16)
            Sst = [sbt([128, 128], f"S{i}_h{h}") for i in range(2)]
            Ssb = [sbt([128, 128], f"Sb{i}_h{h}", BF16) for i in range(2)]
            A("pool", lambda e: e.memset(Sst[0][0][:], 0.0), writes=[Sst[0][1]])
            A("pool", lambda e: e.memset(Ssb[0][0][:], 0.0), writes=[Ssb[0][1]])
            s_par = [0]
            vnew = [sbt([64, 128], f"vnew{i}_h{h}", BF16) for i in range(2)]
            vn_rr = [0]
            tmpd, t_tmpd = Ug, t_Ug
            Vb, t_Vb = sbt([64, NCHK, 128], f"Vb_h{h}", BF16)
            Kbg, t_Kbg = sbt([64, NCHK, 128], f"Kbg_h{h}", BF16)
            Atb, t_Atb = sbt([64, NCHK, 64], f"Atb_h{h}", BF16)

            def gen(ob, t_ob):
                    gh = gt[:, :, h]
                    bh = beta[:, :, h]
                    qT = qk[:, h, :].rearrange("p (c f) -> p c f", f=CH)
                    kT = qk[:, 2 + h, :].rearrange("p (c f) -> p c f", f=CH)
                    t_q, t_k, t_v = t_qk[h], t_qk[2 + h], t_cv[4 + h]
                    bk, tb = bank_mm(64, [(0, NCHK, Um, gh)], [tc, t_g])
                    A("dve", lambda e, bk=bk: e.tensor_copy(out=gc_col[:], in_=bk[0:64, 0:NCHK]), reads=[tb], writes=[t_gcc])
                    A("dve", lambda e, gh=gh: e.tensor_tensor(out=Ug[:], in0=bc_mid(Um, NCHK), in1=bc_last(gh, 64), op=ALU.mult),
                      reads=[tc, t_g], writes=[t_Ug])
                    bk, tb = bank_mm(128, [(c * 64, 64, ones_f[:, :], Ug[:, c, :]) for c in range(NCHK)], [tc, t_Ug])
                    A("act", lambda e, bk=bk: e.copy(out=gcrow[:].rearrange("p c f -> p (c f)"), in_=bk[:, :]),
                      reads=[tb], writes=[t_gcr])
                    yield
                    A("dve", lambda e: e.tensor_tensor(out=dl[:], in0=gcrow[0:64, :, :], in1=bc_last(gc_col[:, :], 64),
                                                       op=ALU.subtract), reads=[t_gcr, t_gcc], writes=[t_dl])
                    A("dve", lambda e: e.tensor_tensor(out=tmpd[:], in0=dl[:], in1=bc_mid(BUm, NCHK), op=ALU.add),
                      reads=[t_dl, tc], writes=[t_tmpd])
                    A("act", lambda e: e.activation(out=decay[:], in_=tmpd[:], func=AF.Exp, scale=-1.0),
                      reads=[t_tmpd], writes=[t_dec])
                    A("dve", lambda e: e.tensor_tensor(out=tmpd[:], in0=dl[:], in1=bc_mid(BLm, NCHK), op=ALU.subtract),
                      reads=[t_dl, tc, t_dec], writes=[t_tmpd])
                    A("act", lambda e: e.activation(out=decayT[:], in_=tmpd[:], func=AF.Exp), reads=[t_tmpd], writes=[t_decT])
                    A("act", lambda e: e.activation(out=egrow[:], in_=gcrow[:], func=AF.Exp), reads=[t_gcr], writes=[t_egr])
                    A("act", lambda e: e.activation(out=kgs[:], in_=dl[:, :, 63], func=AF.Exp), reads=[t_dl], writes=[t_kgs])
                    A("act", lambda e: e.activation(out=egc[:], in_=gc_col[:], func=AF.Exp), reads=[t_gcc], writes=[t_egc])
                    yield
                    for (src, t_src, dst, t_dst, isbf) in ((kT, t_k, Ktok, t_Kt, True),
                                                           (cv[:, 4 + h, :].rearrange("p (c f) -> p c f", f=CH), t_v, Vtok, t_Vt, False)):
                        for half in range(2):
                            bk, tb = cx.bank()
                            bkv = bk[:, :].bitcast(BF16) if isbf else bk[:, :]
                            idm = ident_bf[:] if isbf else ident
                            for c4 in range(4):
                                c = half * 4 + c4
                                A("pe", lambda e, bkv=bkv, c4=c4, c=c, src=src, idm=idm: e.transpose(
                                    out=bkv[0:64, c4 * 128:(c4 + 1) * 128], in_=src[:, c, :], identity=idm),
                                  reads=[t_src, tc], writes=[tb])
                            evac(cx, dst[:, half * 4:half * 4 + 4, :].rearrange("p c f -> p (c f)"), bkv[0:64, 0:512], [tb], [t_dst])
                    bk, tb = bank_mm(64, [(c * 64, 64, kT[:, c, :], kT[:, c, :]) for c in range(NCHK)], [t_k])
                    P0, t_P0 = Pb[0]
                    Pt0, t_Pt0 = Ptb[0]
                    A("dve", lambda e, bk=bk: e.tensor_tensor(out=tmpd[:].rearrange("p c f -> p (c f)"), in0=bk[0:64, :],
                                                              in1=decay[:].rearrange("p c f -> p (c f)"), op=ALU.mult),
                      reads=[tb, t_dec], writes=[t_tmpd])
                    A("dve", lambda e, bh=bh: e.tensor_tensor(out=tmpd[:], in0=tmpd[:], in1=bc_last(bh, 64), op=ALU.mult),
                      reads=[t_tmpd, t_beta], writes=[t_tmpd])
                    A("dve", lambda e, P0=P0: e.tensor_tensor(out=P0[:], in0=tmpd[:], in1=bc_mid(SLm, NCHK), op=ALU.mult),
                      reads=[t_tmpd, tc], writes=[t_P0])
                    yield
                    bk, tb = cx.bank()
                    bkv = bk[:, :].bitcast(BF16)
                    for c in range(NCHK):
                        A("pe", lambda e, bkv=bkv, c=c, P0=P0: e.transpose(out=bkv[0:64, c * 64:(c + 1) * 64], in_=P0[:, c, :],
                                                                         identity=id64_bf), reads=[t_P0, tc], writes=[tb])
                    A("act", lambda e, bkv=bkv, Pt0=Pt0: e.copy(out=Pt0[:].rearrange("p c f -> p (c f)"), in_=bkv[0:64, 0:512]),
                      reads=[tb], writes=[t_Pt0])
                    A("dve", lambda e, Pt0=Pt0: e.tensor_tensor(out=At[:], in0=bc_mid(id64, NCHK), in1=Pt0[:], op=ALU.subtract),
                      reads=[t_Pt0, tc], writes=[t_At])
                    A("act", lambda e: e.copy(out=Atb[:], in_=At[:]), reads=[t_At], writes=[t_Atb])
                    yield
                    for lv in range(5):
                        Pc, t_Pc = Pb[lv % 2]
                        Ptc, t_Ptc = Ptb[lv % 2]
                        Pn, t_Pn = Pb[(lv + 1) % 2]
                        Ptn, t_Ptn = Ptb[(lv + 1) % 2]
                        bk, tb = bank_mm(64, [(c * 64, 64, Ptc[:, c, :], Pc[:, c, :]) for c in range(NCHK)], [t_Pc, t_Ptc])
                        A("dve", lambda e, bk=bk, Pn=Pn: e.tensor_copy(out=Pn[:].rearrange("p c f -> p (c f)"), in_=bk[0:64, :]),
                          reads=[tb], writes=[t_Pn])
                        bk, tb = bank_mm(64, [(c * 64, 64, Pc[:, c, :], Ptc[:, c, :]) for c in range(NCHK)], [t_Pc, t_Ptc])
                        A("act", lambda e, bk=bk, Ptn=Ptn: e.copy(out=Ptn[:].rearrange("p c f -> p (c f)"), in_=bk[0:64, :]),
                          reads=[tb], writes=[t_Ptn])
                        bk, tb = bank_mm(64, [(c * 64, 64, Pn[:, c, :], Atb[:, c, :]) for c in range(NCHK)], [t_Pn, t_Atb])
                        A("dve", lambda e, bk=bk: e.tensor_tensor(out=At[:].rearrange("p c f -> p (c f)"),
                                                                  in0=At[:].rearrange("p c f -> p (c f)"), in1=bk[0:64, :],
                                                                  op=ALU.add), reads=[tb, t_At], writes=[t_At])
                        A("act", lambda e: e.copy(out=Atb[:], in_=At[:]), reads=[t_At], writes=[t_Atb])
                        yield
                    A("dve", lambda e: e.tensor_tensor(out=kg[:], in0=Ktok[:], in1=bc_last(kgs[:, :], 128), op=ALU.mult),
                      reads=[t_Kt, t_kgs], writes=[t_kg])
                    A("dve", lambda e, bh=bh: e.tensor_tensor(out=Vb[:], in0=Vtok[:], in1=bc_last(bh, 128), op=ALU.mult),
                      reads=[t_Vt, t_beta], writes=[t_Vb])
                    A("dve", lambda e, bh=bh: e.tensor_tensor(out=be[:], in0=egc[:], in1=bh, op=ALU.mult),
                      reads=[t_egc, t_beta], writes=[t_be])
                    A("dve", lambda e: e.tensor_tensor(out=Kbg[:], in0=Ktok[:], in1=bc_last(be[:, :], 128), op=ALU.mult),
                      reads=[t_Kt, t_be], writes=[t_Kbg])
                    for half in range(2):
                        bk, tb = bank_mm(64, [(c4 * 128, 128, Atb[:, half * 4 + c4, :], Vb[:, half * 4 + c4, :]) for c4 in range(4)],
                                         [t_Atb, t_Vb])
                        evac(cx, u_sb[:, half * 4:half * 4 + 4, :].rearrange("p c f -> p (c f)"), bk[0:64, :], [tb], [t_usb])
                    bk, tb = bank_mm(128, [(c * 64, 64, Kbg[:, c, :], Atb[:, c, :]) for c in range(NCHK)], [t_Atb, t_Kbg])
                    evac(cx, wT[:].rearrange("p c f -> p (c f)"), bk[:, :], [tb], [t_wT])
                    yield
                    bk, tb = bank_mm(64, [(c * 64, 64, kT[:, c, :], qT[:, c, :]) for c in range(NCHK)], [t_k, t_q])
                    A("dve", lambda e, bk=bk: e.tensor_tensor(out=attnT[:].rearrange("p c f -> p (c f)"), in0=bk[0:64, :],
                                                              in1=decayT[:].rearrange("p c f -> p (c f)"), op=ALU.mult),
                      reads=[tb, t_decT], writes=[t_att])
                    A("dve", lambda e, qT=qT: e.tensor_tensor(out=qgT[:], in0=qT, in1=egrow[:], op=ALU.mult),
                      reads=[t_q, t_egr], writes=[t_qg])
                    yield
                    obk, t_obk = cx.bank()
                    cx.reserved.add(id(obk))
                    for c in range(NCHK):
                        Sc, t_Sc = Sst[s_par[0]]
                        Sn, t_Sn = Sst[1 - s_par[0]]
                        Sbc, t_Sbc = Ssb[s_par[0]]
                        Sbn, t_Sbn = Ssb[1 - s_par[0]]
                        s_par[0] = 1 - s_par[0]
                        vn, t_vn = vnew[vn_rr[0] % 2]
                        vn_rr[0] += 1
                        bk, tb = bank_mm(64, [(0, 128, wT[:, c, :], Sbc[:, :])], [t_wT, t_Sbc])
                        A("dve", lambda e, bk=bk, c=c, vn=vn: e.tensor_tensor(out=vn[:], in0=u_sb[:, c, :], in1=bk[0:64, 0:128],
                                                                             op=ALU.subtract), reads=[tb, t_usb], writes=[t_vn])
                        A("pe", lambda e, c=c, Sbc=Sbc, obk=obk: e.matmul(obk[:, c * 64:(c + 1) * 64], lhsT=Sbc[:, :], rhs=qgT[:, c, :],
                                                                         start=True, stop=False), reads=[t_Sbc, t_qg], writes=[t_obk])
                        A("pe", lambda e, c=c, vn=vn, obk=obk: e.matmul(obk[:, c * 64:(c + 1) * 64], lhsT=vn[:, :], rhs=attnT[:, c, :],
                                                                       start=False, stop=True), reads=[t_vn, t_att], writes=[t_obk])
                        bk, tb = bank_mm(128, [(0, 128, kg[:, c, :], vn[:, :])], [t_kg, t_vn])
                        A("dve", lambda e, bk=bk, c=c, Sc=Sc, Sn=Sn: e.scalar_tensor_tensor(
                            out=Sn[:], in0=Sc[:], scalar=egrow[:, c, 63:64], in1=bk[:, 0:128], op0=ALU.mult, op1=ALU.add),
                          reads=[tb, t_Sc, t_egr], writes=[t_Sn])
                        A("act", lambda e, Sn=Sn, Sbn=Sbn: e.copy(out=Sbn[:], in_=Sn[:]), reads=[t_Sn], writes=[t_Sbn])
                        yield
                    A("act", lambda e, obk=obk: e.copy(out=o_sb[:], in_=obk[:, :]), reads=[t_obk], writes=[t_osb])
                    cx.reserved.discard(id(obk))
                    A("act", lambda e: e.activation(out=sqb[:], in_=o_sb[:], func=AF.Square), reads=[t_osb], writes=[t_sqb])
                    bk, tb = cx.bank()
                    A("pe", lambda e, bk=bk: e.matmul(bk[:, :], lhsT=cx.ones_bf[:], rhs=sqb[:], start=True, stop=True),
                      reads=[t_sqb, tc], writes=[tb])
                    A("act", lambda e, bk=bk: e.activation(out=rstd[:], in_=bk[:, :], func=AF.Ln, bias=EPS, scale=1.0 / HD),
                      reads=[tb], writes=[t_rstd])
                    A("act", lambda e: e.activation(out=rstd[:], in_=rstd[:], func=AF.Exp, scale=-0.5), reads=[t_rstd], writes=[t_rstd])
                    A("dve", lambda e: e.scalar_tensor_tensor(out=o_tmp[:], in0=o_sb[:], scalar=gon[:, 0:1], in1=rstd[:],
                                                              op0=ALU.mult, op1=ALU.mult),
                      reads=[t_osb, t_gon, t_rstd], writes=[t_otmp])
                    A("dve", lambda e, h=h, ob=ob: e.tensor_tensor(out=ob[:, h, :], in0=o_tmp[:], in1=siluz[:, h, :], op=ALU.mult),
                      reads=[t_otmp, t_sz[h]], writes=[t_ob])

            return gen

        o_sb, t_osb = sbt([128, T], "o_sb")
        o_tmp, t_otmp = sbt([128, T], "o_tmp")
        outb = [sbt([128, 2, T], f"outb{i}", BF16) for i in range(2)]
        ov = odnT.rearrange("(h p) t -> p h t", p=128)
        head_fns = [make_head(0), make_head(1)]

        def bank_mm(nparts, items, reads, name=None):
            bk, tb = cx.bank()
            for (c0, ncol, lh, rh) in items:
                A("pe", lambda e, bk=bk, c0=c0, ncol=ncol, lh=lh, rh=rh: e.matmul(
                    bk[0:nparts, c0:c0 + ncol], lhsT=lh, rhs=rh, start=True, stop=True), reads=reads, writes=[tb])
            return bk, tb

        for it in range(ntiles):
            t0 = it * T
            ut, t_u = u_t[0], t_ut[0]
            A("sp", lambda e, ut=ut, t0=t0: e.dma_start(out=ut[:], in_=uT_all[:, t0:t0 + T].rearrange("(k p) t -> p k t", p=128)),
              writes=[t_u], dma_tok=t_u)
            for c in range(8):
                bk, tb = cx.bank()
                for k in range(KC):
                    A("pe", lambda e, bk=bk, c=c, k=k, ut=ut: e.matmul(bk[:, :], lhsT=wres[:, k, c * 128:(c + 1) * 128],
                                                                      rhs=ut[:, k, :], start=(k == 0), stop=(k == KC - 1)),
                      reads=[t_w, t_u], writes=[tb])
                if c < 6:
                    evac(cx, rawbuf[:, c, 3:T + 3], bk[:, :], [tb], [t_raw[c]])
                else:
                    A("act", lambda e, bk=bk, c=c: e.activation(out=siluz[:, c - 6, :], in_=bk[:, :], func=AF.Silu),
                      reads=[tb], writes=[t_sz[c - 6]])
            for c in range(6):
                A("dve", lambda e, c=c: e.tensor_scalar(out=cv[:, c, :], in0=rawbuf[:, c, 0:T],
                                                        scalar1=convw[:, c * 4:c * 4 + 1], scalar2=None, op0=ALU.mult),
                  reads=[t_raw[c], t_cw], writes=[t_cv[c]])
                for j in range(1, 4):
                    A("dve", lambda e, c=c, j=j: e.scalar_tensor_tensor(out=cv[:, c, :], in0=rawbuf[:, c, j:j + T],
                                                                        scalar=convw[:, c * 4 + j:c * 4 + j + 1],
                                                                        in1=cv[:, c, :], op0=ALU.mult, op1=ALU.add),
                      reads=[t_raw[c], t_cw, t_cv[c]], writes=[t_cv[c]])
                A("act", lambda e, c=c: e.copy(out=rawbuf[:, c, 0:3], in_=rawbuf[:, c, T:T + 3]),
                  reads=[t_raw[c]], writes=[t_raw[c]])
                A("act", lambda e, c=c: e.activation(out=cv[:, c, :], in_=cv[:, c, :], func=AF.Silu),
                  reads=[t_cv[c]], writes=[t_cv[c]])
            for idx in range(4):
                A("act", lambda e, idx=idx: e.activation(out=sqb[:], in_=cv[:, idx, :], func=AF.Square),
                  reads=[t_cv[idx]], writes=[t_sqb])
                bk, tb = cx.bank()
                A("pe", lambda e, bk=bk: e.matmul(bk[:, :], lhsT=cx.ones_bf[:], rhs=sqb[:], start=True, stop=True),
                  reads=[t_sqb, tc], writes=[tb])
                A("act", lambda e, bk=bk: e.activation(out=rstd[:], in_=bk[:, :], func=AF.Ln, bias=1e-6, scale=1.0),
                  reads=[tb], writes=[t_rstd])
                A("act", lambda e: e.activation(out=rstd[:], in_=rstd[:], func=AF.Exp, scale=-0.5), reads=[t_rstd], writes=[t_rstd])
                sc = float(HD ** -0.5) if idx < 2 else 1.0
                A("dve", lambda e, idx=idx, sc=sc: e.scalar_tensor_tensor(out=qk[:, idx, :], in0=cv[:, idx, :], scalar=sc,
                                                                          in1=rstd[:], op0=ALU.mult, op1=ALU.mult),
                  reads=[t_cv[idx], t_rstd], writes=[t_qk[idx]])
            bk, tb = cx.bank()
            for ck in range(NCHK):
                for k in range(KC):
                    A("pe", lambda e, bk=bk, ck=ck, k=k, ut=ut: e.matmul(bk[0:64, ck * 4:ck * 4 + 4],
                                                                        lhsT=ut[:, k, ck * CH:(ck + 1) * CH],
                                                                        rhs=wbar[:, k, :], start=(k == 0), stop=(k == KC - 1)),
                      reads=[t_wb, t_u], writes=[tb])
            A("dve", lambda e, bk=bk: e.tensor_copy(out=ba[:].rearrange("p c f -> p (c f)"), in_=bk[0:64, 0:NCHK * 4]),
              reads=[tb], writes=[t_ba])
            A("act", lambda e: e.activation(out=beta[:], in_=ba[:, :, 0:2], func=AF.Sigmoid), reads=[t_ba], writes=[t_beta])
            A("dve", lambda e: e.tensor_tensor(out=spx[:], in0=ba[:, :, 2:4], in1=bc_mid(hv[:, 0:2], NCHK), op=ALU.add),
              reads=[t_ba, t_hv], writes=[t_spx])
            A("act", lambda e: e.activation(out=spx[:], in_=spx[:], func=AF.Exp), reads=[t_spx], writes=[t_spx])
            A("dve", lambda e: e.tensor_scalar(out=spx[:], in0=spx[:], scalar1=1.0, scalar2=None, op0=ALU.add),
              reads=[t_spx], writes=[t_spx])
            A("act", lambda e: e.activation(out=spx[:], in_=spx[:], func=AF.Ln), reads=[t_spx], writes=[t_spx])
            A("dve", lambda e: e.scalar_tensor_tensor(out=gt[:], in0=spx[:], scalar=-1.0, in1=bc_mid(expA[:, 0:2], NCHK),
                                                      op0=ALU.mult, op1=ALU.mult), reads=[t_spx, t_hv], writes=[t_g])
            ob, t_ob = outb[it % 2]
            gens = [head_fns[h](ob, t_ob) for h in range(2)]
            while gens:
                for g in list(gens):
                    try:
                        next(g)
                    except StopIteration:
                        gens.remove(g)
            A("sp", lambda e, ob=ob, t0=t0: e.dma_start(out=ov[:, :, t0:t0 + T], in_=ob[:]), reads=[t_ob], dma_tok=t_ob)
        R.emit()
    return nc


SB_ONE = 128
SB_NM0 = 129
SB_VM0 = 129 + 4 * 512
SB_TOT = 129 + 8 * 512


def sb_consts():
    c = np.zeros((128, SB_TOT), np.float32)
    c[:, 0:128] = np.eye(128, dtype=np.float32)
    t = np.arange(128)[:, None]
    j = np.arange(128)[None, :]
    tri = (t + j < 128).astype(np.float32)
    for q4 in range(4):
        for b in range(4):
            blk = c[:, SB_NM0 + q4 * 512 + b * 128: SB_NM0 + q4 * 512 + (b + 1) * 128]
            if 3 - b > q4:
                blk[:] = 1.0
            elif 3 - b == q4:
                blk[:] = tri
    c[:, SB_VM0:SB_VM0 + 4 * 512] = 1.0 - c[:, SB_NM0:SB_NM0 + 4 * 512]
    c[:, SB_ONE] = 1.0
    return c


def build_stage2_sb(ntiles=SEQ // T, heads=(0, 1), shared=None):
    S = ntiles * T
    NB = S // 128
    nc = shared["nc"] if shared else bass.Bass("TRN2", target_bir_lowering=False)
    uT_all = shared["uT_all"] if shared else nc.dram_tensor("uT_all", [D, S], BF16, kind="ExternalInput").ap()
    uT_rev = nc.dram_tensor("uT_rev", [D, S], BF16, kind="ExternalInput").ap()
    wsb = nc.dram_tensor("wsb", [D, 768], F32, kind="ExternalInput").ap()
    consts_d = nc.dram_tensor("consts_sb", [128, SB_TOT], F32, kind="ExternalInput").ap()
    osbT = nc.dram_tensor("osbT", [256, S], BF16, kind="ExternalOutput").ap()
    with contextlib.ExitStack() as stack:
        cx = Ctx(nc, stack, shared["sems"] if shared else None, "b_" if shared else "")
        R = cx.R
        A = R.add
        tc = cx.t_const
        consts = R.sb([128, 129], F32, "consts")
        A("sp", lambda e: e.dma_start(out=consts[:], in_=consts_d[:, 0:129]), writes=[tc], dma_tok=tc)
        ones_c = consts[:, 128:129]
        masks = R.sb([128, 8 * 512], BF16, "masks")
        t_mk = R.tok("masks")
        A("pool", lambda e: e.dma_start(out=masks[:], in_=consts_d[:, SB_NM0:SB_TOT]), writes=[t_mk], dma_tok=t_mk)
        mneg = R.sb([128, 4 * 512], BF16, "mneg")
        t_mneg = R.tok("mneg")
        A("pool", lambda e: e.tensor_scalar(out=mneg[:], in0=masks[:, 0:4 * 512], scalar1=-1e4, scalar2=None, op0=ALU.mult),
          reads=[t_mk], writes=[t_mneg])
        ident_bf = R.sb([128, 128], BF16, "ident_bf")
        A("dve", lambda e: e.tensor_copy(out=ident_bf[:], in_=consts[:, 0:128]), reads=[tc], writes=[tc])
        ones_f = ones_c.to_broadcast([128, T])
        wres = R.sb([128, KC, 768], BF16, "wres")
        t_w = R.tok("wres")
        wv = wsb.rearrange("(k p) n -> p k n", p=128)
        for half in range(2):
            A("pool", lambda e, half=half: e.dma_start(out=wres[:, :, half * 384:(half + 1) * 384],
                                                      in_=wv[:, :, half * 384:(half + 1) * 384]),
              writes=[t_w], dma_tok=t_w)
        uf, tuf = R.sb([128, KC, T], BF16, "u_f"), R.tok("u_f")
        ur, tur = R.sb([128, KC, T], BF16, "u_r"), R.tok("u_r")
        QT = R.sb([128, S], BF16, "QT")
        KTr = R.sb([128, S], BF16, "KTr")
        dVr = R.sb([128, NB, 128], BF16, "dVr")
        t_Q = [R.tok(f"Q{i}") for i in range(ntiles)]
        t_K = [R.tok(f"K{i}") for i in range(ntiles)]
        t_V = [R.tok(f"V{i}") for i in range(ntiles)]
        VTb = [R.sb([128, T + 1], F32, f"VTb{i}") for i in range(2)]
        t_VT = [R.tok(f"VTb{i}") for i in range(2)]
        dVT = R.sb([128, T], BF16, "dVT")
        t_dVT = R.tok("dVT")
        smb = [[R.sb([128, T], F32, f"smb{i}_{k}") for k in range(2)] for i in range(4)]
        t_sm = [[R.tok(f"smb{i}_{k}") for k in range(2)] for i in range(4)]
        Pb = [[R.sb([128, T], BF16, f"Pb{i}_{k}") for k in range(2)] for i in range(4)]
        t_P = [[R.tok(f"Pb{i}_{k}") for k in range(2)] for i in range(4)]
        Pm = [R.sb([128, T], BF16, f"Pm{i}") for i in range(4)]
        t_Pm = [R.tok(f"Pm{i}") for i in range(4)]
        vsh, t_vsh = R.sb([128, T], F32, "vsh"), R.tok("vsh")
        ATs = R.sb([128, 4, T], BF16, "ATs")
        t_AT = [R.tok(f"ATs_{hh}") for hh in range(2)]
        osb, t_os = R.sb([128, T], BF16, "osb"), R.tok("osb")
        scale = float(HD ** -0.5)
        uva = uT_all.rearrange("(k p) t -> p k t", p=128)
        uvr = uT_rev.rearrange("(k p) t -> p k t", p=128)
        gstep = 0
        for h in heads:
            for n, it in enumerate(reversed(range(ntiles))):
                t0 = it * T
                A("sp", lambda e, t0=t0: e.dma_start(out=uf[:], in_=uva[:, :, t0:t0 + T]), writes=[tuf], dma_tok=tuf)
                A("sp", lambda e, t0=t0: e.dma_start(out=ur[:], in_=uvr[:, :, t0:t0 + T]), writes=[tur], dma_tok=tur)
                for (src, tsrc, col, dst, tdst) in ((uf, tuf, h * 128, QT, t_Q[it]), (ur, tur, 256 + h * 128, KTr, t_K[it])):
                    bk, tb = cx.bank()
                    for k in range(KC):
                        A("pe", lambda e, bk=bk, k=k, src=src, col=col: e.matmul(bk[:, :], lhsT=wres[:, k, col:col + 128],
                                                                                rhs=src[:, k, :], start=(k == 0), stop=(k == KC - 1)),
                          reads=[t_w, tsrc], writes=[tb])
                    evac(cx, dst[:, t0:t0 + T], bk[:, :], [tb], [tdst])
                vt, tvt = VTb[n % 2], t_VT[n % 2]
                vtp, tvtp = VTb[(n + 1) % 2], t_VT[(n + 1) % 2]
                bk, tb = cx.bank()
                for k in range(KC):
                    A("pe", lambda e, bk=bk, k=k, h=h: e.matmul(bk[:, :], lhsT=wres[:, k, 512 + h * 128:512 + (h + 1) * 128],
                                                               rhs=ur[:, k, :], start=(k == 0), stop=(k == KC - 1)),
                      reads=[t_w, tur], writes=[tb])
                A("act", lambda e, bk=bk, vt=vt: e.copy(out=vt[:, 0:T], in_=bk[:, :]), reads=[tb], writes=[tvt])
                if n == 0:
                    A("pool", lambda e, vt=vt: e.memset(vt[:, T:T + 1], 0.0), reads=[tvt], writes=[tvt])
                else:
                    A("act", lambda e, vt=vt, vtp=vtp: e.copy(out=vt[:, T:T + 1], in_=vtp[:, 0:1]), reads=[tvtp, tvt], writes=[tvt])
                A("pool", lambda e, vt=vt: e.tensor_tensor(out=dVT[:], in0=vt[:, 1:T + 1], in1=vt[:, 0:T], op=ALU.subtract),
                  reads=[tvt], writes=[t_dVT])
                bk2, tb2 = cx.bank()
                bkb = bk2[:, :].bitcast(BF16)
                for b in range(4):
                    A("pe", lambda e, bkb=bkb, b=b: e.transpose(out=bkb[:, b * 128:(b + 1) * 128], in_=dVT[:, b * 128:(b + 1) * 128],
                                                               identity=ident_bf[:]), reads=[t_dVT, tc], writes=[tb2])
                evac(cx, dVr[:, it * 4:it * 4 + 4, :].rearrange("p b d -> p (b d)"), bkb[:, 0:T], [tb2], [t_V[it]])
            steps = [(qg, i) for qg in range(NB // 4) for i in range(qg + 1)]

            def issue_scores(qg, i, par):
                j0 = (NB - 4 - 4 * qg + 4 * i) * 128
                kt = j0 // T
                zb = []
                for q4 in range(4):
                    qb = qg * 4 + q4
                    bk, tb = cx.bank()
                    zb.append((bk, tb))
                    A("pe", lambda e, bk=bk, qb=qb, j0=j0, i=i: e.matmul(bk[:, :], lhsT=QT[:, qb * 128:(qb + 1) * 128],
                                                                        rhs=KTr[:, j0:j0 + T], start=True, stop=(i != 0)),
                      reads=[t_Q[qg], t_K[kt]], writes=[tb])
                    if i == 0:
                        A("pe", lambda e, bk=bk, q4=q4: e.matmul(bk[:, :], lhsT=ident_bf[:], rhs=mneg[:, q4 * 512:(q4 + 1) * 512],
                                                                start=False, stop=True), reads=[t_mneg, tc], writes=[tb])
                for q4 in range(4):
                    bk, tb = zb[q4]
                    sm, tsm = smb[q4][par], t_sm[q4][par]
                    A("act", lambda e, bk=bk, sm=sm: e.activation(out=sm[:], in_=bk[:, :], func=AF.Sigmoid, scale=-scale),
                      reads=[tb], writes=[tsm])

            issue_scores(steps[0][0], steps[0][1], gstep % 2)
            obk = t_obk = None
            for n, (qg, i) in enumerate(steps):
                par = gstep % 2
                gstep += 1
                if n + 1 < len(steps):
                    issue_scores(steps[n + 1][0], steps[n + 1][1], gstep % 2)
                ntile = qg + 1
                JB = NB - 4 - 4 * qg + 4 * i
                kt = JB // 4
                if i == 0:
                    obk, t_obk = cx.bank()
                    cx.reserved.add(id(obk))
                    tq0 = qg * T
                    if tq0 == 0:
                        A("pool", lambda e: e.memset(uf[:, :, 0:1], 0.0), writes=[tuf])
                        A("sp", lambda e: e.dma_start(out=uf[:, :, 1:T], in_=uva[:, :, 0:T - 1]), writes=[tuf], dma_tok=tuf)
                    else:
                        A("sp", lambda e, tq0=tq0: e.dma_start(out=uf[:], in_=uva[:, :, tq0 - 1:tq0 + T - 1]), writes=[tuf], dma_tok=tuf)
                for q4 in range(4):
                    sm, tsm = smb[q4][par], t_sm[q4][par]
                    Pc, tPc = Pb[q4][par], t_P[q4][par]
                    Pp, tPp = Pb[q4][1 - par], t_P[q4][1 - par]
                    if i == 0:
                        A("dve", lambda e, sm=sm, Pc=Pc: e.tensor_tensor_scan(out=Pc[:], data0=sm[:], data1=ones_f, initial=ones_c,
                                                                             op0=ALU.mult, op1=ALU.mult),
                          reads=[tsm, tc], writes=[tPc])
                    else:
                        A("dve", lambda e, sm=sm, Pc=Pc, Pp=Pp: e.tensor_tensor_scan(out=Pc[:], data0=sm[:], data1=ones_f,
                                                                                    initial=Pp[:, T - 1:T], op0=ALU.mult, op1=ALU.mult),
                          reads=[tsm, tPp, tc], writes=[tPc])
                if i == 0:
                    for q4 in range(4):
                        vm = masks[:, (4 + q4) * 512:(5 + q4) * 512]
                        A("pool", lambda e, q4=q4, vm=vm, par=par: e.tensor_tensor(out=Pm[q4][:], in0=Pb[q4][par][:], in1=vm, op=ALU.mult),
                          reads=[t_P[q4][par], t_mk], writes=[t_Pm[q4]])
                for hh in range(2):
                    bk2, tb2 = cx.bank()
                    bkb = bk2[:, :].bitcast(BF16)
                    for bb in range(2):
                        b = hh * 2 + bb
                        for q4 in range(4):
                            Pc, tPc = (Pm[q4], t_Pm[q4]) if i == 0 else (Pb[q4][par], t_P[q4][par])
                            A("pe", lambda e, bkb=bkb, b=b, bb=bb, q4=q4, Pc=Pc: e.transpose(
                                out=bkb[:, bb * 512 + q4 * 128:bb * 512 + (q4 + 1) * 128], in_=Pc[:, b * 128:(b + 1) * 128],
                                identity=ident_bf[:]), reads=[tPc, tc], writes=[tb2])
                    A("act", lambda e, bkb=bkb, hh=hh: e.copy(out=ATs[:, hh * 2:hh * 2 + 2, :].rearrange("p b q -> p (b q)"),
                                                             in_=bkb[:, :]), reads=[tb2], writes=[t_AT[hh]])
                for b in range(4):
                    first = (i == 0 and b == 0)
                    last = (i == ntile - 1 and b == 3)
                    A("pe", lambda e, obk=obk, b=b, JB=JB, first=first, last=last: e.matmul(
                        obk[:, :], lhsT=dVr[:, JB + b, :], rhs=ATs[:, b, :], start=first, stop=last),
                      reads=[t_AT[b // 2], t_V[kt]], writes=[t_obk])
                if i == ntile - 1:
                    bkv, tbv = cx.bank()
                    for k in range(KC):
                        A("pe", lambda e, bkv=bkv, k=k, h=h: e.matmul(bkv[:, :], lhsT=wres[:, k, 512 + h * 128:512 + (h + 1) * 128],
                                                                     rhs=uf[:, k, :], start=(k == 0), stop=(k == KC - 1)),
                          reads=[t_w, tuf], writes=[tbv])
                    A("act", lambda e, bkv=bkv: e.copy(out=vsh[:], in_=bkv[:, :]), reads=[tbv], writes=[t_vsh])
                    A("dve", lambda e, obk=obk: e.tensor_tensor(out=osb[:], in0=obk[:, :], in1=vsh[:], op=ALU.add),
                      reads=[t_obk, t_vsh], writes=[t_os])
                    cx.reserved.discard(id(obk))
                    A("sp", lambda e, qg=qg, h=h: e.dma_start(out=osbT[h * 128:(h + 1) * 128, qg * T:(qg + 1) * T], in_=osb[:]),
                      reads=[t_os], dma_tok=t_os)
        R.emit()
    return nc


def build_stage2(ntiles=SEQ // T):
    S = ntiles * T
    nc = bass.Bass("TRN2", target_bir_lowering=False)
    uT_all = nc.dram_tensor("uT_all", [D, S], BF16, kind="ExternalInput").ap()
    with contextlib.ExitStack() as sems:
        shared = {"nc": nc, "uT_all": uT_all, "sems": sems}
        build_stage2_dn(ntiles, shared=shared)
        nc.all_engine_barrier()
        build_stage2_sb(ntiles, shared=shared)
    return nc


def linear_fm(cx, ws, w_dram, kch, xT, t_x, col0, nout, consume):
    R = cx.R
    for c4 in range(0, nout, 4):
        n = min(4, nout - c4)
        wt, tw = ws.load(w_dram, 0, kch, col0 + c4 * 128, n * 128)
        for j in range(n):
            bk, tb = cx.bank()
            for k in range(kch):
                R.add("pe", lambda e, bk=bk, k=k, j=j, wt=wt: e.matmul(bk[:, :], lhsT=wt[:, k, j * 128:(j + 1) * 128],
                                                                      rhs=xT[:, k, :], start=(k == 0), stop=(k == kch - 1)),
                      reads=[tw, t_x], writes=[tb])
            consume(c4 + j, bk, tb)


def build_stage3(ntiles=TOK // T):
    nc = bass.Bass("TRN2", target_bir_lowering=False)
    NTK = ntiles * T
    h1T = nc.dram_tensor("h1T", [D, NTK], F32, kind="ExternalInput").ap()
    uT_d = nc.dram_tensor("uT", [D, NTK], BF16, kind="ExternalInput").ap()
    odn_d = nc.dram_tensor("odnT", [D, NTK], BF16, kind="ExternalInput").ap()
    osb_d = nc.dram_tensor("osbT", [D, NTK], BF16, kind="ExternalInput").ap()
    pT_d = nc.dram_tensor("pT", [PLE, NTK], F32, kind="ExternalInput").ap()
    wgate = nc.dram_tensor("wgate", [D, 2 * D], F32, kind="ExternalInput").ap()
    wbd = nc.dram_tensor("wbd", [D, D], F32, kind="ExternalInput").ap()
    wbs = nc.dram_tensor("wbs", [D, D], F32, kind="ExternalInput").ap()
    wout = nc.dram_tensor("wout", [D, D], F32, kind="ExternalInput").ap()
    wg = nc.dram_tensor("wg", [D, DFF], F32, kind="ExternalInput").ap()
    wu = nc.dram_tensor("wu", [D, DFF], F32, kind="ExternalInput").ap()
    wd = nc.dram_tensor("wd", [DFF, D], F32, kind="ExternalInput").ap()
    wpg = nc.dram_tensor("wpg", [D, D], F32, kind="ExternalInput").ap()
    wpp = nc.dram_tensor("wpp", [PLE, D], F32, kind="ExternalInput").ap()
    gains = nc.dram_tensor("gains", [128, 5 * KC], F32, kind="ExternalInput").ap()
    ident_d = nc.dram_tensor("ident", [128, 128], F32, kind="ExternalInput").ap()
    out = nc.dram_tensor("out", [NTK, D], F32, kind="ExternalOutput").ap()
    with contextlib.ExitStack() as stack:
        cx = Ctx(nc, stack)
        R = cx.R
        A = R.add
        g_all, t_g = load_vec_fm(cx, gains, "gains_sb")
        ident = R.sb([128, 128], F32, "ident")
        A("sp", lambda e: e.dma_start(out=ident[:], in_=ident_d[:, :]), writes=[cx.t_const], dma_tok=cx.t_const)
        big = R.sb([128, KC * T], F32, "big")
        otok = big[:].rearrange("p (s d) -> p s d", s=4)
        fT = big[:].rearrange("p (c t) -> p c t", c=KC)
        t_f = R.tok("fT")
        t_otok = [t_f] * 4
        hT = R.sb([128, KC, T], F32, "hT")
        t_h = R.tok("hT")
        uT = R.sb([128, KC, T], BF16, "uT")
        t_u = R.tok("uT")
        sq = R.sb([128, KC, T], BF16, "sq")
        t_sq = R.tok("sq")
        rstd = R.sb([128, T], F32, "rstd")
        t_rstd = R.tok("rstd")
        actT = R.sb([128, DFF // 128, T], BF16, "actT")
        t_act = R.tok("actT")
        odn = actT[:, 0:KC, :]
        osb = actT[:, KC:2 * KC, :]
        tmp = [R.sb([128, T], F32, f"tmp{i}") for i in range(2)]
        t_tmp = [R.tok(f"tmp{i}") for i in range(2)]
        pTb = R.sb([128, 2, T], BF16, "pTb")
        t_p = R.tok("pTb")
        ws = WStream(cx, KC, 3, "w")
        hv = h1T.rearrange("(c p) t -> p c t", p=128)
        uv = uT_d.rearrange("(c p) t -> p c t", p=128)
        dnv = odn_d.rearrange("(c p) t -> p c t", p=128)
        sbv = osb_d.rearrange("(c p) t -> p c t", p=128)
        pv = pT_d.rearrange("(c p) t -> p c t", p=128)
        for it in range(ntiles):
            r0 = it * T
            A("sp", lambda e, r0=r0: e.dma_start(out=hT[:], in_=hv[:, :, r0:r0 + T]), writes=[t_h], dma_tok=t_h)
            A("sp", lambda e, r0=r0: e.dma_start(out=uT[:], in_=uv[:, :, r0:r0 + T]), writes=[t_u], dma_tok=t_u)
            t_dn = R.tok("odn_ld")
            A("sp", lambda e, r0=r0: e.dma_start(out=odn, in_=dnv[:, :, r0:r0 + T]), writes=[t_act], dma_tok=t_dn)
            A("sp", lambda e, r0=r0: e.dma_start(out=osb, in_=sbv[:, :, r0:r0 + T]), writes=[t_act], dma_tok=t_dn)
            A("pool", lambda e, r0=r0: e.dma_start(out=pTb[:], in_=pv[:, :, r0:r0 + T]), writes=[t_p], dma_tok=t_p)
            for (gcol, wb, xs, tx, second) in ((0, wbd, odn, t_act, False), (D, wbs, osb, t_act, True)):
                for c4 in range(0, KC, 4):
                    gtile, tgw = ws.load(wgate, 0, KC, gcol + c4 * 128, 512)
                    btile, tbw = ws.load(wb, 0, KC, c4 * 128, 512)
                    for j in range(4):
                        c = c4 + j
                        bg, tbg = cx.bank()
                        for k in range(KC):
                            A("pe", lambda e, bg=bg, k=k, j=j, gtile=gtile: e.matmul(
                                bg[:, :], lhsT=gtile[:, k, j * 128:(j + 1) * 128], rhs=uT[:, k, :],
                                start=(k == 0), stop=(k == KC - 1)), reads=[tgw, t_u], writes=[tbg])
                        bb, tbb = cx.bank()
                        for k in range(KC):
                            A("pe", lambda e, bb=bb, k=k, j=j, btile=btile, xs=xs: e.matmul(
                                bb[:, :], lhsT=btile[:, k, j * 128:(j + 1) * 128], rhs=xs[:, k, :],
                                start=(k == 0), stop=(k == KC - 1)), reads=[tbw, tx], writes=[tbb])
                        tt, tm = t_tmp[c % 2], tmp[c % 2]
                        A("act", lambda e, bg=bg, tm=tm: e.activation(out=tm[:], in_=bg[:, :], func=AF.Sigmoid),
                          reads=[tbg], writes=[tt])
                        if not second:
                            A("dve", lambda e, bb=bb, tm=tm, c=c: e.tensor_tensor(out=fT[:, c, :], in0=tm[:], in1=bb[:, :],
                                                                                 op=ALU.mult), reads=[tt, tbb], writes=[t_f])
                        else:
                            A("dve", lambda e, bb=bb, tm=tm: e.tensor_tensor(out=tm[:], in0=tm[:], in1=bb[:, :], op=ALU.mult),
                              reads=[tt, tbb], writes=[tt])
                            A("dve", lambda e, tm=tm, c=c: e.tensor_tensor(out=sq[:, c, :], in0=tm[:], in1=fT[:, c, :],
                                                                          op=ALU.add), reads=[tt, t_f], writes=[t_sq])
            linear_fm(cx, ws, wout, KC, sq, t_sq, 0, KC,
                      lambda c, bk, tb: evac(cx, fT[:, c, :], bk[:, :], [tb], [t_f]))
            resid_norm_add(cx, fT, t_f, g_all[:, 0:KC], t_g, hT, t_h, sq, t_sq, rstd, t_rstd, 1.0)
            rms_fm(cx, hT, t_h, KC, g_all[:, KC:2 * KC], t_g, uT, t_u, sq, t_sq, rstd, t_rstd, D)
            ffn_fm(cx, uT, t_u, wg, wu, wd, actT, t_act, ws, fT, t_f, tmp, t_tmp)
            resid_norm_add(cx, fT, t_f, g_all[:, 2 * KC:3 * KC], t_g, hT, t_h, sq, t_sq, rstd, t_rstd, 0.5)
            rms_fm(cx, hT, t_h, KC, g_all[:, 3 * KC:4 * KC], t_g, uT, t_u, sq, t_sq, rstd, t_rstd, D)
            for c4 in range(0, KC, 4):
                gtile, tgw = ws.load(wpg, 0, KC, c4 * 128, 512)
                ptile, tpw = ws.load(wpp, 0, 2, c4 * 128, 512)
                for j in range(4):
                    c = c4 + j
                    bg, tbg = cx.bank()
                    for k in range(KC):
                        A("pe", lambda e, bg=bg, k=k, j=j, gtile=gtile: e.matmul(
                            bg[:, :], lhsT=gtile[:, k, j * 128:(j + 1) * 128], rhs=uT[:, k, :],
                            start=(k == 0), stop=(k == KC - 1)), reads=[tgw, t_u], writes=[tbg])
                    bp, tbp = cx.bank()
                    for k in range(2):
                        A("pe", lambda e, bp=bp, k=k, j=j, ptile=ptile: e.matmul(
                            bp[:, :], lhsT=ptile[:, k, j * 128:(j + 1) * 128], rhs=pTb[:, k, :],
                            start=(k == 0), stop=(k == 1)), reads=[tpw, t_p], writes=[tbp])
                    tt, tm = t_tmp[c % 2], tmp[c % 2]
                    A("act", lambda e, bg=bg, tm=tm: e.activation(out=tm[:], in_=bg[:, :], func=AF.Sigmoid),
                      reads=[tbg], writes=[tt])
                    A("dve", lambda e, bp=bp, tm=tm, c=c: e.tensor_tensor(out=fT[:, c, :], in0=tm[:], in1=bp[:, :], op=ALU.mult),
                      reads=[tt, tbp], writes=[t_f])
            resid_norm_add(cx, fT, t_f, g_all[:, 4 * KC:5 * KC], t_g, hT, t_h, sq, t_sq, rstd, t_rstd, 1.0)
            transpose_out(cx, hT, t_h, KC, out, r0, otok, t_otok, ident)
        R.emit()
    return nc


_PROGS = {}


def _prog(name, fn):
    if name not in _PROGS:
        _PROGS[name] = fn()
    return _PROGS[name]


def _run(nc, maps):
    import sys, time
    t0 = time.time()
    res = run_bass_kernel_spmd(nc, maps, core_ids=list(range(NCORE)))
    print(f"[kernel] launch done in {time.time() - t0:.1f}s", file=sys.stderr, flush=True)
    return res.results


DN_W = 2048
O1 = 3 * DN_W
O2 = O1 + DN_W
O3 = O2 + 16
O4 = O3 + 16
O5 = O4 + 3 * 2048
O6 = O5 + D


def kernel(x, p, ffn1_norm_pre, ffn1_w_gate, ffn1_w_up, ffn1_w_down, ffn1_norm_post,
           mix_norm_pre, w_in, dn_conv_w, dn_A_log, dn_dt_bias, dn_out_norm,
           w_branch_dn, w_branch_sb, w_out, mix_norm_post,
           ffn2_norm_pre, ffn2_w_gate, ffn2_w_up, ffn2_w_down, ffn2_norm_post,
           ple_norm_pre, ple_w_gate, ple_w_proj, ple_norm_post):
    f32 = lambda a: np.ascontiguousarray(np.asarray(a, dtype=np.float32))
    x = f32(x)[0]
    p = f32(p)[0, 0]
    w_in = f32(w_in)[0]
    ident = np.eye(128, dtype=np.float32)
    g1 = np.concatenate([fm_vec(ffn1_norm_pre[0]), fm_vec(ffn1_norm_post[0]), fm_vec(mix_norm_pre[0])], 1)
    wg1, wu1, wd1 = f32(ffn1_w_gate)[0], f32(ffn1_w_up)[0], f32(ffn1_w_down)[0]
    maps = [{"x": np.ascontiguousarray(x[c * TOK:(c + 1) * TOK]), "wg": wg1, "wu": wu1, "wd": wd1,
             "gains": g1, "ident": ident} for c in range(NCORE)]
    r1 = _run(_prog("s1", build_stage1), maps)
    h1T = [np.asarray(r["h1T"]) for r in r1]
    uT = [np.asarray(r["uT"]) for r in r1]
    uT_all = np.ascontiguousarray(np.concatenate(uT, axis=1))
    uT_rev = np.ascontiguousarray(uT_all[:, ::-1])
    del r1, maps
    conv = f32(dn_conv_w)[0]
    consts = dn_consts()
    sbc = sb_consts()
    maps = []
    for c in range(NCORE):
        hs = (2 * c, 2 * c + 1)
        cols = [w_in[:, off + h * 128: off + (h + 1) * 128] for off in (0, DN_W, 2 * DN_W, O1) for h in hs]
        wdn = np.ascontiguousarray(np.concatenate(cols, axis=1))
        wba = np.ascontiguousarray(np.stack([w_in[:, O2 + hs[0]], w_in[:, O2 + hs[1]], w_in[:, O3 + hs[0]], w_in[:, O3 + hs[1]]], axis=1))
        cw = np.stack([conv[:, off + h * 128: off + (h + 1) * 128] for off in (0, DN_W, 2 * DN_W) for h in hs], axis=0)
        cw = np.ascontiguousarray(cw.transpose(2, 0, 1).reshape(128, 24))
        hv = np.array([dn_dt_bias[0][hs[0]], dn_dt_bias[0][hs[1]], dn_A_log[0][hs[0]], dn_A_log[0][hs[1]]], np.float32)
        scols = [w_in[:, O4 + off + h * 128: O4 + off + (h + 1) * 128] for off in (0, 2048, 4096) for h in hs]
        maps.append({"uT_all": uT_all, "uT_rev": uT_rev, "wdn": wdn, "wba": wba, "convw": cw,
                     "gon": f32(dn_out_norm)[0][:, None].copy(), "hv": np.ascontiguousarray(np.tile(hv[None, :], (64, 1))),
                     "consts_dn": consts, "wsb": np.ascontiguousarray(np.concatenate(scols, axis=1)), "consts_sb": sbc})
    r2 = _run(_prog("s2", build_stage2), maps)
    odn_full = np.ascontiguousarray(np.concatenate([np.asarray(r["odnT"]) for r in r2], axis=0))
    osb_full = np.ascontiguousarray(np.concatenate([np.asarray(r["osbT"]) for r in r2], axis=0))
    del r2, maps, uT_all, uT_rev
    g3 = np.concatenate([fm_vec(mix_norm_post[0]), fm_vec(ffn2_norm_pre[0]), fm_vec(ffn2_norm_post[0]),
                         fm_vec(ple_norm_pre[0]), fm_vec(ple_norm_post[0])], 1)
    wgate = np.ascontiguousarray(w_in[:, O5:O5 + 2 * D])
    shared = {"wgate": wgate, "wbd": f32(w_branch_dn)[0], "wbs": f32(w_branch_sb)[0], "wout": f32(w_out)[0],
              "wg": f32(ffn2_w_gate)[0], "wu": f32(ffn2_w_up)[0], "wd": f32(ffn2_w_down)[0],
              "wpg": f32(ple_w_gate)[0], "wpp": f32(ple_w_proj)[0], "gains": g3, "ident": ident}
    maps = []
    for c in range(NCORE):
        sl = slice(c * TOK, (c + 1) * TOK)
        m = dict(shared)
        m.update({"h1T": h1T[c], "uT": uT[c], "odnT": np.ascontiguousarray(odn_full[:, sl]),
                  "osbT": np.ascontiguousarray(osb_full[:, sl]), "pT": np.ascontiguousarray(p[sl].T)})
        maps.append(m)
    r3 = _run(_prog("s3", build_stage3), maps)
    out = np.concatenate([np.asarray(r["out"]) for r in r3], axis=0)
    return out.reshape(1, SEQ, D).astype(np.float32)
```
